# Optimizing a Trainium2 kernel written in Bass

```python
import jax, jax.numpy as jnp
from jax import lax
import numpy as np

D_MODEL = 2048
BATCH = 4
SEQ = 2048
DEPTH = 1

N_META = 16
CHUNK = 128
MIX_DIM = D_MODEL
N_RET_HEADS = 8
RET_HEAD_DIM = 128
RET_DIM = N_RET_HEADS * RET_HEAD_DIM
N_FOX_HEADS = 8
FOX_HEAD_DIM = 128
FOX_DIM = N_FOX_HEADS * FOX_HEAD_DIM
IN_DIM = 4 * RET_DIM + 3 * FOX_DIM + N_FOX_HEADS
D_FF = 5632
CONV_WIDTH = 3
ROPE_BASE = 10000.0
NORM_EPS = 1e-6

kernel_name = "hymba_retention_fox_convffn_block"


def _rmsnorm(x, gain):
    x32 = x.astype(jnp.float32)
    y = x32 * lax.rsqrt(jnp.mean(x32 * x32, axis=-1, keepdims=True) + NORM_EPS)
    return (y * gain.astype(jnp.float32)).astype(x.dtype)


def _heads(t, n_heads):
    b, l, _ = t.shape
    return t.reshape(b, l, n_heads, -1).transpose(0, 2, 1, 3)


def _rotary(t, pos):
    d = t.shape[-1]
    inv_freq = 1.0 / (ROPE_BASE ** (jnp.arange(0, d, 2, dtype=jnp.float32) / d))
    ang = pos[:, None] * inv_freq[None, :]
    cos, sin = jnp.cos(ang), jnp.sin(ang)
    t1, t2 = t[..., : d // 2], t[..., d // 2:]
    return jnp.concatenate([t1 * cos - t2 * sin, t1 * sin + t2 * cos], axis=-1)


def _decay_matrix(log_g, n):
    idx = jnp.arange(n, dtype=jnp.float32)
    diff = idx[:, None] - idx[None, :]
    return jnp.where(diff >= 0, jnp.exp(jnp.maximum(diff, 0.0)[None] * log_g[:, None, None]), 0.0)


def _ret_intra(q, k, v, dmat):
    s = jnp.einsum('bhid,bhjd->bhij', q, k) * dmat[None]
    return jnp.einsum('bhij,bhjv->bhiv', s, v)


def _retention(q, k, v, log_g):
    b, h, l, dv = v.shape
    m = N_META
    out_m = _ret_intra(q[:, :, :m], k[:, :, :m], v[:, :, :m], _decay_matrix(log_g, m))
    zeta_m = jnp.exp((m - 1 - jnp.arange(m, dtype=jnp.float32))[None, :] * log_g[:, None])
    state0 = jnp.einsum('bhjd,bhjv,hj->bhdv', k[:, :, :m], v[:, :, :m], zeta_m)
    n_chunks = (l - m) // CHUNK

    def to_chunks(t):
        return t[:, :, m:].reshape(b, h, n_chunks, CHUNK, t.shape[-1]).transpose(2, 0, 1, 3, 4)

    pos_c = jnp.arange(CHUNK, dtype=jnp.float32)
    d_c = _decay_matrix(log_g, CHUNK)
    xi = jnp.exp((pos_c + 1.0)[None, :] * log_g[:, None])
    zeta = jnp.exp((CHUNK - 1.0 - pos_c)[None, :] * log_g[:, None])
    g_chunk = jnp.exp(CHUNK * log_g)[None, :, None, None]

    def step(state, qkv):
        qc, kc, vc = qkv
        o = _ret_intra(qc, kc, vc, d_c) + jnp.einsum('bhid,bhdv,hi->bhiv', qc, state, xi)
        state = g_chunk * state + jnp.einsum('bhjd,bhjv,hj->bhdv', kc, vc, zeta)
        return state, o

    _, o = lax.scan(step, state0, (to_chunks(q), to_chunks(k), to_chunks(v)))
    o = o.transpose(1, 2, 0, 3, 4).reshape(b, h, n_chunks * CHUNK, dv)
    return jnp.concatenate([out_m, o], axis=2)


def _forgetting_attention(q, k, v, log_f):
    l = q.shape[2]
    scale = q.shape[-1] ** -0.5
    cum = jnp.cumsum(log_f, axis=-1)
    bounds = [0, N_META] + [N_META + CHUNK * (i + 1) for i in range((l - N_META) // CHUNK)]
    outs = []
    for s, e in zip(bounds[:-1], bounds[1:]):
        logits = jnp.einsum('bhqd,bhkd->bhqk', q[:, :, s:e], k[:, :, :e]).astype(jnp.float32) * scale
        logits = logits + cum[:, :, s:e, None] - cum[:, :, None, :e]
        causal = jnp.arange(s, e)[:, None] >= jnp.arange(e)[None, :]
        p = jax.nn.softmax(jnp.where(causal[None, None], logits, -jnp.inf), axis=-1)
        outs.append(jnp.einsum('bhqk,bhkd->bhqd', p.astype(v.dtype), v[:, :, :e]))
    return jnp.concatenate(outs, axis=2)


def _head_groupnorm(o, gain):
    mu = jnp.mean(o, axis=-1, keepdims=True)
    var = jnp.mean(jnp.square(o - mu), axis=-1, keepdims=True)
    y = (o - mu) * lax.rsqrt(var + NORM_EPS)
    b, h, l, d = o.shape
    return y.transpose(0, 2, 1, 3).reshape(b, l, h * d) * gain.astype(jnp.float32)


def _causal_dwconv(u, w, bias):
    kw = w.shape[0]
    l = u.shape[1]
    up = jnp.pad(u, ((0, 0), (kw - 1, 0), (0, 0)))
    y = bias
    for i in range(kw):
        y = y + w[i] * up[:, i:i + l]
    return y


def setup_inputs(seed: int = 0) -> dict:
    key = jax.random.key(seed)
    ks = jax.random.split(key, 16)
    f32 = jnp.float32
    x = jax.random.normal(ks[0], (BATCH, SEQ, D_MODEL), f32)
    meta_tokens = jax.random.normal(ks[1], (N_META, D_MODEL), f32)
    norm1_gain = 1.0 + 0.01 * jax.random.normal(ks[2], (DEPTH, D_MODEL), f32)
    w_in = jax.random.normal(ks[3], (DEPTH, D_MODEL, IN_DIM), f32) * D_MODEL ** -0.5
    b_forget = (jnp.linspace(1.0, 5.0, N_FOX_HEADS, dtype=f32)[None, :]
                + 0.1 * jax.random.normal(ks[4], (DEPTH, N_FOX_HEADS), f32))
    ret_norm_gain = 1.0 + 0.01 * jax.random.normal(ks[5], (DEPTH, RET_DIM), f32)
    w_out = jax.random.normal(ks[6], (DEPTH, MIX_DIM, D_MODEL), f32) * MIX_DIM ** -0.5
    norm2_gain = 1.0 + 0.01 * jax.random.normal(ks[7], (DEPTH, D_MODEL), f32)
    w_up = jax.random.normal(ks[8], (DEPTH, D_MODEL, 2 * D_FF), f32) * D_MODEL ** -0.5
    conv_w = jax.random.normal(ks[9], (DEPTH, CONV_WIDTH, 2 * D_FF), f32) * CONV_WIDTH ** -0.5
    conv_b = 0.01 * jax.random.normal(ks[10], (DEPTH, 2 * D_FF), f32)
    w_down = jax.random.normal(ks[11], (DEPTH, D_FF, D_MODEL), f32) * D_FF ** -0.5
    final_norm_gain = 1.0 + 0.01 * jax.random.normal(ks[12], (D_MODEL,), f32)
    return {"x": x, "meta_tokens": meta_tokens, "norm1_gain": norm1_gain, "w_in": w_in,
            "b_forget": b_forget, "ret_norm_gain": ret_norm_gain, "w_out": w_out,
            "norm2_gain": norm2_gain, "w_up": w_up, "conv_w": conv_w, "conv_b": conv_b,
            "w_down": w_down, "final_norm_gain": final_norm_gain}


def reference(x, meta_tokens, norm1_gain, w_in, b_forget, ret_norm_gain, w_out,
              norm2_gain, w_up, conv_w, conv_b, w_down, final_norm_gain):
    b = x.shape[0]
    f32 = jnp.float32
    h = jnp.concatenate([jnp.broadcast_to(meta_tokens[None].astype(x.dtype), (b, N_META, D_MODEL)), x], axis=1)
    l = h.shape[1]
    pos = jnp.arange(l, dtype=f32)
    log_g = jnp.log1p(-jnp.exp2(-5.0 - jnp.arange(N_RET_HEADS, dtype=f32)))
    split_at = np.cumsum([RET_DIM] * 4 + [FOX_DIM] * 3)[:].tolist()

    for layer in range(DEPTH):
        a = _rmsnorm(h, norm1_gain[layer])
        proj = a @ w_in[layer]
        r_q, r_k, r_v, r_g, f_q, f_k, f_v, f_f = jnp.split(proj, split_at, axis=-1)

        rq = _rotary(_heads(r_q, N_RET_HEADS).astype(f32), pos)
        rk = _rotary(_heads(r_k, N_RET_HEADS).astype(f32), pos) * RET_HEAD_DIM ** -0.5
        rv = _heads(r_v, N_RET_HEADS).astype(f32)
        ret = _head_groupnorm(_retention(rq, rk, rv, log_g), ret_norm_gain[layer])
        ret = (jax.nn.silu(r_g.astype(f32)) * ret).astype(x.dtype)

        log_f = jax.nn.log_sigmoid(f_f.astype(f32) + b_forget[layer].astype(f32)).transpose(0, 2, 1)
        fox = _forgetting_attention(_heads(f_q, N_FOX_HEADS), _heads(f_k, N_FOX_HEADS),
                                    _heads(f_v, N_FOX_HEADS), log_f)
        fox = fox.transpose(0, 2, 1, 3).reshape(b, l, FOX_DIM).astype(x.dtype)

        h = h + jnp.concatenate([ret, fox], axis=-1) @ w_out[layer]

        c = _rmsnorm(h, norm2_gain[layer])
        u = _causal_dwconv(c @ w_up[layer], conv_w[layer], conv_b[layer])
        gate, val = jnp.split(u, 2, axis=-1)
        h = h + (jax.nn.silu(gate) * val) @ w_down[layer]

    out = _rmsnorm(h, final_norm_gain)
    return out[:, N_META:]
```

```python
import contextlib
import numpy as np
import concourse.bass as bass
import concourse.mybir as mybir
from concourse.bass_utils import run_bass_kernel_spmd

F32 = mybir.dt.float32
BF16 = mybir.dt.bfloat16
AF = mybir.ActivationFunctionType
ALU = mybir.AluOpType
AX = mybir.AxisListType

D = 2048
NB = 17
LT = NB * 128
OWN0 = 1150
NOWN = 1026
NH = 8
DFF = 5632
NFF = 44
IN_DIM = 7176
SCALE = 128 ** -0.5
EPS = 1e-6
NEG = -30000.0
XOFF = 300
MASKW = 768
NSLOT = 8
GRP = 4
DEBUG = False
STOP = None
NHEADS_RUN = 8
NCORES_RUN = 8
STOP2 = None
SKIP = set()

ENGS = ("sp", "act", "pool", "dve", "pe")
SEM_LIMIT = 12000
WAIT_ALL_STREAMS = ("const", "constp")


class _Op:
    __slots__ = ("eng", "fn", "deps", "signal", "stream", "sem_i", "val", "inc", "idx", "batch")


class Prog:
    def __init__(self, nc):
        self.nc = nc
        self.ops = []
        self.eng_ops = {e: [] for e in ENGS}
        self.last_w = {}
        self.readers = {}
        self.last_in_stream = {}
        self.barrier_deps = set()

    def op(self, eng, fn, reads=(), writes=(), dma=None, batch=None):
        o = _Op()
        o.batch = batch
        o.eng = eng
        o.fn = fn
        o.idx = len(self.ops)
        o.stream = ("dma", dma) if dma is not None else ("eng", eng)
        o.inc = 16 if dma is not None else 1
        o.signal = dma is not None
        deps = set(self.barrier_deps)
        writes = list(writes) + [k for k in reads if isinstance(k, tuple) and k[0] == "ps"]
        reads = [k for k in reads if not (isinstance(k, tuple) and k[0] == "ps")]
        for k in reads:
            w = self.last_w.get(k)
            if w is not None:
                deps.add(w)
        for k in writes:
            w = self.last_w.get(k)
            if w is not None:
                deps.add(w)
            for r in self.readers.get(k, ()):
                deps.add(r)
        o.deps = deps
        for k in reads:
            self.readers.setdefault(k, []).append(o.idx)
        for k in writes:
            self.last_w[k] = o.idx
            self.readers[k] = []
        self.ops.append(o)
        self.eng_ops[eng].append(o)
        self.last_in_stream[o.stream] = o.idx
        return o

    def barrier(self):
        self.barrier_deps = set(self.last_in_stream.values())

    def emit(self, final_wait_streams=()):
        nc = self.nc
        ops = self.ops
        for o in ops:
            for d in o.deps:
                p = ops[d]
                if p.stream == ("eng", "pe") and o.eng == "pe":
                    continue
                p.signal = True
        streams = {}
        for o in ops:
            if not o.signal:
                continue
            st = streams.setdefault(o.stream, {"n": 0, "cur": 0})
            if st["cur"] + o.inc > SEM_LIMIT:
                st["n"] += 1
                st["cur"] = 0
            st["cur"] += o.inc
            o.sem_i = (o.stream, st["n"])
            o.val = st["cur"]
        batch_max = {}
        for o in ops:
            if o.signal and o.batch is not None:
                k = (o.sem_i, o.batch)
                batch_max[k] = max(batch_max.get(k, 0), o.val)
        sem_keys = []
        seen = set()
        for o in ops:
            if o.signal and o.sem_i not in seen:
                seen.add(o.sem_i)
                sem_keys.append(o.sem_i)
        with contextlib.ExitStack() as es:
            sems = {}
            for i, k in enumerate(sem_keys):
                sems[k] = es.enter_context(nc.semaphore("s%d" % i))
            last_val = {}
            for o in ops:
                if o.signal:
                    last_val[o.sem_i] = max(last_val.get(o.sem_i, 0), o.val)
            block = es.enter_context(nc.Block())
            handles = {"sp": block.sync, "act": block.scalar, "pool": block.gpsimd,
                       "dve": block.vector, "pe": block.tensor}

            def make(engname):
                def body(eng):
                    waited = {}
                    for o in self.eng_ops[engname]:
                        need = {}
                        for d in o.deps:
                            p = ops[d]
                            if not p.signal:
                                continue
                            if p.stream == ("eng", "pe") and engname == "pe":
                                continue
                            v_ = last_val[p.sem_i] if (p.stream[0] == "dma" and p.stream[1] in WAIT_ALL_STREAMS) else p.val
                            if p.batch is not None:
                                v_ = batch_max[(p.sem_i, p.batch)]
                            if v_ > need.get(p.sem_i, 0):
                                need[p.sem_i] = v_
                        for k, v in need.items():
                            if waited.get(k, 0) < v:
                                eng.wait_ge(sems[k], v)
                                waited[k] = v
                        ins = o.fn(eng)
                        if o.signal:
                            ins.then_inc(sems[o.sem_i], o.inc)
                    if engname == "sp":
                        for k in sem_keys:
                            if k[0][0] == "dma" and k[0][1].startswith(final_wait_streams):
                                eng.wait_ge(sems[k], last_val[k])
                return body

            for e in ENGS:
                handles[e](make(e))
        return len(ops)


def build_nc():
    nc = bass.Bass("TRN2", target_bir_lowering=False)

    def din(name, shape):
        return nc.dram_tensor(name, list(shape), F32, kind="ExternalInput").ap()

    xl = din("xl", [LT, D])
    w_in = din("w_in", [D, IN_DIM])
    w_out = din("w_out", [D, D])
    w_up = din("w_up", [D, 2 * DFF])
    w_down = din("w_down", [DFF, D])
    g1 = din("g1", [D]); g2 = din("g2", [D]); gf = din("gf", [D])
    rng = din("rng", [1024])
    bfg = din("bfg", [NB * 8])
    convw = din("convw", [128, 2 * NFF * 3])
    convb = din("convb", [128, 2 * NFF])
    cosT = din("cosT", [128, NB * 64]); sinT = din("sinT", [128, NB * 64])
    dkT = din("dkT", [128, NB * 8]); dqT = din("dqT", [128, NB * 8]); kbT = din("kbT", [128, NB * 8])
    wffd = din("wffd", [128, 16 * 8])
    c_ident = din("c_ident", [128, 128]); c_tri = din("c_tri", [128, 128]); c_ones = din("c_ones", [128, 128])
    c_maskR = din("c_maskR", [128, 128]); c_maskT = din("c_maskT", [128, MASKW]); c_sel = din("c_sel", [128, 1024])
    y = nc.dram_tensor("y", [1024, D], F32, kind="ExternalOutput").ap()
    dbg = {}
    if DEBUG:
        dbg["d_aT"] = nc.dram_tensor("d_aT", [128, 16 * LT], BF16, kind="ExternalOutput").ap()
        dbg["d_mix"] = nc.dram_tensor("d_mix", [128, 16 * NOWN], BF16, kind="ExternalOutput").ap()
        dbg["d_h1"] = nc.dram_tensor("d_h1", [128, 8 * D], F32, kind="ExternalOutput").ap()
        dbg["d_cum"] = nc.dram_tensor("d_cum", [128, NB * 8], F32, kind="ExternalOutput").ap()
        dbg["d_cT"] = nc.dram_tensor("d_cT", [128, 16 * NOWN], BF16, kind="ExternalOutput").ap()
        dbg["d_hh"] = nc.dram_tensor("d_hh", [2, D], F32, kind="ExternalOutput").ap()

    w_in_v = w_in.rearrange("(c p) n -> p c n", p=128)
    w_out_v = w_out.rearrange("(c p) n -> p c n", p=128)
    w_up_v = w_up.rearrange("(c p) n -> p c n", p=128)

    with contextlib.ExitStack() as es:
        def sb(name, shape, dt):
            return es.enter_context(nc.sbuf_tensor(name, list(shape), dt))

        def ps(name, shape, dt):
            return es.enter_context(nc.psum_tensor(name, list(shape), dt))

        R1 = sb("R1", [128, 16 * LT], BF16)
        aT = R1[:, :].rearrange("p (c t) -> p c t", c=16)
        R1f = R1.bitcast(F32)
        h2 = R1f[:, 0:8 * D].rearrange("p (b f) -> p b f", b=8)
        mixT = sb("mixT", [128, 16, NOWN], BF16)
        ringT = sb("ringT", [128, NSLOT * 2048], BF16)
        ring = [ringT[:, i * 2048:(i + 1) * 2048] for i in range(NSLOT)]
        R3N = 20480
        R3 = sb("R3", [128, R3N], BF16)
        R3f = R3.bitcast(F32)
        gb = sb("gb", [128, D], F32)
        ident_b = sb("ident_b", [128, 128], BF16)
        ones_b = sb("ones_b", [128, 128], BF16)
        maskT = sb("maskT", [128, MASKW], BF16)
        sel = sb("sel", [128, 1024], BF16)
        ident_f = sb("ident_f", [128, 128], F32)
        tri_f = sb("tri_f", [128, 128], F32)
        ones_f = sb("ones_f", [128, 128], F32)
        maskR = sb("maskR", [128, 128], F32)
        convw_s = sb("convw_s", [128, 2 * NFF * 3], F32)
        convb_s = sb("convb_s", [128, 2 * NFF], F32)
        bfg_s = sb("bfg_s", [128, NB * 8], F32)
        kb_s = sb("kb_s", [128, NB, 8], F32)
        cos_s = sb("cos_s", [128, NB, 64], F32)
        sin_s = sb("sin_s", [128, NB, 64], F32)
        dk_s = sb("dk_s", [128, NB, 8], F32)
        dq_s = sb("dq_s", [128, NB, 8], F32)
        spt = sb("spt", [128, NB * 8], F32)
        cumn = sb("cumn", [128, NB * 8], F32)
        tot = sb("tot", [128, NB * 8], F32)
        pre = sb("pre", [128, NB * 8], F32)
        biasK = sb("biasK", [128, NB * 8], F32)
        Rb = sb("Rb", [128, 9 * 128], BF16)
        wff = sb("wff", [128, 16, 8], BF16)
        st1 = sb("st1", [128, 32], F32)
        st2 = sb("st2", [128, 64], F32)

        pp = [ps("pp%d" % i, [128, 1024], F32) for i in range(4)]
        ppb = [p.bitcast(BF16) for p in pp]

        def bank(i):
            return pp[i // 2][:, (i % 2) * 512:(i % 2) * 512 + 512]

        P = Prog(nc)
        slot_ctr = [0]

        def next_slot():
            s = slot_ctr[0] % NSLOT
            slot_ctr[0] += 1
            return s

        def dma_sp(out, in_, reads=(), writes=(), stream="const"):
            P.op("sp", lambda e: e.dma_start(out=out, in_=in_), reads=reads, writes=writes, dma=stream)

        def dma_cast(out, in_, reads=(), writes=(), stream="constp", batch=None):
            P.op("pool", lambda e: e.dma_start(out=out, in_=in_), reads=reads, writes=writes, dma=stream, batch=batch)

        def load_slice(wv, c0, ncols=128):
            s = next_slot()
            v = ring[s][:, 0:16 * ncols].rearrange("p (c n) -> p c n", c=16)
            for hh in range(2):
                dma_cast(v[:, 8 * hh:8 * hh + 8, :], wv[:, 8 * hh:8 * hh + 8, c0:c0 + ncols], writes=[("ring", s, hh)], stream="ring%d" % s,
                         batch=slot_ctr[0])
            return s, v

        def rkeys(s):
            return [("ring", s, 0), ("ring", s, 1)]

        dma_sp(ident_f[:], c_ident, writes=["ident_f"])
        dma_sp(tri_f[:], c_tri, writes=["tri_f"])
        dma_sp(ones_f[:], c_ones, writes=["ones_f"])
        dma_sp(maskR[:], c_maskR, writes=["maskR"])
        dma_cast(ident_b[:], c_ident, writes=["ident_b"])
        dma_cast(ones_b[:], c_ones, writes=["ones_b"])
        dma_cast(maskT[:], c_maskT, writes=["maskT"])
        dma_cast(sel[:], c_sel, writes=["sel"])
        dma_sp(convw_s[:], convw, writes=["convw"])
        dma_sp(convb_s[:], convb, writes=["convb"])
        dma_sp(bfg_s[:], bfg.partition_broadcast(128), writes=["bfg"])
        dma_sp(kb_s[:, :, :].rearrange("p b h -> p (b h)"), kbT, writes=["kb"])
        dma_sp(cos_s[:, :, :].rearrange("p b d -> p (b d)"), cosT, writes=["cos"])
        dma_sp(sin_s[:, :, :].rearrange("p b d -> p (b d)"), sinT, writes=["sin"])
        dma_sp(dk_s[:, :, :].rearrange("p b h -> p (b h)"), dkT, writes=["dk"])
        dma_sp(dq_s[:, :, :].rearrange("p b h -> p (b h)"), dqT, writes=["dq"])
        dma_sp(gb[:], g1.partition_broadcast(128), writes=["gb"], stream="gb")
        dma_cast(wff[:, :, :].rearrange("p c n -> p (c n)"), wffd, writes=[("wff", 0), ("wff", 1)])
        P.op("dve", lambda e: e.memset(pre[:], 0.0), writes=["pre"])

        def rms_rstd(src_ap, junk_ap, col, rkeys_, jkey, extra_writes=()):
            npart = src_ap.shape[0]
            c = st1[0:npart, col:col + 1]
            P.op("act", lambda e: e.activation(out=junk_ap, in_=src_ap, func=AF.Square, accum_out=c),
                 reads=rkeys_, writes=[jkey, ("st1", col)] + list(extra_writes))
            P.op("dve", lambda e: e.tensor_scalar(out=c, in0=c, scalar1=1.0 / D, scalar2=EPS, op0=ALU.mult, op1=ALU.add),
                 reads=[("st1", col)], writes=[("st1", col)])
            P.op("act", lambda e: e.sqrt(out=c, in_=c), reads=[("st1", col)], writes=[("st1", col)])
            P.op("dve", lambda e: e.reciprocal(out=c, in_=c), reads=[("st1", col)], writes=[("st1", col)])
            return c

        xsA = [R3f[:, 0:2048], R3f[:, 2048:4096], R3f[:, 4096:6144]]
        xnA = [R3[:, 12288:14336], R3[:, 14336:16384]]
        junkA = R3[:, 16384:18432]
        b6 = bank(6)

        def aTk(tb):
            return [("aT", tb, 0), ("aT", tb, 1)]

        def A1(tb):
            xs = xsA[tb % 3]
            dma_sp(xs, xl[tb * 128:(tb + 1) * 128, :], writes=[("xs", tb % 3)], stream="xs%d" % (tb % 3))
            rms_rstd(xs, junkA, tb % 3, [("xs", tb % 3)], "junkA")

        def A2(tb):
            xs = xsA[tb % 3]
            c = st1[:, tb % 3:tb % 3 + 1]
            xn = xnA[tb % 2]
            P.op("dve", lambda e: e.scalar_tensor_tensor(out=xn, in0=xs, scalar=c, in1=gb[:], op0=ALU.mult, op1=ALU.mult),
                 reads=[("xs", tb % 3), ("st1", tb % 3), "gb"], writes=[("xnA", tb % 2)])

        def A3(tb):
            xn = xnA[tb % 2]
            pv = ppb[tb % 2]
            for cc in range(16):
                P.op("pe", lambda e, cc=cc: e.transpose(out=pv[:, cc * 128:(cc + 1) * 128], in_=xn[:, cc * 128:(cc + 1) * 128], identity=ident_b[:]),
                     reads=[("xnA", tb % 2), "ident_b"], writes=[("ps", 2 * (tb % 2)), ("ps", 2 * (tb % 2) + 1)])
            pv3 = pv[:, 0:2048].rearrange("p (c t) -> p c t", c=16)
            P.op("act", lambda e: e.copy(out=aT[:, 0:8, tb * 128:(tb + 1) * 128], in_=pv3[:, 0:8, :]),
                 reads=[("ps", 2 * (tb % 2))], writes=[("aT", tb, 0)])
            P.op("dve", lambda e: e.tensor_copy(out=aT[:, 8:16, tb * 128:(tb + 1) * 128], in_=pv3[:, 8:16, :]),
                 reads=[("ps", 2 * (tb % 2) + 1)], writes=[("aT", tb, 1)])

        def A4(tb):
            for kc in range(16):
                P.op("pe", lambda e, kc=kc: e.matmul(b6[:, tb * 8:tb * 8 + 8], lhsT=aT[:, kc, tb * 128:(tb + 1) * 128], rhs=wff[:, kc, :],
                                                    start=(kc == 0), stop=(kc == 15)),
                     reads=aTk(tb) + [("wff", 0), ("wff", 1)], writes=[("ps", 6)])

        for i in range(NB + 3):
            if i < NB:
                A1(i)
            if 0 <= i - 1 < NB:
                A2(i - 1)
            if 0 <= i - 2 < NB:
                A3(i - 2)
            if 0 <= i - 3 < NB:
                A4(i - 3)


        def aT_range_keys(l0, l1):
            ks = []
            for tb in range(l0 // 128, (l1 - 1) // 128 + 1):
                ks += aTk(tb)
            return ks

        if DEBUG:
            dma_sp(dbg["d_aT"], R1[:, :], reads=aT_range_keys(0, LT), stream="st")

        if STOP == "A":
            P.emit(final_wait_streams="st")
            return nc
        P.barrier()
        dma_sp(gb[:, 0:1024], rng.partition_broadcast(128), writes=["gb"], stream="gb")

        b7 = bank(7)
        P.op("dve", lambda e: e.tensor_tensor(out=spt[:], in0=b6[:, 0:NB * 8], in1=bfg_s[:], op=ALU.add), reads=[("ps", 6), "bfg"], writes=["spt"])
        P.op("act", lambda e: e.activation(out=spt[:], in_=spt[:], func=AF.Exp, scale=-1.0), reads=["spt"], writes=["spt"])
        P.op("act", lambda e: e.activation(out=spt[:], in_=spt[:], func=AF.Ln, bias=1.0, scale=1.0), reads=["spt"], writes=["spt"])
        P.op("pe", lambda e: e.matmul(b7[:, 0:136], lhsT=tri_f[:], rhs=spt[:], start=True, stop=True), reads=["spt", "tri_f"], writes=[("ps", 7)])
        P.op("pe", lambda e: e.matmul(b7[:, 136:272], lhsT=ones_f[:], rhs=spt[:], start=True, stop=True), reads=["spt", "ones_f"], writes=[("ps", 7)])
        P.op("dve", lambda e: e.tensor_copy(out=cumn[:], in_=b7[:, 0:136]), reads=[("ps", 7)], writes=["cumn"])
        P.op("dve", lambda e: e.tensor_copy(out=tot[:], in_=b7[:, 136:272]), reads=[("ps", 7)], writes=["tot"])
        for b in range(1, NB):
            P.op("dve", lambda e, b=b: e.tensor_tensor(out=pre[:, b * 8:b * 8 + 8], in0=pre[:, (b - 1) * 8:b * 8], in1=tot[:, (b - 1) * 8:b * 8], op=ALU.add),
                 reads=["pre", "tot"], writes=["pre"])
        P.op("dve", lambda e: e.tensor_tensor(out=cumn[:], in0=cumn[:], in1=pre[:], op=ALU.add), reads=["cumn", "pre"], writes=["cumn"])
        P.op("dve", lambda e: e.tensor_tensor(out=biasK[:], in0=cumn[:], in1=kb_s[:, :, :].rearrange("p b h -> p (b h)"), op=ALU.add),
             reads=["cumn", "kb"], writes=["biasK"])
        for c in range(9):
            tb = 8 + c
            dst = pp[2][0:8, c * 128:(c + 1) * 128] if c < 8 else pp[3][0:8, 512:640]
            P.op("pe", lambda e, dst=dst, tb=tb: e.transpose(out=dst, in_=cumn[:, tb * 8:tb * 8 + 8], identity=ident_f[:]),
                 reads=["cumn", "ident_f"], writes=[("ps", 4), ("ps", 5)] if c < 8 else [("ps", 7)])
        P.op("dve", lambda e: e.memset(Rb[:], 0.0), writes=["Rb"])
        P.op("act", lambda e: e.activation(out=Rb[0:8, 0:1024], in_=pp[2][0:8, 0:1024], func=AF.Copy, scale=-1.0 / SCALE),
             reads=[("ps", 4), ("ps", 5)], writes=["Rb"])
        P.op("act", lambda e: e.activation(out=Rb[0:8, 1024:1152], in_=pp[3][0:8, 512:640], func=AF.Copy, scale=-1.0 / SCALE),
             reads=[("ps", 7)], writes=["Rb"])
        if DEBUG:
            dma_sp(dbg["d_cum"], cumn[:], reads=["cumn"], stream="st")
        if STOP == "B0":
            P.emit(final_wait_streams="st")
            return nc

        o_ = 0
        def carve(n):
            nonlocal o_
            a = o_
            o_ += n
            return a
        fkT = R3[:, carve(LT):o_]
        _a = carve(1028)
        fqT = R3[:, _a:_a + NOWN]
        rvfv = R3[:, carve(NB * 256):o_].rearrange("p (b n) -> p b n", b=NB)
        rkt = R3[:, carve(NB * 128):o_].rearrange("p (b n) -> p b n", b=NB)
        rqT = R3[:, carve(1152):o_]
        rkT = R3[:, carve(1152):o_]
        sg = R3[:, carve(1152):o_].rearrange("p (b n) -> p b n", b=9)
        rqt = R3[:, carve(1152):o_].rearrange("p (b n) -> p b n", b=9)
        PTt = [R3[:, carve(342):o_] for _ in range(3)]
        smt = [R3[:, carve(128):o_] for _ in range(2)]
        Sbf = [R3[:, carve(128):o_] for _ in range(2)]
        junkB = R3[:, carve(128):o_]
        assert o_ % 2 == 0
        fo = o_ // 2
        def carvef(n):
            nonlocal fo
            a = fo
            fo += n
            return a
        o_all = R3f[:, carvef(1152):fo].rearrange("p (b n) -> p b n", b=9)
        rden = R3f[:, carvef(342):fo]
        rtA = R3f[:, carvef(64):fo]
        rtB = R3f[:, carvef(64):fo]
        assert fo * 2 <= R3N, fo * 2

        def rotary(src, dst, tbl, dec_ap, rk, wk):
            C = cos_s[:, tbl, :]
            S = sin_s[:, tbl, :]
            t1 = src[:, 0:64]
            t2 = src[:, 64:128]
            rd = list(rk) + ["cos", "sin", "dk", "dq"]
            P.op("dve", lambda e: e.scalar_tensor_tensor(out=rtA, in0=t1, scalar=dec_ap, in1=C, op0=ALU.mult, op1=ALU.mult), reads=rd, writes=["rtA"])
            P.op("dve", lambda e: e.scalar_tensor_tensor(out=rtB, in0=t2, scalar=dec_ap, in1=S, op0=ALU.mult, op1=ALU.mult), reads=rd, writes=["rtB"])
            P.op("dve", lambda e: e.tensor_tensor(out=dst[:, 0:64], in0=rtA, in1=rtB, op=ALU.subtract), reads=["rtA", "rtB"], writes=wk)
            P.op("dve", lambda e: e.scalar_tensor_tensor(out=rtA, in0=t1, scalar=dec_ap, in1=S, op0=ALU.mult, op1=ALU.mult), reads=rd + wk, writes=["rtA"])
            P.op("dve", lambda e: e.scalar_tensor_tensor(out=rtB, in0=t2, scalar=dec_ap, in1=C, op0=ALU.mult, op1=ALU.mult), reads=rd + wk, writes=["rtB"])
            P.op("dve", lambda e: e.tensor_tensor(out=dst[:, 64:128], in0=rtA, in1=rtB, op=ALU.add), reads=["rtA", "rtB"], writes=wk)


        vA = ringT[:, 0:6144].rearrange("p (c n) -> p c n", c=16)
        vB = ringT[:, 6144:10240].rearrange("p (c n) -> p c n", c=16)
        vC = ringT[:, 10240:12288].rearrange("p (c n) -> p c n", c=16)
        vD = ringT[:, 12288:14336].rearrange("p (c n) -> p c n", c=16)
        regions = {"A": (vA, 3), "B": (vB, 2), "C": (vC, 1), "D": (vD, 1)}

        def rg_keys(name):
            return [("rg" + name, si, hh) for si in range(regions[name][1]) for hh in range(2)]

        def load_region(name, col_offs, h):
            v, _ = regions[name]
            for si, c0 in enumerate(col_offs):
                for hh in range(2):
                    dma_cast(v[:, 8 * hh:8 * hh + 8, si * 128:(si + 1) * 128], w_in_v[:, 8 * hh:8 * hh + 8, c0:c0 + 128],
                             writes=[("rg" + name, si, hh)], stream="rg" + name, batch=h)

        deferred_tail = [None]

        def load_head(h):
            load_region("A", [1024 + h * 128, 2048 + h * 128, 6144 + h * 128], h)
            load_region("B", [h * 128, 3072 + h * 128], h)
            load_region("C", [4096 + h * 128], h)
            load_region("D", [5120 + h * 128], h)
        load_head(0)
        for h in range(NHEADS_RUN):
            for tb in range(NB):
                bA = 2 * (tb % 2)
                bB = bA + 1
                for kc in range(16):
                    P.op("pe", lambda e, tb=tb, kc=kc, bA=bA: e.matmul(
                        bank(bA)[:, 0:384], lhsT=aT[:, kc, tb * 128:(tb + 1) * 128], rhs=vA[:, kc, :],
                        start=(kc == 0), stop=(kc == 15)),
                        reads=aTk(tb) + rg_keys("A"), writes=[("ps", bA)])
                    if tb >= 8:
                        P.op("pe", lambda e, tb=tb, kc=kc, bB=bB: e.matmul(
                            bank(bB)[:, 0:256], lhsT=aT[:, kc, tb * 128:(tb + 1) * 128], rhs=vB[:, kc, :],
                            start=(kc == 0), stop=(kc == 15)),
                            reads=aTk(tb) + rg_keys("B"), writes=[("ps", bB)])
                rotary(bank(bA), rkt[:, tb, :], tb, dk_s[:, tb, h:h + 1], [("ps", bA)], [("rkt", tb)])
                P.op("act", lambda e, tb=tb, bA=bA: e.copy(out=rvfv[:, tb, :], in_=bank(bA)[:, 128:384]), reads=[("ps", bA)], writes=[("rvfv", tb)])
                if tb == 3 and deferred_tail[0] is not None:
                    deferred_tail[0]()
                    deferred_tail[0] = None
                if tb >= 8:
                    rotary(bank(bB), rqt[:, tb - 8, :], tb, dq_s[:, tb, h:h + 1], [("ps", bB)], [("rqt", tb - 8)])
                    P.op("act", lambda e, tb=tb, bB=bB: e.copy(out=sg[:, tb - 8, :], in_=bank(bB)[:, 128:256]), reads=[("ps", bB)], writes=["sg"])
            P.op("act", lambda e: e.activation(out=sg[:, :, :], in_=sg[:, :, :], func=AF.Silu), reads=["sg"], writes=["sg"])
            for nb in range(5):
                n0 = nb * 512
                nw = min(512, LT - n0)
                bk = 4 + nb % 2
                for kc in range(16):
                    P.op("pe", lambda e, kc=kc, n0=n0, nw=nw, bk=bk: e.matmul(bank(bk)[:, 0:nw], lhsT=vD[:, kc, :], rhs=aT[:, kc, n0:n0 + nw],
                                                                          start=(kc == 0), stop=(kc == 15)),
                         reads=aT_range_keys(n0, n0 + nw) + rg_keys("D"), writes=[("ps", bk)])
                if nb % 2 == 0:
                    P.op("act", lambda e, n0=n0, nw=nw, bk=bk: e.copy(out=fkT[:, n0:n0 + nw], in_=bank(bk)[:, 0:nw]), reads=[("ps", bk)], writes=[("fkT", nb)])
                else:
                    P.op("dve", lambda e, n0=n0, nw=nw, bk=bk: e.tensor_copy(out=fkT[:, n0:n0 + nw], in_=bank(bk)[:, 0:nw]), reads=[("ps", bk)], writes=[("fkT", nb)])
            for g in range(3):
                n0 = OWN0 + 342 * g
                bk = 4 + (g + 1) % 2
                for kc in range(16):
                    P.op("pe", lambda e, kc=kc, n0=n0, bk=bk: e.matmul(bank(bk)[:, 0:342], lhsT=vC[:, kc, :], rhs=aT[:, kc, n0:n0 + 342],
                                                                    start=(kc == 0), stop=(kc == 15)),
                         reads=aT_range_keys(n0, n0 + 342) + rg_keys("C"), writes=[("ps", bk)])
                P.op("act", lambda e, g=g, bk=bk: e.copy(out=fqT[:, 342 * g:342 * g + 342], in_=bank(bk)[:, 0:342]), reads=[("ps", bk)], writes=[("fqT", g)])

            if h + 1 < NHEADS_RUN:
                load_head(h + 1)
            for c in range(9):
                P.op("pe", lambda e, c=c: e.transpose(out=ppb[2][:, c * 128:(c + 1) * 128], in_=rqt[:, c, :], identity=ident_b[:]),
                     reads=[("rqt", c), "ident_b"], writes=[("ps", 4), ("ps", 5)])
                P.op("pe", lambda e, c=c: e.transpose(out=ppb[3][:, c * 128:(c + 1) * 128], in_=rkt[:, 8 + c, :], identity=ident_b[:]),
                     reads=[("rkt", 8 + c), "ident_b"], writes=[("ps", 6), ("ps", 7)])
            P.op("act", lambda e: e.copy(out=rqT, in_=ppb[2][:, 0:1152]), reads=[("ps", 4), ("ps", 5)], writes=["rqT"])
            P.op("dve", lambda e: e.tensor_copy(out=rkT, in_=ppb[3][:, 0:1152]), reads=[("ps", 6), ("ps", 7)], writes=["rkT"])
            Sps = bank(7)[:, 0:128]
            ret_steps = []

            def r_init():
                for b in range(8):
                    P.op("pe", lambda e, b=b: e.matmul(Sps, lhsT=rkt[:, b, :], rhs=rvfv[:, b, 0:128], start=(b == 0), stop=(b == 7), skip_group_check=True),
                         reads=[("rkt", b), ("rvfv", b)], writes=[("ps", 7)])
            ret_steps.append(r_init)

            def r_a(c):
                sTp = bank(6)[:, (c % 2) * 128:(c % 2) * 128 + 128]
                if c > 0:
                    P.op("pe", lambda e: e.matmul(Sps, lhsT=rkt[:, 7 + c, :], rhs=rvfv[:, 7 + c, 0:128], start=False, stop=True, skip_group_check=True),
                         reads=[("rkt", 7 + c), ("rvfv", 7 + c)], writes=[("ps", 7)])
                P.op("act", lambda e: e.copy(out=Sbf[c % 2], in_=Sps), reads=[("ps", 7)], writes=[("Sbf", c % 2)])
                P.op("pe", lambda e: e.matmul(sTp, lhsT=rkT[:, c * 128:(c + 1) * 128], rhs=rqT[:, c * 128:(c + 1) * 128], start=True, stop=True),
                     reads=["rkT", "rqT"], writes=[("ps", 6)])
                P.op("dve", lambda e: e.tensor_tensor(out=smt[c % 2], in0=sTp, in1=maskR[:], op=ALU.mult),
                     reads=[("ps", 6), "maskR"], writes=[("smt", c % 2)])

            def r_b(c):
                op_ = bank(4)[:, (c % 2) * 128:(c % 2) * 128 + 128]
                P.op("pe", lambda e: e.matmul(op_, lhsT=rqT[:, c * 128:(c + 1) * 128], rhs=Sbf[c % 2], start=True, stop=False),
                     reads=["rqT", ("Sbf", c % 2)], writes=[("ps", 4)])
                P.op("pe", lambda e: e.matmul(op_, lhsT=smt[c % 2], rhs=rvfv[:, 8 + c, 0:128], start=False, stop=True),
                     reads=[("smt", c % 2), ("rvfv", 8 + c)], writes=[("ps", 4)])
                P.op("dve", lambda e: e.tensor_copy(out=o_all[:, c, :], in_=op_), reads=[("ps", 4)], writes=[("o_all", c)])
                P.op("act", lambda e: e.activation(out=junkB, in_=op_, func=AF.Square, accum_out=st2[:, 16 + c:17 + c]),
                     reads=[("ps", 4)], writes=["junkB", ("st2q", c)])
            for c in range(9):
                ret_steps.append(lambda c=c: r_a(c))
                ret_steps.append(lambda c=c: r_b(c))

            def r_tail(h=h):
                oall_keys = [("o_all", c) for c in range(9)]
                sq_keys = [("st2q", c) for c in range(9)]
                mean = st2[:, 0:9]
                ssq = st2[:, 16:25]
                msq = st2[:, 32:41]
                rstd = st2[:, 48:57]
                P.op("dve", lambda e: e.reduce_sum(out=mean, in_=o_all[:, :, :], axis=AX.X), reads=oall_keys, writes=["st2m"])
                P.op("pool", lambda e: e.tensor_scalar_mul(out=mean, in0=mean, scalar1=1.0 / 128), reads=["st2m"], writes=["st2m"])
                P.op("pool", lambda e: e.tensor_tensor(out=msq, in0=mean, in1=mean, op=ALU.mult), reads=["st2m"], writes=["st2s"])
                P.op("pool", lambda e: e.tensor_scalar(out=rstd, in0=ssq, scalar1=1.0 / 128, scalar2=EPS, op0=ALU.mult, op1=ALU.add), reads=sq_keys, writes=["st2r"])
                P.op("pool", lambda e: e.tensor_tensor(out=rstd, in0=rstd, in1=msq, op=ALU.subtract), reads=["st2r", "st2s"], writes=["st2r"])
                P.op("act", lambda e: e.activation(out=rstd, in_=rstd, func=AF.Ln), reads=["st2r"], writes=["st2r"])
                P.op("act", lambda e: e.activation(out=rstd, in_=rstd, func=AF.Exp, scale=-0.5), reads=["st2r"], writes=["st2r"])
                for c in range(9):
                    P.op("pool", lambda e, c=c: e.tensor_scalar(out=o_all[:, c, :], in0=o_all[:, c, :], scalar1=st2[:, c:c + 1], scalar2=st2[:, 48 + c:49 + c],
                                                               op0=ALU.subtract, op1=ALU.mult),
                         reads=[("o_all", c), "st2m", "st2r"], writes=[("o_all", c)])
                    P.op("pool", lambda e, c=c: e.tensor_tensor(out=o_all[:, c, :], in0=o_all[:, c, :], in1=gb[:, h * 128:(h + 1) * 128], op=ALU.mult),
                         reads=[("o_all", c), "gb"], writes=[("o_all", c)])
                    P.op("pool", lambda e, c=c: e.tensor_tensor(out=rqt[:, c, :], in0=o_all[:, c, :], in1=sg[:, c, :], op=ALU.mult),
                         reads=[("o_all", c), "sg"], writes=[("rqt", c)])

            def r_tail_pe(h=h):
                for c in range(9):
                    P.op("pe", lambda e, c=c: e.transpose(out=ppb[2][:, c * 128:(c + 1) * 128], in_=rqt[:, c, :], identity=ident_b[:]),
                         reads=[("rqt", c), "ident_b"], writes=[("ps", 4), ("ps", 5)])
                P.op("act", lambda e: e.copy(out=mixT[:, h, :], in_=ppb[2][:, 126:1152]), reads=[("ps", 4), ("ps", 5)], writes=[("mixh", h)])

            tiles = []
            for g in range(3):
                q0 = OWN0 + 342 * g
                kmax = (q0 + 342 - 1) // 128
                for kb in range(kmax + 1):
                    tiles.append((g, kb, kmax, q0))
            oTp = bank(2)[:, 0:342]
            dnp = bank(3)[:, 0:342]

            def f_s(ti, h=h):
                g, kb, kmax, q0 = tiles[ti]
                sbk = (0, 1, 5)[ti % 3]
                sb_ = bank(sbk)[:, 0:342]
                delta = 128 * kb - q0
                need_mask = (128 * kb + 127) > q0
                P.op("pe", lambda e: e.matmul(sb_, lhsT=fkT[:, kb * 128:(kb + 1) * 128], rhs=fqT[:, 342 * g:342 * g + 342], start=True, stop=False),
                     reads=[("fkT", kb // 4), ("fqT", g)], writes=[("ps", sbk)])
                P.op("pe", lambda e: e.matmul(sb_, lhsT=sel[:, h * 128:(h + 1) * 128], rhs=Rb[:, 126 + 342 * g:126 + 342 * g + 342],
                                              start=False, stop=(not need_mask)),
                     reads=["sel", "Rb"], writes=[("ps", sbk)])
                if need_mask:
                    off = XOFF - delta
                    assert 0 <= off and off + 342 <= MASKW, off
                    P.op("pe", lambda e: e.matmul(sb_, lhsT=ident_b[:], rhs=maskT[:, off:off + 342], start=False, stop=True),
                         reads=["ident_b", "maskT"], writes=[("ps", sbk)])
                pt = PTt[ti % 3]
                P.op("act", lambda e: e.activation(out=pt, in_=sb_, func=AF.Exp, bias=biasK[:, kb * 8 + h:kb * 8 + h + 1], scale=SCALE),
                     reads=[("ps", sbk), "biasK"], writes=[("PT", ti % 3)])

            def f_pv(ti, h=h):
                g, kb, kmax, q0 = tiles[ti]
                pt = PTt[ti % 3]
                ptk = ("PT", ti % 3)
                P.op("pe", lambda e: e.matmul(oTp, lhsT=rvfv[:, kb, 128:256], rhs=pt, start=(kb == 0), stop=(kb == kmax)),
                     reads=[("rvfv", kb), ptk], writes=[("ps", 2)])
                P.op("pe", lambda e: e.matmul(dnp, lhsT=ones_b[:], rhs=pt, start=(kb == 0), stop=(kb == kmax)),
                     reads=["ones_b", ptk], writes=[("ps", 3)])
                if kb == kmax:
                    P.op("dve", lambda e: e.reciprocal(out=rden, in_=dnp), reads=[("ps", 3)], writes=["rden"])
                    P.op("dve", lambda e: e.tensor_tensor(out=mixT[:, 8 + h, 342 * g:342 * g + 342], in0=oTp, in1=rden, op=ALU.mult),
                         reads=[("ps", 2), "rden"], writes=[("mixf", h, g)])

            fox_steps = []
            nt_ = len(tiles)
            def f_first():
                f_s(0)
                f_s(1)
            fox_steps.append(f_first)
            for ti in range(nt_):
                def st(ti=ti):
                    if ti + 2 < nt_:
                        f_s(ti + 2)
                    f_pv(ti)
                fox_steps.append(st)
            fi = 0
            per = [3, 2]
            for ri, rs in enumerate(ret_steps):
                rs()
                k = 1 if ri == 0 else per[ri % 2]
                for _ in range(k):
                    if fi < len(fox_steps):
                        fox_steps[fi]()
                        fi += 1
            while fi < len(fox_steps):
                fox_steps[fi]()
                fi += 1
            r_tail()
            deferred_tail[0] = r_tail_pe
        deferred_tail[0]()


        mix_all = [("mixh", h) for h in range(NH)] + [("mixf", h, g) for h in range(NH) for g in range(3)]
        if DEBUG:
            dma_sp(dbg["d_mix"], mixT[:, :, :].rearrange("p c t -> p (c t)"), reads=mix_all, stream="st")
        if STOP == "B":
            P.emit(final_wait_streams="st")
            return nc
        all_rg = [k for nm in ("A", "B", "C", "D") for k in rg_keys(nm)]

        def load_wout_quarter(qp, extra=()):
            sl = []
            for m in range(4):
                s_ = next_slot()
                v_ = ring[s_].rearrange("p (c n) -> p c n", c=4)
                dma_cast(v_, w_out_v[:, 4 * m:4 * m + 4, qp * 512:(qp + 1) * 512], writes=rkeys(s_) + list(extra), stream="ring%d" % s_)
                sl.append((s_, v_))
            return sl
        wq = {0: load_wout_quarter(0, all_rg), 1: load_wout_quarter(1, all_rg)}

        P.barrier()

        dma_sp(gb[:], g2.partition_broadcast(128), writes=["gb"], stream="gb")
        xsC = [R3f[:, 0:512], R3f[:, 512:1024], R3f[:, 1024:1536]]
        hnC = [R3[:, 4096:6144], R3[:, 6144:8192]]
        junkC = R3[:, 8192:10240]
        hh = R3f[:, 6144:8192]
        blocks = [(-1, 0, OWN0)] + [(tb, 2 + 128 * tb, 1152 + 128 * tb) for tb in range(8)]
        xi = 0
        ubk = 0

        def c_norm1(bi, tb):
            src = hh[:, :] if tb < 0 else h2[:, tb, :]
            col = 4 + bi % 2
            hk = [("h1", bi, q) for q in range(4)]
            c = rms_rstd(src, junkC, col, hk, "junkC")
            hn = hnC[bi % 2]
            P.op("dve", lambda e: e.scalar_tensor_tensor(out=hn, in0=src, scalar=c, in1=gb[:], op0=ALU.mult, op1=ALU.mult),
                 reads=hk + [("st1", col), "gb"], writes=[("hnC", bi % 2)])

        def c_norm2(bi, tb, c0):
            hn = hnC[bi % 2]
            pv = ppb[2 + bi % 2]
            pk = [("ps", 4 + 2 * (bi % 2)), ("ps", 5 + 2 * (bi % 2))]
            for cc in range(16):
                P.op("pe", lambda e, cc=cc: e.transpose(out=pv[:, cc * 128:(cc + 1) * 128], in_=hn[:, cc * 128:(cc + 1) * 128], identity=ident_b[:]),
                     reads=[("hnC", bi % 2), "ident_b"], writes=pk)
            pv3 = pv[:, 0:2048].rearrange("p (c t) -> p c t", c=16)
            if tb < 0:
                P.op("act", lambda e: e.copy(out=mixT[:, :, 0:2], in_=pv3[:, :, 0:2]), reads=pk + mix_all, writes=[("cT", bi)])
            else:
                P.op("act", lambda e: e.copy(out=mixT[:, 0:8, c0:c0 + 128], in_=pv3[:, 0:8, :]), reads=pk[0:1] + mix_all, writes=[("cT", bi)])
                P.op("dve", lambda e: e.tensor_copy(out=mixT[:, 8:16, c0:c0 + 128], in_=pv3[:, 8:16, :]), reads=pk[1:2] + mix_all, writes=[("cTb", bi)])

        for qp in range(4):
            slots = wq[qp]
            pend = None
            for bi, (tb, c0, l0) in enumerate(blocks):
                xs = xsC[xi % 3]
                xk = ("xs", xi % 3)
                xstream = "xs%d" % (xi % 3)
                xi += 1
                dma_sp(xs, xl[l0:l0 + 128, qp * 512:(qp + 1) * 512], writes=[xk], stream=xstream)
                bk = ubk % 4
                ubk += 1
                ck = [("cT", bi), ("cTb", bi)] + ([("cT", 1), ("cTb", 1)] if tb < 0 else [])
                for kc in range(16):
                    s_, v_ = slots[kc // 4]
                    P.op("pe", lambda e, kc=kc, v_=v_, c0=c0, bk=bk: e.matmul(
                        bank(bk), lhsT=mixT[:, kc, c0:c0 + 128], rhs=v_[:, kc % 4, :], start=(kc == 0), stop=(kc == 15)),
                        reads=mix_all + ck + rkeys(s_), writes=[("ps", bk)])
                dst = hh[:, qp * 512:(qp + 1) * 512] if tb < 0 else h2[:, tb, qp * 512:(qp + 1) * 512]
                P.op("dve", lambda e, dst=dst, bk=bk, xs=xs: e.tensor_tensor(out=dst, in0=bank(bk), in1=xs, op=ALU.add),
                     reads=[("ps", bk), xk], writes=[("h1", bi, qp)])
                if qp == 3:
                    c_norm1(bi, tb)
                    if pend is not None:
                        c_norm2(*pend)
                    pend = (bi, tb, c0)
            if qp == 3:
                c_norm2(*pend)
            if qp + 2 < 4:
                wq[qp + 2] = load_wout_quarter(qp + 2)

        cT = mixT
        cT_all = [("cT", bi) for bi in range(9)] + [("cTb", bi) for bi in range(1, 9)]
        if DEBUG:
            dma_sp(dbg["d_h1"], R1f[:, 0:8 * D], reads=[("h1", bi, hp) for bi in range(1, 9) for hp in range(4)], stream="st")
        if DEBUG:
            dma_sp(dbg["d_cT"], mixT[:, :, :].rearrange("p c t -> p (c t)"), reads=cT_all, stream="st")
            dma_sp(dbg["d_hh"], hh[0:2, :], reads=[("h1", 0, q) for q in range(4)], stream="st")
        if STOP == "C":
            P.emit(final_wait_streams="st")
            return nc
        P.barrier()

        gated = [[R3[:, (gs * GRP + jj) * 1024:(gs * GRP + jj + 1) * 1024] for jj in range(GRP)] for gs in range(2)]
        fb = 2 * GRP * 1024 // 2
        Yg = [R3f[:, fb + i * 1024:fb + (i + 1) * 1024] for i in range(2)]
        Yv = [R3f[:, fb + 2048 + i * 1024:fb + 2048 + (i + 1) * 1024] for i in range(2)]
        sb0 = 2 * (fb + 4096)
        Sg = [R3[:, sb0 + i * 1024:sb0 + (i + 1) * 1024] for i in range(2)]
        assert sb0 + 2048 <= R3N
        nblk = [(0, 342), (342, 683), (683, 1024)]
        ub = [0]

        h2_keys = lambda tb: [("h2", tb, n) for n in range(4)]
        mixflat = mixT[:, :, :].rearrange("p c t -> p (c t)")
        otE = [mixflat[:, 0:4096].bitcast(F32), mixflat[:, 4096:8192].bitcast(F32)]
        junkE = mixflat[:, 8192:10240]

        def final_block(tb):
            col = 8 + tb % 2
            c = rms_rstd(h2[:, tb, :], junkE, col, h2_keys(tb), "junkE", extra_writes=cT_all)
            ot = otE[tb % 2]
            P.op("dve", lambda e: e.scalar_tensor_tensor(out=ot, in0=h2[:, tb, :], scalar=c, in1=gb[:], op0=ALU.mult, op1=ALU.mult),
                 reads=h2_keys(tb) + [("st1", col), "gb"], writes=[("ot", tb % 2)] + cT_all)
            dma_sp(y[tb * 128:(tb + 1) * 128, :], ot, reads=[("ot", tb % 2)], stream="st%d" % (tb % 2))

        def wdown_group(gi, dslots, last=False):
            gs = gi % 2
            t = 0
            for tb in range(8):
                for n in range(4):
                    bk = 4 + t % 4
                    t += 1
                    for jj in range(GRP):
                        s, v = dslots[jj]
                        P.op("pe", lambda e, jj=jj, v=v, tb=tb, n=n, bk=bk, gs=gs: e.matmul(
                            bank(bk), lhsT=gated[gs][jj][:, tb * 128:(tb + 1) * 128], rhs=v[:, n * 512:(n + 1) * 512],
                            start=(jj == 0), stop=(jj == GRP - 1)),
                            reads=[("gated", gs, jj)] + rkeys(s), writes=[("ps", bk)])
                    P.op("dve", lambda e, tb=tb, n=n, bk=bk: e.tensor_tensor(out=h2[:, tb, n * 512:(n + 1) * 512], in0=h2[:, tb, n * 512:(n + 1) * 512], in1=bank(bk), op=ALU.add),
                         reads=[("ps", bk), ("h2", tb, n)], writes=[("h2", tb, n)])
                if last:
                    final_block(tb)

        def load_down(gi):
            dslots = []
            for jj in range(GRP):
                j = gi * GRP + jj
                s = next_slot()
                dma_cast(ring[s][:, :], w_down[j * 128:(j + 1) * 128, :], writes=rkeys(s), stream="ring%d" % s)
                dslots.append((s, ring[s]))
            return dslots

        prev = None
        pair_i = 0
        for gi in range(NFF // GRP):
            gs = gi % 2
            for jj in range(GRP):
                j = gi * GRP + jj
                pi = pair_i % 2
                pair_i += 1
                for half in range(2):
                    cidx = half * NFF + j
                    s, v = load_slice(w_up_v, half * DFF + j * 128)
                    Y = (Yg if half == 0 else Yv)[pi]
                    yk = ("Y", half, pi)
                    for (r0, r1) in nblk:
                        ln = r1 - r0
                        bk = ub[0] % 4
                        ub[0] += 1
                        for kc in range(16):
                            P.op("pe", lambda e, kc=kc, v=v, r0=r0, ln=ln, bk=bk: e.matmul(bank(bk)[:, 0:ln + 2], lhsT=v[:, kc, :], rhs=cT[:, kc, r0:r0 + ln + 2],
                                                                                    start=(kc == 0), stop=(kc == 15)),
                                 reads=cT_all + rkeys(s), writes=[("ps", bk)])
                        u = bank(bk)
                        P.op("act", lambda e, u=u, Y=Y, r0=r0, r1=r1, ln=ln, cidx=cidx: e.activation(
                            out=Y[:, r0:r1], in_=u[:, 2:ln + 2], func=AF.Identity, bias=convb_s[:, cidx:cidx + 1], scale=convw_s[:, cidx * 3 + 2:cidx * 3 + 3]),
                            reads=[("ps", bk), "convw", "convb"], writes=[yk])
                        P.op("dve", lambda e, u=u, Y=Y, r0=r0, r1=r1, ln=ln, cidx=cidx: e.scalar_tensor_tensor(
                            out=Y[:, r0:r1], in0=u[:, 1:ln + 1], scalar=convw_s[:, cidx * 3 + 1:cidx * 3 + 2], in1=Y[:, r0:r1], op0=ALU.mult, op1=ALU.add),
                            reads=[("ps", bk), "convw", yk], writes=[yk])
                        P.op("dve", lambda e, u=u, Y=Y, r0=r0, r1=r1, ln=ln, cidx=cidx: e.scalar_tensor_tensor(
                            out=Y[:, r0:r1], in0=u[:, 0:ln], scalar=convw_s[:, cidx * 3:cidx * 3 + 1], in1=Y[:, r0:r1], op0=ALU.mult, op1=ALU.add),
                            reads=[("ps", bk), "convw", yk], writes=[yk])
                    if half == 0:
                        P.op("act", lambda e, Y=Y, pi=pi: e.activation(out=Sg[pi], in_=Y, func=AF.Silu), reads=[yk], writes=[("Sg", pi)])
                    else:
                        P.op("dve", lambda e, Y=Y, pi=pi, gs=gs, jj=jj: e.tensor_tensor(out=gated[gs][jj], in0=Y, in1=Sg[pi], op=ALU.mult),
                             reads=[yk, ("Sg", pi)], writes=[("gated", gs, jj)])
            if prev is not None:
                wdown_group(prev, load_down(prev))
            prev = gi
        dma_sp(gb[:], gf.partition_broadcast(128), writes=["gb"], stream="gb")
        wdown_group(prev, load_down(prev), last=True)

        P.emit(final_wait_streams="st")
    return nc


_NC_CACHE = {}


def _consts():
    c = {}
    c["c_ident"] = np.eye(128, dtype=np.float32)
    c["c_tri"] = np.triu(np.ones((128, 128), np.float32))
    c["c_ones"] = np.ones((128, 128), np.float32)
    c["c_maskR"] = np.triu(np.ones((128, 128), np.float32))
    p = np.arange(128)[:, None]
    xx = np.arange(MASKW)[None, :]
    c["c_maskT"] = np.where(xx - XOFF < p, NEG, 0.0).astype(np.float32)
    sel = np.zeros((128, 8, 128), np.float32)
    for h in range(8):
        sel[h, h, :] = 1.0
    c["c_sel"] = sel.reshape(128, 1024)
    return c


def _core_tables(T0):
    l = np.arange(LT)
    t = l - 1152 + T0
    valid = t >= 0
    inv_freq = 1.0 / (10000.0 ** (np.arange(0, 128, 2, dtype=np.float64) / 128.0))
    ang = np.where(valid, t, 0)[:, None].astype(np.float64) * inv_freq[None, :]
    def pm(a):
        n = a.shape[1]
        return np.ascontiguousarray(a.reshape(NB, 128, n).transpose(1, 0, 2).reshape(128, NB * n))
    tabs = {"cosT": pm(np.cos(ang).astype(np.float32)), "sinT": pm(np.sin(ang).astype(np.float32))}
    log_g = np.log1p(-np.exp2(-5.0 - np.arange(8, dtype=np.float64)))
    rel = (l - 1152).astype(np.float64)
    tabs["dqT"] = pm(np.exp(rel[:, None] * log_g[None, :]).astype(np.float32))
    tabs["dkT"] = pm((np.exp(-rel[:, None] * log_g[None, :]) * SCALE).astype(np.float32))
    tabs["kbT"] = pm(np.repeat(np.where(valid, 0.0, NEG).astype(np.float32)[:, None], 8, axis=1))
    return tabs


def kernel(x, meta_tokens, norm1_gain, w_in, b_forget, ret_norm_gain, w_out, norm2_gain, w_up,
           conv_w, conv_b, w_down, final_norm_gain):
    f32 = np.float32
    x = np.asarray(x, f32)
    B = x.shape[0]
    if "nc" not in _NC_CACHE:
        _NC_CACHE["nc"] = build_nc()
    nc = _NC_CACHE["nc"]
    consts = _consts()
    shared = {
        "w_in": np.ascontiguousarray(np.asarray(w_in, f32)[0]),
        "w_out": np.ascontiguousarray(np.asarray(w_out, f32)[0]),
        "w_up": np.ascontiguousarray(np.asarray(w_up, f32)[0]),
        "w_down": np.ascontiguousarray(np.asarray(w_down, f32)[0]),
        "g1": np.ascontiguousarray(np.asarray(norm1_gain, f32)[0]),
        "g2": np.ascontiguousarray(np.asarray(norm2_gain, f32)[0]),
        "gf": np.ascontiguousarray(np.asarray(final_norm_gain, f32)),
        "rng": np.ascontiguousarray(np.asarray(ret_norm_gain, f32)[0]),
        "wffd": np.ascontiguousarray(np.asarray(w_in, f32)[0][:, 7168:7176].reshape(16, 128, 8).transpose(1, 0, 2).reshape(128, 128)),
        "bfg": np.ascontiguousarray(np.tile(np.asarray(b_forget, f32)[0], NB)),
        "convw": np.ascontiguousarray(np.asarray(conv_w, f32)[0].reshape(3, 2 * NFF, 128).transpose(2, 1, 0).reshape(128, 2 * NFF * 3)),
        "convb": np.ascontiguousarray(np.asarray(conv_b, f32)[0].reshape(2 * NFF, 128).T),
    }
    shared.update(consts)
    meta = np.asarray(meta_tokens, f32)
    in_maps = []
    for core in range(8):
        b, s = core // 2, core % 2
        T0 = 16 + 1024 * s
        full = np.concatenate([meta, x[b]], axis=0)
        xl = np.zeros((LT, D), f32)
        t_lo = T0 - 1152
        src_lo = max(t_lo, 0)
        xl[src_lo - t_lo:, :] = full[src_lo:T0 + 1024]
        m = dict(shared)
        m["xl"] = xl
        m.update(_core_tables(T0))
        in_maps.append(m)
    res = run_bass_kernel_spmd(nc, in_maps[:NCORES_RUN], core_ids=list(range(NCORES_RUN)))
    out = np.zeros((B, 2048, D), f32)
    for core in range(NCORES_RUN):
        b, s = core // 2, core % 2
        out[b, 1024 * s:1024 * (s + 1), :] = res.results[core]["y"]
    if DEBUG:
        kernel.debug = res.results
    return out
```

```python
import contextlib
import numpy as np
import concourse.bass as bass
import concourse.mybir as mybir
from concourse.bass_utils import run_bass_kernel_spmd

F32 = mybir.dt.float32
BF16 = mybir.dt.bfloat16
AF = mybir.ActivationFunctionType
ALU = mybir.AluOpType
AX = mybir.AxisListType

D = 2048
NB = 17
LT = NB * 128
OWN0 = 1150
NOWN = 1026
NH = 8
DFF = 5632
NFF = 44
IN_DIM = 7176
SCALE = 128 ** -0.5
EPS = 1e-6
NEG = -30000.0
XOFF = 300
MASKW = 768
NSLOT = 8
GRP = 4
DEBUG = False
STOP = None
NHEADS_RUN = 8
NCORES_RUN = 8
STOP2 = None
SKIP = set()

ENGS = ("sp", "act", "pool", "dve", "pe")
SEM_LIMIT = 12000
WAIT_ALL_STREAMS = ("const", "constp")


class _Op:
    __slots__ = ("eng", "fn", "deps", "signal", "stream", "sem_i", "val", "inc", "idx", "batch")


class Prog:
    def __init__(self, nc):
        self.nc = nc
        self.ops = []
        self.eng_ops = {e: [] for e in ENGS}
        self.last_w = {}
        self.readers = {}
        self.last_in_stream = {}
        self.barrier_deps = set()

    def op(self, eng, fn, reads=(), writes=(), dma=None, batch=None):
        o = _Op()
        o.batch = batch
        o.eng = eng
        o.fn = fn
        o.idx = len(self.ops)
        o.stream = ("dma", dma) if dma is not None else ("eng", eng)
        o.inc = 16 if dma is not None else 1
        o.signal = dma is not None
        deps = set(self.barrier_deps)
        writes = list(writes) + [k for k in reads if isinstance(k, tuple) and k[0] == "ps"]
        reads = [k for k in reads if not (isinstance(k, tuple) and k[0] == "ps")]
        for k in reads:
            w = self.last_w.get(k)
            if w is not None:
                deps.add(w)
        for k in writes:
            w = self.last_w.get(k)
            if w is not None:
                deps.add(w)
            for r in self.readers.get(k, ()):
                deps.add(r)
        o.deps = deps
        for k in reads:
            self.readers.setdefault(k, []).append(o.idx)
        for k in writes:
            self.last_w[k] = o.idx
            self.readers[k] = []
        self.ops.append(o)
        self.eng_ops[eng].append(o)
        self.last_in_stream[o.stream] = o.idx
        return o

    def barrier(self):
        self.barrier_deps = set(self.last_in_stream.values())

    def emit(self, final_wait_streams=()):
        nc = self.nc
        ops = self.ops
        for o in ops:
            for d in o.deps:
                p = ops[d]
                if p.stream == ("eng", "pe") and o.eng == "pe":
                    continue
                p.signal = True
        streams = {}
        for o in ops:
            if not o.signal:
                continue
            st = streams.setdefault(o.stream, {"n": 0, "cur": 0})
            if st["cur"] + o.inc > SEM_LIMIT:
                st["n"] += 1
                st["cur"] = 0
            st["cur"] += o.inc
            o.sem_i = (o.stream, st["n"])
            o.val = st["cur"]
        batch_max = {}
        for o in ops:
            if o.signal and o.batch is not None:
                k = (o.sem_i, o.batch)
                batch_max[k] = max(batch_max.get(k, 0), o.val)
        sem_keys = []
        seen = set()
        for o in ops:
            if o.signal and o.sem_i not in seen:
                seen.add(o.sem_i)
                sem_keys.append(o.sem_i)
        with contextlib.ExitStack() as es:
            sems = {}
            for i, k in enumerate(sem_keys):
                sems[k] = es.enter_context(nc.semaphore("s%d" % i))
            last_val = {}
            for o in ops:
                if o.signal:
                    last_val[o.sem_i] = max(last_val.get(o.sem_i, 0), o.val)
            block = es.enter_context(nc.Block())
            handles = {"sp": block.sync, "act": block.scalar, "pool": block.gpsimd,
                       "dve": block.vector, "pe": block.tensor}

            def make(engname):
                def body(eng):
                    waited = {}
                    for o in self.eng_ops[engname]:
                        need = {}
                        for d in o.deps:
                            p = ops[d]
                            if not p.signal:
                                continue
                            if p.stream == ("eng", "pe") and engname == "pe":
                                continue
                            v_ = last_val[p.sem_i] if (p.stream[0] == "dma" and p.stream[1] in WAIT_ALL_STREAMS) else p.val
                            if p.batch is not None:
                                v_ = batch_max[(p.sem_i, p.batch)]
                            if v_ > need.get(p.sem_i, 0):
                                need[p.sem_i] = v_
                        for k, v in need.items():
                            if waited.get(k, 0) < v:
                                eng.wait_ge(sems[k], v)
                                waited[k] = v
                        ins = o.fn(eng)
                        if o.signal:
                            ins.then_inc(sems[o.sem_i], o.inc)
                    if engname == "sp":
                        for k in sem_keys:
                            if k[0][0] == "dma" and k[0][1].startswith(final_wait_streams):
                                eng.wait_ge(sems[k], last_val[k])
                return body

            for e in ENGS:
                handles[e](make(e))
        return len(ops)


def build_nc():
    nc = bass.Bass("TRN2", target_bir_lowering=False)

    def din(name, shape):
        return nc.dram_tensor(name, list(shape), F32, kind="ExternalInput").ap()

    xl = din("xl", [LT, D])
    w_in = din("w_in", [D, IN_DIM])
    w_out = din("w_out", [D, D])
    w_up = din("w_up", [D, 2 * DFF])
    w_down = din("w_down", [DFF, D])
    g1 = din("g1", [D]); g2 = din("g2", [D]); gf = din("gf", [D])
    rng = din("rng", [1024])
    bfg = din("bfg", [NB * 8])
    convw = din("convw", [128, 2 * NFF * 3])
    convb = din("convb", [128, 2 * NFF])
    cosT = din("cosT", [128, NB * 64]); sinT = din("sinT", [128, NB * 64])
    dkT = din("dkT", [128, NB * 8]); dqT = din("dqT", [128, NB * 8]); kbT = din("kbT", [128, NB * 8])
    wffd = din("wffd", [128, 16 * 8])
    c_ident = din("c_ident", [128, 128]); c_tri = din("c_tri", [128, 128]); c_ones = din("c_ones", [128, 128])
    c_maskR = din("c_maskR", [128, 128]); c_maskT = din("c_maskT", [128, MASKW]); c_sel = din("c_sel", [128, 1024])
    y = nc.dram_tensor("y", [1024, D], F32, kind="ExternalOutput").ap()
    dbg = {}
    if DEBUG:
        dbg["d_aT"] = nc.dram_tensor("d_aT", [128, 16 * LT], BF16, kind="ExternalOutput").ap()
        dbg["d_mix"] = nc.dram_tensor("d_mix", [128, 16 * NOWN], BF16, kind="ExternalOutput").ap()
        dbg["d_h1"] = nc.dram_tensor("d_h1", [128, 8 * D], F32, kind="ExternalOutput").ap()
        dbg["d_cum"] = nc.dram_tensor("d_cum", [128, NB * 8], F32, kind="ExternalOutput").ap()
        dbg["d_cT"] = nc.dram_tensor("d_cT", [128, 16 * NOWN], BF16, kind="ExternalOutput").ap()
        dbg["d_hh"] = nc.dram_tensor("d_hh", [2, D], F32, kind="ExternalOutput").ap()

    w_in_v = w_in.rearrange("(c p) n -> p c n", p=128)
    w_out_v = w_out.rearrange("(c p) n -> p c n", p=128)
    w_up_v = w_up.rearrange("(c p) n -> p c n", p=128)

    with contextlib.ExitStack() as es:
        def sb(name, shape, dt):
            return es.enter_context(nc.sbuf_tensor(name, list(shape), dt))

        def ps(name, shape, dt):
            return es.enter_context(nc.psum_tensor(name, list(shape), dt))

        R1 = sb("R1", [128, 16 * LT], BF16)
        aT = R1[:, :].rearrange("p (c t) -> p c t", c=16)
        R1f = R1.bitcast(F32)
        h2 = R1f[:, 0:8 * D].rearrange("p (b f) -> p b f", b=8)
        mixT = sb("mixT", [128, 16, NOWN], BF16)
        ringT = sb("ringT", [128, NSLOT * 2048], BF16)
        ring = [ringT[:, i * 2048:(i + 1) * 2048] for i in range(NSLOT)]
        R3N = 20480
        R3 = sb("R3", [128, R3N], BF16)
        R3f = R3.bitcast(F32)
        gb = sb("gb", [128, D], F32)
        ident_b = sb("ident_b", [128, 128], BF16)
        ones_b = sb("ones_b", [128, 128], BF16)
        maskT = sb("maskT", [128, MASKW], BF16)
        sel = sb("sel", [128, 1024], BF16)
        ident_f = sb("ident_f", [128, 128], F32)
        tri_f = sb("tri_f", [128, 128], F32)
        ones_f = sb("ones_f", [128, 128], F32)
        maskR = sb("maskR", [128, 128], F32)
        convw_s = sb("convw_s", [128, 2 * NFF * 3], F32)
        convb_s = sb("convb_s", [128, 2 * NFF], F32)
        bfg_s = sb("bfg_s", [128, NB * 8], F32)
        kb_s = sb("kb_s", [128, NB, 8], F32)
        cos_s = sb("cos_s", [128, NB, 64], F32)
        sin_s = sb("sin_s", [128, NB, 64], F32)
        dk_s = sb("dk_s", [128, NB, 8], F32)
        dq_s = sb("dq_s", [128, NB, 8], F32)
        spt = sb("spt", [128, NB * 8], F32)
        cumn = sb("cumn", [128, NB * 8], F32)
        tot = sb("tot", [128, NB * 8], F32)
        pre = sb("pre", [128, NB * 8], F32)
        biasK = sb("biasK", [128, NB * 8], F32)
        Rb = sb("Rb", [128, 9 * 128], BF16)
        wff = sb("wff", [128, 16, 8], BF16)
        st1 = sb("st1", [128, 32], F32)
        st2 = sb("st2", [128, 64], F32)

        pp = [ps("pp%d" % i, [128, 1024], F32) for i in range(4)]
        ppb = [p.bitcast(BF16) for p in pp]

        def bank(i):
            return pp[i // 2][:, (i % 2) * 512:(i % 2) * 512 + 512]

        P = Prog(nc)
        slot_ctr = [0]

        def next_slot():
            s = slot_ctr[0] % NSLOT
            slot_ctr[0] += 1
            return s

        def dma_sp(out, in_, reads=(), writes=(), stream="const"):
            P.op("sp", lambda e: e.dma_start(out=out, in_=in_), reads=reads, writes=writes, dma=stream)

        def dma_cast(out, in_, reads=(), writes=(), stream="constp", batch=None):
            P.op("pool", lambda e: e.dma_start(out=out, in_=in_), reads=reads, writes=writes, dma=stream, batch=batch)

        def load_slice(wv, c0, ncols=128):
            s = next_slot()
            v = ring[s][:, 0:16 * ncols].rearrange("p (c n) -> p c n", c=16)
            for hh in range(2):
                dma_cast(v[:, 8 * hh:8 * hh + 8, :], wv[:, 8 * hh:8 * hh + 8, c0:c0 + ncols], writes=[("ring", s, hh)], stream="ring%d" % s,
                         batch=slot_ctr[0])
            return s, v

        def rkeys(s):
            return [("ring", s, 0), ("ring", s, 1)]

        dma_sp(ident_f[:], c_ident, writes=["ident_f"])
        dma_sp(tri_f[:], c_tri, writes=["tri_f"])
        dma_sp(ones_f[:], c_ones, writes=["ones_f"])
        dma_sp(maskR[:], c_maskR, writes=["maskR"])
        dma_cast(ident_b[:], c_ident, writes=["ident_b"])
        dma_cast(ones_b[:], c_ones, writes=["ones_b"])
        dma_cast(maskT[:], c_maskT, writes=["maskT"])
        dma_cast(sel[:], c_sel, writes=["sel"])
        dma_sp(convw_s[:], convw, writes=["convw"])
        dma_sp(convb_s[:], convb, writes=["convb"])
        dma_sp(bfg_s[:], bfg.partition_broadcast(128), writes=["bfg"])
        dma_sp(kb_s[:, :, :].rearrange("p b h -> p (b h)"), kbT, writes=["kb"])
        dma_sp(cos_s[:, :, :].rearrange("p b d -> p (b d)"), cosT, writes=["cos"])
        dma_sp(sin_s[:, :, :].rearrange("p b d -> p (b d)"), sinT, writes=["sin"])
        dma_sp(dk_s[:, :, :].rearrange("p b h -> p (b h)"), dkT, writes=["dk"])
        dma_sp(dq_s[:, :, :].rearrange("p b h -> p (b h)"), dqT, writes=["dq"])
        dma_sp(gb[:], g1.partition_broadcast(128), writes=["gb"], stream="gb")
        dma_cast(wff[:, :, :].rearrange("p c n -> p (c n)"), wffd, writes=[("wff", 0), ("wff", 1)])
        P.op("dve", lambda e: e.memset(pre[:], 0.0), writes=["pre"])

        def rms_rstd(src_ap, junk_ap, col, rkeys_, jkey, extra_writes=()):
            npart = src_ap.shape[0]
            c = st1[0:npart, col:col + 1]
            P.op("act", lambda e: e.activation(out=junk_ap, in_=src_ap, func=AF.Square, accum_out=c),
                 reads=rkeys_, writes=[jkey, ("st1", col)] + list(extra_writes))
            P.op("dve", lambda e: e.tensor_scalar(out=c, in0=c, scalar1=1.0 / D, scalar2=EPS, op0=ALU.mult, op1=ALU.add),
                 reads=[("st1", col)], writes=[("st1", col)])
            P.op("act", lambda e: e.sqrt(out=c, in_=c), reads=[("st1", col)], writes=[("st1", col)])
            P.op("dve", lambda e: e.reciprocal(out=c, in_=c), reads=[("st1", col)], writes=[("st1", col)])
            return c

        xsA = [R3f[:, 0:2048], R3f[:, 2048:4096], R3f[:, 4096:6144]]
        xnA = [R3[:, 12288:14336], R3[:, 14336:16384]]
        junkA = R3[:, 16384:18432]
        b6 = bank(6)

        def aTk(tb):
            return [("aT", tb, 0), ("aT", tb, 1)]

        def A1(tb):
            xs = xsA[tb % 3]
            dma_sp(xs, xl[tb * 128:(tb + 1) * 128, :], writes=[("xs", tb % 3)], stream="xs%d" % (tb % 3))
            rms_rstd(xs, junkA, tb % 3, [("xs", tb % 3)], "junkA")

        def A2(tb):
            xs = xsA[tb % 3]
            c = st1[:, tb % 3:tb % 3 + 1]
            xn = xnA[tb % 2]
            P.op("dve", lambda e: e.scalar_tensor_tensor(out=xn, in0=xs, scalar=c, in1=gb[:], op0=ALU.mult, op1=ALU.mult),
                 reads=[("xs", tb % 3), ("st1", tb % 3), "gb"], writes=[("xnA", tb % 2)])

        def A3(tb):
            xn = xnA[tb % 2]
            pv = ppb[tb % 2]
            for cc in range(16):
                P.op("pe", lambda e, cc=cc: e.transpose(out=pv[:, cc * 128:(cc + 1) * 128], in_=xn[:, cc * 128:(cc + 1) * 128], identity=ident_b[:]),
                     reads=[("xnA", tb % 2), "ident_b"], writes=[("ps", 2 * (tb % 2)), ("ps", 2 * (tb % 2) + 1)])
            pv3 = pv[:, 0:2048].rearrange("p (c t) -> p c t", c=16)
            P.op("act", lambda e: e.copy(out=aT[:, 0:8, tb * 128:(tb + 1) * 128], in_=pv3[:, 0:8, :]),
                 reads=[("ps", 2 * (tb % 2))], writes=[("aT", tb, 0)])
            P.op("dve", lambda e: e.tensor_copy(out=aT[:, 8:16, tb * 128:(tb + 1) * 128], in_=pv3[:, 8:16, :]),
                 reads=[("ps", 2 * (tb % 2) + 1)], writes=[("aT", tb, 1)])

        def A4(tb):
            for kc in range(16):
                P.op("pe", lambda e, kc=kc: e.matmul(b6[:, tb * 8:tb * 8 + 8], lhsT=aT[:, kc, tb * 128:(tb + 1) * 128], rhs=wff[:, kc, :],
                                                    start=(kc == 0), stop=(kc == 15)),
                     reads=aTk(tb) + [("wff", 0), ("wff", 1)], writes=[("ps", 6)])

        for i in range(NB + 3):
            if i < NB:
                A1(i)
            if 0 <= i - 1 < NB:
                A2(i - 1)
            if 0 <= i - 2 < NB:
                A3(i - 2)
            if 0 <= i - 3 < NB:
                A4(i - 3)


        def aT_range_keys(l0, l1):
            ks = []
            for tb in range(l0 // 128, (l1 - 1) // 128 + 1):
                ks += aTk(tb)
            return ks

        if DEBUG:
            dma_sp(dbg["d_aT"], R1[:, :], reads=aT_range_keys(0, LT), stream="st")

        if STOP == "A":
            P.emit(final_wait_streams="st")
            return nc
        P.barrier()
        dma_sp(gb[:, 0:1024], rng.partition_broadcast(128), writes=["gb"], stream="gb")

        b7 = bank(7)
        P.op("dve", lambda e: e.tensor_tensor(out=spt[:], in0=b6[:, 0:NB * 8], in1=bfg_s[:], op=ALU.add), reads=[("ps", 6), "bfg"], writes=["spt"])
        P.op("act", lambda e: e.activation(out=spt[:], in_=spt[:], func=AF.Exp, scale=-1.0), reads=["spt"], writes=["spt"])
        P.op("act", lambda e: e.activation(out=spt[:], in_=spt[:], func=AF.Ln, bias=1.0, scale=1.0), reads=["spt"], writes=["spt"])
        P.op("pe", lambda e: e.matmul(b7[:, 0:136], lhsT=tri_f[:], rhs=spt[:], start=True, stop=True), reads=["spt", "tri_f"], writes=[("ps", 7)])
        P.op("pe", lambda e: e.matmul(b7[:, 136:272], lhsT=ones_f[:], rhs=spt[:], start=True, stop=True), reads=["spt", "ones_f"], writes=[("ps", 7)])
        P.op("dve", lambda e: e.tensor_copy(out=cumn[:], in_=b7[:, 0:136]), reads=[("ps", 7)], writes=["cumn"])
        P.op("dve", lambda e: e.tensor_copy(out=tot[:], in_=b7[:, 136:272]), reads=[("ps", 7)], writes=["tot"])
        for b in range(1, NB):
            P.op("dve", lambda e, b=b: e.tensor_tensor(out=pre[:, b * 8:b * 8 + 8], in0=pre[:, (b - 1) * 8:b * 8], in1=tot[:, (b - 1) * 8:b * 8], op=ALU.add),
                 reads=["pre", "tot"], writes=["pre"])
        P.op("dve", lambda e: e.tensor_tensor(out=cumn[:], in0=cumn[:], in1=pre[:], op=ALU.add), reads=["cumn", "pre"], writes=["cumn"])
        P.op("dve", lambda e: e.tensor_tensor(out=biasK[:], in0=cumn[:], in1=kb_s[:, :, :].rearrange("p b h -> p (b h)"), op=ALU.add),
             reads=["cumn", "kb"], writes=["biasK"])
        for c in range(9):
            tb = 8 + c
            dst = pp[2][0:8, c * 128:(c + 1) * 128] if c < 8 else pp[3][0:8, 512:640]
            P.op("pe", lambda e, dst=dst, tb=tb: e.transpose(out=dst, in_=cumn[:, tb * 8:tb * 8 + 8], identity=ident_f[:]),
                 reads=["cumn", "ident_f"], writes=[("ps", 4), ("ps", 5)] if c < 8 else [("ps", 7)])
        P.op("dve", lambda e: e.memset(Rb[:], 0.0), writes=["Rb"])
        P.op("act", lambda e: e.activation(out=Rb[0:8, 0:1024], in_=pp[2][0:8, 0:1024], func=AF.Copy, scale=-1.0 / SCALE),
             reads=[("ps", 4), ("ps", 5)], writes=["Rb"])
        P.op("act", lambda e: e.activation(out=Rb[0:8, 1024:1152], in_=pp[3][0:8, 512:640], func=AF.Copy, scale=-1.0 / SCALE),
             reads=[("ps", 7)], writes=["Rb"])
        if DEBUG:
            dma_sp(dbg["d_cum"], cumn[:], reads=["cumn"], stream="st")
        if STOP == "B0":
            P.emit(final_wait_streams="st")
            return nc

        o_ = 0
        def carve(n):
            nonlocal o_
            a = o_
            o_ += n
            return a
        fkT = R3[:, carve(LT):o_]
        _a = carve(1028)
        fqT = R3[:, _a:_a + NOWN]
        rvfv = R3[:, carve(NB * 256):o_].rearrange("p (b n) -> p b n", b=NB)
        rkt = R3[:, carve(NB * 128):o_].rearrange("p (b n) -> p b n", b=NB)
        rqT = R3[:, carve(1152):o_]
        rkT = R3[:, carve(1152):o_]
        sg = R3[:, carve(1152):o_].rearrange("p (b n) -> p b n", b=9)
        rqt = R3[:, carve(1152):o_].rearrange("p (b n) -> p b n", b=9)
        PTt = [R3[:, carve(342):o_] for _ in range(3)]
        smt = [R3[:, carve(128):o_] for _ in range(2)]
        Sbf = [R3[:, carve(128):o_] for _ in range(2)]
        junkB = R3[:, carve(128):o_]
        assert o_ % 2 == 0
        fo = o_ // 2
        def carvef(n):
            nonlocal fo
            a = fo
            fo += n
            return a
        o_all = R3f[:, carvef(1152):fo].rearrange("p (b n) -> p b n", b=9)
        rden = R3f[:, carvef(342):fo]
        rtA = R3f[:, carvef(64):fo]
        rtB = R3f[:, carvef(64):fo]
        ro = R3[:, fo * 2:fo * 2 + 1152].rearrange("p (b n) -> p b n", b=9)
        assert fo * 2 + 1152 <= R3N, fo * 2

        def rotary(src, dst, tbl, dec_ap, rk, wk):
            C = cos_s[:, tbl, :]
            S = sin_s[:, tbl, :]
            t1 = src[:, 0:64]
            t2 = src[:, 64:128]
            rd = list(rk) + ["cos", "sin", "dk", "dq"]
            P.op("dve", lambda e: e.scalar_tensor_tensor(out=rtA, in0=t1, scalar=dec_ap, in1=C, op0=ALU.mult, op1=ALU.mult), reads=rd, writes=["rtA"])
            P.op("dve", lambda e: e.scalar_tensor_tensor(out=rtB, in0=t2, scalar=dec_ap, in1=S, op0=ALU.mult, op1=ALU.mult), reads=rd, writes=["rtB"])
            P.op("dve", lambda e: e.tensor_tensor(out=dst[:, 0:64], in0=rtA, in1=rtB, op=ALU.subtract), reads=["rtA", "rtB"], writes=wk)
            P.op("dve", lambda e: e.scalar_tensor_tensor(out=rtA, in0=t1, scalar=dec_ap, in1=S, op0=ALU.mult, op1=ALU.mult), reads=rd + wk, writes=["rtA"])
            P.op("dve", lambda e: e.scalar_tensor_tensor(out=rtB, in0=t2, scalar=dec_ap, in1=C, op0=ALU.mult, op1=ALU.mult), reads=rd + wk, writes=["rtB"])
            P.op("dve", lambda e: e.tensor_tensor(out=dst[:, 64:128], in0=rtA, in1=rtB, op=ALU.add), reads=["rtA", "rtB"], writes=wk)


        vA = ringT[:, 0:6144].rearrange("p (c n) -> p c n", c=16)
        vB = ringT[:, 6144:10240].rearrange("p (c n) -> p c n", c=16)
        vC = ringT[:, 10240:12288].rearrange("p (c n) -> p c n", c=16)
        vD = ringT[:, 12288:14336].rearrange("p (c n) -> p c n", c=16)
        regions = {"A": (vA, 3), "B": (vB, 2), "C": (vC, 1), "D": (vD, 1)}

        def rg_keys(name):
            return [("rg" + name, si, hh) for si in range(regions[name][1]) for hh in range(2)]

        def load_region(name, col_offs, h):
            v, _ = regions[name]
            for si, c0 in enumerate(col_offs):
                for hh in range(2):
                    dma_cast(v[:, 8 * hh:8 * hh + 8, si * 128:(si + 1) * 128], w_in_v[:, 8 * hh:8 * hh + 8, c0:c0 + 128],
                             writes=[("rg" + name, si, hh)], stream="rg" + name, batch=h)

        all_rg = [k for nm in ("A", "B", "C", "D") for k in rg_keys(nm)]

        def load_wout_quarter(qp, extra=()):
            sl = []
            for m in range(4):
                s_ = next_slot()
                v_ = ring[s_].rearrange("p (c n) -> p c n", c=4)
                dma_cast(v_, w_out_v[:, 4 * m:4 * m + 4, qp * 512:(qp + 1) * 512], writes=rkeys(s_) + list(extra), stream="ring%d" % s_)
                sl.append((s_, v_))
            return sl
        wq = {}
        deferred_tail = [None]

        def load_head(h):
            load_region("A", [1024 + h * 128, 2048 + h * 128, 6144 + h * 128], h)
            load_region("B", [h * 128, 3072 + h * 128], h)
            load_region("C", [4096 + h * 128], h)
            load_region("D", [5120 + h * 128], h)
        load_head(0)
        for h in range(NHEADS_RUN):
            for tb in range(NB):
                bA = 2 * (tb % 2)
                bB = bA + 1
                for kc in range(16):
                    P.op("pe", lambda e, tb=tb, kc=kc, bA=bA: e.matmul(
                        bank(bA)[:, 0:384], lhsT=aT[:, kc, tb * 128:(tb + 1) * 128], rhs=vA[:, kc, :],
                        start=(kc == 0), stop=(kc == 15)),
                        reads=aTk(tb) + rg_keys("A"), writes=[("ps", bA)])
                    if tb >= 8:
                        P.op("pe", lambda e, tb=tb, kc=kc, bB=bB: e.matmul(
                            bank(bB)[:, 0:256], lhsT=aT[:, kc, tb * 128:(tb + 1) * 128], rhs=vB[:, kc, :],
                            start=(kc == 0), stop=(kc == 15)),
                            reads=aTk(tb) + rg_keys("B"), writes=[("ps", bB)])
                rotary(bank(bA), rkt[:, tb, :], tb, dk_s[:, tb, h:h + 1], [("ps", bA)], [("rkt", tb)])
                P.op("act", lambda e, tb=tb, bA=bA: e.copy(out=rvfv[:, tb, :], in_=bank(bA)[:, 128:384]), reads=[("ps", bA)], writes=[("rvfv", tb)])
                if tb >= 8:
                    rotary(bank(bB), rqt[:, tb - 8, :], tb, dq_s[:, tb, h:h + 1], [("ps", bB)], [("rqt", tb - 8)])
                    P.op("act", lambda e, tb=tb, bB=bB: e.copy(out=sg[:, tb - 8, :], in_=bank(bB)[:, 128:256]), reads=[("ps", bB)], writes=["sg"])
            P.op("act", lambda e: e.activation(out=sg[:, :, :], in_=sg[:, :, :], func=AF.Silu), reads=["sg"], writes=["sg"])
            if deferred_tail[0] is not None:
                deferred_tail[0]()
                deferred_tail[0] = None
            for nb in range(5):
                n0 = nb * 512
                nw = min(512, LT - n0)
                bk = 4 + nb % 2
                for kc in range(16):
                    P.op("pe", lambda e, kc=kc, n0=n0, nw=nw, bk=bk: e.matmul(bank(bk)[:, 0:nw], lhsT=vD[:, kc, :], rhs=aT[:, kc, n0:n0 + nw],
                                                                          start=(kc == 0), stop=(kc == 15)),
                         reads=aT_range_keys(n0, n0 + nw) + rg_keys("D"), writes=[("ps", bk)])
                if nb % 2 == 0:
                    P.op("act", lambda e, n0=n0, nw=nw, bk=bk: e.copy(out=fkT[:, n0:n0 + nw], in_=bank(bk)[:, 0:nw]), reads=[("ps", bk)], writes=[("fkT", nb)])
                else:
                    P.op("dve", lambda e, n0=n0, nw=nw, bk=bk: e.tensor_copy(out=fkT[:, n0:n0 + nw], in_=bank(bk)[:, 0:nw]), reads=[("ps", bk)], writes=[("fkT", nb)])
            for g in range(3):
                n0 = OWN0 + 342 * g
                bk = 4 + (g + 1) % 2
                for kc in range(16):
                    P.op("pe", lambda e, kc=kc, n0=n0, bk=bk: e.matmul(bank(bk)[:, 0:342], lhsT=vC[:, kc, :], rhs=aT[:, kc, n0:n0 + 342],
                                                                    start=(kc == 0), stop=(kc == 15)),
                         reads=aT_range_keys(n0, n0 + 342) + rg_keys("C"), writes=[("ps", bk)])
                P.op("act", lambda e, g=g, bk=bk: e.copy(out=fqT[:, 342 * g:342 * g + 342], in_=bank(bk)[:, 0:342]), reads=[("ps", bk)], writes=[("fqT", g)])

            if h + 1 < NHEADS_RUN:
                load_head(h + 1)
            else:
                wq[0] = load_wout_quarter(0, all_rg)
                wq[1] = load_wout_quarter(1, all_rg)
            for c in range(9):
                P.op("pe", lambda e, c=c: e.transpose(out=ppb[2][:, c * 128:(c + 1) * 128], in_=rqt[:, c, :], identity=ident_b[:]),
                     reads=[("rqt", c), "ident_b"], writes=[("ps", 4), ("ps", 5)])
                P.op("pe", lambda e, c=c: e.transpose(out=ppb[3][:, c * 128:(c + 1) * 128], in_=rkt[:, 8 + c, :], identity=ident_b[:]),
                     reads=[("rkt", 8 + c), "ident_b"], writes=[("ps", 6), ("ps", 7)])
            P.op("act", lambda e: e.copy(out=rqT, in_=ppb[2][:, 0:1152]), reads=[("ps", 4), ("ps", 5)], writes=["rqT"])
            P.op("dve", lambda e: e.tensor_copy(out=rkT, in_=ppb[3][:, 0:1152]), reads=[("ps", 6), ("ps", 7)], writes=["rkT"])
            Sps = bank(7)[:, 0:128]
            ret_steps = []

            def r_init():
                for b in range(8):
                    P.op("pe", lambda e, b=b: e.matmul(Sps, lhsT=rkt[:, b, :], rhs=rvfv[:, b, 0:128], start=(b == 0), stop=(b == 7), skip_group_check=True),
                         reads=[("rkt", b), ("rvfv", b)], writes=[("ps", 7)])
            ret_steps.append(r_init)

            def r_a(c):
                sTp = bank(6)[:, (c % 2) * 128:(c % 2) * 128 + 128]
                if c > 0:
                    P.op("pe", lambda e: e.matmul(Sps, lhsT=rkt[:, 7 + c, :], rhs=rvfv[:, 7 + c, 0:128], start=False, stop=True, skip_group_check=True),
                         reads=[("rkt", 7 + c), ("rvfv", 7 + c)], writes=[("ps", 7)])
                P.op("act", lambda e: e.copy(out=Sbf[c % 2], in_=Sps), reads=[("ps", 7)], writes=[("Sbf", c % 2)])
                P.op("pe", lambda e: e.matmul(sTp, lhsT=rkT[:, c * 128:(c + 1) * 128], rhs=rqT[:, c * 128:(c + 1) * 128], start=True, stop=True),
                     reads=["rkT", "rqT"], writes=[("ps", 6)])
                P.op("dve", lambda e: e.tensor_tensor(out=smt[c % 2], in0=sTp, in1=maskR[:], op=ALU.mult),
                     reads=[("ps", 6), "maskR"], writes=[("smt", c % 2)])

            def r_b(c):
                op_ = bank(4)[:, (c % 2) * 128:(c % 2) * 128 + 128]
                P.op("pe", lambda e: e.matmul(op_, lhsT=rqT[:, c * 128:(c + 1) * 128], rhs=Sbf[c % 2], start=True, stop=False),
                     reads=["rqT", ("Sbf", c % 2)], writes=[("ps", 4)])
                P.op("pe", lambda e: e.matmul(op_, lhsT=smt[c % 2], rhs=rvfv[:, 8 + c, 0:128], start=False, stop=True),
                     reads=[("smt", c % 2), ("rvfv", 8 + c)], writes=[("ps", 4)])
                P.op("dve", lambda e: e.tensor_copy(out=o_all[:, c, :], in_=op_), reads=[("ps", 4)], writes=[("o_all", c)])
                P.op("act", lambda e: e.activation(out=junkB, in_=op_, func=AF.Square, accum_out=st2[:, 16 + c:17 + c]),
                     reads=[("ps", 4)], writes=["junkB", ("st2q", c)])
            for c in range(9):
                ret_steps.append(lambda c=c: r_a(c))
                ret_steps.append(lambda c=c: r_b(c))

            def r_tail(h=h):
                oall_keys = [("o_all", c) for c in range(9)]
                sq_keys = [("st2q", c) for c in range(9)]
                mean = st2[:, 0:9]
                ssq = st2[:, 16:25]
                msq = st2[:, 32:41]
                rstd = st2[:, 48:57]
                P.op("dve", lambda e: e.reduce_sum(out=mean, in_=o_all[:, :, :], axis=AX.X), reads=oall_keys, writes=["st2m"])
                P.op("pool", lambda e: e.tensor_scalar_mul(out=mean, in0=mean, scalar1=1.0 / 128), reads=["st2m"], writes=["st2m"])
                P.op("pool", lambda e: e.tensor_tensor(out=msq, in0=mean, in1=mean, op=ALU.mult), reads=["st2m"], writes=["st2s"])
                P.op("pool", lambda e: e.tensor_scalar(out=rstd, in0=ssq, scalar1=1.0 / 128, scalar2=EPS, op0=ALU.mult, op1=ALU.add), reads=sq_keys, writes=["st2r"])
                P.op("pool", lambda e: e.tensor_tensor(out=rstd, in0=rstd, in1=msq, op=ALU.subtract), reads=["st2r", "st2s"], writes=["st2r"])
                P.op("act", lambda e: e.activation(out=rstd, in_=rstd, func=AF.Ln), reads=["st2r"], writes=["st2r"])
                P.op("act", lambda e: e.activation(out=rstd, in_=rstd, func=AF.Exp, scale=-0.5), reads=["st2r"], writes=["st2r"])
                for c in range(9):
                    P.op("pool", lambda e, c=c: e.tensor_scalar(out=o_all[:, c, :], in0=o_all[:, c, :], scalar1=st2[:, c:c + 1], scalar2=st2[:, 48 + c:49 + c],
                                                               op0=ALU.subtract, op1=ALU.mult),
                         reads=[("o_all", c), "st2m", "st2r"], writes=[("o_all", c)])
                    P.op("pool", lambda e, c=c: e.tensor_tensor(out=o_all[:, c, :], in0=o_all[:, c, :], in1=gb[:, h * 128:(h + 1) * 128], op=ALU.mult),
                         reads=[("o_all", c), "gb"], writes=[("o_all", c)])
                    P.op("pool", lambda e, c=c: e.tensor_tensor(out=ro[:, c, :], in0=o_all[:, c, :], in1=sg[:, c, :], op=ALU.mult),
                         reads=[("o_all", c), "sg"], writes=[("ro", c)])

            def r_tail_pe(h=h):
                for c in range(9):
                    P.op("pe", lambda e, c=c: e.transpose(out=ppb[2][:, c * 128:(c + 1) * 128], in_=ro[:, c, :], identity=ident_b[:]),
                         reads=[("ro", c), "ident_b"], writes=[("ps", 4), ("ps", 5)])
                P.op("act", lambda e: e.copy(out=mixT[:, h, :], in_=ppb[2][:, 126:1152]), reads=[("ps", 4), ("ps", 5)], writes=[("mixh", h)])

            tiles = []
            for g in range(3):
                q0 = OWN0 + 342 * g
                kmax = (q0 + 342 - 1) // 128
                for kb in range(kmax + 1):
                    tiles.append((g, kb, kmax, q0))
            oTp = bank(2)[:, 0:342]
            dnp = bank(3)[:, 0:342]

            def f_s(ti, h=h):
                g, kb, kmax, q0 = tiles[ti]
                sbk = (0, 1, 5)[ti % 3]
                sb_ = bank(sbk)[:, 0:342]
                delta = 128 * kb - q0
                need_mask = (128 * kb + 127) > q0
                P.op("pe", lambda e: e.matmul(sb_, lhsT=fkT[:, kb * 128:(kb + 1) * 128], rhs=fqT[:, 342 * g:342 * g + 342], start=True, stop=False),
                     reads=[("fkT", kb // 4), ("fqT", g)], writes=[("ps", sbk)])
                P.op("pe", lambda e: e.matmul(sb_, lhsT=sel[:, h * 128:(h + 1) * 128], rhs=Rb[:, 126 + 342 * g:126 + 342 * g + 342],
                                              start=False, stop=(not need_mask)),
                     reads=["sel", "Rb"], writes=[("ps", sbk)])
                if need_mask:
                    off = XOFF - delta
                    assert 0 <= off and off + 342 <= MASKW, off
                    P.op("pe", lambda e: e.matmul(sb_, lhsT=ident_b[:], rhs=maskT[:, off:off + 342], start=False, stop=True),
                         reads=["ident_b", "maskT"], writes=[("ps", sbk)])
                pt = PTt[ti % 3]
                P.op("act", lambda e: e.activation(out=pt, in_=sb_, func=AF.Exp, bias=biasK[:, kb * 8 + h:kb * 8 + h + 1], scale=SCALE),
                     reads=[("ps", sbk), "biasK"], writes=[("PT", ti % 3)])

            def f_pv(ti, h=h):
                g, kb, kmax, q0 = tiles[ti]
                pt = PTt[ti % 3]
                ptk = ("PT", ti % 3)
                P.op("pe", lambda e: e.matmul(oTp, lhsT=rvfv[:, kb, 128:256], rhs=pt, start=(kb == 0), stop=(kb == kmax)),
                     reads=[("rvfv", kb), ptk], writes=[("ps", 2)])
                P.op("pe", lambda e: e.matmul(dnp, lhsT=ones_b[:], rhs=pt, start=(kb == 0), stop=(kb == kmax)),
                     reads=["ones_b", ptk], writes=[("ps", 3)])
                if kb == kmax:
                    P.op("dve", lambda e: e.reciprocal(out=rden, in_=dnp), reads=[("ps", 3)], writes=["rden"])
                    P.op("dve", lambda e: e.tensor_tensor(out=mixT[:, 8 + h, 342 * g:342 * g + 342], in0=oTp, in1=rden, op=ALU.mult),
                         reads=[("ps", 2), "rden"], writes=[("mixf", h, g)])

            fox_steps = []
            nt_ = len(tiles)
            def f_first():
                f_s(0)
                f_s(1)
            fox_steps.append(f_first)
            for ti in range(nt_):
                def st(ti=ti):
                    if ti + 2 < nt_:
                        f_s(ti + 2)
                    f_pv(ti)
                fox_steps.append(st)
            fi = 0
            per = [3, 2]
            for ri, rs in enumerate(ret_steps):
                rs()
                k = 1 if ri == 0 else per[ri % 2]
                for _ in range(k):
                    if fi < len(fox_steps):
                        fox_steps[fi]()
                        fi += 1
            while fi < len(fox_steps):
                fox_steps[fi]()
                fi += 1
            r_tail()
            deferred_tail[0] = r_tail_pe
        deferred_tail[0]()


        mix_all = [("mixh", h) for h in range(NH)] + [("mixf", h, g) for h in range(NH) for g in range(3)]
        if DEBUG:
            dma_sp(dbg["d_mix"], mixT[:, :, :].rearrange("p c t -> p (c t)"), reads=mix_all, stream="st")
        if STOP == "B":
            P.emit(final_wait_streams="st")
            return nc

        P.barrier()

        dma_sp(gb[:], g2.partition_broadcast(128), writes=["gb"], stream="gb")
        xsC = [R3f[:, 0:512], R3f[:, 512:1024], R3f[:, 1024:1536]]
        hnC = [R3[:, 4096:6144], R3[:, 6144:8192]]
        junkC = R3[:, 8192:10240]
        hh = R3f[:, 6144:8192]
        blocks = [(-1, 0, OWN0)] + [(tb, 2 + 128 * tb, 1152 + 128 * tb) for tb in range(8)]
        xi = 0
        ubk = 0

        def c_norm1(bi, tb):
            src = hh[:, :] if tb < 0 else h2[:, tb, :]
            col = 4 + bi % 2
            hk = [("h1", bi, q) for q in range(4)]
            c = rms_rstd(src, junkC, col, hk, "junkC")
            hn = hnC[bi % 2]
            P.op("dve", lambda e: e.scalar_tensor_tensor(out=hn, in0=src, scalar=c, in1=gb[:], op0=ALU.mult, op1=ALU.mult),
                 reads=hk + [("st1", col), "gb"], writes=[("hnC", bi % 2)])

        def c_norm2(bi, tb, c0):
            hn = hnC[bi % 2]
            pv = ppb[2 + bi % 2]
            pk = [("ps", 4 + 2 * (bi % 2)), ("ps", 5 + 2 * (bi % 2))]
            for cc in range(16):
                P.op("pe", lambda e, cc=cc: e.transpose(out=pv[:, cc * 128:(cc + 1) * 128], in_=hn[:, cc * 128:(cc + 1) * 128], identity=ident_b[:]),
                     reads=[("hnC", bi % 2), "ident_b"], writes=pk)
            pv3 = pv[:, 0:2048].rearrange("p (c t) -> p c t", c=16)
            if tb < 0:
                P.op("act", lambda e: e.copy(out=mixT[:, :, 0:2], in_=pv3[:, :, 0:2]), reads=pk + mix_all, writes=[("cT", bi)])
            else:
                P.op("act", lambda e: e.copy(out=mixT[:, 0:8, c0:c0 + 128], in_=pv3[:, 0:8, :]), reads=pk[0:1] + mix_all, writes=[("cT", bi)])
                P.op("dve", lambda e: e.tensor_copy(out=mixT[:, 8:16, c0:c0 + 128], in_=pv3[:, 8:16, :]), reads=pk[1:2] + mix_all, writes=[("cTb", bi)])

        for qp in range(4):
            slots = wq[qp]
            pend = None
            for bi, (tb, c0, l0) in enumerate(blocks):
                xs = xsC[xi % 3]
                xk = ("xs", xi % 3)
                xstream = "xs%d" % (xi % 3)
                xi += 1
                dma_sp(xs, xl[l0:l0 + 128, qp * 512:(qp + 1) * 512], writes=[xk], stream=xstream)
                bk = ubk % 4
                ubk += 1
                ck = [("cT", bi), ("cTb", bi)] + ([("cT", 1), ("cTb", 1)] if tb < 0 else [])
                for kc in range(16):
                    s_, v_ = slots[kc // 4]
                    P.op("pe", lambda e, kc=kc, v_=v_, c0=c0, bk=bk: e.matmul(
                        bank(bk), lhsT=mixT[:, kc, c0:c0 + 128], rhs=v_[:, kc % 4, :], start=(kc == 0), stop=(kc == 15)),
                        reads=mix_all + ck + rkeys(s_), writes=[("ps", bk)])
                dst = hh[:, qp * 512:(qp + 1) * 512] if tb < 0 else h2[:, tb, qp * 512:(qp + 1) * 512]
                P.op("dve", lambda e, dst=dst, bk=bk, xs=xs: e.tensor_tensor(out=dst, in0=bank(bk), in1=xs, op=ALU.add),
                     reads=[("ps", bk), xk], writes=[("h1", bi, qp)])
                if qp == 3:
                    c_norm1(bi, tb)
                    if pend is not None:
                        c_norm2(*pend)
                    pend = (bi, tb, c0)
            if qp == 3:
                c_norm2(*pend)
            if qp + 2 < 4:
                wq[qp + 2] = load_wout_quarter(qp + 2)

        cT = mixT
        cT_all = [("cT", bi) for bi in range(9)] + [("cTb", bi) for bi in range(1, 9)]
        if DEBUG:
            dma_sp(dbg["d_h1"], R1f[:, 0:8 * D], reads=[("h1", bi, hp) for bi in range(1, 9) for hp in range(4)], stream="st")
        if DEBUG:
            dma_sp(dbg["d_cT"], mixT[:, :, :].rearrange("p c t -> p (c t)"), reads=cT_all, stream="st")
            dma_sp(dbg["d_hh"], hh[0:2, :], reads=[("h1", 0, q) for q in range(4)], stream="st")
        if STOP == "C":
            P.emit(final_wait_streams="st")
            return nc
        P.barrier()

        gated = [[R3[:, (gs * GRP + jj) * 1024:(gs * GRP + jj + 1) * 1024] for jj in range(GRP)] for gs in range(2)]
        fb = 2 * GRP * 1024 // 2
        Yg = [R3f[:, fb + i * 1024:fb + (i + 1) * 1024] for i in range(2)]
        Yv = [R3f[:, fb + 2048 + i * 1024:fb + 2048 + (i + 1) * 1024] for i in range(2)]
        sb0 = 2 * (fb + 4096)
        Sg = [R3[:, sb0 + i * 1024:sb0 + (i + 1) * 1024] for i in range(2)]
        assert sb0 + 2048 <= R3N
        nblk = [(0, 342), (342, 683), (683, 1024)]
        ub = [0]

        h2_keys = lambda tb: [("h2", tb, n) for n in range(4)]
        mixflat = mixT[:, :, :].rearrange("p c t -> p (c t)")
        otE = [mixflat[:, 0:4096].bitcast(F32), mixflat[:, 4096:8192].bitcast(F32)]
        junkE = mixflat[:, 8192:10240]

        def final_block(tb):
            col = 8 + tb % 2
            c = rms_rstd(h2[:, tb, :], junkE, col, h2_keys(tb), "junkE", extra_writes=cT_all)
            ot = otE[tb % 2]
            P.op("dve", lambda e: e.scalar_tensor_tensor(out=ot, in0=h2[:, tb, :], scalar=c, in1=gb[:], op0=ALU.mult, op1=ALU.mult),
                 reads=h2_keys(tb) + [("st1", col), "gb"], writes=[("ot", tb % 2)] + cT_all)
            dma_sp(y[tb * 128:(tb + 1) * 128, :], ot, reads=[("ot", tb % 2)], stream="st%d" % (tb % 2))

        def wdown_group(gi, dslots, last=False):
            gs = gi % 2
            t = 0
            for tb in range(8):
                for n in range(4):
                    bk = 4 + t % 4
                    t += 1
                    for jj in range(GRP):
                        s, v = dslots[jj]
                        P.op("pe", lambda e, jj=jj, v=v, tb=tb, n=n, bk=bk, gs=gs: e.matmul(
                            bank(bk), lhsT=gated[gs][jj][:, tb * 128:(tb + 1) * 128], rhs=v[:, n * 512:(n + 1) * 512],
                            start=(jj == 0), stop=(jj == GRP - 1)),
                            reads=[("gated", gs, jj)] + rkeys(s), writes=[("ps", bk)])
                    P.op("dve", lambda e, tb=tb, n=n, bk=bk: e.tensor_tensor(out=h2[:, tb, n * 512:(n + 1) * 512], in0=h2[:, tb, n * 512:(n + 1) * 512], in1=bank(bk), op=ALU.add),
                         reads=[("ps", bk), ("h2", tb, n)], writes=[("h2", tb, n)])
                if last:
                    final_block(tb)

        def load_down(gi):
            dslots = []
            for jj in range(GRP):
                j = gi * GRP + jj
                s = next_slot()
                dma_cast(ring[s][:, :], w_down[j * 128:(j + 1) * 128, :], writes=rkeys(s), stream="ring%d" % s)
                dslots.append((s, ring[s]))
            return dslots

        prev = None
        pair_i = 0
        for gi in range(NFF // GRP):
            gs = gi % 2
            for jj in range(GRP):
                j = gi * GRP + jj
                pi = pair_i % 2
                pair_i += 1
                for half in range(2):
                    cidx = half * NFF + j
                    s, v = load_slice(w_up_v, half * DFF + j * 128)
                    Y = (Yg if half == 0 else Yv)[pi]
                    yk = ("Y", half, pi)
                    for (r0, r1) in nblk:
                        ln = r1 - r0
                        bk = ub[0] % 4
                        ub[0] += 1
                        for kc in range(16):
                            P.op("pe", lambda e, kc=kc, v=v, r0=r0, ln=ln, bk=bk: e.matmul(bank(bk)[:, 0:ln + 2], lhsT=v[:, kc, :], rhs=cT[:, kc, r0:r0 + ln + 2],
                                                                                    start=(kc == 0), stop=(kc == 15)),
                                 reads=cT_all + rkeys(s), writes=[("ps", bk)])
                        u = bank(bk)
                        P.op("act", lambda e, u=u, Y=Y, r0=r0, r1=r1, ln=ln, cidx=cidx: e.activation(
                            out=Y[:, r0:r1], in_=u[:, 2:ln + 2], func=AF.Identity, bias=convb_s[:, cidx:cidx + 1], scale=convw_s[:, cidx * 3 + 2:cidx * 3 + 3]),
                            reads=[("ps", bk), "convw", "convb"], writes=[yk])
                        P.op("dve", lambda e, u=u, Y=Y, r0=r0, r1=r1, ln=ln, cidx=cidx: e.scalar_tensor_tensor(
                            out=Y[:, r0:r1], in0=u[:, 1:ln + 1], scalar=convw_s[:, cidx * 3 + 1:cidx * 3 + 2], in1=Y[:, r0:r1], op0=ALU.mult, op1=ALU.add),
                            reads=[("ps", bk), "convw", yk], writes=[yk])
                        P.op("dve", lambda e, u=u, Y=Y, r0=r0, r1=r1, ln=ln, cidx=cidx: e.scalar_tensor_tensor(
                            out=Y[:, r0:r1], in0=u[:, 0:ln], scalar=convw_s[:, cidx * 3:cidx * 3 + 1], in1=Y[:, r0:r1], op0=ALU.mult, op1=ALU.add),
                            reads=[("ps", bk), "convw", yk], writes=[yk])
                    if half == 0:
                        P.op("act", lambda e, Y=Y, pi=pi: e.activation(out=Sg[pi], in_=Y, func=AF.Silu), reads=[yk], writes=[("Sg", pi)])
                    else:
                        P.op("dve", lambda e, Y=Y, pi=pi, gs=gs, jj=jj: e.tensor_tensor(out=gated[gs][jj], in0=Y, in1=Sg[pi], op=ALU.mult),
                             reads=[yk, ("Sg", pi)], writes=[("gated", gs, jj)])
            if prev is not None:
                wdown_group(prev, load_down(prev))
            prev = gi
        dma_sp(gb[:], gf.partition_broadcast(128), writes=["gb"], stream="gb")
        wdown_group(prev, load_down(prev), last=True)

        P.emit(final_wait_streams="st")
    return nc


_NC_CACHE = {}


def _consts():
    c = {}
    c["c_ident"] = np.eye(128, dtype=np.float32)
    c["c_tri"] = np.triu(np.ones((128, 128), np.float32))
    c["c_ones"] = np.ones((128, 128), np.float32)
    c["c_maskR"] = np.triu(np.ones((128, 128), np.float32))
    p = np.arange(128)[:, None]
    xx = np.arange(MASKW)[None, :]
    c["c_maskT"] = np.where(xx - XOFF < p, NEG, 0.0).astype(np.float32)
    sel = np.zeros((128, 8, 128), np.float32)
    for h in range(8):
        sel[h, h, :] = 1.0
    c["c_sel"] = sel.reshape(128, 1024)
    return c


def _core_tables(T0):
    l = np.arange(LT)
    t = l - 1152 + T0
    valid = t >= 0
    inv_freq = 1.0 / (10000.0 ** (np.arange(0, 128, 2, dtype=np.float64) / 128.0))
    ang = np.where(valid, t, 0)[:, None].astype(np.float64) * inv_freq[None, :]
    def pm(a):
        n = a.shape[1]
        return np.ascontiguousarray(a.reshape(NB, 128, n).transpose(1, 0, 2).reshape(128, NB * n))
    tabs = {"cosT": pm(np.cos(ang).astype(np.float32)), "sinT": pm(np.sin(ang).astype(np.float32))}
    log_g = np.log1p(-np.exp2(-5.0 - np.arange(8, dtype=np.float64)))
    rel = (l - 1152).astype(np.float64)
    tabs["dqT"] = pm(np.exp(rel[:, None] * log_g[None, :]).astype(np.float32))
    tabs["dkT"] = pm((np.exp(-rel[:, None] * log_g[None, :]) * SCALE).astype(np.float32))
    tabs["kbT"] = pm(np.repeat(np.where(valid, 0.0, NEG).astype(np.float32)[:, None], 8, axis=1))
    return tabs


def kernel(x, meta_tokens, norm1_gain, w_in, b_forget, ret_norm_gain, w_out, norm2_gain, w_up,
           conv_w, conv_b, w_down, final_norm_gain):
    f32 = np.float32
    x = np.asarray(x, f32)
    B = x.shape[0]
    if "nc" not in _NC_CACHE:
        _NC_CACHE["nc"] = build_nc()
    nc = _NC_CACHE["nc"]
    consts = _consts()
    shared = {
        "w_in": np.ascontiguousarray(np.asarray(w_in, f32)[0]),
        "w_out": np.ascontiguousarray(np.asarray(w_out, f32)[0]),
        "w_up": np.ascontiguousarray(np.asarray(w_up, f32)[0]),
        "w_down": np.ascontiguousarray(np.asarray(w_down, f32)[0]),
        "g1": np.ascontiguousarray(np.asarray(norm1_gain, f32)[0]),
        "g2": np.ascontiguousarray(np.asarray(norm2_gain, f32)[0]),
        "gf": np.ascontiguousarray(np.asarray(final_norm_gain, f32)),
        "rng": np.ascontiguousarray(np.asarray(ret_norm_gain, f32)[0]),
        "wffd": np.ascontiguousarray(np.asarray(w_in, f32)[0][:, 7168:7176].reshape(16, 128, 8).transpose(1, 0, 2).reshape(128, 128)),
        "bfg": np.ascontiguousarray(np.tile(np.asarray(b_forget, f32)[0], NB)),
        "convw": np.ascontiguousarray(np.asarray(conv_w, f32)[0].reshape(3, 2 * NFF, 128).transpose(2, 1, 0).reshape(128, 2 * NFF * 3)),
        "convb": np.ascontiguousarray(np.asarray(conv_b, f32)[0].reshape(2 * NFF, 128).T),
    }
    shared.update(consts)
    meta = np.asarray(meta_tokens, f32)
    in_maps = []
    for core in range(8):
        b, s = core // 2, core % 2
        T0 = 16 + 1024 * s
        full = np.concatenate([meta, x[b]], axis=0)
        xl = np.zeros((LT, D), f32)
        t_lo = T0 - 1152
        src_lo = max(t_lo, 0)
        xl[src_lo - t_lo:, :] = full[src_lo:T0 + 1024]
        m = dict(shared)
        m["xl"] = xl
        m.update(_core_tables(T0))
        in_maps.append(m)
    res = run_bass_kernel_spmd(nc, in_maps[:NCORES_RUN], core_ids=list(range(NCORES_RUN)))
    out = np.zeros((B, 2048, D), f32)
    for core in range(NCORES_RUN):
        b, s = core // 2, core % 2
        out[b, 1024 * s:1024 * (s + 1), :] = res.results[core]["y"]
    if DEBUG:
        kernel.debug = res.results
    return out
```

```python
import contextlib
import numpy as np
import concourse.bass as bass
import concourse.mybir as mybir
from concourse.bass_utils import run_bass_kernel_spmd

F32 = mybir.dt.float32
BF16 = mybir.dt.bfloat16
AF = mybir.ActivationFunctionType
ALU = mybir.AluOpType
AX = mybir.AxisListType

D = 2048
NB = 17
LT = NB * 128
OWN0 = 1150
NOWN = 1026
NH = 8
DFF = 5632
NFF = 44
IN_DIM = 7176
SCALE = 128 ** -0.5
EPS = 1e-6
NEG = -30000.0
XOFF = 300
MASKW = 768
NSLOT = 8
GRP = 4
DEBUG = False
STOP = None
NHEADS_RUN = 8
NCORES_RUN = 8
STOP2 = None
SKIP = set()

ENGS = ("sp", "act", "pool", "dve", "pe")
SEM_LIMIT = 12000
WAIT_ALL_STREAMS = ("const", "constp")


class _Op:
    __slots__ = ("eng", "fn", "deps", "signal", "stream", "sem_i", "val", "inc", "idx", "batch")


class Prog:
    def __init__(self, nc):
        self.nc = nc
        self.ops = []
        self.eng_ops = {e: [] for e in ENGS}
        self.last_w = {}
        self.readers = {}
        self.last_in_stream = {}
        self.barrier_deps = set()

    def op(self, eng, fn, reads=(), writes=(), dma=None, batch=None):
        o = _Op()
        o.batch = batch
        o.eng = eng
        o.fn = fn
        o.idx = len(self.ops)
        o.stream = ("dma", dma) if dma is not None else ("eng", eng)
        o.inc = 16 if dma is not None else 1
        o.signal = dma is not None
        deps = set(self.barrier_deps)
        writes = list(writes) + [k for k in reads if isinstance(k, tuple) and k[0] == "ps"]
        reads = [k for k in reads if not (isinstance(k, tuple) and k[0] == "ps")]
        for k in reads:
            w = self.last_w.get(k)
            if w is not None:
                deps.add(w)
        for k in writes:
            w = self.last_w.get(k)
            if w is not None:
                deps.add(w)
            for r in self.readers.get(k, ()):
                deps.add(r)
        o.deps = deps
        for k in reads:
            self.readers.setdefault(k, []).append(o.idx)
        for k in writes:
            self.last_w[k] = o.idx
            self.readers[k] = []
        self.ops.append(o)
        self.eng_ops[eng].append(o)
        self.last_in_stream[o.stream] = o.idx
        return o

    def barrier(self):
        self.barrier_deps = set(self.last_in_stream.values())

    def emit(self, final_wait_streams=()):
        nc = self.nc
        ops = self.ops
        for o in ops:
            for d in o.deps:
                p = ops[d]
                if p.stream == ("eng", "pe") and o.eng == "pe":
                    continue
                p.signal = True
        streams = {}
        for o in ops:
            if not o.signal:
                continue
            st = streams.setdefault(o.stream, {"n": 0, "cur": 0})
            if st["cur"] + o.inc > SEM_LIMIT:
                st["n"] += 1
                st["cur"] = 0
            st["cur"] += o.inc
            o.sem_i = (o.stream, st["n"])
            o.val = st["cur"]
        batch_max = {}
        for o in ops:
            if o.signal and o.batch is not None:
                k = (o.sem_i, o.batch)
                batch_max[k] = max(batch_max.get(k, 0), o.val)
        sem_keys = []
        seen = set()
        for o in ops:
            if o.signal and o.sem_i not in seen:
                seen.add(o.sem_i)
                sem_keys.append(o.sem_i)
        with contextlib.ExitStack() as es:
            sems = {}
            for i, k in enumerate(sem_keys):
                sems[k] = es.enter_context(nc.semaphore("s%d" % i))
            last_val = {}
            for o in ops:
                if o.signal:
                    last_val[o.sem_i] = max(last_val.get(o.sem_i, 0), o.val)
            block = es.enter_context(nc.Block())
            handles = {"sp": block.sync, "act": block.scalar, "pool": block.gpsimd,
                       "dve": block.vector, "pe": block.tensor}

            def make(engname):
                def body(eng):
                    waited = {}
                    for o in self.eng_ops[engname]:
                        need = {}
                        for d in o.deps:
                            p = ops[d]
                            if not p.signal:
                                continue
                            if p.stream == ("eng", "pe") and engname == "pe":
                                continue
                            v_ = last_val[p.sem_i] if (p.stream[0] == "dma" and p.stream[1] in WAIT_ALL_STREAMS) else p.val
                            if p.batch is not None:
                                v_ = batch_max[(p.sem_i, p.batch)]
                            if v_ > need.get(p.sem_i, 0):
                                need[p.sem_i] = v_
                        for k, v in need.items():
                            if waited.get(k, 0) < v:
                                eng.wait_ge(sems[k], v)
                                waited[k] = v
                        ins = o.fn(eng)
                        if o.signal:
                            ins.then_inc(sems[o.sem_i], o.inc)
                    if engname == "sp":
                        for k in sem_keys:
                            if k[0][0] == "dma" and k[0][1].startswith(final_wait_streams):
                                eng.wait_ge(sems[k], last_val[k])
                return body

            for e in ENGS:
                handles[e](make(e))
        return len(ops)


def build_nc():
    nc = bass.Bass("TRN2", target_bir_lowering=False)

    def din(name, shape):
        return nc.dram_tensor(name, list(shape), F32, kind="ExternalInput").ap()

    xl = din("xl", [LT, D])
    w_in = din("w_in", [D, IN_DIM])
    w_out = din("w_out", [D, D])
    w_up = din("w_up", [D, 2 * DFF])
    w_down = din("w_down", [DFF, D])
    g1 = din("g1", [D]); g2 = din("g2", [D]); gf = din("gf", [D])
    rng = din("rng", [1024])
    bfg = din("bfg", [NB * 8])
    convw = din("convw", [128, 2 * NFF * 3])
    convb = din("convb", [128, 2 * NFF])
    cosT = din("cosT", [128, NB * 64]); sinT = din("sinT", [128, NB * 64])
    dkT = din("dkT", [128, NB * 8]); dqT = din("dqT", [128, NB * 8]); kbT = din("kbT", [128, NB * 8])
    wffd = din("wffd", [128, 16 * 8])
    c_ident = din("c_ident", [128, 128]); c_tri = din("c_tri", [128, 128]); c_ones = din("c_ones", [128, 128])
    c_maskR = din("c_maskR", [128, 128]); c_maskT = din("c_maskT", [128, MASKW]); c_sel = din("c_sel", [128, 1024])
    y = nc.dram_tensor("y", [1024, D], F32, kind="ExternalOutput").ap()
    dbg = {}
    if DEBUG:
        dbg["d_aT"] = nc.dram_tensor("d_aT", [128, 16 * LT], BF16, kind="ExternalOutput").ap()
        dbg["d_mix"] = nc.dram_tensor("d_mix", [128, 16 * NOWN], BF16, kind="ExternalOutput").ap()
        dbg["d_h1"] = nc.dram_tensor("d_h1", [128, 8 * D], F32, kind="ExternalOutput").ap()
        dbg["d_cum"] = nc.dram_tensor("d_cum", [128, NB * 8], F32, kind="ExternalOutput").ap()
        dbg["d_cT"] = nc.dram_tensor("d_cT", [128, 16 * NOWN], BF16, kind="ExternalOutput").ap()
        dbg["d_hh"] = nc.dram_tensor("d_hh", [2, D], F32, kind="ExternalOutput").ap()

    w_in_v = w_in.rearrange("(c p) n -> p c n", p=128)
    w_out_v = w_out.rearrange("(c p) n -> p c n", p=128)
    w_up_v = w_up.rearrange("(c p) n -> p c n", p=128)

    with contextlib.ExitStack() as es:
        def sb(name, shape, dt):
            return es.enter_context(nc.sbuf_tensor(name, list(shape), dt))

        def ps(name, shape, dt):
            return es.enter_context(nc.psum_tensor(name, list(shape), dt))

        R1 = sb("R1", [128, 16 * LT], BF16)
        aT = R1[:, :].rearrange("p (c t) -> p c t", c=16)
        R1f = R1.bitcast(F32)
        h2 = R1f[:, 0:8 * D].rearrange("p (b f) -> p b f", b=8)
        mixT = sb("mixT", [128, 16, NOWN], BF16)
        ringT = sb("ringT", [128, NSLOT * 2048], BF16)
        ring = [ringT[:, i * 2048:(i + 1) * 2048] for i in range(NSLOT)]
        R3N = 20480
        R3 = sb("R3", [128, R3N], BF16)
        R3f = R3.bitcast(F32)
        gb = sb("gb", [128, D], F32)
        ident_b = sb("ident_b", [128, 128], BF16)
        ones_b = sb("ones_b", [128, 128], BF16)
        maskT = sb("maskT", [128, MASKW], BF16)
        sel = sb("sel", [128, 1024], BF16)
        ident_f = sb("ident_f", [128, 128], F32)
        tri_f = sb("tri_f", [128, 128], F32)
        ones_f = sb("ones_f", [128, 128], F32)
        maskR = sb("maskR", [128, 128], F32)
        convw_s = sb("convw_s", [128, 2 * NFF * 3], F32)
        convb_s = sb("convb_s", [128, 2 * NFF], F32)
        bfg_s = sb("bfg_s", [128, NB * 8], F32)
        kb_s = sb("kb_s", [128, NB, 8], F32)
        cos_s = sb("cos_s", [128, NB, 64], F32)
        sin_s = sb("sin_s", [128, NB, 64], F32)
        dk_s = sb("dk_s", [128, NB, 8], F32)
        dq_s = sb("dq_s", [128, NB, 8], F32)
        spt = sb("spt", [128, NB * 8], F32)
        cumn = sb("cumn", [128, NB * 8], F32)
        tot = sb("tot", [128, NB * 8], F32)
        pre = sb("pre", [128, NB * 8], F32)
        biasK = sb("biasK", [128, NB * 8], F32)
        Rb = sb("Rb", [128, 9 * 128], BF16)
        wff = sb("wff", [128, 16, 8], BF16)
        st1 = sb("st1", [128, 32], F32)
        st2 = sb("st2", [128, 64], F32)

        pp = [ps("pp%d" % i, [128, 1024], F32) for i in range(4)]
        ppb = [p.bitcast(BF16) for p in pp]

        def bank(i):
            return pp[i // 2][:, (i % 2) * 512:(i % 2) * 512 + 512]

        P = Prog(nc)
        slot_ctr = [0]

        def next_slot():
            s = slot_ctr[0] % NSLOT
            slot_ctr[0] += 1
            return s

        def dma_sp(out, in_, reads=(), writes=(), stream="const"):
            P.op("sp", lambda e: e.dma_start(out=out, in_=in_), reads=reads, writes=writes, dma=stream)

        def dma_cast(out, in_, reads=(), writes=(), stream="constp", batch=None):
            P.op("pool", lambda e: e.dma_start(out=out, in_=in_), reads=reads, writes=writes, dma=stream, batch=batch)

        def load_slice(wv, c0, ncols=128):
            s = next_slot()
            v = ring[s][:, 0:16 * ncols].rearrange("p (c n) -> p c n", c=16)
            for hh in range(2):
                dma_cast(v[:, 8 * hh:8 * hh + 8, :], wv[:, 8 * hh:8 * hh + 8, c0:c0 + ncols], writes=[("ring", s, hh)], stream="ring%d" % s,
                         batch=slot_ctr[0])
            return s, v

        def rkeys(s):
            return [("ring", s, 0), ("ring", s, 1)]

        dma_sp(ident_f[:], c_ident, writes=["ident_f"])
        dma_sp(tri_f[:], c_tri, writes=["tri_f"])
        dma_sp(ones_f[:], c_ones, writes=["ones_f"])
        dma_sp(maskR[:], c_maskR, writes=["maskR"])
        dma_cast(ident_b[:], c_ident, writes=["ident_b"])
        dma_cast(ones_b[:], c_ones, writes=["ones_b"])
        dma_cast(maskT[:], c_maskT, writes=["maskT"])
        dma_cast(sel[:], c_sel, writes=["sel"])
        dma_sp(convw_s[:], convw, writes=["convw"])
        dma_sp(convb_s[:], convb, writes=["convb"])
        dma_sp(bfg_s[:], bfg.partition_broadcast(128), writes=["bfg"])
        dma_sp(kb_s[:, :, :].rearrange("p b h -> p (b h)"), kbT, writes=["kb"])
        dma_sp(cos_s[:, :, :].rearrange("p b d -> p (b d)"), cosT, writes=["cos"])
        dma_sp(sin_s[:, :, :].rearrange("p b d -> p (b d)"), sinT, writes=["sin"])
        dma_sp(dk_s[:, :, :].rearrange("p b h -> p (b h)"), dkT, writes=["dk"])
        dma_sp(dq_s[:, :, :].rearrange("p b h -> p (b h)"), dqT, writes=["dq"])
        dma_sp(gb[:], g1.partition_broadcast(128), writes=["gb"], stream="gb")
        dma_cast(wff[:, :, :].rearrange("p c n -> p (c n)"), wffd, writes=[("wff", 0), ("wff", 1)])
        P.op("dve", lambda e: e.memset(pre[:], 0.0), writes=["pre"])

        def rms_rstd(src_ap, junk_ap, col, rkeys_, jkey, extra_writes=()):
            npart = src_ap.shape[0]
            c = st1[0:npart, col:col + 1]
            P.op("act", lambda e: e.activation(out=junk_ap, in_=src_ap, func=AF.Square, accum_out=c),
                 reads=rkeys_, writes=[jkey, ("st1", col)] + list(extra_writes))
            P.op("dve", lambda e: e.tensor_scalar(out=c, in0=c, scalar1=1.0 / D, scalar2=EPS, op0=ALU.mult, op1=ALU.add),
                 reads=[("st1", col)], writes=[("st1", col)])
            P.op("act", lambda e: e.sqrt(out=c, in_=c), reads=[("st1", col)], writes=[("st1", col)])
            P.op("dve", lambda e: e.reciprocal(out=c, in_=c), reads=[("st1", col)], writes=[("st1", col)])
            return c

        xsA = [R3f[:, 0:2048], R3f[:, 2048:4096], R3f[:, 4096:6144]]
        xnA = [R3[:, 12288:14336], R3[:, 14336:16384]]
        junkA = R3[:, 16384:18432]
        b6 = bank(6)

        def aTk(tb):
            return [("aT", tb, 0), ("aT", tb, 1)]

        def A1(tb):
            xs = xsA[tb % 3]
            dma_sp(xs, xl[tb * 128:(tb + 1) * 128, :], writes=[("xs", tb % 3)], stream="xs%d" % (tb % 3))
            rms_rstd(xs, junkA, tb % 3, [("xs", tb % 3)], "junkA")

        def A2(tb):
            xs = xsA[tb % 3]
            c = st1[:, tb % 3:tb % 3 + 1]
            xn = xnA[tb % 2]
            P.op("dve", lambda e: e.scalar_tensor_tensor(out=xn, in0=xs, scalar=c, in1=gb[:], op0=ALU.mult, op1=ALU.mult),
                 reads=[("xs", tb % 3), ("st1", tb % 3), "gb"], writes=[("xnA", tb % 2)])

        def A3(tb):
            xn = xnA[tb % 2]
            pv = ppb[tb % 2]
            for cc in range(16):
                P.op("pe", lambda e, cc=cc: e.transpose(out=pv[:, cc * 128:(cc + 1) * 128], in_=xn[:, cc * 128:(cc + 1) * 128], identity=ident_b[:]),
                     reads=[("xnA", tb % 2), "ident_b"], writes=[("ps", 2 * (tb % 2)), ("ps", 2 * (tb % 2) + 1)])
            pv3 = pv[:, 0:2048].rearrange("p (c t) -> p c t", c=16)
            P.op("act", lambda e: e.copy(out=aT[:, 0:8, tb * 128:(tb + 1) * 128], in_=pv3[:, 0:8, :]),
                 reads=[("ps", 2 * (tb % 2))], writes=[("aT", tb, 0)])
            P.op("dve", lambda e: e.tensor_copy(out=aT[:, 8:16, tb * 128:(tb + 1) * 128], in_=pv3[:, 8:16, :]),
                 reads=[("ps", 2 * (tb % 2) + 1)], writes=[("aT", tb, 1)])

        def A4(tb):
            for kc in range(16):
                P.op("pe", lambda e, kc=kc: e.matmul(b6[:, tb * 8:tb * 8 + 8], lhsT=aT[:, kc, tb * 128:(tb + 1) * 128], rhs=wff[:, kc, :],
                                                    start=(kc == 0), stop=(kc == 15)),
                     reads=aTk(tb) + [("wff", 0), ("wff", 1)], writes=[("ps", 6)])

        for i in range(NB + 3):
            if i < NB:
                A1(i)
            if 0 <= i - 1 < NB:
                A2(i - 1)
            if 0 <= i - 2 < NB:
                A3(i - 2)
            if 0 <= i - 3 < NB:
                A4(i - 3)


        def aT_range_keys(l0, l1):
            ks = []
            for tb in range(l0 // 128, (l1 - 1) // 128 + 1):
                ks += aTk(tb)
            return ks

        if DEBUG:
            dma_sp(dbg["d_aT"], R1[:, :], reads=aT_range_keys(0, LT), stream="st")

        if STOP == "A":
            P.emit(final_wait_streams="st")
            return nc
        P.barrier()
        dma_sp(gb[:, 0:1024], rng.partition_broadcast(128), writes=["gb"], stream="gb")

        b7 = bank(7)
        P.op("dve", lambda e: e.tensor_tensor(out=spt[:], in0=b6[:, 0:NB * 8], in1=bfg_s[:], op=ALU.add), reads=[("ps", 6), "bfg"], writes=["spt"])
        P.op("act", lambda e: e.activation(out=spt[:], in_=spt[:], func=AF.Exp, scale=-1.0), reads=["spt"], writes=["spt"])
        P.op("act", lambda e: e.activation(out=spt[:], in_=spt[:], func=AF.Ln, bias=1.0, scale=1.0), reads=["spt"], writes=["spt"])
        P.op("pe", lambda e: e.matmul(b7[:, 0:136], lhsT=tri_f[:], rhs=spt[:], start=True, stop=True), reads=["spt", "tri_f"], writes=[("ps", 7)])
        P.op("pe", lambda e: e.matmul(b7[:, 136:272], lhsT=ones_f[:], rhs=spt[:], start=True, stop=True), reads=["spt", "ones_f"], writes=[("ps", 7)])
        P.op("dve", lambda e: e.tensor_copy(out=cumn[:], in_=b7[:, 0:136]), reads=[("ps", 7)], writes=["cumn"])
        P.op("dve", lambda e: e.tensor_copy(out=tot[:], in_=b7[:, 136:272]), reads=[("ps", 7)], writes=["tot"])
        for b in range(1, NB):
            P.op("dve", lambda e, b=b: e.tensor_tensor(out=pre[:, b * 8:b * 8 + 8], in0=pre[:, (b - 1) * 8:b * 8], in1=tot[:, (b - 1) * 8:b * 8], op=ALU.add),
                 reads=["pre", "tot"], writes=["pre"])
        P.op("dve", lambda e: e.tensor_tensor(out=cumn[:], in0=cumn[:], in1=pre[:], op=ALU.add), reads=["cumn", "pre"], writes=["cumn"])
        P.op("dve", lambda e: e.tensor_tensor(out=biasK[:], in0=cumn[:], in1=kb_s[:, :, :].rearrange("p b h -> p (b h)"), op=ALU.add),
             reads=["cumn", "kb"], writes=["biasK"])
        for c in range(9):
            tb = 8 + c
            dst = pp[2][0:8, c * 128:(c + 1) * 128] if c < 8 else pp[3][0:8, 512:640]
            P.op("pe", lambda e, dst=dst, tb=tb: e.transpose(out=dst, in_=cumn[:, tb * 8:tb * 8 + 8], identity=ident_f[:]),
                 reads=["cumn", "ident_f"], writes=[("ps", 4), ("ps", 5)] if c < 8 else [("ps", 7)])
        P.op("dve", lambda e: e.memset(Rb[:], 0.0), writes=["Rb"])
        P.op("act", lambda e: e.activation(out=Rb[0:8, 0:1024], in_=pp[2][0:8, 0:1024], func=AF.Copy, scale=-1.0 / SCALE),
             reads=[("ps", 4), ("ps", 5)], writes=["Rb"])
        P.op("act", lambda e: e.activation(out=Rb[0:8, 1024:1152], in_=pp[3][0:8, 512:640], func=AF.Copy, scale=-1.0 / SCALE),
             reads=[("ps", 7)], writes=["Rb"])
        if DEBUG:
            dma_sp(dbg["d_cum"], cumn[:], reads=["cumn"], stream="st")
        if STOP == "B0":
            P.emit(final_wait_streams="st")
            return nc

        o_ = 0
        def carve(n):
            nonlocal o_
            a = o_
            o_ += n
            return a
        fkT = R3[:, carve(LT):o_]
        _a = carve(1028)
        fqT = R3[:, _a:_a + NOWN]
        rvfv = R3[:, carve(NB * 256):o_].rearrange("p (b n) -> p b n", b=NB)
        rkt = R3[:, carve(NB * 128):o_].rearrange("p (b n) -> p b n", b=NB)
        rqT = R3[:, carve(1152):o_]
        rkT = R3[:, carve(1152):o_]
        sg = R3[:, carve(1152):o_].rearrange("p (b n) -> p b n", b=9)
        rqt = R3[:, carve(1152):o_].rearrange("p (b n) -> p b n", b=9)
        PTt = [R3[:, carve(342):o_] for _ in range(3)]
        smt = [R3[:, carve(128):o_] for _ in range(2)]
        Sbf = [R3[:, carve(128):o_] for _ in range(2)]
        junkB = R3[:, carve(128):o_]
        assert o_ % 2 == 0
        fo = o_ // 2
        def carvef(n):
            nonlocal fo
            a = fo
            fo += n
            return a
        o_all = R3f[:, carvef(1152):fo].rearrange("p (b n) -> p b n", b=9)
        rden = R3f[:, carvef(342):fo]
        rtA = R3f[:, carvef(64):fo]
        rtB = R3f[:, carvef(64):fo]
        ro = R3[:, fo * 2:fo * 2 + 1152].rearrange("p (b n) -> p b n", b=9)
        assert fo * 2 + 1152 <= R3N, fo * 2

        def rotary(src, dst, tbl, dec_ap, rk, wk):
            C = cos_s[:, tbl, :]
            S = sin_s[:, tbl, :]
            t1 = src[:, 0:64]
            t2 = src[:, 64:128]
            rd = list(rk) + ["cos", "sin", "dk", "dq"]
            P.op("dve", lambda e: e.scalar_tensor_tensor(out=rtA, in0=t1, scalar=dec_ap, in1=C, op0=ALU.mult, op1=ALU.mult), reads=rd, writes=["rtA"])
            P.op("dve", lambda e: e.scalar_tensor_tensor(out=rtB, in0=t2, scalar=dec_ap, in1=S, op0=ALU.mult, op1=ALU.mult), reads=rd, writes=["rtB"])
            P.op("dve", lambda e: e.tensor_tensor(out=dst[:, 0:64], in0=rtA, in1=rtB, op=ALU.subtract), reads=["rtA", "rtB"], writes=wk)
            P.op("dve", lambda e: e.scalar_tensor_tensor(out=rtA, in0=t1, scalar=dec_ap, in1=S, op0=ALU.mult, op1=ALU.mult), reads=rd + wk, writes=["rtA"])
            P.op("dve", lambda e: e.scalar_tensor_tensor(out=rtB, in0=t2, scalar=dec_ap, in1=C, op0=ALU.mult, op1=ALU.mult), reads=rd + wk, writes=["rtB"])
            P.op("dve", lambda e: e.tensor_tensor(out=dst[:, 64:128], in0=rtA, in1=rtB, op=ALU.add), reads=["rtA", "rtB"], writes=wk)


        vA = ringT[:, 0:6144].rearrange("p (c n) -> p c n", c=16)
        vB = ringT[:, 6144:10240].rearrange("p (c n) -> p c n", c=16)
        vC = ringT[:, 10240:12288].rearrange("p (c n) -> p c n", c=16)
        vD = ringT[:, 12288:14336].rearrange("p (c n) -> p c n", c=16)
        regions = {"A": (vA, 3), "B": (vB, 2), "C": (vC, 1), "D": (vD, 1)}

        def rg_keys(name):
            return [("rg" + name, si, hh) for si in range(regions[name][1]) for hh in range(2)]

        def load_region(name, col_offs, h):
            v, _ = regions[name]
            for si, c0 in enumerate(col_offs):
                for hh in range(2):
                    dma_cast(v[:, 8 * hh:8 * hh + 8, si * 128:(si + 1) * 128], w_in_v[:, 8 * hh:8 * hh + 8, c0:c0 + 128],
                             writes=[("rg" + name, si, hh)], stream="rg" + name, batch=h)

        all_rg = [k for nm in ("A", "B", "C", "D") for k in rg_keys(nm)]

        def load_wout_quarter(qp, extra=()):
            sl = []
            for m in range(4):
                s_ = next_slot()
                v_ = ring[s_].rearrange("p (c n) -> p c n", c=4)
                dma_cast(v_, w_out_v[:, 4 * m:4 * m + 4, qp * 512:(qp + 1) * 512], writes=rkeys(s_) + list(extra), stream="ring%d" % s_)
                sl.append((s_, v_))
            return sl
        wq = {}
        deferred_tail = [None]

        def load_head(h):
            load_region("A", [1024 + h * 128, 2048 + h * 128, 6144 + h * 128], h)
            load_region("B", [h * 128, 3072 + h * 128], h)
            load_region("C", [4096 + h * 128], h)
            load_region("D", [5120 + h * 128], h)
        load_head(0)
        for h in range(NHEADS_RUN):
            for tb in range(NB):
                bA = 2 * (tb % 2)
                bB = bA + 1
                for kc in range(16):
                    P.op("pe", lambda e, tb=tb, kc=kc, bA=bA: e.matmul(
                        bank(bA)[:, 0:384], lhsT=aT[:, kc, tb * 128:(tb + 1) * 128], rhs=vA[:, kc, :],
                        start=(kc == 0), stop=(kc == 15)),
                        reads=aTk(tb) + rg_keys("A"), writes=[("ps", bA)])
                    if tb >= 8:
                        P.op("pe", lambda e, tb=tb, kc=kc, bB=bB: e.matmul(
                            bank(bB)[:, 0:256], lhsT=aT[:, kc, tb * 128:(tb + 1) * 128], rhs=vB[:, kc, :],
                            start=(kc == 0), stop=(kc == 15)),
                            reads=aTk(tb) + rg_keys("B"), writes=[("ps", bB)])
                P.op("act", lambda e, tb=tb, bA=bA: e.copy(out=rvfv[:, tb, :], in_=bank(bA)[:, 128:384]), reads=[("ps", bA)], writes=[("rvfv", tb)])
                if tb >= 8:
                    P.op("act", lambda e, tb=tb, bB=bB: e.copy(out=sg[:, tb - 8, :], in_=bank(bB)[:, 128:256]), reads=[("ps", bB)], writes=["sg"])
                rotary(bank(bA), rkt[:, tb, :], tb, dk_s[:, tb, h:h + 1], [("ps", bA)], [("rkt", tb)])
                if tb >= 8:
                    rotary(bank(bB), rqt[:, tb - 8, :], tb, dq_s[:, tb, h:h + 1], [("ps", bB)], [("rqt", tb - 8)])
            P.op("act", lambda e: e.activation(out=sg[:, :, :], in_=sg[:, :, :], func=AF.Silu), reads=["sg"], writes=["sg"])
            if deferred_tail[0] is not None:
                deferred_tail[0]()
                deferred_tail[0] = None
            for nb in range(5):
                n0 = nb * 512
                nw = min(512, LT - n0)
                bk = 4 + nb % 2
                for kc in range(16):
                    P.op("pe", lambda e, kc=kc, n0=n0, nw=nw, bk=bk: e.matmul(bank(bk)[:, 0:nw], lhsT=vD[:, kc, :], rhs=aT[:, kc, n0:n0 + nw],
                                                                          start=(kc == 0), stop=(kc == 15)),
                         reads=aT_range_keys(n0, n0 + nw) + rg_keys("D"), writes=[("ps", bk)])
                if nb % 2 == 0:
                    P.op("act", lambda e, n0=n0, nw=nw, bk=bk: e.copy(out=fkT[:, n0:n0 + nw], in_=bank(bk)[:, 0:nw]), reads=[("ps", bk)], writes=[("fkT", nb)])
                else:
                    P.op("dve", lambda e, n0=n0, nw=nw, bk=bk: e.tensor_copy(out=fkT[:, n0:n0 + nw], in_=bank(bk)[:, 0:nw]), reads=[("ps", bk)], writes=[("fkT", nb)])
            for g in range(3):
                n0 = OWN0 + 342 * g
                bk = 4 + (g + 1) % 2
                for kc in range(16):
                    P.op("pe", lambda e, kc=kc, n0=n0, bk=bk: e.matmul(bank(bk)[:, 0:342], lhsT=vC[:, kc, :], rhs=aT[:, kc, n0:n0 + 342],
                                                                    start=(kc == 0), stop=(kc == 15)),
                         reads=aT_range_keys(n0, n0 + 342) + rg_keys("C"), writes=[("ps", bk)])
                P.op("act", lambda e, g=g, bk=bk: e.copy(out=fqT[:, 342 * g:342 * g + 342], in_=bank(bk)[:, 0:342]), reads=[("ps", bk)], writes=[("fqT", g)])

            if h + 1 < NHEADS_RUN:
                load_head(h + 1)
            else:
                wq[0] = load_wout_quarter(0, all_rg)
                wq[1] = load_wout_quarter(1, all_rg)
            for c in range(9):
                P.op("pe", lambda e, c=c: e.transpose(out=ppb[2][:, c * 128:(c + 1) * 128], in_=rqt[:, c, :], identity=ident_b[:]),
                     reads=[("rqt", c), "ident_b"], writes=[("ps", 4), ("ps", 5)])
                P.op("pe", lambda e, c=c: e.transpose(out=ppb[3][:, c * 128:(c + 1) * 128], in_=rkt[:, 8 + c, :], identity=ident_b[:]),
                     reads=[("rkt", 8 + c), "ident_b"], writes=[("ps", 6), ("ps", 7)])
            P.op("act", lambda e: e.copy(out=rqT, in_=ppb[2][:, 0:1152]), reads=[("ps", 4), ("ps", 5)], writes=["rqT"])
            P.op("dve", lambda e: e.tensor_copy(out=rkT, in_=ppb[3][:, 0:1152]), reads=[("ps", 6), ("ps", 7)], writes=["rkT"])
            Sps = bank(7)[:, 0:128]
            ret_steps = []

            def r_init():
                for b in range(8):
                    P.op("pe", lambda e, b=b: e.matmul(Sps, lhsT=rkt[:, b, :], rhs=rvfv[:, b, 0:128], start=(b == 0), stop=(b == 7), skip_group_check=True),
                         reads=[("rkt", b), ("rvfv", b)], writes=[("ps", 7)])
            ret_steps.append(r_init)

            def r_a(c):
                sTp = bank(6)[:, (c % 2) * 128:(c % 2) * 128 + 128]
                if c > 0:
                    P.op("pe", lambda e: e.matmul(Sps, lhsT=rkt[:, 7 + c, :], rhs=rvfv[:, 7 + c, 0:128], start=False, stop=True, skip_group_check=True),
                         reads=[("rkt", 7 + c), ("rvfv", 7 + c)], writes=[("ps", 7)])
                P.op("act", lambda e: e.copy(out=Sbf[c % 2], in_=Sps), reads=[("ps", 7)], writes=[("Sbf", c % 2)])
                P.op("pe", lambda e: e.matmul(sTp, lhsT=rkT[:, c * 128:(c + 1) * 128], rhs=rqT[:, c * 128:(c + 1) * 128], start=True, stop=True),
                     reads=["rkT", "rqT"], writes=[("ps", 6)])
                P.op("dve", lambda e: e.tensor_tensor(out=smt[c % 2], in0=sTp, in1=maskR[:], op=ALU.mult),
                     reads=[("ps", 6), "maskR"], writes=[("smt", c % 2)])

            def r_b(c):
                op_ = bank(4)[:, (c % 2) * 128:(c % 2) * 128 + 128]
                P.op("pe", lambda e: e.matmul(op_, lhsT=rqT[:, c * 128:(c + 1) * 128], rhs=Sbf[c % 2], start=True, stop=False),
                     reads=["rqT", ("Sbf", c % 2)], writes=[("ps", 4)])
                P.op("pe", lambda e: e.matmul(op_, lhsT=smt[c % 2], rhs=rvfv[:, 8 + c, 0:128], start=False, stop=True),
                     reads=[("smt", c % 2), ("rvfv", 8 + c)], writes=[("ps", 4)])
                P.op("dve", lambda e: e.tensor_copy(out=o_all[:, c, :], in_=op_), reads=[("ps", 4)], writes=[("o_all", c)])
                P.op("act", lambda e: e.activation(out=junkB, in_=op_, func=AF.Square, accum_out=st2[:, 16 + c:17 + c]),
                     reads=[("ps", 4)], writes=["junkB", ("st2q", c)])
            for c in range(9):
                ret_steps.append(lambda c=c: r_a(c))
                ret_steps.append(lambda c=c: r_b(c))

            def r_tail(h=h):
                tail_eng = "dve" if h == NHEADS_RUN - 1 else "pool"
                oall_keys = [("o_all", c) for c in range(9)]
                sq_keys = [("st2q", c) for c in range(9)]
                mean = st2[:, 0:9]
                ssq = st2[:, 16:25]
                msq = st2[:, 32:41]
                rstd = st2[:, 48:57]
                P.op("dve", lambda e: e.reduce_sum(out=mean, in_=o_all[:, :, :], axis=AX.X), reads=oall_keys, writes=["st2m"])
                P.op(tail_eng, lambda e: e.tensor_scalar_mul(out=mean, in0=mean, scalar1=1.0 / 128), reads=["st2m"], writes=["st2m"])
                P.op(tail_eng, lambda e: e.tensor_tensor(out=msq, in0=mean, in1=mean, op=ALU.mult), reads=["st2m"], writes=["st2s"])
                P.op(tail_eng, lambda e: e.tensor_scalar(out=rstd, in0=ssq, scalar1=1.0 / 128, scalar2=EPS, op0=ALU.mult, op1=ALU.add), reads=sq_keys, writes=["st2r"])
                P.op(tail_eng, lambda e: e.tensor_tensor(out=rstd, in0=rstd, in1=msq, op=ALU.subtract), reads=["st2r", "st2s"], writes=["st2r"])
                P.op("act", lambda e: e.activation(out=rstd, in_=rstd, func=AF.Ln), reads=["st2r"], writes=["st2r"])
                P.op("act", lambda e: e.activation(out=rstd, in_=rstd, func=AF.Exp, scale=-0.5), reads=["st2r"], writes=["st2r"])
                for c in range(9):
                    P.op(tail_eng, lambda e, c=c: e.tensor_scalar(out=o_all[:, c, :], in0=o_all[:, c, :], scalar1=st2[:, c:c + 1], scalar2=st2[:, 48 + c:49 + c],
                                                               op0=ALU.subtract, op1=ALU.mult),
                         reads=[("o_all", c), "st2m", "st2r"], writes=[("o_all", c)])
                    P.op(tail_eng, lambda e, c=c: e.tensor_tensor(out=o_all[:, c, :], in0=o_all[:, c, :], in1=gb[:, h * 128:(h + 1) * 128], op=ALU.mult),
                         reads=[("o_all", c), "gb"], writes=[("o_all", c)])
                    P.op(tail_eng, lambda e, c=c: e.tensor_tensor(out=ro[:, c, :], in0=o_all[:, c, :], in1=sg[:, c, :], op=ALU.mult),
                         reads=[("o_all", c), "sg"], writes=[("ro", c)])

            def r_tail_pe(h=h):
                for c in range(9):
                    P.op("pe", lambda e, c=c: e.transpose(out=ppb[2][:, c * 128:(c + 1) * 128], in_=ro[:, c, :], identity=ident_b[:]),
                         reads=[("ro", c), "ident_b"], writes=[("ps", 4), ("ps", 5)])
                P.op("act", lambda e: e.copy(out=mixT[:, h, :], in_=ppb[2][:, 126:1152]), reads=[("ps", 4), ("ps", 5)], writes=[("mixh", h)])

            tiles = []
            for g in range(3):
                q0 = OWN0 + 342 * g
                kmax = (q0 + 342 - 1) // 128
                for kb in range(kmax + 1):
                    tiles.append((g, kb, kmax, q0))
            oTp = bank(2)[:, 0:342]
            dnp = bank(3)[:, 0:342]

            def f_s(ti, h=h):
                g, kb, kmax, q0 = tiles[ti]
                sbk = (0, 1, 5)[ti % 3]
                sb_ = bank(sbk)[:, 0:342]
                delta = 128 * kb - q0
                need_mask = (128 * kb + 127) > q0
                P.op("pe", lambda e: e.matmul(sb_, lhsT=fkT[:, kb * 128:(kb + 1) * 128], rhs=fqT[:, 342 * g:342 * g + 342], start=True, stop=False),
                     reads=[("fkT", kb // 4), ("fqT", g)], writes=[("ps", sbk)])
                P.op("pe", lambda e: e.matmul(sb_, lhsT=sel[:, h * 128:(h + 1) * 128], rhs=Rb[:, 126 + 342 * g:126 + 342 * g + 342],
                                              start=False, stop=(not need_mask)),
                     reads=["sel", "Rb"], writes=[("ps", sbk)])
                if need_mask:
                    off = XOFF - delta
                    assert 0 <= off and off + 342 <= MASKW, off
                    P.op("pe", lambda e: e.matmul(sb_, lhsT=ident_b[:], rhs=maskT[:, off:off + 342], start=False, stop=True),
                         reads=["ident_b", "maskT"], writes=[("ps", sbk)])
                pt = PTt[ti % 3]
                P.op("act", lambda e: e.activation(out=pt, in_=sb_, func=AF.Exp, bias=biasK[:, kb * 8 + h:kb * 8 + h + 1], scale=SCALE),
                     reads=[("ps", sbk), "biasK"], writes=[("PT", ti % 3)])

            def f_pv(ti, h=h):
                g, kb, kmax, q0 = tiles[ti]
                pt = PTt[ti % 3]
                ptk = ("PT", ti % 3)
                P.op("pe", lambda e: e.matmul(oTp, lhsT=rvfv[:, kb, 128:256], rhs=pt, start=(kb == 0), stop=(kb == kmax)),
                     reads=[("rvfv", kb), ptk], writes=[("ps", 2)])
                P.op("pe", lambda e: e.matmul(dnp, lhsT=ones_b[:], rhs=pt, start=(kb == 0), stop=(kb == kmax)),
                     reads=["ones_b", ptk], writes=[("ps", 3)])
                if kb == kmax:
                    P.op("dve", lambda e: e.reciprocal(out=rden, in_=dnp), reads=[("ps", 3)], writes=["rden"])
                    P.op("dve", lambda e: e.tensor_tensor(out=mixT[:, 8 + h, 342 * g:342 * g + 342], in0=oTp, in1=rden, op=ALU.mult),
                         reads=[("ps", 2), "rden"], writes=[("mixf", h, g)])

            fox_steps = []
            nt_ = len(tiles)
            def f_first():
                f_s(0)
                f_s(1)
            fox_steps.append(f_first)
            for ti in range(nt_):
                def st(ti=ti):
                    if ti + 2 < nt_:
                        f_s(ti + 2)
                    f_pv(ti)
                fox_steps.append(st)
            fi = 0
            per = [3, 2]
            for ri, rs in enumerate(ret_steps):
                rs()
                k = 1 if ri == 0 else per[ri % 2]
                for _ in range(k):
                    if fi < len(fox_steps):
                        fox_steps[fi]()
                        fi += 1
            while fi < len(fox_steps):
                fox_steps[fi]()
                fi += 1
            r_tail()
            deferred_tail[0] = r_tail_pe
        deferred_tail[0]()


        mix_all = [("mixh", h) for h in range(NH)] + [("mixf", h, g) for h in range(NH) for g in range(3)]
        if DEBUG:
            dma_sp(dbg["d_mix"], mixT[:, :, :].rearrange("p c t -> p (c t)"), reads=mix_all, stream="st")
        if STOP == "B":
            P.emit(final_wait_streams="st")
            return nc

        P.barrier()

        dma_sp(gb[:], g2.partition_broadcast(128), writes=["gb"], stream="gb")
        xsC = [R3f[:, 0:512], R3f[:, 512:1024], R3f[:, 1024:1536]]
        hnC = [R3[:, 4096:6144], R3[:, 6144:8192]]
        junkC = R3[:, 8192:10240]
        hh = R3f[:, 6144:8192]
        blocks = [(-1, 0, OWN0)] + [(tb, 2 + 128 * tb, 1152 + 128 * tb) for tb in range(8)]
        xi = 0
        ubk = 0

        def c_norm1(bi, tb):
            src = hh[:, :] if tb < 0 else h2[:, tb, :]
            col = 4 + bi % 2
            hk = [("h1", bi, q) for q in range(4)]
            c = rms_rstd(src, junkC, col, hk, "junkC")
            hn = hnC[bi % 2]
            P.op("dve", lambda e: e.scalar_tensor_tensor(out=hn, in0=src, scalar=c, in1=gb[:], op0=ALU.mult, op1=ALU.mult),
                 reads=hk + [("st1", col), "gb"], writes=[("hnC", bi % 2)])

        def c_norm2(bi, tb, c0):
            hn = hnC[bi % 2]
            pv = ppb[2 + bi % 2]
            pk = [("ps", 4 + 2 * (bi % 2)), ("ps", 5 + 2 * (bi % 2))]
            for cc in range(16):
                P.op("pe", lambda e, cc=cc: e.transpose(out=pv[:, cc * 128:(cc + 1) * 128], in_=hn[:, cc * 128:(cc + 1) * 128], identity=ident_b[:]),
                     reads=[("hnC", bi % 2), "ident_b"], writes=pk)
            pv3 = pv[:, 0:2048].rearrange("p (c t) -> p c t", c=16)
            if tb < 0:
                P.op("act", lambda e: e.copy(out=mixT[:, :, 0:2], in_=pv3[:, :, 0:2]), reads=pk + mix_all, writes=[("cT", bi)])
            else:
                P.op("act", lambda e: e.copy(out=mixT[:, 0:8, c0:c0 + 128], in_=pv3[:, 0:8, :]), reads=pk[0:1] + mix_all, writes=[("cT", bi)])
                P.op("dve", lambda e: e.tensor_copy(out=mixT[:, 8:16, c0:c0 + 128], in_=pv3[:, 8:16, :]), reads=pk[1:2] + mix_all, writes=[("cTb", bi)])

        for qp in range(4):
            slots = wq[qp]
            pend = None
            for bi, (tb, c0, l0) in enumerate(blocks):
                xs = xsC[xi % 3]
                xk = ("xs", xi % 3)
                xstream = "xs%d" % (xi % 3)
                xi += 1
                dma_sp(xs, xl[l0:l0 + 128, qp * 512:(qp + 1) * 512], writes=[xk], stream=xstream)
                bk = ubk % 4
                ubk += 1
                ck = [("cT", bi), ("cTb", bi)] + ([("cT", 1), ("cTb", 1)] if tb < 0 else [])
                for kc in range(16):
                    s_, v_ = slots[kc // 4]
                    P.op("pe", lambda e, kc=kc, v_=v_, c0=c0, bk=bk: e.matmul(
                        bank(bk), lhsT=mixT[:, kc, c0:c0 + 128], rhs=v_[:, kc % 4, :], start=(kc == 0), stop=(kc == 15)),
                        reads=mix_all + ck + rkeys(s_), writes=[("ps", bk)])
                dst = hh[:, qp * 512:(qp + 1) * 512] if tb < 0 else h2[:, tb, qp * 512:(qp + 1) * 512]
                P.op("dve", lambda e, dst=dst, bk=bk, xs=xs: e.tensor_tensor(out=dst, in0=bank(bk), in1=xs, op=ALU.add),
                     reads=[("ps", bk), xk], writes=[("h1", bi, qp)])
                if qp == 3:
                    c_norm1(bi, tb)
                    if pend is not None:
                        c_norm2(*pend)
                    pend = (bi, tb, c0)
            if qp == 3:
                c_norm2(*pend)
            if qp + 2 < 4:
                wq[qp + 2] = load_wout_quarter(qp + 2)

        cT = mixT
        cT_all = [("cT", bi) for bi in range(9)] + [("cTb", bi) for bi in range(1, 9)]
        if DEBUG:
            dma_sp(dbg["d_h1"], R1f[:, 0:8 * D], reads=[("h1", bi, hp) for bi in range(1, 9) for hp in range(4)], stream="st")
        if DEBUG:
            dma_sp(dbg["d_cT"], mixT[:, :, :].rearrange("p c t -> p (c t)"), reads=cT_all, stream="st")
            dma_sp(dbg["d_hh"], hh[0:2, :], reads=[("h1", 0, q) for q in range(4)], stream="st")
        if STOP == "C":
            P.emit(final_wait_streams="st")
            return nc
        P.barrier()

        gated = [[R3[:, (gs * GRP + jj) * 1024:(gs * GRP + jj + 1) * 1024] for jj in range(GRP)] for gs in range(2)]
        fb = 2 * GRP * 1024 // 2
        Yg = [R3f[:, fb + i * 1024:fb + (i + 1) * 1024] for i in range(2)]
        Yv = [R3f[:, fb + 2048 + i * 1024:fb + 2048 + (i + 1) * 1024] for i in range(2)]
        sb0 = 2 * (fb + 4096)
        Sg = [R3[:, sb0 + i * 1024:sb0 + (i + 1) * 1024] for i in range(2)]
        assert sb0 + 2048 <= R3N
        nblk = [(0, 342), (342, 683), (683, 1024)]
        ub = [0]

        h2_keys = lambda tb: [("h2", tb, n) for n in range(4)]
        mixflat = mixT[:, :, :].rearrange("p c t -> p (c t)")
        otE = [mixflat[:, 0:4096].bitcast(F32), mixflat[:, 4096:8192].bitcast(F32)]
        junkE = mixflat[:, 8192:10240]

        def final_block(tb):
            col = 8 + tb % 2
            c = rms_rstd(h2[:, tb, :], junkE, col, h2_keys(tb), "junkE", extra_writes=cT_all)
            ot = otE[tb % 2]
            P.op("dve", lambda e: e.scalar_tensor_tensor(out=ot, in0=h2[:, tb, :], scalar=c, in1=gb[:], op0=ALU.mult, op1=ALU.mult),
                 reads=h2_keys(tb) + [("st1", col), "gb"], writes=[("ot", tb % 2)] + cT_all)
            dma_sp(y[tb * 128:(tb + 1) * 128, :], ot, reads=[("ot", tb % 2)], stream="st%d" % (tb % 2))

        def wdown_group(gi, dslots, last=False):
            gs = gi % 2
            t = 0
            for tb in range(8):
                for n in range(4):
                    bk = 4 + t % 4
                    t += 1
                    for jj in range(GRP):
                        s, v = dslots[jj]
                        P.op("pe", lambda e, jj=jj, v=v, tb=tb, n=n, bk=bk, gs=gs: e.matmul(
                            bank(bk), lhsT=gated[gs][jj][:, tb * 128:(tb + 1) * 128], rhs=v[:, n * 512:(n + 1) * 512],
                            start=(jj == 0), stop=(jj == GRP - 1)),
                            reads=[("gated", gs, jj)] + rkeys(s), writes=[("ps", bk)])
                    P.op("dve", lambda e, tb=tb, n=n, bk=bk: e.tensor_tensor(out=h2[:, tb, n * 512:(n + 1) * 512], in0=h2[:, tb, n * 512:(n + 1) * 512], in1=bank(bk), op=ALU.add),
                         reads=[("ps", bk), ("h2", tb, n)], writes=[("h2", tb, n)])
                if last:
                    final_block(tb)

        def load_down(gi):
            dslots = []
            for jj in range(GRP):
                j = gi * GRP + jj
                s = next_slot()
                dma_cast(ring[s][:, :], w_down[j * 128:(j + 1) * 128, :], writes=rkeys(s), stream="ring%d" % s)
                dslots.append((s, ring[s]))
            return dslots

        prev = None
        pair_i = 0
        for gi in range(NFF // GRP):
            gs = gi % 2
            for jj in range(GRP):
                j = gi * GRP + jj
                pi = pair_i % 2
                pair_i += 1
                for half in range(2):
                    cidx = half * NFF + j
                    s, v = load_slice(w_up_v, half * DFF + j * 128)
                    Y = (Yg if half == 0 else Yv)[pi]
                    yk = ("Y", half, pi)
                    for (r0, r1) in nblk:
                        ln = r1 - r0
                        bk = ub[0] % 4
                        ub[0] += 1
                        for kc in range(16):
                            P.op("pe", lambda e, kc=kc, v=v, r0=r0, ln=ln, bk=bk: e.matmul(bank(bk)[:, 0:ln + 2], lhsT=v[:, kc, :], rhs=cT[:, kc, r0:r0 + ln + 2],
                                                                                    start=(kc == 0), stop=(kc == 15)),
                                 reads=cT_all + rkeys(s), writes=[("ps", bk)])
                        u = bank(bk)
                        P.op("act", lambda e, u=u, Y=Y, r0=r0, r1=r1, ln=ln, cidx=cidx: e.activation(
                            out=Y[:, r0:r1], in_=u[:, 2:ln + 2], func=AF.Identity, bias=convb_s[:, cidx:cidx + 1], scale=convw_s[:, cidx * 3 + 2:cidx * 3 + 3]),
                            reads=[("ps", bk), "convw", "convb"], writes=[yk])
                        P.op("dve", lambda e, u=u, Y=Y, r0=r0, r1=r1, ln=ln, cidx=cidx: e.scalar_tensor_tensor(
                            out=Y[:, r0:r1], in0=u[:, 1:ln + 1], scalar=convw_s[:, cidx * 3 + 1:cidx * 3 + 2], in1=Y[:, r0:r1], op0=ALU.mult, op1=ALU.add),
                            reads=[("ps", bk), "convw", yk], writes=[yk])
                        P.op("dve", lambda e, u=u, Y=Y, r0=r0, r1=r1, ln=ln, cidx=cidx: e.scalar_tensor_tensor(
                            out=Y[:, r0:r1], in0=u[:, 0:ln], scalar=convw_s[:, cidx * 3:cidx * 3 + 1], in1=Y[:, r0:r1], op0=ALU.mult, op1=ALU.add),
                            reads=[("ps", bk), "convw", yk], writes=[yk])
                    if half == 0:
                        P.op("act", lambda e, Y=Y, pi=pi: e.activation(out=Sg[pi], in_=Y, func=AF.Silu), reads=[yk], writes=[("Sg", pi)])
                    else:
                        P.op("dve", lambda e, Y=Y, pi=pi, gs=gs, jj=jj: e.tensor_tensor(out=gated[gs][jj], in0=Y, in1=Sg[pi], op=ALU.mult),
                             reads=[yk, ("Sg", pi)], writes=[("gated", gs, jj)])
            if prev is not None:
                wdown_group(prev, load_down(prev))
            prev = gi
        dma_sp(gb[:], gf.partition_broadcast(128), writes=["gb"], stream="gb")
        wdown_group(prev, load_down(prev), last=True)

        P.emit(final_wait_streams="st")
    return nc


_NC_CACHE = {}


def _consts():
    c = {}
    c["c_ident"] = np.eye(128, dtype=np.float32)
    c["c_tri"] = np.triu(np.ones((128, 128), np.float32))
    c["c_ones"] = np.ones((128, 128), np.float32)
    c["c_maskR"] = np.triu(np.ones((128, 128), np.float32))
    p = np.arange(128)[:, None]
    xx = np.arange(MASKW)[None, :]
    c["c_maskT"] = np.where(xx - XOFF < p, NEG, 0.0).astype(np.float32)
    sel = np.zeros((128, 8, 128), np.float32)
    for h in range(8):
        sel[h, h, :] = 1.0
    c["c_sel"] = sel.reshape(128, 1024)
    return c


def _core_tables(T0):
    l = np.arange(LT)
    t = l - 1152 + T0
    valid = t >= 0
    inv_freq = 1.0 / (10000.0 ** (np.arange(0, 128, 2, dtype=np.float64) / 128.0))
    ang = np.where(valid, t, 0)[:, None].astype(np.float64) * inv_freq[None, :]
    def pm(a):
        n = a.shape[1]
        return np.ascontiguousarray(a.reshape(NB, 128, n).transpose(1, 0, 2).reshape(128, NB * n))
    tabs = {"cosT": pm(np.cos(ang).astype(np.float32)), "sinT": pm(np.sin(ang).astype(np.float32))}
    log_g = np.log1p(-np.exp2(-5.0 - np.arange(8, dtype=np.float64)))
    rel = (l - 1152).astype(np.float64)
    tabs["dqT"] = pm(np.exp(rel[:, None] * log_g[None, :]).astype(np.float32))
    tabs["dkT"] = pm((np.exp(-rel[:, None] * log_g[None, :]) * SCALE).astype(np.float32))
    tabs["kbT"] = pm(np.repeat(np.where(valid, 0.0, NEG).astype(np.float32)[:, None], 8, axis=1))
    return tabs


def kernel(x, meta_tokens, norm1_gain, w_in, b_forget, ret_norm_gain, w_out, norm2_gain, w_up,
           conv_w, conv_b, w_down, final_norm_gain):
    f32 = np.float32
    x = np.asarray(x, f32)
    B = x.shape[0]
    if "nc" not in _NC_CACHE:
        _NC_CACHE["nc"] = build_nc()
    nc = _NC_CACHE["nc"]
    consts = _consts()
    shared = {
        "w_in": np.ascontiguousarray(np.asarray(w_in, f32)[0]),
        "w_out": np.ascontiguousarray(np.asarray(w_out, f32)[0]),
        "w_up": np.ascontiguousarray(np.asarray(w_up, f32)[0]),
        "w_down": np.ascontiguousarray(np.asarray(w_down, f32)[0]),
        "g1": np.ascontiguousarray(np.asarray(norm1_gain, f32)[0]),
        "g2": np.ascontiguousarray(np.asarray(norm2_gain, f32)[0]),
        "gf": np.ascontiguousarray(np.asarray(final_norm_gain, f32)),
        "rng": np.ascontiguousarray(np.asarray(ret_norm_gain, f32)[0]),
        "wffd": np.ascontiguousarray(np.asarray(w_in, f32)[0][:, 7168:7176].reshape(16, 128, 8).transpose(1, 0, 2).reshape(128, 128)),
        "bfg": np.ascontiguousarray(np.tile(np.asarray(b_forget, f32)[0], NB)),
        "convw": np.ascontiguousarray(np.asarray(conv_w, f32)[0].reshape(3, 2 * NFF, 128).transpose(2, 1, 0).reshape(128, 2 * NFF * 3)),
        "convb": np.ascontiguousarray(np.asarray(conv_b, f32)[0].reshape(2 * NFF, 128).T),
    }
    shared.update(consts)
    meta = np.asarray(meta_tokens, f32)
    in_maps = []
    for core in range(8):
        b, s = core // 2, core % 2
        T0 = 16 + 1024 * s
        full = np.concatenate([meta, x[b]], axis=0)
        xl = np.zeros((LT, D), f32)
        t_lo = T0 - 1152
        src_lo = max(t_lo, 0)
        xl[src_lo - t_lo:, :] = full[src_lo:T0 + 1024]
        m = dict(shared)
        m["xl"] = xl
        m.update(_core_tables(T0))
        in_maps.append(m)
    res = run_bass_kernel_spmd(nc, in_maps[:NCORES_RUN], core_ids=list(range(NCORES_RUN)))
    out = np.zeros((B, 2048, D), f32)
    for core in range(NCORES_RUN):
        b, s = core // 2, core % 2
        out[b, 1024 * s:1024 * (s + 1), :] = res.results[core]["y"]
    if DEBUG:
        kernel.debug = res.results
    return out
```

```python
import contextlib
import numpy as np
import concourse.bass as bass
import concourse.mybir as mybir
from concourse.bass_utils import run_bass_kernel_spmd

F32 = mybir.dt.float32
BF16 = mybir.dt.bfloat16
AF = mybir.ActivationFunctionType
ALU = mybir.AluOpType
AX = mybir.AxisListType

D = 2048
NB = 17
LT = NB * 128
OWN0 = 1150
NOWN = 1026
NH = 8
DFF = 5632
NFF = 44
IN_DIM = 7176
SCALE = 128 ** -0.5
EPS = 1e-6
NEG = -30000.0
XOFF = 300
MASKW = 768
NSLOT = 8
GRP = 4
DEBUG = False
STOP = None
NHEADS_RUN = 8
NCORES_RUN = 8
STOP2 = None
SKIP = set()

ENGS = ("sp", "act", "pool", "dve", "pe")
SEM_LIMIT = 12000
WAIT_ALL_STREAMS = ("const", "constp")


class _Op:
    __slots__ = ("eng", "fn", "deps", "signal", "stream", "sem_i", "val", "inc", "idx", "batch")


class Prog:
    def __init__(self, nc):
        self.nc = nc
        self.ops = []
        self.eng_ops = {e: [] for e in ENGS}
        self.last_w = {}
        self.readers = {}
        self.last_in_stream = {}
        self.barrier_deps = set()

    def op(self, eng, fn, reads=(), writes=(), dma=None, batch=None):
        o = _Op()
        o.batch = batch
        o.eng = eng
        o.fn = fn
        o.idx = len(self.ops)
        o.stream = ("dma", dma) if dma is not None else ("eng", eng)
        o.inc = 16 if dma is not None else 1
        o.signal = dma is not None
        deps = set(self.barrier_deps)
        writes = list(writes) + [k for k in reads if isinstance(k, tuple) and k[0] == "ps"]
        reads = [k for k in reads if not (isinstance(k, tuple) and k[0] == "ps")]
        for k in reads:
            w = self.last_w.get(k)
            if w is not None:
                deps.add(w)
        for k in writes:
            w = self.last_w.get(k)
            if w is not None:
                deps.add(w)
            for r in self.readers.get(k, ()):
                deps.add(r)
        o.deps = deps
        for k in reads:
            self.readers.setdefault(k, []).append(o.idx)
        for k in writes:
            self.last_w[k] = o.idx
            self.readers[k] = []
        self.ops.append(o)
        self.eng_ops[eng].append(o)
        self.last_in_stream[o.stream] = o.idx
        return o

    def barrier(self):
        self.barrier_deps = set(self.last_in_stream.values())

    def emit(self, final_wait_streams=()):
        nc = self.nc
        ops = self.ops
        for o in ops:
            for d in o.deps:
                p = ops[d]
                if p.stream == ("eng", "pe") and o.eng == "pe":
                    continue
                p.signal = True
        streams = {}
        for o in ops:
            if not o.signal:
                continue
            st = streams.setdefault(o.stream, {"n": 0, "cur": 0})
            if st["cur"] + o.inc > SEM_LIMIT:
                st["n"] += 1
                st["cur"] = 0
            st["cur"] += o.inc
            o.sem_i = (o.stream, st["n"])
            o.val = st["cur"]
        batch_max = {}
        for o in ops:
            if o.signal and o.batch is not None:
                k = (o.sem_i, o.batch)
                batch_max[k] = max(batch_max.get(k, 0), o.val)
        sem_keys = []
        seen = set()
        for o in ops:
            if o.signal and o.sem_i not in seen:
                seen.add(o.sem_i)
                sem_keys.append(o.sem_i)
        with contextlib.ExitStack() as es:
            sems = {}
            for i, k in enumerate(sem_keys):
                sems[k] = es.enter_context(nc.semaphore("s%d" % i))
            last_val = {}
            for o in ops:
                if o.signal:
                    last_val[o.sem_i] = max(last_val.get(o.sem_i, 0), o.val)
            block = es.enter_context(nc.Block())
            handles = {"sp": block.sync, "act": block.scalar, "pool": block.gpsimd,
                       "dve": block.vector, "pe": block.tensor}

            def make(engname):
                def body(eng):
                    waited = {}
                    for o in self.eng_ops[engname]:
                        need = {}
                        for d in o.deps:
                            p = ops[d]
                            if not p.signal:
                                continue
                            if p.stream == ("eng", "pe") and engname == "pe":
                                continue
                            v_ = last_val[p.sem_i] if (p.stream[0] == "dma" and p.stream[1] in WAIT_ALL_STREAMS) else p.val
                            if p.batch is not None:
                                v_ = batch_max[(p.sem_i, p.batch)]
                            if v_ > need.get(p.sem_i, 0):
                                need[p.sem_i] = v_
                        for k, v in need.items():
                            if waited.get(k, 0) < v:
                                eng.wait_ge(sems[k], v)
                                waited[k] = v
                        ins = o.fn(eng)
                        if o.signal:
                            ins.then_inc(sems[o.sem_i], o.inc)
                    if engname == "sp":
                        for k in sem_keys:
                            if k[0][0] == "dma" and k[0][1].startswith(final_wait_streams):
                                eng.wait_ge(sems[k], last_val[k])
                return body

            for e in ENGS:
                handles[e](make(e))
        return len(ops)


def build_nc():
    nc = bass.Bass("TRN2", target_bir_lowering=False)

    def din(name, shape):
        return nc.dram_tensor(name, list(shape), F32, kind="ExternalInput").ap()

    xl = din("xl", [LT, D])
    w_in = din("w_in", [D, IN_DIM])
    w_out = din("w_out", [D, D])
    w_up = din("w_up", [D, 2 * DFF])
    w_down = din("w_down", [DFF, D])
    g1 = din("g1", [D]); g2 = din("g2", [D]); gf = din("gf", [D])
    rng = din("rng", [1024])
    bfg = din("bfg", [NB * 8])
    convw = din("convw", [128, 2 * NFF * 3])
    convb = din("convb", [128, 2 * NFF])
    cosT = din("cosT", [128, NB * 64]); sinT = din("sinT", [128, NB * 64])
    dkT = din("dkT", [128, NB * 8]); dqT = din("dqT", [128, NB * 8]); kbT = din("kbT", [128, NB * 8])
    wffd = din("wffd", [128, 16 * 8])
    c_ident = din("c_ident", [128, 128]); c_tri = din("c_tri", [128, 128]); c_ones = din("c_ones", [128, 128])
    c_maskR = din("c_maskR", [128, 128]); c_maskT = din("c_maskT", [128, MASKW]); c_sel = din("c_sel", [128, 1024])
    y = nc.dram_tensor("y", [1024, D], F32, kind="ExternalOutput").ap()
    dbg = {}
    if DEBUG:
        dbg["d_aT"] = nc.dram_tensor("d_aT", [128, 16 * LT], BF16, kind="ExternalOutput").ap()
        dbg["d_mix"] = nc.dram_tensor("d_mix", [128, 16 * NOWN], BF16, kind="ExternalOutput").ap()
        dbg["d_h1"] = nc.dram_tensor("d_h1", [128, 8 * D], F32, kind="ExternalOutput").ap()
        dbg["d_cum"] = nc.dram_tensor("d_cum", [128, NB * 8], F32, kind="ExternalOutput").ap()
        dbg["d_cT"] = nc.dram_tensor("d_cT", [128, 16 * NOWN], BF16, kind="ExternalOutput").ap()
        dbg["d_hh"] = nc.dram_tensor("d_hh", [2, D], F32, kind="ExternalOutput").ap()

    w_in_v = w_in.rearrange("(c p) n -> p c n", p=128)
    w_out_v = w_out.rearrange("(c p) n -> p c n", p=128)
    w_up_v = w_up.rearrange("(c p) n -> p c n", p=128)

    with contextlib.ExitStack() as es:
        def sb(name, shape, dt):
            return es.enter_context(nc.sbuf_tensor(name, list(shape), dt))

        def ps(name, shape, dt):
            return es.enter_context(nc.psum_tensor(name, list(shape), dt))

        R1 = sb("R1", [128, 16 * LT], BF16)
        aT = R1[:, :].rearrange("p (c t) -> p c t", c=16)
        R1f = R1.bitcast(F32)
        h2 = R1f[:, 0:8 * D].rearrange("p (b f) -> p b f", b=8)
        mixT = sb("mixT", [128, 16, NOWN], BF16)
        ringT = sb("ringT", [128, NSLOT * 2048], BF16)
        ring = [ringT[:, i * 2048:(i + 1) * 2048] for i in range(NSLOT)]
        R3N = 20480
        R3 = sb("R3", [128, R3N], BF16)
        R3f = R3.bitcast(F32)
        gb = sb("gb", [128, D], F32)
        ident_b = sb("ident_b", [128, 128], BF16)
        ones_b = sb("ones_b", [128, 128], BF16)
        maskT = sb("maskT", [128, MASKW], BF16)
        sel = sb("sel", [128, 1024], BF16)
        ident_f = sb("ident_f", [128, 128], F32)
        tri_f = sb("tri_f", [128, 128], F32)
        ones_f = sb("ones_f", [128, 128], F32)
        maskR = sb("maskR", [128, 128], F32)
        convw_s = sb("convw_s", [128, 2 * NFF * 3], F32)
        convb_s = sb("convb_s", [128, 2 * NFF], F32)
        bfg_s = sb("bfg_s", [128, NB * 8], F32)
        kb_s = sb("kb_s", [128, NB, 8], F32)
        cos_s = sb("cos_s", [128, NB, 64], F32)
        sin_s = sb("sin_s", [128, NB, 64], F32)
        dk_s = sb("dk_s", [128, NB, 8], F32)
        dq_s = sb("dq_s", [128, NB, 8], F32)
        spt = sb("spt", [128, NB * 8], F32)
        cumn = sb("cumn", [128, NB * 8], F32)
        tot = sb("tot", [128, NB * 8], F32)
        pre = sb("pre", [128, NB * 8], F32)
        biasK = sb("biasK", [128, NB * 8], F32)
        Rb = sb("Rb", [128, 9 * 128], BF16)
        wff = sb("wff", [128, 16, 8], BF16)
        st1 = sb("st1", [128, 32], F32)
        eps_t = sb("eps_t", [128, 1], F32)
        st2 = sb("st2", [128, 64], F32)

        pp = [ps("pp%d" % i, [128, 1024], F32) for i in range(4)]
        ppb = [p.bitcast(BF16) for p in pp]

        def bank(i):
            return pp[i // 2][:, (i % 2) * 512:(i % 2) * 512 + 512]

        P = Prog(nc)
        slot_ctr = [0]

        def next_slot():
            s = slot_ctr[0] % NSLOT
            slot_ctr[0] += 1
            return s

        def dma_sp(out, in_, reads=(), writes=(), stream="const"):
            P.op("sp", lambda e: e.dma_start(out=out, in_=in_), reads=reads, writes=writes, dma=stream)

        def dma_cast(out, in_, reads=(), writes=(), stream="constp", batch=None):
            P.op("pool", lambda e: e.dma_start(out=out, in_=in_), reads=reads, writes=writes, dma=stream, batch=batch)

        def load_slice(wv, c0, ncols=128):
            s = next_slot()
            v = ring[s][:, 0:16 * ncols].rearrange("p (c n) -> p c n", c=16)
            for hh in range(2):
                dma_cast(v[:, 8 * hh:8 * hh + 8, :], wv[:, 8 * hh:8 * hh + 8, c0:c0 + ncols], writes=[("ring", s, hh)], stream="ring%d" % s,
                         batch=slot_ctr[0])
            return s, v

        def rkeys(s):
            return [("ring", s, 0), ("ring", s, 1)]

        dma_sp(ident_f[:], c_ident, writes=["ident_f"])
        dma_sp(tri_f[:], c_tri, writes=["tri_f"])
        dma_sp(ones_f[:], c_ones, writes=["ones_f"])
        dma_sp(maskR[:], c_maskR, writes=["maskR"])
        dma_cast(ident_b[:], c_ident, writes=["ident_b"])
        dma_cast(ones_b[:], c_ones, writes=["ones_b"])
        dma_cast(maskT[:], c_maskT, writes=["maskT"])
        dma_cast(sel[:], c_sel, writes=["sel"])
        dma_sp(convw_s[:], convw, writes=["convw"])
        dma_sp(convb_s[:], convb, writes=["convb"])
        dma_sp(bfg_s[:], bfg.partition_broadcast(128), writes=["bfg"])
        dma_sp(kb_s[:, :, :].rearrange("p b h -> p (b h)"), kbT, writes=["kb"])
        dma_sp(cos_s[:, :, :].rearrange("p b d -> p (b d)"), cosT, writes=["cos"])
        dma_sp(sin_s[:, :, :].rearrange("p b d -> p (b d)"), sinT, writes=["sin"])
        dma_sp(dk_s[:, :, :].rearrange("p b h -> p (b h)"), dkT, writes=["dk"])
        dma_sp(dq_s[:, :, :].rearrange("p b h -> p (b h)"), dqT, writes=["dq"])
        dma_sp(gb[:], g1.partition_broadcast(128), writes=["gb"], stream="gb")
        dma_cast(wff[:, :, :].rearrange("p c n -> p (c n)"), wffd, writes=[("wff", 0), ("wff", 1)])
        P.op("dve", lambda e: e.memset(pre[:], 0.0), writes=["pre"])
        P.op("dve", lambda e: e.memset(eps_t[:], EPS), writes=["eps_t"])

        def rms_rstd(src_ap, junk_ap, col, rkeys_, jkey, extra_writes=()):
            npart = src_ap.shape[0]
            c = st1[0:npart, col:col + 1]
            P.op("act", lambda e: e.activation(out=junk_ap, in_=src_ap, func=AF.Square, accum_out=c),
                 reads=rkeys_, writes=[jkey, ("st1", col)] + list(extra_writes))
            P.op("act", lambda e: e.activation(out=c, in_=c, func=AF.Ln, bias=eps_t[0:npart, 0:1], scale=1.0 / D),
                 reads=[("st1", col), "eps_t"], writes=[("st1", col)])
            P.op("act", lambda e: e.activation(out=c, in_=c, func=AF.Exp, scale=-0.5), reads=[("st1", col)], writes=[("st1", col)])
            return c

        xsA = [R3f[:, 0:2048], R3f[:, 2048:4096], R3f[:, 4096:6144]]
        xnA = [R3[:, 12288:14336], R3[:, 14336:16384]]
        junkA = R3[:, 16384:18432]
        b6 = bank(6)

        def aTk(tb):
            return [("aT", tb, 0), ("aT", tb, 1)]

        def A1(tb):
            xs = xsA[tb % 3]
            dma_sp(xs, xl[tb * 128:(tb + 1) * 128, :], writes=[("xs", tb % 3)], stream="xs%d" % (tb % 3))
            rms_rstd(xs, junkA, tb % 3, [("xs", tb % 3)], "junkA")

        def A2(tb):
            xs = xsA[tb % 3]
            c = st1[:, tb % 3:tb % 3 + 1]
            xn = xnA[tb % 2]
            P.op("dve", lambda e: e.scalar_tensor_tensor(out=xn, in0=xs, scalar=c, in1=gb[:], op0=ALU.mult, op1=ALU.mult),
                 reads=[("xs", tb % 3), ("st1", tb % 3), "gb"], writes=[("xnA", tb % 2)])

        def A3(tb):
            xn = xnA[tb % 2]
            pv = ppb[tb % 2]
            for cc in range(16):
                P.op("pe", lambda e, cc=cc: e.transpose(out=pv[:, cc * 128:(cc + 1) * 128], in_=xn[:, cc * 128:(cc + 1) * 128], identity=ident_b[:]),
                     reads=[("xnA", tb % 2), "ident_b"], writes=[("ps", 2 * (tb % 2)), ("ps", 2 * (tb % 2) + 1)])
            pv3 = pv[:, 0:2048].rearrange("p (c t) -> p c t", c=16)
            P.op("act", lambda e: e.copy(out=aT[:, 0:8, tb * 128:(tb + 1) * 128], in_=pv3[:, 0:8, :]),
                 reads=[("ps", 2 * (tb % 2))], writes=[("aT", tb, 0)])
            P.op("dve", lambda e: e.tensor_copy(out=aT[:, 8:16, tb * 128:(tb + 1) * 128], in_=pv3[:, 8:16, :]),
                 reads=[("ps", 2 * (tb % 2) + 1)], writes=[("aT", tb, 1)])

        def A4(tb):
            for kc in range(16):
                P.op("pe", lambda e, kc=kc: e.matmul(b6[:, tb * 8:tb * 8 + 8], lhsT=aT[:, kc, tb * 128:(tb + 1) * 128], rhs=wff[:, kc, :],
                                                    start=(kc == 0), stop=(kc == 15)),
                     reads=aTk(tb) + [("wff", 0), ("wff", 1)], writes=[("ps", 6)])

        for i in range(NB + 3):
            if i < NB:
                A1(i)
            if 0 <= i - 1 < NB:
                A2(i - 1)
            if 0 <= i - 2 < NB:
                A3(i - 2)
            if 0 <= i - 3 < NB:
                A4(i - 3)


        def aT_range_keys(l0, l1):
            ks = []
            for tb in range(l0 // 128, (l1 - 1) // 128 + 1):
                ks += aTk(tb)
            return ks

        if DEBUG:
            dma_sp(dbg["d_aT"], R1[:, :], reads=aT_range_keys(0, LT), stream="st")

        if STOP == "A":
            P.emit(final_wait_streams="st")
            return nc
        P.barrier()
        dma_sp(gb[:, 0:1024], rng.partition_broadcast(128), writes=["gb"], stream="gb")

        b7 = bank(7)
        P.op("dve", lambda e: e.tensor_tensor(out=spt[:], in0=b6[:, 0:NB * 8], in1=bfg_s[:], op=ALU.add), reads=[("ps", 6), "bfg"], writes=["spt"])
        P.op("act", lambda e: e.activation(out=spt[:], in_=spt[:], func=AF.Exp, scale=-1.0), reads=["spt"], writes=["spt"])
        P.op("act", lambda e: e.activation(out=spt[:], in_=spt[:], func=AF.Ln, bias=1.0, scale=1.0), reads=["spt"], writes=["spt"])
        P.op("pe", lambda e: e.matmul(b7[:, 0:136], lhsT=tri_f[:], rhs=spt[:], start=True, stop=True), reads=["spt", "tri_f"], writes=[("ps", 7)])
        P.op("pe", lambda e: e.matmul(b7[:, 136:272], lhsT=ones_f[:], rhs=spt[:], start=True, stop=True), reads=["spt", "ones_f"], writes=[("ps", 7)])
        P.op("dve", lambda e: e.tensor_copy(out=cumn[:], in_=b7[:, 0:136]), reads=[("ps", 7)], writes=["cumn"])
        P.op("dve", lambda e: e.tensor_copy(out=tot[:], in_=b7[:, 136:272]), reads=[("ps", 7)], writes=["tot"])
        for b in range(1, NB):
            P.op("dve", lambda e, b=b: e.tensor_tensor(out=pre[:, b * 8:b * 8 + 8], in0=pre[:, (b - 1) * 8:b * 8], in1=tot[:, (b - 1) * 8:b * 8], op=ALU.add),
                 reads=["pre", "tot"], writes=["pre"])
        P.op("dve", lambda e: e.tensor_tensor(out=cumn[:], in0=cumn[:], in1=pre[:], op=ALU.add), reads=["cumn", "pre"], writes=["cumn"])
        P.op("dve", lambda e: e.tensor_tensor(out=biasK[:], in0=cumn[:], in1=kb_s[:, :, :].rearrange("p b h -> p (b h)"), op=ALU.add),
             reads=["cumn", "kb"], writes=["biasK"])
        for c in range(9):
            tb = 8 + c
            dst = pp[2][0:8, c * 128:(c + 1) * 128] if c < 8 else pp[3][0:8, 512:640]
            P.op("pe", lambda e, dst=dst, tb=tb: e.transpose(out=dst, in_=cumn[:, tb * 8:tb * 8 + 8], identity=ident_f[:]),
                 reads=["cumn", "ident_f"], writes=[("ps", 4), ("ps", 5)] if c < 8 else [("ps", 7)])
        P.op("dve", lambda e: e.memset(Rb[:], 0.0), writes=["Rb"])
        P.op("act", lambda e: e.activation(out=Rb[0:8, 0:1024], in_=pp[2][0:8, 0:1024], func=AF.Copy, scale=-1.0 / SCALE),
             reads=[("ps", 4), ("ps", 5)], writes=["Rb"])
        P.op("act", lambda e: e.activation(out=Rb[0:8, 1024:1152], in_=pp[3][0:8, 512:640], func=AF.Copy, scale=-1.0 / SCALE),
             reads=[("ps", 7)], writes=["Rb"])
        if DEBUG:
            dma_sp(dbg["d_cum"], cumn[:], reads=["cumn"], stream="st")
        if STOP == "B0":
            P.emit(final_wait_streams="st")
            return nc

        o_ = 0
        def carve(n):
            nonlocal o_
            a = o_
            o_ += n
            return a
        fkT = R3[:, carve(LT):o_]
        _a = carve(1028)
        fqT = R3[:, _a:_a + NOWN]
        rvfv = R3[:, carve(NB * 256):o_].rearrange("p (b n) -> p b n", b=NB)
        rkt = R3[:, carve(NB * 128):o_].rearrange("p (b n) -> p b n", b=NB)
        rqT = R3[:, carve(1152):o_]
        rkT = R3[:, carve(1152):o_]
        sg = R3[:, carve(1152):o_].rearrange("p (b n) -> p b n", b=9)
        rqt = R3[:, carve(1152):o_].rearrange("p (b n) -> p b n", b=9)
        PTt = [R3[:, carve(342):o_] for _ in range(3)]
        smt = [R3[:, carve(128):o_] for _ in range(2)]
        Sbf = [R3[:, carve(128):o_] for _ in range(2)]
        junkB = R3[:, carve(128):o_]
        assert o_ % 2 == 0
        fo = o_ // 2
        def carvef(n):
            nonlocal fo
            a = fo
            fo += n
            return a
        o_all = R3f[:, carvef(1152):fo].rearrange("p (b n) -> p b n", b=9)
        rden = R3f[:, carvef(342):fo]
        rtA = R3f[:, carvef(64):fo]
        rtB = R3f[:, carvef(64):fo]
        ro = R3[:, fo * 2:fo * 2 + 1152].rearrange("p (b n) -> p b n", b=9)
        assert fo * 2 + 1152 <= R3N, fo * 2

        def rotary(src, dst, tbl, dec_ap, rk, wk):
            C = cos_s[:, tbl, :]
            S = sin_s[:, tbl, :]
            t1 = src[:, 0:64]
            t2 = src[:, 64:128]
            rd = list(rk) + ["cos", "sin", "dk", "dq"]
            P.op("dve", lambda e: e.scalar_tensor_tensor(out=rtA, in0=t1, scalar=dec_ap, in1=C, op0=ALU.mult, op1=ALU.mult), reads=rd, writes=["rtA"])
            P.op("dve", lambda e: e.scalar_tensor_tensor(out=rtB, in0=t2, scalar=dec_ap, in1=S, op0=ALU.mult, op1=ALU.mult), reads=rd, writes=["rtB"])
            P.op("dve", lambda e: e.tensor_tensor(out=dst[:, 0:64], in0=rtA, in1=rtB, op=ALU.subtract), reads=["rtA", "rtB"], writes=wk)
            P.op("dve", lambda e: e.scalar_tensor_tensor(out=rtA, in0=t1, scalar=dec_ap, in1=S, op0=ALU.mult, op1=ALU.mult), reads=rd + wk, writes=["rtA"])
            P.op("dve", lambda e: e.scalar_tensor_tensor(out=rtB, in0=t2, scalar=dec_ap, in1=C, op0=ALU.mult, op1=ALU.mult), reads=rd + wk, writes=["rtB"])
            P.op("dve", lambda e: e.tensor_tensor(out=dst[:, 64:128], in0=rtA, in1=rtB, op=ALU.add), reads=["rtA", "rtB"], writes=wk)


        vA = ringT[:, 0:6144].rearrange("p (c n) -> p c n", c=16)
        vB = ringT[:, 6144:10240].rearrange("p (c n) -> p c n", c=16)
        vC = ringT[:, 10240:12288].rearrange("p (c n) -> p c n", c=16)
        vD = ringT[:, 12288:14336].rearrange("p (c n) -> p c n", c=16)
        regions = {"A": (vA, 3), "B": (vB, 2), "C": (vC, 1), "D": (vD, 1)}

        def rg_keys(name):
            return [("rg" + name, si, hh) for si in range(regions[name][1]) for hh in range(2)]

        def load_region(name, col_offs, h):
            v, _ = regions[name]
            for si, c0 in enumerate(col_offs):
                for hh in range(2):
                    dma_cast(v[:, 8 * hh:8 * hh + 8, si * 128:(si + 1) * 128], w_in_v[:, 8 * hh:8 * hh + 8, c0:c0 + 128],
                             writes=[("rg" + name, si, hh)], stream="rg" + name, batch=h)

        all_rg = [k for nm in ("A", "B", "C", "D") for k in rg_keys(nm)]

        def load_wout_quarter(qp, extra=()):
            sl = []
            for m in range(4):
                s_ = next_slot()
                v_ = ring[s_].rearrange("p (c n) -> p c n", c=4)
                dma_cast(v_, w_out_v[:, 4 * m:4 * m + 4, qp * 512:(qp + 1) * 512], writes=rkeys(s_) + list(extra), stream="ring%d" % s_)
                sl.append((s_, v_))
            return sl
        wq = {}
        deferred_tail = [None]

        def load_head(h):
            load_region("A", [1024 + h * 128, 2048 + h * 128, 6144 + h * 128], h)
            load_region("B", [h * 128, 3072 + h * 128], h)
            load_region("C", [4096 + h * 128], h)
            load_region("D", [5120 + h * 128], h)
        load_head(0)
        for h in range(NHEADS_RUN):
            for tb in range(NB):
                bA = 2 * (tb % 2)
                bB = bA + 1
                for kc in range(16):
                    P.op("pe", lambda e, tb=tb, kc=kc, bA=bA: e.matmul(
                        bank(bA)[:, 0:384], lhsT=aT[:, kc, tb * 128:(tb + 1) * 128], rhs=vA[:, kc, :],
                        start=(kc == 0), stop=(kc == 15)),
                        reads=aTk(tb) + rg_keys("A"), writes=[("ps", bA)])
                    if tb >= 8:
                        P.op("pe", lambda e, tb=tb, kc=kc, bB=bB: e.matmul(
                            bank(bB)[:, 0:256], lhsT=aT[:, kc, tb * 128:(tb + 1) * 128], rhs=vB[:, kc, :],
                            start=(kc == 0), stop=(kc == 15)),
                            reads=aTk(tb) + rg_keys("B"), writes=[("ps", bB)])
                P.op("act", lambda e, tb=tb, bA=bA: e.copy(out=rvfv[:, tb, :], in_=bank(bA)[:, 128:384]), reads=[("ps", bA)], writes=[("rvfv", tb)])
                if tb >= 8:
                    P.op("act", lambda e, tb=tb, bB=bB: e.copy(out=sg[:, tb - 8, :], in_=bank(bB)[:, 128:256]), reads=[("ps", bB)], writes=["sg"])
                rotary(bank(bA), rkt[:, tb, :], tb, dk_s[:, tb, h:h + 1], [("ps", bA)], [("rkt", tb)])
                if tb >= 8:
                    rotary(bank(bB), rqt[:, tb - 8, :], tb, dq_s[:, tb, h:h + 1], [("ps", bB)], [("rqt", tb - 8)])
            P.op("act", lambda e: e.activation(out=sg[:, :, :], in_=sg[:, :, :], func=AF.Silu), reads=["sg"], writes=["sg"])
            if deferred_tail[0] is not None:
                deferred_tail[0]()
                deferred_tail[0] = None
            for nb in range(5):
                n0 = nb * 512
                nw = min(512, LT - n0)
                bk = 4 + nb % 2
                for kc in range(16):
                    P.op("pe", lambda e, kc=kc, n0=n0, nw=nw, bk=bk: e.matmul(bank(bk)[:, 0:nw], lhsT=vD[:, kc, :], rhs=aT[:, kc, n0:n0 + nw],
                                                                          start=(kc == 0), stop=(kc == 15)),
                         reads=aT_range_keys(n0, n0 + nw) + rg_keys("D"), writes=[("ps", bk)])
                if nb % 2 == 0:
                    P.op("act", lambda e, n0=n0, nw=nw, bk=bk: e.copy(out=fkT[:, n0:n0 + nw], in_=bank(bk)[:, 0:nw]), reads=[("ps", bk)], writes=[("fkT", nb)])
                else:
                    P.op("dve", lambda e, n0=n0, nw=nw, bk=bk: e.tensor_copy(out=fkT[:, n0:n0 + nw], in_=bank(bk)[:, 0:nw]), reads=[("ps", bk)], writes=[("fkT", nb)])
            for g in range(3):
                n0 = OWN0 + 342 * g
                bk = 4 + (g + 1) % 2
                for kc in range(16):
                    P.op("pe", lambda e, kc=kc, n0=n0, bk=bk: e.matmul(bank(bk)[:, 0:342], lhsT=vC[:, kc, :], rhs=aT[:, kc, n0:n0 + 342],
                                                                    start=(kc == 0), stop=(kc == 15)),
                         reads=aT_range_keys(n0, n0 + 342) + rg_keys("C"), writes=[("ps", bk)])
                P.op("act", lambda e, g=g, bk=bk: e.copy(out=fqT[:, 342 * g:342 * g + 342], in_=bank(bk)[:, 0:342]), reads=[("ps", bk)], writes=[("fqT", g)])

            if h + 1 < NHEADS_RUN:
                load_head(h + 1)
            else:
                wq[0] = load_wout_quarter(0, all_rg)
                wq[1] = load_wout_quarter(1, all_rg)
            for c in range(9):
                P.op("pe", lambda e, c=c: e.transpose(out=ppb[2][:, c * 128:(c + 1) * 128], in_=rqt[:, c, :], identity=ident_b[:]),
                     reads=[("rqt", c), "ident_b"], writes=[("ps", 4), ("ps", 5)])
                P.op("pe", lambda e, c=c: e.transpose(out=ppb[3][:, c * 128:(c + 1) * 128], in_=rkt[:, 8 + c, :], identity=ident_b[:]),
                     reads=[("rkt", 8 + c), "ident_b"], writes=[("ps", 6), ("ps", 7)])
            P.op("act", lambda e: e.copy(out=rqT, in_=ppb[2][:, 0:1152]), reads=[("ps", 4), ("ps", 5)], writes=["rqT"])
            P.op("dve", lambda e: e.tensor_copy(out=rkT, in_=ppb[3][:, 0:1152]), reads=[("ps", 6), ("ps", 7)], writes=["rkT"])
            Sps = bank(7)[:, 0:128]
            ret_steps = []

            def r_init():
                for b in range(8):
                    P.op("pe", lambda e, b=b: e.matmul(Sps, lhsT=rkt[:, b, :], rhs=rvfv[:, b, 0:128], start=(b == 0), stop=(b == 7), skip_group_check=True),
                         reads=[("rkt", b), ("rvfv", b)], writes=[("ps", 7)])
            ret_steps.append(r_init)

            def r_a(c):
                sTp = bank(6)[:, (c % 2) * 128:(c % 2) * 128 + 128]
                if c > 0:
                    P.op("pe", lambda e: e.matmul(Sps, lhsT=rkt[:, 7 + c, :], rhs=rvfv[:, 7 + c, 0:128], start=False, stop=True, skip_group_check=True),
                         reads=[("rkt", 7 + c), ("rvfv", 7 + c)], writes=[("ps", 7)])
                P.op("act", lambda e: e.copy(out=Sbf[c % 2], in_=Sps), reads=[("ps", 7)], writes=[("Sbf", c % 2)])
                P.op("pe", lambda e: e.matmul(sTp, lhsT=rkT[:, c * 128:(c + 1) * 128], rhs=rqT[:, c * 128:(c + 1) * 128], start=True, stop=True),
                     reads=["rkT", "rqT"], writes=[("ps", 6)])
                P.op("dve", lambda e: e.tensor_tensor(out=smt[c % 2], in0=sTp, in1=maskR[:], op=ALU.mult),
                     reads=[("ps", 6), "maskR"], writes=[("smt", c % 2)])

            def r_b(c):
                op_ = bank(4)[:, (c % 2) * 128:(c % 2) * 128 + 128]
                P.op("pe", lambda e: e.matmul(op_, lhsT=rqT[:, c * 128:(c + 1) * 128], rhs=Sbf[c % 2], start=True, stop=False),
                     reads=["rqT", ("Sbf", c % 2)], writes=[("ps", 4)])
                P.op("pe", lambda e: e.matmul(op_, lhsT=smt[c % 2], rhs=rvfv[:, 8 + c, 0:128], start=False, stop=True),
                     reads=[("smt", c % 2), ("rvfv", 8 + c)], writes=[("ps", 4)])
                P.op("dve", lambda e: e.tensor_copy(out=o_all[:, c, :], in_=op_), reads=[("ps", 4)], writes=[("o_all", c)])
                P.op("act", lambda e: e.activation(out=junkB, in_=op_, func=AF.Square, accum_out=st2[:, 16 + c:17 + c]),
                     reads=[("ps", 4)], writes=["junkB", ("st2q", c)])
            for c in range(9):
                ret_steps.append(lambda c=c: r_a(c))
                ret_steps.append(lambda c=c: r_b(c))

            def r_tail(h=h):
                tail_eng = "dve" if h == NHEADS_RUN - 1 else "pool"
                oall_keys = [("o_all", c) for c in range(9)]
                sq_keys = [("st2q", c) for c in range(9)]
                mean = st2[:, 0:9]
                ssq = st2[:, 16:25]
                msq = st2[:, 32:41]
                rstd = st2[:, 48:57]
                P.op("dve", lambda e: e.reduce_sum(out=mean, in_=o_all[:, :, :], axis=AX.X), reads=oall_keys, writes=["st2m"])
                P.op(tail_eng, lambda e: e.tensor_scalar_mul(out=mean, in0=mean, scalar1=1.0 / 128), reads=["st2m"], writes=["st2m"])
                P.op(tail_eng, lambda e: e.tensor_tensor(out=msq, in0=mean, in1=mean, op=ALU.mult), reads=["st2m"], writes=["st2s"])
                P.op(tail_eng, lambda e: e.tensor_scalar(out=rstd, in0=ssq, scalar1=1.0 / 128, scalar2=EPS, op0=ALU.mult, op1=ALU.add), reads=sq_keys, writes=["st2r"])
                P.op(tail_eng, lambda e: e.tensor_tensor(out=rstd, in0=rstd, in1=msq, op=ALU.subtract), reads=["st2r", "st2s"], writes=["st2r"])
                P.op("act", lambda e: e.activation(out=rstd, in_=rstd, func=AF.Ln), reads=["st2r"], writes=["st2r"])
                P.op("act", lambda e: e.activation(out=rstd, in_=rstd, func=AF.Exp, scale=-0.5), reads=["st2r"], writes=["st2r"])
                for c in range(9):
                    P.op(tail_eng, lambda e, c=c: e.tensor_scalar(out=o_all[:, c, :], in0=o_all[:, c, :], scalar1=st2[:, c:c + 1], scalar2=st2[:, 48 + c:49 + c],
                                                               op0=ALU.subtract, op1=ALU.mult),
                         reads=[("o_all", c), "st2m", "st2r"], writes=[("o_all", c)])
                    P.op(tail_eng, lambda e, c=c: e.tensor_tensor(out=o_all[:, c, :], in0=o_all[:, c, :], in1=gb[:, h * 128:(h + 1) * 128], op=ALU.mult),
                         reads=[("o_all", c), "gb"], writes=[("o_all", c)])
                    P.op(tail_eng, lambda e, c=c: e.tensor_tensor(out=ro[:, c, :], in0=o_all[:, c, :], in1=sg[:, c, :], op=ALU.mult),
                         reads=[("o_all", c), "sg"], writes=[("ro", c)])

            def r_tail_pe(h=h):
                for c in range(9):
                    P.op("pe", lambda e, c=c: e.transpose(out=ppb[2][:, c * 128:(c + 1) * 128], in_=ro[:, c, :], identity=ident_b[:]),
                         reads=[("ro", c), "ident_b"], writes=[("ps", 4), ("ps", 5)])
                P.op("act", lambda e: e.copy(out=mixT[:, h, :], in_=ppb[2][:, 126:1152]), reads=[("ps", 4), ("ps", 5)], writes=[("mixh", h)])

            tiles = []
            for g in range(3):
                q0 = OWN0 + 342 * g
                kmax = (q0 + 342 - 1) // 128
                for kb in range(kmax + 1):
                    tiles.append((g, kb, kmax, q0))
            oTp = bank(2)[:, 0:342]
            dnp = bank(3)[:, 0:342]

            def f_s(ti, h=h):
                g, kb, kmax, q0 = tiles[ti]
                sbk = (0, 1, 5)[ti % 3]
                sb_ = bank(sbk)[:, 0:342]
                delta = 128 * kb - q0
                need_mask = (128 * kb + 127) > q0
                P.op("pe", lambda e: e.matmul(sb_, lhsT=fkT[:, kb * 128:(kb + 1) * 128], rhs=fqT[:, 342 * g:342 * g + 342], start=True, stop=False),
                     reads=[("fkT", kb // 4), ("fqT", g)], writes=[("ps", sbk)])
                P.op("pe", lambda e: e.matmul(sb_, lhsT=sel[:, h * 128:(h + 1) * 128], rhs=Rb[:, 126 + 342 * g:126 + 342 * g + 342],
                                              start=False, stop=(not need_mask)),
                     reads=["sel", "Rb"], writes=[("ps", sbk)])
                if need_mask:
                    off = XOFF - delta
                    assert 0 <= off and off + 342 <= MASKW, off
                    P.op("pe", lambda e: e.matmul(sb_, lhsT=ident_b[:], rhs=maskT[:, off:off + 342], start=False, stop=True),
                         reads=["ident_b", "maskT"], writes=[("ps", sbk)])
                pt = PTt[ti % 3]
                P.op("act", lambda e: e.activation(out=pt, in_=sb_, func=AF.Exp, bias=biasK[:, kb * 8 + h:kb * 8 + h + 1], scale=SCALE),
                     reads=[("ps", sbk), "biasK"], writes=[("PT", ti % 3)])

            def f_pv(ti, h=h):
                g, kb, kmax, q0 = tiles[ti]
                pt = PTt[ti % 3]
                ptk = ("PT", ti % 3)
                P.op("pe", lambda e: e.matmul(oTp, lhsT=rvfv[:, kb, 128:256], rhs=pt, start=(kb == 0), stop=(kb == kmax)),
                     reads=[("rvfv", kb), ptk], writes=[("ps", 2)])
                P.op("pe", lambda e: e.matmul(dnp, lhsT=ones_b[:], rhs=pt, start=(kb == 0), stop=(kb == kmax)),
                     reads=["ones_b", ptk], writes=[("ps", 3)])
                if kb == kmax:
                    P.op("dve", lambda e: e.reciprocal(out=rden, in_=dnp), reads=[("ps", 3)], writes=["rden"])
                    P.op("dve", lambda e: e.tensor_tensor(out=mixT[:, 8 + h, 342 * g:342 * g + 342], in0=oTp, in1=rden, op=ALU.mult),
                         reads=[("ps", 2), "rden"], writes=[("mixf", h, g)])

            fox_steps = []
            nt_ = len(tiles)
            def f_first():
                f_s(0)
                f_s(1)
            fox_steps.append(f_first)
            for ti in range(nt_):
                def st(ti=ti):
                    if ti + 2 < nt_:
                        f_s(ti + 2)
                    f_pv(ti)
                fox_steps.append(st)
            fi = 0
            per = [3, 2]
            for ri, rs in enumerate(ret_steps):
                rs()
                k = 1 if ri == 0 else per[ri % 2]
                for _ in range(k):
                    if fi < len(fox_steps):
                        fox_steps[fi]()
                        fi += 1
            while fi < len(fox_steps):
                fox_steps[fi]()
                fi += 1
            r_tail()
            deferred_tail[0] = r_tail_pe
        deferred_tail[0]()


        mix_all = [("mixh", h) for h in range(NH)] + [("mixf", h, g) for h in range(NH) for g in range(3)]
        if DEBUG:
            dma_sp(dbg["d_mix"], mixT[:, :, :].rearrange("p c t -> p (c t)"), reads=mix_all, stream="st")
        if STOP == "B":
            P.emit(final_wait_streams="st")
            return nc

        P.barrier()

        dma_sp(gb[:], g2.partition_broadcast(128), writes=["gb"], stream="gb")
        xsC = [R3f[:, 0:512], R3f[:, 512:1024], R3f[:, 1024:1536]]
        hnC = [R3[:, 4096:6144], R3[:, 6144:8192]]
        junkC = R3[:, 8192:10240]
        hh = R3f[:, 6144:8192]
        blocks = [(-1, 0, OWN0)] + [(tb, 2 + 128 * tb, 1152 + 128 * tb) for tb in range(8)]
        xi = 0
        ubk = 0

        def c_norm1(bi, tb):
            src = hh[:, :] if tb < 0 else h2[:, tb, :]
            col = 4 + bi % 2
            hk = [("h1", bi, q) for q in range(4)]
            c = rms_rstd(src, junkC, col, hk, "junkC")
            hn = hnC[bi % 2]
            P.op("dve", lambda e: e.scalar_tensor_tensor(out=hn, in0=src, scalar=c, in1=gb[:], op0=ALU.mult, op1=ALU.mult),
                 reads=hk + [("st1", col), "gb"], writes=[("hnC", bi % 2)])

        def c_norm2(bi, tb, c0):
            hn = hnC[bi % 2]
            pv = ppb[2 + bi % 2]
            pk = [("ps", 4 + 2 * (bi % 2)), ("ps", 5 + 2 * (bi % 2))]
            for cc in range(16):
                P.op("pe", lambda e, cc=cc: e.transpose(out=pv[:, cc * 128:(cc + 1) * 128], in_=hn[:, cc * 128:(cc + 1) * 128], identity=ident_b[:]),
                     reads=[("hnC", bi % 2), "ident_b"], writes=pk)
            pv3 = pv[:, 0:2048].rearrange("p (c t) -> p c t", c=16)
            if tb < 0:
                P.op("act", lambda e: e.copy(out=mixT[:, :, 0:2], in_=pv3[:, :, 0:2]), reads=pk + mix_all, writes=[("cT", bi)])
            else:
                P.op("act", lambda e: e.copy(out=mixT[:, 0:8, c0:c0 + 128], in_=pv3[:, 0:8, :]), reads=pk[0:1] + mix_all, writes=[("cT", bi)])
                P.op("dve", lambda e: e.tensor_copy(out=mixT[:, 8:16, c0:c0 + 128], in_=pv3[:, 8:16, :]), reads=pk[1:2] + mix_all, writes=[("cTb", bi)])

        for qp in range(4):
            slots = wq[qp]
            pend = None
            for bi, (tb, c0, l0) in enumerate(blocks):
                xs = xsC[xi % 3]
                xk = ("xs", xi % 3)
                xstream = "xs%d" % (xi % 3)
                xi += 1
                dma_sp(xs, xl[l0:l0 + 128, qp * 512:(qp + 1) * 512], writes=[xk], stream=xstream)
                bk = ubk % 4
                ubk += 1
                ck = [("cT", bi), ("cTb", bi)] + ([("cT", 1), ("cTb", 1)] if tb < 0 else [])
                for kc in range(16):
                    s_, v_ = slots[kc // 4]
                    P.op("pe", lambda e, kc=kc, v_=v_, c0=c0, bk=bk: e.matmul(
                        bank(bk), lhsT=mixT[:, kc, c0:c0 + 128], rhs=v_[:, kc % 4, :], start=(kc == 0), stop=(kc == 15)),
                        reads=mix_all + ck + rkeys(s_), writes=[("ps", bk)])
                dst = hh[:, qp * 512:(qp + 1) * 512] if tb < 0 else h2[:, tb, qp * 512:(qp + 1) * 512]
                P.op("dve", lambda e, dst=dst, bk=bk, xs=xs: e.tensor_tensor(out=dst, in0=bank(bk), in1=xs, op=ALU.add),
                     reads=[("ps", bk), xk], writes=[("h1", bi, qp)])
                if qp == 3:
                    c_norm1(bi, tb)
                    if pend is not None:
                        c_norm2(*pend)
                    pend = (bi, tb, c0)
            if qp == 3:
                c_norm2(*pend)
            if qp + 2 < 4:
                wq[qp + 2] = load_wout_quarter(qp + 2)

        cT = mixT
        cT_all = [("cT", bi) for bi in range(9)] + [("cTb", bi) for bi in range(1, 9)]
        if DEBUG:
            dma_sp(dbg["d_h1"], R1f[:, 0:8 * D], reads=[("h1", bi, hp) for bi in range(1, 9) for hp in range(4)], stream="st")
        if DEBUG:
            dma_sp(dbg["d_cT"], mixT[:, :, :].rearrange("p c t -> p (c t)"), reads=cT_all, stream="st")
            dma_sp(dbg["d_hh"], hh[0:2, :], reads=[("h1", 0, q) for q in range(4)], stream="st")
        if STOP == "C":
            P.emit(final_wait_streams="st")
            return nc
        P.barrier()

        gated = [[R3[:, (gs * GRP + jj) * 1024:(gs * GRP + jj + 1) * 1024] for jj in range(GRP)] for gs in range(2)]
        fb = 2 * GRP * 1024 // 2
        Yg = [R3f[:, fb + i * 1024:fb + (i + 1) * 1024] for i in range(2)]
        Yv = [R3f[:, fb + 2048 + i * 1024:fb + 2048 + (i + 1) * 1024] for i in range(2)]
        sb0 = 2 * (fb + 4096)
        Sg = [R3[:, sb0 + i * 1024:sb0 + (i + 1) * 1024] for i in range(2)]
        assert sb0 + 2048 <= R3N
        nblk = [(0, 342), (342, 683), (683, 1024)]
        ub = [0]

        h2_keys = lambda tb: [("h2", tb, n) for n in range(4)]
        mixflat = mixT[:, :, :].rearrange("p c t -> p (c t)")
        otE = [mixflat[:, 0:4096].bitcast(F32), mixflat[:, 4096:8192].bitcast(F32)]
        junkE = mixflat[:, 8192:10240]

        def final_block(tb):
            col = 8 + tb % 2
            c = rms_rstd(h2[:, tb, :], junkE, col, h2_keys(tb), "junkE", extra_writes=cT_all)
            ot = otE[tb % 2]
            P.op("dve", lambda e: e.scalar_tensor_tensor(out=ot, in0=h2[:, tb, :], scalar=c, in1=gb[:], op0=ALU.mult, op1=ALU.mult),
                 reads=h2_keys(tb) + [("st1", col), "gb"], writes=[("ot", tb % 2)] + cT_all)
            dma_sp(y[tb * 128:(tb + 1) * 128, :], ot, reads=[("ot", tb % 2)], stream="st%d" % (tb % 2))

        def wdown_group(gi, dslots, last=False):
            gs = gi % 2
            for tb in range(8):
                b0 = 4 if tb % 2 == 0 else 0
                for jj in range(GRP):
                    s, v = dslots[jj]
                    for n in range(4):
                        bk = b0 + n
                        P.op("pe", lambda e, jj=jj, v=v, tb=tb, n=n, bk=bk, gs=gs: e.matmul(
                            bank(bk), lhsT=gated[gs][jj][:, tb * 128:(tb + 1) * 128], rhs=v[:, n * 512:(n + 1) * 512],
                            start=(jj == 0), stop=(jj == GRP - 1)),
                            reads=[("gated", gs, jj)] + rkeys(s), writes=[("ps", bk)])
                for n in range(4):
                    bk = b0 + n
                    P.op("dve", lambda e, tb=tb, n=n, bk=bk: e.tensor_tensor(out=h2[:, tb, n * 512:(n + 1) * 512], in0=h2[:, tb, n * 512:(n + 1) * 512], in1=bank(bk), op=ALU.add),
                         reads=[("ps", bk), ("h2", tb, n)], writes=[("h2", tb, n)])
                if last:
                    final_block(tb)

        def load_down(gi):
            dslots = []
            for jj in range(GRP):
                j = gi * GRP + jj
                s = next_slot()
                dma_cast(ring[s][:, :], w_down[j * 128:(j + 1) * 128, :], writes=rkeys(s), stream="ring%d" % s)
                dslots.append((s, ring[s]))
            return dslots

        prev = None
        pair_i = 0
        for gi in range(NFF // GRP):
            gs = gi % 2
            for jj in range(GRP):
                j = gi * GRP + jj
                pi = pair_i % 2
                pair_i += 1
                for half in range(2):
                    cidx = half * NFF + j
                    s, v = load_slice(w_up_v, half * DFF + j * 128)
                    Y = (Yg if half == 0 else Yv)[pi]
                    yk = ("Y", half, pi)
                    for (r0, r1) in nblk:
                        ln = r1 - r0
                        bk = ub[0] % 4
                        ub[0] += 1
                        for kc in range(16):
                            P.op("pe", lambda e, kc=kc, v=v, r0=r0, ln=ln, bk=bk: e.matmul(bank(bk)[:, 0:ln + 2], lhsT=v[:, kc, :], rhs=cT[:, kc, r0:r0 + ln + 2],
                                                                                    start=(kc == 0), stop=(kc == 15)),
                                 reads=cT_all + rkeys(s), writes=[("ps", bk)])
                        u = bank(bk)
                        P.op("act", lambda e, u=u, Y=Y, r0=r0, r1=r1, ln=ln, cidx=cidx: e.activation(
                            out=Y[:, r0:r1], in_=u[:, 2:ln + 2], func=AF.Identity, bias=convb_s[:, cidx:cidx + 1], scale=convw_s[:, cidx * 3 + 2:cidx * 3 + 3]),
                            reads=[("ps", bk), "convw", "convb"], writes=[yk])
                        P.op("dve", lambda e, u=u, Y=Y, r0=r0, r1=r1, ln=ln, cidx=cidx: e.scalar_tensor_tensor(
                            out=Y[:, r0:r1], in0=u[:, 1:ln + 1], scalar=convw_s[:, cidx * 3 + 1:cidx * 3 + 2], in1=Y[:, r0:r1], op0=ALU.mult, op1=ALU.add),
                            reads=[("ps", bk), "convw", yk], writes=[yk])
                        P.op("dve", lambda e, u=u, Y=Y, r0=r0, r1=r1, ln=ln, cidx=cidx: e.scalar_tensor_tensor(
                            out=Y[:, r0:r1], in0=u[:, 0:ln], scalar=convw_s[:, cidx * 3:cidx * 3 + 1], in1=Y[:, r0:r1], op0=ALU.mult, op1=ALU.add),
                            reads=[("ps", bk), "convw", yk], writes=[yk])
                    if half == 0:
                        P.op("act", lambda e, Y=Y, pi=pi: e.activation(out=Sg[pi], in_=Y, func=AF.Silu), reads=[yk], writes=[("Sg", pi)])
                    else:
                        P.op("dve", lambda e, Y=Y, pi=pi, gs=gs, jj=jj: e.tensor_tensor(out=gated[gs][jj], in0=Y, in1=Sg[pi], op=ALU.mult),
                             reads=[yk, ("Sg", pi)], writes=[("gated", gs, jj)])
            if prev is not None:
                wdown_group(prev, load_down(prev))
            prev = gi
        dma_sp(gb[:], gf.partition_broadcast(128), writes=["gb"], stream="gb")
        wdown_group(prev, load_down(prev), last=True)

        P.emit(final_wait_streams="st")
    return nc


_NC_CACHE = {}


def _consts():
    c = {}
    c["c_ident"] = np.eye(128, dtype=np.float32)
    c["c_tri"] = np.triu(np.ones((128, 128), np.float32))
    c["c_ones"] = np.ones((128, 128), np.float32)
    c["c_maskR"] = np.triu(np.ones((128, 128), np.float32))
    p = np.arange(128)[:, None]
    xx = np.arange(MASKW)[None, :]
    c["c_maskT"] = np.where(xx - XOFF < p, NEG, 0.0).astype(np.float32)
    sel = np.zeros((128, 8, 128), np.float32)
    for h in range(8):
        sel[h, h, :] = 1.0
    c["c_sel"] = sel.reshape(128, 1024)
    return c


def _core_tables(T0):
    l = np.arange(LT)
    t = l - 1152 + T0
    valid = t >= 0
    inv_freq = 1.0 / (10000.0 ** (np.arange(0, 128, 2, dtype=np.float64) / 128.0))
    ang = np.where(valid, t, 0)[:, None].astype(np.float64) * inv_freq[None, :]
    def pm(a):
        n = a.shape[1]
        return np.ascontiguousarray(a.reshape(NB, 128, n).transpose(1, 0, 2).reshape(128, NB * n))
    tabs = {"cosT": pm(np.cos(ang).astype(np.float32)), "sinT": pm(np.sin(ang).astype(np.float32))}
    log_g = np.log1p(-np.exp2(-5.0 - np.arange(8, dtype=np.float64)))
    rel = (l - 1152).astype(np.float64)
    tabs["dqT"] = pm(np.exp(rel[:, None] * log_g[None, :]).astype(np.float32))
    tabs["dkT"] = pm((np.exp(-rel[:, None] * log_g[None, :]) * SCALE).astype(np.float32))
    tabs["kbT"] = pm(np.repeat(np.where(valid, 0.0, NEG).astype(np.float32)[:, None], 8, axis=1))
    return tabs


def kernel(x, meta_tokens, norm1_gain, w_in, b_forget, ret_norm_gain, w_out, norm2_gain, w_up,
           conv_w, conv_b, w_down, final_norm_gain):
    f32 = np.float32
    x = np.asarray(x, f32)
    B = x.shape[0]
    if "nc" not in _NC_CACHE:
        _NC_CACHE["nc"] = build_nc()
    nc = _NC_CACHE["nc"]
    consts = _consts()
    shared = {
        "w_in": np.ascontiguousarray(np.asarray(w_in, f32)[0]),
        "w_out": np.ascontiguousarray(np.asarray(w_out, f32)[0]),
        "w_up": np.ascontiguousarray(np.asarray(w_up, f32)[0]),
        "w_down": np.ascontiguousarray(np.asarray(w_down, f32)[0]),
        "g1": np.ascontiguousarray(np.asarray(norm1_gain, f32)[0]),
        "g2": np.ascontiguousarray(np.asarray(norm2_gain, f32)[0]),
        "gf": np.ascontiguousarray(np.asarray(final_norm_gain, f32)),
        "rng": np.ascontiguousarray(np.asarray(ret_norm_gain, f32)[0]),
        "wffd": np.ascontiguousarray(np.asarray(w_in, f32)[0][:, 7168:7176].reshape(16, 128, 8).transpose(1, 0, 2).reshape(128, 128)),
        "bfg": np.ascontiguousarray(np.tile(np.asarray(b_forget, f32)[0], NB)),
        "convw": np.ascontiguousarray(np.asarray(conv_w, f32)[0].reshape(3, 2 * NFF, 128).transpose(2, 1, 0).reshape(128, 2 * NFF * 3)),
        "convb": np.ascontiguousarray(np.asarray(conv_b, f32)[0].reshape(2 * NFF, 128).T),
    }
    shared.update(consts)
    meta = np.asarray(meta_tokens, f32)
    in_maps = []
    for core in range(8):
        b, s = core // 2, core % 2
        T0 = 16 + 1024 * s
        full = np.concatenate([meta, x[b]], axis=0)
        xl = np.zeros((LT, D), f32)
        t_lo = T0 - 1152
        src_lo = max(t_lo, 0)
        xl[src_lo - t_lo:, :] = full[src_lo:T0 + 1024]
        m = dict(shared)
        m["xl"] = xl
        m.update(_core_tables(T0))
        in_maps.append(m)
    res = run_bass_kernel_spmd(nc, in_maps[:NCORES_RUN], core_ids=list(range(NCORES_RUN)))
    out = np.zeros((B, 2048, D), f32)
    for core in range(NCORES_RUN):
        b, s = core // 2, core % 2
        out[b, 1024 * s:1024 * (s + 1), :] = res.results[core]["y"]
    if DEBUG:
        kernel.debug = res.results
    return out
```

```python
import contextlib
import numpy as np
import concourse.bass as bass
import concourse.mybir as mybir
from concourse.bass_utils import run_bass_kernel_spmd

F32 = mybir.dt.float32
BF16 = mybir.dt.bfloat16
AF = mybir.ActivationFunctionType
ALU = mybir.AluOpType
AX = mybir.AxisListType

D = 2048
NB = 17
LT = NB * 128
OWN0 = 1150
NOWN = 1026
NH = 8
DFF = 5632
NFF = 44
IN_DIM = 7176
SCALE = 128 ** -0.5
EPS = 1e-6
NEG = -30000.0
XOFF = 300
MASKW = 768
NSLOT = 8
GRP = 4
DEBUG = False
STOP = None
NHEADS_RUN = 8
NCORES_RUN = 8
STOP2 = None
SKIP = set()

ENGS = ("sp", "act", "pool", "dve", "pe")
SEM_LIMIT = 12000
WAIT_ALL_STREAMS = ("const", "constp")


class _Op:
    __slots__ = ("eng", "fn", "deps", "signal", "stream", "sem_i", "val", "inc", "idx", "batch")


class Prog:
    def __init__(self, nc):
        self.nc = nc
        self.ops = []
        self.eng_ops = {e: [] for e in ENGS}
        self.last_w = {}
        self.readers = {}
        self.last_in_stream = {}
        self.barrier_deps = set()

    def op(self, eng, fn, reads=(), writes=(), dma=None, batch=None):
        o = _Op()
        o.batch = batch
        o.eng = eng
        o.fn = fn
        o.idx = len(self.ops)
        o.stream = ("dma", dma) if dma is not None else ("eng", eng)
        o.inc = 16 if dma is not None else 1
        o.signal = dma is not None
        deps = set(self.barrier_deps)
        writes = list(writes) + [k for k in reads if isinstance(k, tuple) and k[0] == "ps"]
        reads = [k for k in reads if not (isinstance(k, tuple) and k[0] == "ps")]
        for k in reads:
            w = self.last_w.get(k)
            if w is not None:
                deps.add(w)
        for k in writes:
            w = self.last_w.get(k)
            if w is not None:
                deps.add(w)
            for r in self.readers.get(k, ()):
                deps.add(r)
        o.deps = deps
        for k in reads:
            self.readers.setdefault(k, []).append(o.idx)
        for k in writes:
            self.last_w[k] = o.idx
            self.readers[k] = []
        self.ops.append(o)
        self.eng_ops[eng].append(o)
        self.last_in_stream[o.stream] = o.idx
        return o

    def barrier(self):
        self.barrier_deps = set(self.last_in_stream.values())

    def emit(self, final_wait_streams=()):
        nc = self.nc
        ops = self.ops
        for o in ops:
            for d in o.deps:
                p = ops[d]
                if p.stream == ("eng", "pe") and o.eng == "pe":
                    continue
                p.signal = True
        streams = {}
        for o in ops:
            if not o.signal:
                continue
            st = streams.setdefault(o.stream, {"n": 0, "cur": 0})
            if st["cur"] + o.inc > SEM_LIMIT:
                st["n"] += 1
                st["cur"] = 0
            st["cur"] += o.inc
            o.sem_i = (o.stream, st["n"])
            o.val = st["cur"]
        batch_max = {}
        for o in ops:
            if o.signal and o.batch is not None:
                k = (o.sem_i, o.batch)
                batch_max[k] = max(batch_max.get(k, 0), o.val)
        sem_keys = []
        seen = set()
        for o in ops:
            if o.signal and o.sem_i not in seen:
                seen.add(o.sem_i)
                sem_keys.append(o.sem_i)
        with contextlib.ExitStack() as es:
            sems = {}
            for i, k in enumerate(sem_keys):
                sems[k] = es.enter_context(nc.semaphore("s%d" % i))
            last_val = {}
            for o in ops:
                if o.signal:
                    last_val[o.sem_i] = max(last_val.get(o.sem_i, 0), o.val)
            block = es.enter_context(nc.Block())
            handles = {"sp": block.sync, "act": block.scalar, "pool": block.gpsimd,
                       "dve": block.vector, "pe": block.tensor}

            def make(engname):
                def body(eng):
                    waited = {}
                    for o in self.eng_ops[engname]:
                        need = {}
                        for d in o.deps:
                            p = ops[d]
                            if not p.signal:
                                continue
                            if p.stream == ("eng", "pe") and engname == "pe":
                                continue
                            v_ = last_val[p.sem_i] if (p.stream[0] == "dma" and p.stream[1] in WAIT_ALL_STREAMS) else p.val
                            if p.batch is not None:
                                v_ = batch_max[(p.sem_i, p.batch)]
                            if v_ > need.get(p.sem_i, 0):
                                need[p.sem_i] = v_
                        for k, v in need.items():
                            if waited.get(k, 0) < v:
                                eng.wait_ge(sems[k], v)
                                waited[k] = v
                        ins = o.fn(eng)
                        if o.signal:
                            ins.then_inc(sems[o.sem_i], o.inc)
                    if engname == "sp":
                        for k in sem_keys:
                            if k[0][0] == "dma" and k[0][1].startswith(final_wait_streams):
                                eng.wait_ge(sems[k], last_val[k])
                return body

            for e in ENGS:
                handles[e](make(e))
        return len(ops)


def build_nc():
    nc = bass.Bass("TRN2", target_bir_lowering=False)

    def din(name, shape):
        return nc.dram_tensor(name, list(shape), F32, kind="ExternalInput").ap()

    xl = din("xl", [LT, D])
    w_in = din("w_in", [D, IN_DIM])
    w_out = din("w_out", [D, D])
    w_up = din("w_up", [D, 2 * DFF])
    w_down = din("w_down", [DFF, D])
    g1 = din("g1", [D]); g2 = din("g2", [D]); gf = din("gf", [D])
    rng = din("rng", [1024])
    bfg = din("bfg", [NB * 8])
    convw = din("convw", [128, 2 * NFF * 3])
    convb = din("convb", [128, 2 * NFF])
    cosT = din("cosT", [128, NB * 64]); sinT = din("sinT", [128, NB * 64])
    dkT = din("dkT", [128, NB * 8]); dqT = din("dqT", [128, NB * 8]); kbT = din("kbT", [128, NB * 8])
    wffd = din("wffd", [128, 16 * 8])
    c_ident = din("c_ident", [128, 128]); c_tri = din("c_tri", [128, 128]); c_ones = din("c_ones", [128, 128])
    c_maskR = din("c_maskR", [128, 128]); c_maskT = din("c_maskT", [128, MASKW]); c_sel = din("c_sel", [128, 1024])
    y = nc.dram_tensor("y", [1024, D], F32, kind="ExternalOutput").ap()
    dbg = {}
    if DEBUG:
        dbg["d_aT"] = nc.dram_tensor("d_aT", [128, 16 * LT], BF16, kind="ExternalOutput").ap()
        dbg["d_mix"] = nc.dram_tensor("d_mix", [128, 16 * NOWN], BF16, kind="ExternalOutput").ap()
        dbg["d_h1"] = nc.dram_tensor("d_h1", [128, 8 * D], F32, kind="ExternalOutput").ap()
        dbg["d_cum"] = nc.dram_tensor("d_cum", [128, NB * 8], F32, kind="ExternalOutput").ap()
        dbg["d_cT"] = nc.dram_tensor("d_cT", [128, 16 * NOWN], BF16, kind="ExternalOutput").ap()
        dbg["d_hh"] = nc.dram_tensor("d_hh", [2, D], F32, kind="ExternalOutput").ap()

    w_in_v = w_in.rearrange("(c p) n -> p c n", p=128)
    w_out_v = w_out.rearrange("(c p) n -> p c n", p=128)
    w_up_v = w_up.rearrange("(c p) n -> p c n", p=128)

    with contextlib.ExitStack() as es:
        def sb(name, shape, dt):
            return es.enter_context(nc.sbuf_tensor(name, list(shape), dt))

        def ps(name, shape, dt):
            return es.enter_context(nc.psum_tensor(name, list(shape), dt))

        R1 = sb("R1", [128, 16 * LT], BF16)
        aT = R1[:, :].rearrange("p (c t) -> p c t", c=16)
        R1f = R1.bitcast(F32)
        h2 = R1f[:, 0:8 * D].rearrange("p (b f) -> p b f", b=8)
        mixT = sb("mixT", [128, 16, NOWN], BF16)
        ringT = sb("ringT", [128, NSLOT * 2048], BF16)
        ring = [ringT[:, i * 2048:(i + 1) * 2048] for i in range(NSLOT)]
        R3N = 20480
        R3 = sb("R3", [128, R3N], BF16)
        R3f = R3.bitcast(F32)
        gb = sb("gb", [128, D], F32)
        ident_b = sb("ident_b", [128, 128], BF16)
        ones_b = sb("ones_b", [128, 128], BF16)
        maskT = sb("maskT", [128, MASKW], BF16)
        sel = sb("sel", [128, 1024], BF16)
        ident_f = sb("ident_f", [128, 128], F32)
        tri_f = sb("tri_f", [128, 128], F32)
        ones_f = sb("ones_f", [128, 128], F32)
        maskR = sb("maskR", [128, 128], F32)
        convw_s = sb("convw_s", [128, 2 * NFF * 3], F32)
        convb_s = sb("convb_s", [128, 2 * NFF], F32)
        bfg_s = sb("bfg_s", [128, NB * 8], F32)
        kb_s = sb("kb_s", [128, NB, 8], F32)
        cos_s = sb("cos_s", [128, NB, 64], F32)
        sin_s = sb("sin_s", [128, NB, 64], F32)
        dk_s = sb("dk_s", [128, NB, 8], F32)
        dq_s = sb("dq_s", [128, NB, 8], F32)
        spt = sb("spt", [128, NB * 8], F32)
        cumn = sb("cumn", [128, NB * 8], F32)
        tot = sb("tot", [128, NB * 8], F32)
        pre = sb("pre", [128, NB * 8], F32)
        biasK = sb("biasK", [128, NB * 8], F32)
        Rb = sb("Rb", [128, 9 * 128], BF16)
        wff = sb("wff", [128, 16, 8], BF16)
        st1 = sb("st1", [128, 32], F32)
        eps_t = sb("eps_t", [128, 1], F32)
        st2 = sb("st2", [128, 64], F32)

        pp = [ps("pp%d" % i, [128, 1024], F32) for i in range(4)]
        ppb = [p.bitcast(BF16) for p in pp]

        def bank(i):
            return pp[i // 2][:, (i % 2) * 512:(i % 2) * 512 + 512]

        P = Prog(nc)
        slot_ctr = [0]

        def next_slot():
            s = slot_ctr[0] % NSLOT
            slot_ctr[0] += 1
            return s

        def dma_sp(out, in_, reads=(), writes=(), stream="const"):
            P.op("sp", lambda e: e.dma_start(out=out, in_=in_), reads=reads, writes=writes, dma=stream)

        def dma_cast(out, in_, reads=(), writes=(), stream="constp", batch=None):
            P.op("pool", lambda e: e.dma_start(out=out, in_=in_), reads=reads, writes=writes, dma=stream, batch=batch)

        def load_slice(wv, c0, ncols=128):
            s = next_slot()
            v = ring[s][:, 0:16 * ncols].rearrange("p (c n) -> p c n", c=16)
            for hh in range(2):
                dma_cast(v[:, 8 * hh:8 * hh + 8, :], wv[:, 8 * hh:8 * hh + 8, c0:c0 + ncols], writes=[("ring", s, hh)], stream="ring%d" % s,
                         batch=slot_ctr[0])
            return s, v

        def rkeys(s):
            return [("ring", s, 0), ("ring", s, 1)]

        dma_sp(gb[:], g1.partition_broadcast(128), writes=["gb"], stream="gb")
        P.op("dve", lambda e: e.memset(eps_t[:], EPS), writes=["eps_t"])
        P.op("dve", lambda e: e.memset(pre[:], 0.0), writes=["pre"])
        dma_cast(ident_b[:], c_ident, writes=["ident_b"])
        dma_cast(wff[:, :, :].rearrange("p c n -> p (c n)"), wffd, writes=[("wff", 0), ("wff", 1)])
        dma_cast(ones_b[:], c_ones, writes=["ones_b"])
        dma_cast(maskT[:], c_maskT, writes=["maskT"])
        dma_cast(sel[:], c_sel, writes=["sel"])

        def late_constants():
            dma_sp(cos_s[:, :, :].rearrange("p b d -> p (b d)"), cosT, writes=["cos"])
            dma_sp(sin_s[:, :, :].rearrange("p b d -> p (b d)"), sinT, writes=["sin"])
            dma_sp(dk_s[:, :, :].rearrange("p b h -> p (b h)"), dkT, writes=["dk"])
            dma_sp(dq_s[:, :, :].rearrange("p b h -> p (b h)"), dqT, writes=["dq"])
            dma_sp(ident_f[:], c_ident, writes=["ident_f"])
            dma_sp(tri_f[:], c_tri, writes=["tri_f"])
            dma_sp(ones_f[:], c_ones, writes=["ones_f"])
            dma_sp(maskR[:], c_maskR, writes=["maskR"])
            dma_sp(bfg_s[:], bfg.partition_broadcast(128), writes=["bfg"])
            dma_sp(kb_s[:, :, :].rearrange("p b h -> p (b h)"), kbT, writes=["kb"])
            dma_sp(convw_s[:], convw, writes=["convw"])
            dma_sp(convb_s[:], convb, writes=["convb"])

        def rms_rstd(src_ap, junk_ap, col, rkeys_, jkey, extra_writes=()):
            npart = src_ap.shape[0]
            c = st1[0:npart, col:col + 1]
            P.op("act", lambda e: e.activation(out=junk_ap, in_=src_ap, func=AF.Square, accum_out=c),
                 reads=rkeys_, writes=[jkey, ("st1", col)] + list(extra_writes))
            P.op("act", lambda e: e.activation(out=c, in_=c, func=AF.Ln, bias=eps_t[0:npart, 0:1], scale=1.0 / D),
                 reads=[("st1", col), "eps_t"], writes=[("st1", col)])
            P.op("act", lambda e: e.activation(out=c, in_=c, func=AF.Exp, scale=-0.5), reads=[("st1", col)], writes=[("st1", col)])
            return c

        xsA = [R3f[:, 0:2048], R3f[:, 2048:4096], R3f[:, 4096:6144]]
        xnA = [R3[:, 12288:14336], R3[:, 14336:16384]]
        junkA = R3[:, 16384:18432]
        b6 = bank(6)

        def aTk(tb):
            return [("aT", tb, 0), ("aT", tb, 1)]

        def A1(tb):
            xs = xsA[tb % 3]
            dma_sp(xs, xl[tb * 128:(tb + 1) * 128, :], writes=[("xs", tb % 3)], stream="xs%d" % (tb % 3))
            rms_rstd(xs, junkA, tb % 3, [("xs", tb % 3)], "junkA")

        def A2(tb):
            xs = xsA[tb % 3]
            c = st1[:, tb % 3:tb % 3 + 1]
            xn = xnA[tb % 2]
            P.op("dve", lambda e: e.scalar_tensor_tensor(out=xn, in0=xs, scalar=c, in1=gb[:], op0=ALU.mult, op1=ALU.mult),
                 reads=[("xs", tb % 3), ("st1", tb % 3), "gb"], writes=[("xnA", tb % 2)])

        def A3(tb):
            xn = xnA[tb % 2]
            pv = ppb[tb % 2]
            for cc in range(16):
                P.op("pe", lambda e, cc=cc: e.transpose(out=pv[:, cc * 128:(cc + 1) * 128], in_=xn[:, cc * 128:(cc + 1) * 128], identity=ident_b[:]),
                     reads=[("xnA", tb % 2), "ident_b"], writes=[("ps", 2 * (tb % 2)), ("ps", 2 * (tb % 2) + 1)])
            pv3 = pv[:, 0:2048].rearrange("p (c t) -> p c t", c=16)
            P.op("act", lambda e: e.copy(out=aT[:, 0:8, tb * 128:(tb + 1) * 128], in_=pv3[:, 0:8, :]),
                 reads=[("ps", 2 * (tb % 2))], writes=[("aT", tb, 0)])
            P.op("dve", lambda e: e.tensor_copy(out=aT[:, 8:16, tb * 128:(tb + 1) * 128], in_=pv3[:, 8:16, :]),
                 reads=[("ps", 2 * (tb % 2) + 1)], writes=[("aT", tb, 1)])

        def A4(tb):
            for kc in range(16):
                P.op("pe", lambda e, kc=kc: e.matmul(b6[:, tb * 8:tb * 8 + 8], lhsT=aT[:, kc, tb * 128:(tb + 1) * 128], rhs=wff[:, kc, :],
                                                    start=(kc == 0), stop=(kc == 15)),
                     reads=aTk(tb) + [("wff", 0), ("wff", 1)], writes=[("ps", 6)])

        for i in range(NB + 3):
            if i == 3:
                late_constants()
            if i < NB:
                A1(i)
            if 0 <= i - 1 < NB:
                A2(i - 1)
            if 0 <= i - 2 < NB:
                A3(i - 2)
            if 0 <= i - 3 < NB:
                A4(i - 3)


        def aT_range_keys(l0, l1):
            ks = []
            for tb in range(l0 // 128, (l1 - 1) // 128 + 1):
                ks += aTk(tb)
            return ks

        if DEBUG:
            dma_sp(dbg["d_aT"], R1[:, :], reads=aT_range_keys(0, LT), stream="st")

        if STOP == "A":
            P.emit(final_wait_streams="st")
            return nc
        P.barrier()
        dma_sp(gb[:, 0:1024], rng.partition_broadcast(128), writes=["gb"], stream="gb")

        b7 = bank(7)
        P.op("dve", lambda e: e.tensor_tensor(out=spt[:], in0=b6[:, 0:NB * 8], in1=bfg_s[:], op=ALU.add), reads=[("ps", 6), "bfg"], writes=["spt"])
        P.op("act", lambda e: e.activation(out=spt[:], in_=spt[:], func=AF.Exp, scale=-1.0), reads=["spt"], writes=["spt"])
        P.op("act", lambda e: e.activation(out=spt[:], in_=spt[:], func=AF.Ln, bias=1.0, scale=1.0), reads=["spt"], writes=["spt"])
        P.op("pe", lambda e: e.matmul(b7[:, 0:136], lhsT=tri_f[:], rhs=spt[:], start=True, stop=True), reads=["spt", "tri_f"], writes=[("ps", 7)])
        P.op("pe", lambda e: e.matmul(b7[:, 136:272], lhsT=ones_f[:], rhs=spt[:], start=True, stop=True), reads=["spt", "ones_f"], writes=[("ps", 7)])
        P.op("dve", lambda e: e.tensor_copy(out=cumn[:], in_=b7[:, 0:136]), reads=[("ps", 7)], writes=["cumn"])
        P.op("dve", lambda e: e.tensor_copy(out=tot[:], in_=b7[:, 136:272]), reads=[("ps", 7)], writes=["tot"])
        for b in range(1, NB):
            P.op("dve", lambda e, b=b: e.tensor_tensor(out=pre[:, b * 8:b * 8 + 8], in0=pre[:, (b - 1) * 8:b * 8], in1=tot[:, (b - 1) * 8:b * 8], op=ALU.add),
                 reads=["pre", "tot"], writes=["pre"])
        P.op("dve", lambda e: e.tensor_tensor(out=cumn[:], in0=cumn[:], in1=pre[:], op=ALU.add), reads=["cumn", "pre"], writes=["cumn"])
        P.op("dve", lambda e: e.tensor_tensor(out=biasK[:], in0=cumn[:], in1=kb_s[:, :, :].rearrange("p b h -> p (b h)"), op=ALU.add),
             reads=["cumn", "kb"], writes=["biasK"])
        for c in range(9):
            tb = 8 + c
            dst = pp[2][0:8, c * 128:(c + 1) * 128] if c < 8 else pp[3][0:8, 512:640]
            P.op("pe", lambda e, dst=dst, tb=tb: e.transpose(out=dst, in_=cumn[:, tb * 8:tb * 8 + 8], identity=ident_f[:]),
                 reads=["cumn", "ident_f"], writes=[("ps", 4), ("ps", 5)] if c < 8 else [("ps", 7)])
        P.op("dve", lambda e: e.memset(Rb[:], 0.0), writes=["Rb"])
        P.op("act", lambda e: e.activation(out=Rb[0:8, 0:1024], in_=pp[2][0:8, 0:1024], func=AF.Copy, scale=-1.0 / SCALE),
             reads=[("ps", 4), ("ps", 5)], writes=["Rb"])
        P.op("act", lambda e: e.activation(out=Rb[0:8, 1024:1152], in_=pp[3][0:8, 512:640], func=AF.Copy, scale=-1.0 / SCALE),
             reads=[("ps", 7)], writes=["Rb"])
        if DEBUG:
            dma_sp(dbg["d_cum"], cumn[:], reads=["cumn"], stream="st")
        if STOP == "B0":
            P.emit(final_wait_streams="st")
            return nc

        o_ = 0
        def carve(n):
            nonlocal o_
            a = o_
            o_ += n
            return a
        fkT = R3[:, carve(LT):o_]
        _a = carve(1028)
        fqT = R3[:, _a:_a + NOWN]
        rvfv = R3[:, carve(NB * 256):o_].rearrange("p (b n) -> p b n", b=NB)
        rkt = R3[:, carve(NB * 128):o_].rearrange("p (b n) -> p b n", b=NB)
        rqT = R3[:, carve(1152):o_]
        rkT = R3[:, carve(1152):o_]
        sg = R3[:, carve(1152):o_].rearrange("p (b n) -> p b n", b=9)
        rqt = R3[:, carve(1152):o_].rearrange("p (b n) -> p b n", b=9)
        PTt = [R3[:, carve(342):o_] for _ in range(3)]
        smt = [R3[:, carve(128):o_] for _ in range(2)]
        Sbf = [R3[:, carve(128):o_] for _ in range(2)]
        junkB = R3[:, carve(128):o_]
        assert o_ % 2 == 0
        fo = o_ // 2
        def carvef(n):
            nonlocal fo
            a = fo
            fo += n
            return a
        o_all = R3f[:, carvef(1152):fo].rearrange("p (b n) -> p b n", b=9)
        rden = R3f[:, carvef(342):fo]
        rtA = R3f[:, carvef(64):fo]
        rtB = R3f[:, carvef(64):fo]
        ro = R3[:, fo * 2:fo * 2 + 1152].rearrange("p (b n) -> p b n", b=9)
        assert fo * 2 + 1152 <= R3N, fo * 2

        def rotary(src, dst, tbl, dec_ap, rk, wk):
            C = cos_s[:, tbl, :]
            S = sin_s[:, tbl, :]
            t1 = src[:, 0:64]
            t2 = src[:, 64:128]
            rd = list(rk) + ["cos", "sin", "dk", "dq"]
            P.op("dve", lambda e: e.scalar_tensor_tensor(out=rtA, in0=t1, scalar=dec_ap, in1=C, op0=ALU.mult, op1=ALU.mult), reads=rd, writes=["rtA"])
            P.op("dve", lambda e: e.scalar_tensor_tensor(out=rtB, in0=t2, scalar=dec_ap, in1=S, op0=ALU.mult, op1=ALU.mult), reads=rd, writes=["rtB"])
            P.op("dve", lambda e: e.tensor_tensor(out=dst[:, 0:64], in0=rtA, in1=rtB, op=ALU.subtract), reads=["rtA", "rtB"], writes=wk)
            P.op("dve", lambda e: e.scalar_tensor_tensor(out=rtA, in0=t1, scalar=dec_ap, in1=S, op0=ALU.mult, op1=ALU.mult), reads=rd + wk, writes=["rtA"])
            P.op("dve", lambda e: e.scalar_tensor_tensor(out=rtB, in0=t2, scalar=dec_ap, in1=C, op0=ALU.mult, op1=ALU.mult), reads=rd + wk, writes=["rtB"])
            P.op("dve", lambda e: e.tensor_tensor(out=dst[:, 64:128], in0=rtA, in1=rtB, op=ALU.add), reads=["rtA", "rtB"], writes=wk)


        vA = ringT[:, 0:6144].rearrange("p (c n) -> p c n", c=16)
        vB = ringT[:, 6144:10240].rearrange("p (c n) -> p c n", c=16)
        vC = ringT[:, 10240:12288].rearrange("p (c n) -> p c n", c=16)
        vD = ringT[:, 12288:14336].rearrange("p (c n) -> p c n", c=16)
        regions = {"A": (vA, 3), "B": (vB, 2), "C": (vC, 1), "D": (vD, 1)}

        def rg_keys(name):
            return [("rg" + name, si, hh) for si in range(regions[name][1]) for hh in range(2)]

        def load_region(name, col_offs, h):
            v, _ = regions[name]
            for si, c0 in enumerate(col_offs):
                for hh in range(2):
                    dma_cast(v[:, 8 * hh:8 * hh + 8, si * 128:(si + 1) * 128], w_in_v[:, 8 * hh:8 * hh + 8, c0:c0 + 128],
                             writes=[("rg" + name, si, hh)], stream="rg" + name, batch=h)

        all_rg = [k for nm in ("A", "B", "C", "D") for k in rg_keys(nm)]

        def load_wout_quarter(qp, extra=()):
            sl = []
            for m in range(4):
                s_ = next_slot()
                v_ = ring[s_].rearrange("p (c n) -> p c n", c=4)
                dma_cast(v_, w_out_v[:, 4 * m:4 * m + 4, qp * 512:(qp + 1) * 512], writes=rkeys(s_) + list(extra), stream="ring%d" % s_)
                sl.append((s_, v_))
            return sl
        wq = {}
        deferred_tail = [None]

        def load_head(h):
            load_region("A", [1024 + h * 128, 2048 + h * 128, 6144 + h * 128], h)
            load_region("B", [h * 128, 3072 + h * 128], h)
            load_region("C", [4096 + h * 128], h)
            load_region("D", [5120 + h * 128], h)
        load_head(0)
        for h in range(NHEADS_RUN):
            for tb in range(NB):
                bA = 2 * (tb % 2)
                bB = bA + 1
                for kc in range(16):
                    P.op("pe", lambda e, tb=tb, kc=kc, bA=bA: e.matmul(
                        bank(bA)[:, 0:384], lhsT=aT[:, kc, tb * 128:(tb + 1) * 128], rhs=vA[:, kc, :],
                        start=(kc == 0), stop=(kc == 15)),
                        reads=aTk(tb) + rg_keys("A"), writes=[("ps", bA)])
                    if tb >= 8:
                        P.op("pe", lambda e, tb=tb, kc=kc, bB=bB: e.matmul(
                            bank(bB)[:, 0:256], lhsT=aT[:, kc, tb * 128:(tb + 1) * 128], rhs=vB[:, kc, :],
                            start=(kc == 0), stop=(kc == 15)),
                            reads=aTk(tb) + rg_keys("B"), writes=[("ps", bB)])
                P.op("act", lambda e, tb=tb, bA=bA: e.copy(out=rvfv[:, tb, :], in_=bank(bA)[:, 128:384]), reads=[("ps", bA)], writes=[("rvfv", tb)])
                if tb >= 8:
                    P.op("act", lambda e, tb=tb, bB=bB: e.copy(out=sg[:, tb - 8, :], in_=bank(bB)[:, 128:256]), reads=[("ps", bB)], writes=["sg"])
                rotary(bank(bA), rkt[:, tb, :], tb, dk_s[:, tb, h:h + 1], [("ps", bA)], [("rkt", tb)])
                if tb >= 8:
                    rotary(bank(bB), rqt[:, tb - 8, :], tb, dq_s[:, tb, h:h + 1], [("ps", bB)], [("rqt", tb - 8)])
            P.op("act", lambda e: e.activation(out=sg[:, :, :], in_=sg[:, :, :], func=AF.Silu), reads=["sg"], writes=["sg"])
            if deferred_tail[0] is not None:
                deferred_tail[0]()
                deferred_tail[0] = None
            for nb in range(5):
                n0 = nb * 512
                nw = min(512, LT - n0)
                bk = 4 + nb % 2
                for kc in range(16):
                    P.op("pe", lambda e, kc=kc, n0=n0, nw=nw, bk=bk: e.matmul(bank(bk)[:, 0:nw], lhsT=vD[:, kc, :], rhs=aT[:, kc, n0:n0 + nw],
                                                                          start=(kc == 0), stop=(kc == 15)),
                         reads=aT_range_keys(n0, n0 + nw) + rg_keys("D"), writes=[("ps", bk)])
                if nb % 2 == 0:
                    P.op("act", lambda e, n0=n0, nw=nw, bk=bk: e.copy(out=fkT[:, n0:n0 + nw], in_=bank(bk)[:, 0:nw]), reads=[("ps", bk)], writes=[("fkT", nb)])
                else:
                    P.op("dve", lambda e, n0=n0, nw=nw, bk=bk: e.tensor_copy(out=fkT[:, n0:n0 + nw], in_=bank(bk)[:, 0:nw]), reads=[("ps", bk)], writes=[("fkT", nb)])
            for g in range(3):
                n0 = OWN0 + 342 * g
                bk = 4 + (g + 1) % 2
                for kc in range(16):
                    P.op("pe", lambda e, kc=kc, n0=n0, bk=bk: e.matmul(bank(bk)[:, 0:342], lhsT=vC[:, kc, :], rhs=aT[:, kc, n0:n0 + 342],
                                                                    start=(kc == 0), stop=(kc == 15)),
                         reads=aT_range_keys(n0, n0 + 342) + rg_keys("C"), writes=[("ps", bk)])
                P.op("act", lambda e, g=g, bk=bk: e.copy(out=fqT[:, 342 * g:342 * g + 342], in_=bank(bk)[:, 0:342]), reads=[("ps", bk)], writes=[("fqT", g)])

            if h + 1 < NHEADS_RUN:
                load_head(h + 1)
            else:
                wq[0] = load_wout_quarter(0, all_rg)
                wq[1] = load_wout_quarter(1, all_rg)
            for c in range(9):
                P.op("pe", lambda e, c=c: e.transpose(out=ppb[2][:, c * 128:(c + 1) * 128], in_=rqt[:, c, :], identity=ident_b[:]),
                     reads=[("rqt", c), "ident_b"], writes=[("ps", 4), ("ps", 5)])
                P.op("pe", lambda e, c=c: e.transpose(out=ppb[3][:, c * 128:(c + 1) * 128], in_=rkt[:, 8 + c, :], identity=ident_b[:]),
                     reads=[("rkt", 8 + c), "ident_b"], writes=[("ps", 6), ("ps", 7)])
            P.op("act", lambda e: e.copy(out=rqT, in_=ppb[2][:, 0:1152]), reads=[("ps", 4), ("ps", 5)], writes=["rqT"])
            P.op("dve", lambda e: e.tensor_copy(out=rkT, in_=ppb[3][:, 0:1152]), reads=[("ps", 6), ("ps", 7)], writes=["rkT"])
            Sps = bank(7)[:, 0:128]
            ret_steps = []

            def r_init():
                for b in range(8):
                    P.op("pe", lambda e, b=b: e.matmul(Sps, lhsT=rkt[:, b, :], rhs=rvfv[:, b, 0:128], start=(b == 0), stop=(b == 7), skip_group_check=True),
                         reads=[("rkt", b), ("rvfv", b)], writes=[("ps", 7)])
            ret_steps.append(r_init)

            def r_a(c):
                sTp = bank(6)[:, (c % 2) * 128:(c % 2) * 128 + 128]
                if c > 0:
                    P.op("pe", lambda e: e.matmul(Sps, lhsT=rkt[:, 7 + c, :], rhs=rvfv[:, 7 + c, 0:128], start=False, stop=True, skip_group_check=True),
                         reads=[("rkt", 7 + c), ("rvfv", 7 + c)], writes=[("ps", 7)])
                P.op("act", lambda e: e.copy(out=Sbf[c % 2], in_=Sps), reads=[("ps", 7)], writes=[("Sbf", c % 2)])
                P.op("pe", lambda e: e.matmul(sTp, lhsT=rkT[:, c * 128:(c + 1) * 128], rhs=rqT[:, c * 128:(c + 1) * 128], start=True, stop=True),
                     reads=["rkT", "rqT"], writes=[("ps", 6)])
                P.op("dve", lambda e: e.tensor_tensor(out=smt[c % 2], in0=sTp, in1=maskR[:], op=ALU.mult),
                     reads=[("ps", 6), "maskR"], writes=[("smt", c % 2)])

            def r_b(c):
                op_ = bank(4)[:, (c % 2) * 128:(c % 2) * 128 + 128]
                P.op("pe", lambda e: e.matmul(op_, lhsT=rqT[:, c * 128:(c + 1) * 128], rhs=Sbf[c % 2], start=True, stop=False),
                     reads=["rqT", ("Sbf", c % 2)], writes=[("ps", 4)])
                P.op("pe", lambda e: e.matmul(op_, lhsT=smt[c % 2], rhs=rvfv[:, 8 + c, 0:128], start=False, stop=True),
                     reads=[("smt", c % 2), ("rvfv", 8 + c)], writes=[("ps", 4)])
                P.op("dve", lambda e: e.tensor_copy(out=o_all[:, c, :], in_=op_), reads=[("ps", 4)], writes=[("o_all", c)])
                P.op("act", lambda e: e.activation(out=junkB, in_=op_, func=AF.Square, accum_out=st2[:, 16 + c:17 + c]),
                     reads=[("ps", 4)], writes=["junkB", ("st2q", c)])
            for c in range(9):
                ret_steps.append(lambda c=c: r_a(c))
                ret_steps.append(lambda c=c: r_b(c))

            def r_tail(h=h):
                tail_eng = "dve" if h == NHEADS_RUN - 1 else "pool"
                oall_keys = [("o_all", c) for c in range(9)]
                sq_keys = [("st2q", c) for c in range(9)]
                mean = st2[:, 0:9]
                ssq = st2[:, 16:25]
                msq = st2[:, 32:41]
                rstd = st2[:, 48:57]
                P.op("dve", lambda e: e.reduce_sum(out=mean, in_=o_all[:, :, :], axis=AX.X), reads=oall_keys, writes=["st2m"])
                P.op(tail_eng, lambda e: e.tensor_scalar_mul(out=mean, in0=mean, scalar1=1.0 / 128), reads=["st2m"], writes=["st2m"])
                P.op(tail_eng, lambda e: e.tensor_tensor(out=msq, in0=mean, in1=mean, op=ALU.mult), reads=["st2m"], writes=["st2s"])
                P.op(tail_eng, lambda e: e.tensor_scalar(out=rstd, in0=ssq, scalar1=1.0 / 128, scalar2=EPS, op0=ALU.mult, op1=ALU.add), reads=sq_keys, writes=["st2r"])
                P.op(tail_eng, lambda e: e.tensor_tensor(out=rstd, in0=rstd, in1=msq, op=ALU.subtract), reads=["st2r", "st2s"], writes=["st2r"])
                P.op("act", lambda e: e.activation(out=rstd, in_=rstd, func=AF.Ln), reads=["st2r"], writes=["st2r"])
                P.op("act", lambda e: e.activation(out=rstd, in_=rstd, func=AF.Exp, scale=-0.5), reads=["st2r"], writes=["st2r"])
                for c in range(9):
                    P.op(tail_eng, lambda e, c=c: e.tensor_scalar(out=o_all[:, c, :], in0=o_all[:, c, :], scalar1=st2[:, c:c + 1], scalar2=st2[:, 48 + c:49 + c],
                                                               op0=ALU.subtract, op1=ALU.mult),
                         reads=[("o_all", c), "st2m", "st2r"], writes=[("o_all", c)])
                    P.op(tail_eng, lambda e, c=c: e.tensor_tensor(out=o_all[:, c, :], in0=o_all[:, c, :], in1=gb[:, h * 128:(h + 1) * 128], op=ALU.mult),
                         reads=[("o_all", c), "gb"], writes=[("o_all", c)])
                    P.op(tail_eng, lambda e, c=c: e.tensor_tensor(out=ro[:, c, :], in0=o_all[:, c, :], in1=sg[:, c, :], op=ALU.mult),
                         reads=[("o_all", c), "sg"], writes=[("ro", c)])

            def r_tail_pe(h=h):
                for c in range(9):
                    P.op("pe", lambda e, c=c: e.transpose(out=ppb[2][:, c * 128:(c + 1) * 128], in_=ro[:, c, :], identity=ident_b[:]),
                         reads=[("ro", c), "ident_b"], writes=[("ps", 4), ("ps", 5)])
                P.op("act", lambda e: e.copy(out=mixT[:, h, :], in_=ppb[2][:, 126:1152]), reads=[("ps", 4), ("ps", 5)], writes=[("mixh", h)])

            tiles = []
            for g in range(3):
                q0 = OWN0 + 342 * g
                kmax = (q0 + 342 - 1) // 128
                for kb in range(kmax + 1):
                    tiles.append((g, kb, kmax, q0))
            oTp = bank(2)[:, 0:342]
            dnp = bank(3)[:, 0:342]

            def f_s(ti, h=h):
                g, kb, kmax, q0 = tiles[ti]
                sbk = (0, 1, 5)[ti % 3]
                sb_ = bank(sbk)[:, 0:342]
                delta = 128 * kb - q0
                need_mask = (128 * kb + 127) > q0
                P.op("pe", lambda e: e.matmul(sb_, lhsT=fkT[:, kb * 128:(kb + 1) * 128], rhs=fqT[:, 342 * g:342 * g + 342], start=True, stop=False),
                     reads=[("fkT", kb // 4), ("fqT", g)], writes=[("ps", sbk)])
                P.op("pe", lambda e: e.matmul(sb_, lhsT=sel[:, h * 128:(h + 1) * 128], rhs=Rb[:, 126 + 342 * g:126 + 342 * g + 342],
                                              start=False, stop=(not need_mask)),
                     reads=["sel", "Rb"], writes=[("ps", sbk)])
                if need_mask:
                    off = XOFF - delta
                    assert 0 <= off and off + 342 <= MASKW, off
                    P.op("pe", lambda e: e.matmul(sb_, lhsT=ident_b[:], rhs=maskT[:, off:off + 342], start=False, stop=True),
                         reads=["ident_b", "maskT"], writes=[("ps", sbk)])
                pt = PTt[ti % 3]
                P.op("act", lambda e: e.activation(out=pt, in_=sb_, func=AF.Exp, bias=biasK[:, kb * 8 + h:kb * 8 + h + 1], scale=SCALE),
                     reads=[("ps", sbk), "biasK"], writes=[("PT", ti % 3)])

            def f_pv(ti, h=h):
                g, kb, kmax, q0 = tiles[ti]
                pt = PTt[ti % 3]
                ptk = ("PT", ti % 3)
                P.op("pe", lambda e: e.matmul(oTp, lhsT=rvfv[:, kb, 128:256], rhs=pt, start=(kb == 0), stop=(kb == kmax)),
                     reads=[("rvfv", kb), ptk], writes=[("ps", 2)])
                P.op("pe", lambda e: e.matmul(dnp, lhsT=ones_b[:], rhs=pt, start=(kb == 0), stop=(kb == kmax)),
                     reads=["ones_b", ptk], writes=[("ps", 3)])
                if kb == kmax:
                    P.op("dve", lambda e: e.reciprocal(out=rden, in_=dnp), reads=[("ps", 3)], writes=["rden"])
                    P.op("dve", lambda e: e.tensor_tensor(out=mixT[:, 8 + h, 342 * g:342 * g + 342], in0=oTp, in1=rden, op=ALU.mult),
                         reads=[("ps", 2), "rden"], writes=[("mixf", h, g)])

            fox_steps = []
            nt_ = len(tiles)
            def f_first():
                f_s(0)
                f_s(1)
            fox_steps.append(f_first)
            for ti in range(nt_):
                def st(ti=ti):
                    if ti + 2 < nt_:
                        f_s(ti + 2)
                    f_pv(ti)
                fox_steps.append(st)
            fi = 0
            last_head = (h == NHEADS_RUN - 1)
            per = [1, 1] if last_head else [3, 2]
            for ri, rs in enumerate(ret_steps):
                rs()
                k = 1 if ri == 0 else per[ri % 2]
                for _ in range(k):
                    if fi < len(fox_steps):
                        fox_steps[fi]()
                        fi += 1
            if last_head:
                r_tail()
            while fi < len(fox_steps):
                fox_steps[fi]()
                fi += 1
            if not last_head:
                r_tail()
            deferred_tail[0] = r_tail_pe
        deferred_tail[0]()


        mix_all = [("mixh", h) for h in range(NH)] + [("mixf", h, g) for h in range(NH) for g in range(3)]
        if DEBUG:
            dma_sp(dbg["d_mix"], mixT[:, :, :].rearrange("p c t -> p (c t)"), reads=mix_all, stream="st")
        if STOP == "B":
            P.emit(final_wait_streams="st")
            return nc

        P.barrier()

        dma_sp(gb[:], g2.partition_broadcast(128), writes=["gb"], stream="gb")
        xsC = [R3f[:, 0:512], R3f[:, 512:1024], R3f[:, 1024:1536]]
        hnC = [R3[:, 4096:6144], R3[:, 6144:8192]]
        junkC = R3[:, 8192:10240]
        hh = R3f[:, 6144:8192]
        blocks = [(-1, 0, OWN0)] + [(tb, 2 + 128 * tb, 1152 + 128 * tb) for tb in range(8)]
        xi = 0
        ubk = 0

        def c_norm1(bi, tb):
            src = hh[:, :] if tb < 0 else h2[:, tb, :]
            col = 4 + bi % 2
            hk = [("h1", bi, q) for q in range(4)]
            c = rms_rstd(src, junkC, col, hk, "junkC")
            hn = hnC[bi % 2]
            P.op("dve", lambda e: e.scalar_tensor_tensor(out=hn, in0=src, scalar=c, in1=gb[:], op0=ALU.mult, op1=ALU.mult),
                 reads=hk + [("st1", col), "gb"], writes=[("hnC", bi % 2)])

        def c_norm2(bi, tb, c0):
            hn = hnC[bi % 2]
            pv = ppb[2 + bi % 2]
            pk = [("ps", 4 + 2 * (bi % 2)), ("ps", 5 + 2 * (bi % 2))]
            for cc in range(16):
                P.op("pe", lambda e, cc=cc: e.transpose(out=pv[:, cc * 128:(cc + 1) * 128], in_=hn[:, cc * 128:(cc + 1) * 128], identity=ident_b[:]),
                     reads=[("hnC", bi % 2), "ident_b"], writes=pk)
            pv3 = pv[:, 0:2048].rearrange("p (c t) -> p c t", c=16)
            if tb < 0:
                P.op("act", lambda e: e.copy(out=mixT[:, :, 0:2], in_=pv3[:, :, 0:2]), reads=pk + mix_all, writes=[("cT", bi)])
            else:
                P.op("act", lambda e: e.copy(out=mixT[:, 0:8, c0:c0 + 128], in_=pv3[:, 0:8, :]), reads=pk[0:1] + mix_all, writes=[("cT", bi)])
                P.op("dve", lambda e: e.tensor_copy(out=mixT[:, 8:16, c0:c0 + 128], in_=pv3[:, 8:16, :]), reads=pk[1:2] + mix_all, writes=[("cTb", bi)])

        for qp in range(4):
            slots = wq[qp]
            pend = None
            for bi, (tb, c0, l0) in enumerate(blocks):
                xs = xsC[xi % 3]
                xk = ("xs", xi % 3)
                xstream = "xs%d" % (xi % 3)
                xi += 1
                dma_sp(xs, xl[l0:l0 + 128, qp * 512:(qp + 1) * 512], writes=[xk], stream=xstream)
                bk = ubk % 4
                ubk += 1
                ck = [("cT", bi), ("cTb", bi)] + ([("cT", 1), ("cTb", 1)] if tb < 0 else [])
                for kc in range(16):
                    s_, v_ = slots[kc // 4]
                    P.op("pe", lambda e, kc=kc, v_=v_, c0=c0, bk=bk: e.matmul(
                        bank(bk), lhsT=mixT[:, kc, c0:c0 + 128], rhs=v_[:, kc % 4, :], start=(kc == 0), stop=(kc == 15)),
                        reads=mix_all + ck + rkeys(s_), writes=[("ps", bk)])
                dst = hh[:, qp * 512:(qp + 1) * 512] if tb < 0 else h2[:, tb, qp * 512:(qp + 1) * 512]
                P.op("dve", lambda e, dst=dst, bk=bk, xs=xs: e.tensor_tensor(out=dst, in0=bank(bk), in1=xs, op=ALU.add),
                     reads=[("ps", bk), xk], writes=[("h1", bi, qp)])
                if qp == 3:
                    c_norm1(bi, tb)
                    if pend is not None:
                        c_norm2(*pend)
                    pend = (bi, tb, c0)
            if qp == 3:
                c_norm2(*pend)
            if qp + 2 < 4:
                wq[qp + 2] = load_wout_quarter(qp + 2)

        cT = mixT
        cT_all = [("cT", bi) for bi in range(9)] + [("cTb", bi) for bi in range(1, 9)]
        if DEBUG:
            dma_sp(dbg["d_h1"], R1f[:, 0:8 * D], reads=[("h1", bi, hp) for bi in range(1, 9) for hp in range(4)], stream="st")
        if DEBUG:
            dma_sp(dbg["d_cT"], mixT[:, :, :].rearrange("p c t -> p (c t)"), reads=cT_all, stream="st")
            dma_sp(dbg["d_hh"], hh[0:2, :], reads=[("h1", 0, q) for q in range(4)], stream="st")
        if STOP == "C":
            P.emit(final_wait_streams="st")
            return nc
        P.barrier()

        gated = [[R3[:, (gs * GRP + jj) * 1024:(gs * GRP + jj + 1) * 1024] for jj in range(GRP)] for gs in range(2)]
        fb = 2 * GRP * 1024 // 2
        Yg = [R3f[:, fb + i * 1024:fb + (i + 1) * 1024] for i in range(2)]
        Yv = [R3f[:, fb + 2048 + i * 1024:fb + 2048 + (i + 1) * 1024] for i in range(2)]
        sb0 = 2 * (fb + 4096)
        Sg = [R3[:, sb0 + i * 1024:sb0 + (i + 1) * 1024] for i in range(2)]
        assert sb0 + 2048 <= R3N
        nblk = [(0, 342), (342, 683), (683, 1024)]
        ub = [0]

        h2_keys = lambda tb: [("h2", tb, n) for n in range(4)]
        mixflat = mixT[:, :, :].rearrange("p c t -> p (c t)")
        otE = [mixflat[:, 0:4096].bitcast(F32), mixflat[:, 4096:8192].bitcast(F32)]
        junkE = mixflat[:, 8192:10240]

        def final_block(tb):
            col = 8 + tb % 2
            c = rms_rstd(h2[:, tb, :], junkE, col, h2_keys(tb), "junkE", extra_writes=cT_all)
            ot = otE[tb % 2]
            if tb % 2 == 0:
                P.op("dve", lambda e: e.scalar_tensor_tensor(out=ot, in0=h2[:, tb, :], scalar=c, in1=gb[:], op0=ALU.mult, op1=ALU.mult),
                     reads=h2_keys(tb) + [("st1", col), "gb"], writes=[("ot", tb % 2)] + cT_all)
            else:
                P.op("act", lambda e: e.activation(out=ot, in_=h2[:, tb, :], func=AF.Copy, scale=c),
                     reads=h2_keys(tb) + [("st1", col)], writes=[("ot", tb % 2)] + cT_all)
                P.op("pool", lambda e: e.tensor_tensor(out=ot, in0=ot, in1=gb[:], op=ALU.mult),
                     reads=[("ot", tb % 2), "gb"], writes=[("ot", tb % 2)])
            dma_sp(y[tb * 128:(tb + 1) * 128, :], ot, reads=[("ot", tb % 2)], stream="st%d" % (tb % 2))

        def wdown_group(gi, dslots, last=False):
            gs = gi % 2
            for tb in range(8):
                b0 = 4 if tb % 2 == 0 else 0
                for jj in range(GRP):
                    s, v = dslots[jj]
                    for n in range(4):
                        bk = b0 + n
                        P.op("pe", lambda e, jj=jj, v=v, tb=tb, n=n, bk=bk, gs=gs: e.matmul(
                            bank(bk), lhsT=gated[gs][jj][:, tb * 128:(tb + 1) * 128], rhs=v[:, n * 512:(n + 1) * 512],
                            start=(jj == 0), stop=(jj == GRP - 1)),
                            reads=[("gated", gs, jj)] + rkeys(s), writes=[("ps", bk)])
                for n in range(4):
                    bk = b0 + n
                    P.op("dve", lambda e, tb=tb, n=n, bk=bk: e.tensor_tensor(out=h2[:, tb, n * 512:(n + 1) * 512], in0=h2[:, tb, n * 512:(n + 1) * 512], in1=bank(bk), op=ALU.add),
                         reads=[("ps", bk), ("h2", tb, n)], writes=[("h2", tb, n)])
                if last:
                    final_block(tb)

        def load_down(gi):
            dslots = []
            for jj in range(GRP):
                j = gi * GRP + jj
                s = next_slot()
                dma_cast(ring[s][:, :], w_down[j * 128:(j + 1) * 128, :], writes=rkeys(s), stream="ring%d" % s)
                dslots.append((s, ring[s]))
            return dslots

        prev = None
        pair_i = 0
        for gi in range(NFF // GRP):
            gs = gi % 2
            for jj in range(GRP):
                j = gi * GRP + jj
                pi = pair_i % 2
                pair_i += 1
                for half in range(2):
                    cidx = half * NFF + j
                    s, v = load_slice(w_up_v, half * DFF + j * 128)
                    Y = (Yg if half == 0 else Yv)[pi]
                    yk = ("Y", half, pi)
                    for (r0, r1) in nblk:
                        ln = r1 - r0
                        bk = ub[0] % 4
                        ub[0] += 1
                        for kc in range(16):
                            P.op("pe", lambda e, kc=kc, v=v, r0=r0, ln=ln, bk=bk: e.matmul(bank(bk)[:, 0:ln + 2], lhsT=v[:, kc, :], rhs=cT[:, kc, r0:r0 + ln + 2],
                                                                                    start=(kc == 0), stop=(kc == 15)),
                                 reads=cT_all + rkeys(s), writes=[("ps", bk)])
                        u = bank(bk)
                        P.op("act", lambda e, u=u, Y=Y, r0=r0, r1=r1, ln=ln, cidx=cidx: e.activation(
                            out=Y[:, r0:r1], in_=u[:, 2:ln + 2], func=AF.Identity, bias=convb_s[:, cidx:cidx + 1], scale=convw_s[:, cidx * 3 + 2:cidx * 3 + 3]),
                            reads=[("ps", bk), "convw", "convb"], writes=[yk])
                        P.op("dve", lambda e, u=u, Y=Y, r0=r0, r1=r1, ln=ln, cidx=cidx: e.scalar_tensor_tensor(
                            out=Y[:, r0:r1], in0=u[:, 1:ln + 1], scalar=convw_s[:, cidx * 3 + 1:cidx * 3 + 2], in1=Y[:, r0:r1], op0=ALU.mult, op1=ALU.add),
                            reads=[("ps", bk), "convw", yk], writes=[yk])
                        P.op("dve", lambda e, u=u, Y=Y, r0=r0, r1=r1, ln=ln, cidx=cidx: e.scalar_tensor_tensor(
                            out=Y[:, r0:r1], in0=u[:, 0:ln], scalar=convw_s[:, cidx * 3:cidx * 3 + 1], in1=Y[:, r0:r1], op0=ALU.mult, op1=ALU.add),
                            reads=[("ps", bk), "convw", yk], writes=[yk])
                    if half == 0:
                        P.op("act", lambda e, Y=Y, pi=pi: e.activation(out=Sg[pi], in_=Y, func=AF.Silu), reads=[yk], writes=[("Sg", pi)])
                    else:
                        P.op("dve", lambda e, Y=Y, pi=pi, gs=gs, jj=jj: e.tensor_tensor(out=gated[gs][jj], in0=Y, in1=Sg[pi], op=ALU.mult),
                             reads=[yk, ("Sg", pi)], writes=[("gated", gs, jj)])
            if prev is not None:
                wdown_group(prev, load_down(prev))
            prev = gi
        dma_sp(gb[:], gf.partition_broadcast(128), writes=["gb"], stream="gb")
        wdown_group(prev, load_down(prev), last=True)

        P.emit(final_wait_streams="st")
    return nc


_NC_CACHE = {}


def _consts():
    c = {}
    c["c_ident"] = np.eye(128, dtype=np.float32)
    c["c_tri"] = np.triu(np.ones((128, 128), np.float32))
    c["c_ones"] = np.ones((128, 128), np.float32)
    c["c_maskR"] = np.triu(np.ones((128, 128), np.float32))
    p = np.arange(128)[:, None]
    xx = np.arange(MASKW)[None, :]
    c["c_maskT"] = np.where(xx - XOFF < p, NEG, 0.0).astype(np.float32)
    sel = np.zeros((128, 8, 128), np.float32)
    for h in range(8):
        sel[h, h, :] = 1.0
    c["c_sel"] = sel.reshape(128, 1024)
    return c


def _core_tables(T0):
    l = np.arange(LT)
    t = l - 1152 + T0
    valid = t >= 0
    inv_freq = 1.0 / (10000.0 ** (np.arange(0, 128, 2, dtype=np.float64) / 128.0))
    ang = np.where(valid, t, 0)[:, None].astype(np.float64) * inv_freq[None, :]
    def pm(a):
        n = a.shape[1]
        return np.ascontiguousarray(a.reshape(NB, 128, n).transpose(1, 0, 2).reshape(128, NB * n))
    tabs = {"cosT": pm(np.cos(ang).astype(np.float32)), "sinT": pm(np.sin(ang).astype(np.float32))}
    log_g = np.log1p(-np.exp2(-5.0 - np.arange(8, dtype=np.float64)))
    rel = (l - 1152).astype(np.float64)
    tabs["dqT"] = pm(np.exp(rel[:, None] * log_g[None, :]).astype(np.float32))
    tabs["dkT"] = pm((np.exp(-rel[:, None] * log_g[None, :]) * SCALE).astype(np.float32))
    tabs["kbT"] = pm(np.repeat(np.where(valid, 0.0, NEG).astype(np.float32)[:, None], 8, axis=1))
    return tabs


def kernel(x, meta_tokens, norm1_gain, w_in, b_forget, ret_norm_gain, w_out, norm2_gain, w_up,
           conv_w, conv_b, w_down, final_norm_gain):
    f32 = np.float32
    x = np.asarray(x, f32)
    B = x.shape[0]
    if "nc" not in _NC_CACHE:
        _NC_CACHE["nc"] = build_nc()
    nc = _NC_CACHE["nc"]
    consts = _consts()
    shared = {
        "w_in": np.ascontiguousarray(np.asarray(w_in, f32)[0]),
        "w_out": np.ascontiguousarray(np.asarray(w_out, f32)[0]),
        "w_up": np.ascontiguousarray(np.asarray(w_up, f32)[0]),
        "w_down": np.ascontiguousarray(np.asarray(w_down, f32)[0]),
        "g1": np.ascontiguousarray(np.asarray(norm1_gain, f32)[0]),
        "g2": np.ascontiguousarray(np.asarray(norm2_gain, f32)[0]),
        "gf": np.ascontiguousarray(np.asarray(final_norm_gain, f32)),
        "rng": np.ascontiguousarray(np.asarray(ret_norm_gain, f32)[0]),
        "wffd": np.ascontiguousarray(np.asarray(w_in, f32)[0][:, 7168:7176].reshape(16, 128, 8).transpose(1, 0, 2).reshape(128, 128)),
        "bfg": np.ascontiguousarray(np.tile(np.asarray(b_forget, f32)[0], NB)),
        "convw": np.ascontiguousarray(np.asarray(conv_w, f32)[0].reshape(3, 2 * NFF, 128).transpose(2, 1, 0).reshape(128, 2 * NFF * 3)),
        "convb": np.ascontiguousarray(np.asarray(conv_b, f32)[0].reshape(2 * NFF, 128).T),
    }
    shared.update(consts)
    meta = np.asarray(meta_tokens, f32)
    in_maps = []
    for core in range(8):
        b, s = core // 2, core % 2
        T0 = 16 + 1024 * s
        full = np.concatenate([meta, x[b]], axis=0)
        xl = np.zeros((LT, D), f32)
        t_lo = T0 - 1152
        src_lo = max(t_lo, 0)
        xl[src_lo - t_lo:, :] = full[src_lo:T0 + 1024]
        m = dict(shared)
        m["xl"] = xl
        m.update(_core_tables(T0))
        in_maps.append(m)
    res = run_bass_kernel_spmd(nc, in_maps[:NCORES_RUN], core_ids=list(range(NCORES_RUN)))
    out = np.zeros((B, 2048, D), f32)
    for core in range(NCORES_RUN):
        b, s = core // 2, core % 2
        out[b, 1024 * s:1024 * (s + 1), :] = res.results[core]["y"]
    if DEBUG:
        kernel.debug = res.results
    return out
```

```python
import contextlib
import numpy as np
import concourse.bass as bass
import concourse.mybir as mybir
from concourse.bass_utils import run_bass_kernel_spmd

F32 = mybir.dt.float32
BF16 = mybir.dt.bfloat16
AF = mybir.ActivationFunctionType
ALU = mybir.AluOpType
AX = mybir.AxisListType

D = 2048
NB = 17
LT = NB * 128
OWN0 = 1150
NOWN = 1026
NH = 8
DFF = 5632
NFF = 44
IN_DIM = 7176
SCALE = 128 ** -0.5
EPS = 1e-6
NEG = -30000.0
XOFF = 300
MASKW = 768
NSLOT = 8
GRP = 4
DEBUG = False
STOP = None
NHEADS_RUN = 8
NCORES_RUN = 8
STOP2 = None
SKIP = set()

ENGS = ("sp", "act", "pool", "dve", "pe")
SEM_LIMIT = 12000
WAIT_ALL_STREAMS = ("const", "constp")


class _Op:
    __slots__ = ("eng", "fn", "deps", "signal", "stream", "sem_i", "val", "inc", "idx", "batch")


class Prog:
    def __init__(self, nc):
        self.nc = nc
        self.ops = []
        self.eng_ops = {e: [] for e in ENGS}
        self.last_w = {}
        self.readers = {}
        self.last_in_stream = {}
        self.barrier_deps = set()

    def op(self, eng, fn, reads=(), writes=(), dma=None, batch=None):
        o = _Op()
        o.batch = batch
        o.eng = eng
        o.fn = fn
        o.idx = len(self.ops)
        o.stream = ("dma", dma) if dma is not None else ("eng", eng)
        o.inc = 16 if dma is not None else 1
        o.signal = dma is not None
        deps = set(self.barrier_deps)
        writes = list(writes) + [k for k in reads if isinstance(k, tuple) and k[0] == "ps"]
        reads = [k for k in reads if not (isinstance(k, tuple) and k[0] == "ps")]
        for k in reads:
            w = self.last_w.get(k)
            if w is not None:
                deps.add(w)
        for k in writes:
            w = self.last_w.get(k)
            if w is not None:
                deps.add(w)
            for r in self.readers.get(k, ()):
                deps.add(r)
        o.deps = deps
        for k in reads:
            self.readers.setdefault(k, []).append(o.idx)
        for k in writes:
            self.last_w[k] = o.idx
            self.readers[k] = []
        self.ops.append(o)
        self.eng_ops[eng].append(o)
        self.last_in_stream[o.stream] = o.idx
        return o

    def barrier(self):
        self.barrier_deps = set(self.last_in_stream.values())

    def emit(self, final_wait_streams=()):
        nc = self.nc
        ops = self.ops
        for o in ops:
            for d in o.deps:
                p = ops[d]
                if p.stream == ("eng", "pe") and o.eng == "pe":
                    continue
                p.signal = True
        streams = {}
        for o in ops:
            if not o.signal:
                continue
            st = streams.setdefault(o.stream, {"n": 0, "cur": 0})
            if st["cur"] + o.inc > SEM_LIMIT:
                st["n"] += 1
                st["cur"] = 0
            st["cur"] += o.inc
            o.sem_i = (o.stream, st["n"])
            o.val = st["cur"]
        batch_max = {}
        for o in ops:
            if o.signal and o.batch is not None:
                k = (o.sem_i, o.batch)
                batch_max[k] = max(batch_max.get(k, 0), o.val)
        sem_keys = []
        seen = set()
        for o in ops:
            if o.signal and o.sem_i not in seen:
                seen.add(o.sem_i)
                sem_keys.append(o.sem_i)
        with contextlib.ExitStack() as es:
            sems = {}
            for i, k in enumerate(sem_keys):
                sems[k] = es.enter_context(nc.semaphore("s%d" % i))
            last_val = {}
            for o in ops:
                if o.signal:
                    last_val[o.sem_i] = max(last_val.get(o.sem_i, 0), o.val)
            block = es.enter_context(nc.Block())
            handles = {"sp": block.sync, "act": block.scalar, "pool": block.gpsimd,
                       "dve": block.vector, "pe": block.tensor}

            def make(engname):
                def body(eng):
                    waited = {}
                    for o in self.eng_ops[engname]:
                        need = {}
                        for d in o.deps:
                            p = ops[d]
                            if not p.signal:
                                continue
                            if p.stream == ("eng", "pe") and engname == "pe":
                                continue
                            v_ = last_val[p.sem_i] if (p.stream[0] == "dma" and p.stream[1] in WAIT_ALL_STREAMS) else p.val
                            if p.batch is not None:
                                v_ = batch_max[(p.sem_i, p.batch)]
                            if v_ > need.get(p.sem_i, 0):
                                need[p.sem_i] = v_
                        for k, v in need.items():
                            if waited.get(k, 0) < v:
                                eng.wait_ge(sems[k], v)
                                waited[k] = v
                        ins = o.fn(eng)
                        if o.signal:
                            ins.then_inc(sems[o.sem_i], o.inc)
                    if engname == "sp":
                        for k in sem_keys:
                            if k[0][0] == "dma" and k[0][1].startswith(final_wait_streams):
                                eng.wait_ge(sems[k], last_val[k])
                return body

            for e in ENGS:
                handles[e](make(e))
        return len(ops)


def build_nc():
    nc = bass.Bass("TRN2", target_bir_lowering=False)

    def din(name, shape):
        return nc.dram_tensor(name, list(shape), F32, kind="ExternalInput").ap()

    xl = din("xl", [LT, D])
    w_in = din("w_in", [D, IN_DIM])
    w_out = din("w_out", [D, D])
    w_up = din("w_up", [D, 2 * DFF])
    w_down = din("w_down", [DFF, D])
    g1 = din("g1", [D]); g2 = din("g2", [D]); gf = din("gf", [D])
    rng = din("rng", [1024])
    bfg = din("bfg", [NB * 8])
    convw = din("convw", [128, 2 * NFF * 3])
    convb = din("convb", [128, 2 * NFF])
    cosT = din("cosT", [128, NB * 64]); sinT = din("sinT", [128, NB * 64])
    dkT = din("dkT", [128, NB * 8]); dqT = din("dqT", [128, NB * 8]); kbT = din("kbT", [128, NB * 8])
    wffd = din("wffd", [128, 16 * 8])
    c_ident = din("c_ident", [128, 128]); c_tri = din("c_tri", [128, 128]); c_ones = din("c_ones", [128, 128])
    c_maskR = din("c_maskR", [128, 128]); c_maskT = din("c_maskT", [128, MASKW]); c_sel = din("c_sel", [128, 1024])
    y = nc.dram_tensor("y", [1024, D], F32, kind="ExternalOutput").ap()
    dbg = {}
    if DEBUG:
        dbg["d_aT"] = nc.dram_tensor("d_aT", [128, 16 * LT], BF16, kind="ExternalOutput").ap()
        dbg["d_mix"] = nc.dram_tensor("d_mix", [128, 16 * NOWN], BF16, kind="ExternalOutput").ap()
        dbg["d_h1"] = nc.dram_tensor("d_h1", [128, 8 * D], F32, kind="ExternalOutput").ap()
        dbg["d_cum"] = nc.dram_tensor("d_cum", [128, NB * 8], F32, kind="ExternalOutput").ap()
        dbg["d_cT"] = nc.dram_tensor("d_cT", [128, 16 * NOWN], BF16, kind="ExternalOutput").ap()
        dbg["d_hh"] = nc.dram_tensor("d_hh", [2, D], F32, kind="ExternalOutput").ap()

    w_in_v = w_in.rearrange("(c p) n -> p c n", p=128)
    w_out_v = w_out.rearrange("(c p) n -> p c n", p=128)
    w_up_v = w_up.rearrange("(c p) n -> p c n", p=128)

    with contextlib.ExitStack() as es:
        def sb(name, shape, dt):
            return es.enter_context(nc.sbuf_tensor(name, list(shape), dt))

        def ps(name, shape, dt):
            return es.enter_context(nc.psum_tensor(name, list(shape), dt))

        R1 = sb("R1", [128, 16 * LT], BF16)
        aT = R1[:, :].rearrange("p (c t) -> p c t", c=16)
        R1f = R1.bitcast(F32)
        h2 = R1f[:, 0:8 * D].rearrange("p (b f) -> p b f", b=8)
        mixT = sb("mixT", [128, 16, NOWN], BF16)
        ringT = sb("ringT", [128, NSLOT * 2048], BF16)
        ring = [ringT[:, i * 2048:(i + 1) * 2048] for i in range(NSLOT)]
        R3N = 20992
        R3 = sb("R3", [128, R3N], BF16)
        R3f = R3.bitcast(F32)
        gb = sb("gb", [128, D], F32)
        ident_b = sb("ident_b", [128, 128], BF16)
        ones_b = sb("ones_b", [128, 128], BF16)
        maskT = sb("maskT", [128, MASKW], BF16)
        sel = sb("sel", [128, 1024], BF16)
        ident_f = sb("ident_f", [128, 128], F32)
        tri_f = sb("tri_f", [128, 128], F32)
        ones_f = sb("ones_f", [128, 128], F32)
        maskR = sb("maskR", [128, 128], F32)
        convw_s = sb("convw_s", [128, 2 * NFF * 3], F32)
        convb_s = sb("convb_s", [128, 2 * NFF], F32)
        bfg_s = sb("bfg_s", [128, NB * 8], F32)
        kb_s = sb("kb_s", [128, NB, 8], F32)
        cos_s = sb("cos_s", [128, NB, 64], F32)
        sin_s = sb("sin_s", [128, NB, 64], F32)
        dk_s = sb("dk_s", [128, NB, 8], F32)
        dq_s = sb("dq_s", [128, NB, 8], F32)
        ndk_s = sb("ndk_s", [128, NB, 8], F32)
        ndq_s = sb("ndq_s", [128, NB, 8], F32)
        spt = sb("spt", [128, NB * 8], F32)
        cumn = sb("cumn", [128, NB * 8], F32)
        tot = sb("tot", [128, NB * 8], F32)
        pre = sb("pre", [128, NB * 8], F32)
        biasK = sb("biasK", [128, NB * 8], F32)
        Rb = sb("Rb", [128, 9 * 128], BF16)
        wff = sb("wff", [128, 16, 8], BF16)
        st1 = sb("st1", [128, 32], F32)
        eps_t = sb("eps_t", [128, 1], F32)
        st2 = sb("st2", [128, 64], F32)

        pp = [ps("pp%d" % i, [128, 1024], F32) for i in range(4)]
        ppb = [p.bitcast(BF16) for p in pp]

        def bank(i):
            return pp[i // 2][:, (i % 2) * 512:(i % 2) * 512 + 512]

        P = Prog(nc)
        slot_ctr = [0]

        def next_slot():
            s = slot_ctr[0] % NSLOT
            slot_ctr[0] += 1
            return s

        def dma_sp(out, in_, reads=(), writes=(), stream="const"):
            P.op("sp", lambda e: e.dma_start(out=out, in_=in_), reads=reads, writes=writes, dma=stream)

        def dma_cast(out, in_, reads=(), writes=(), stream="constp", batch=None):
            P.op("pool", lambda e: e.dma_start(out=out, in_=in_), reads=reads, writes=writes, dma=stream, batch=batch)

        def load_slice(wv, c0, ncols=128):
            s = next_slot()
            v = ring[s][:, 0:16 * ncols].rearrange("p (c n) -> p c n", c=16)
            for hh in range(2):
                dma_cast(v[:, 8 * hh:8 * hh + 8, :], wv[:, 8 * hh:8 * hh + 8, c0:c0 + ncols], writes=[("ring", s, hh)], stream="ring%d" % s,
                         batch=slot_ctr[0])
            return s, v

        def rkeys(s):
            return [("ring", s, 0), ("ring", s, 1)]

        dma_sp(gb[:], g1.partition_broadcast(128), writes=["gb"], stream="gb")
        P.op("dve", lambda e: e.memset(eps_t[:], EPS), writes=["eps_t"])
        P.op("dve", lambda e: e.memset(pre[:], 0.0), writes=["pre"])
        dma_cast(ident_b[:], c_ident, writes=["ident_b"])
        dma_cast(wff[:, :, :].rearrange("p c n -> p (c n)"), wffd, writes=[("wff", 0), ("wff", 1)])
        dma_cast(ones_b[:], c_ones, writes=["ones_b"])
        dma_cast(maskT[:], c_maskT, writes=["maskT"])
        dma_cast(sel[:], c_sel, writes=["sel"])

        def late_constants():
            dma_sp(cos_s[:, :, :].rearrange("p b d -> p (b d)"), cosT, writes=["cos"])
            dma_sp(sin_s[:, :, :].rearrange("p b d -> p (b d)"), sinT, writes=["sin"])
            dma_sp(dk_s[:, :, :].rearrange("p b h -> p (b h)"), dkT, writes=["dk"])
            dma_sp(dq_s[:, :, :].rearrange("p b h -> p (b h)"), dqT, writes=["dq"])
            P.op("dve", lambda e: e.tensor_scalar_mul(out=ndk_s[:, :, :], in0=dk_s[:, :, :], scalar1=-1.0), reads=["dk"], writes=["ndk"])
            P.op("dve", lambda e: e.tensor_scalar_mul(out=ndq_s[:, :, :], in0=dq_s[:, :, :], scalar1=-1.0), reads=["dq"], writes=["ndq"])
            dma_sp(ident_f[:], c_ident, writes=["ident_f"])
            dma_sp(tri_f[:], c_tri, writes=["tri_f"])
            dma_sp(ones_f[:], c_ones, writes=["ones_f"])
            dma_sp(maskR[:], c_maskR, writes=["maskR"])
            dma_sp(bfg_s[:], bfg.partition_broadcast(128), writes=["bfg"])
            dma_sp(kb_s[:, :, :].rearrange("p b h -> p (b h)"), kbT, writes=["kb"])
            dma_sp(convw_s[:], convw, writes=["convw"])
            dma_sp(convb_s[:], convb, writes=["convb"])

        def rms_rstd(src_ap, junk_ap, col, rkeys_, jkey, extra_writes=()):
            npart = src_ap.shape[0]
            c = st1[0:npart, col:col + 1]
            P.op("act", lambda e: e.activation(out=junk_ap, in_=src_ap, func=AF.Square, accum_out=c),
                 reads=rkeys_, writes=[jkey, ("st1", col)] + list(extra_writes))
            P.op("act", lambda e: e.activation(out=c, in_=c, func=AF.Ln, bias=eps_t[0:npart, 0:1], scale=1.0 / D),
                 reads=[("st1", col), "eps_t"], writes=[("st1", col)])
            P.op("act", lambda e: e.activation(out=c, in_=c, func=AF.Exp, scale=-0.5), reads=[("st1", col)], writes=[("st1", col)])
            return c

        xsA = [R3f[:, 0:2048], R3f[:, 2048:4096], R3f[:, 4096:6144]]
        xnA = [R3[:, 12288:14336], R3[:, 14336:16384]]
        junkA = R3[:, 16384:18432]
        b6 = bank(6)

        def aTk(tb):
            return [("aT", tb, 0), ("aT", tb, 1)]

        def A1(tb):
            xs = xsA[tb % 3]
            dma_sp(xs, xl[tb * 128:(tb + 1) * 128, :], writes=[("xs", tb % 3)], stream="xs%d" % (tb % 3))
            rms_rstd(xs, junkA, tb % 3, [("xs", tb % 3)], "junkA")

        def A2(tb):
            xs = xsA[tb % 3]
            c = st1[:, tb % 3:tb % 3 + 1]
            xn = xnA[tb % 2]
            P.op("dve", lambda e: e.scalar_tensor_tensor(out=xn, in0=xs, scalar=c, in1=gb[:], op0=ALU.mult, op1=ALU.mult),
                 reads=[("xs", tb % 3), ("st1", tb % 3), "gb"], writes=[("xnA", tb % 2)])

        def A3(tb):
            xn = xnA[tb % 2]
            pv = ppb[tb % 2]
            for cc in range(16):
                P.op("pe", lambda e, cc=cc: e.transpose(out=pv[:, cc * 128:(cc + 1) * 128], in_=xn[:, cc * 128:(cc + 1) * 128], identity=ident_b[:]),
                     reads=[("xnA", tb % 2), "ident_b"], writes=[("ps", 2 * (tb % 2)), ("ps", 2 * (tb % 2) + 1)])
            pv3 = pv[:, 0:2048].rearrange("p (c t) -> p c t", c=16)
            P.op("act", lambda e: e.copy(out=aT[:, 0:8, tb * 128:(tb + 1) * 128], in_=pv3[:, 0:8, :]),
                 reads=[("ps", 2 * (tb % 2))], writes=[("aT", tb, 0)])
            P.op("dve", lambda e: e.tensor_copy(out=aT[:, 8:16, tb * 128:(tb + 1) * 128], in_=pv3[:, 8:16, :]),
                 reads=[("ps", 2 * (tb % 2) + 1)], writes=[("aT", tb, 1)])

        def A4(tb):
            for kc in range(16):
                P.op("pe", lambda e, kc=kc: e.matmul(b6[:, tb * 8:tb * 8 + 8], lhsT=aT[:, kc, tb * 128:(tb + 1) * 128], rhs=wff[:, kc, :],
                                                    start=(kc == 0), stop=(kc == 15)),
                     reads=aTk(tb) + [("wff", 0), ("wff", 1)], writes=[("ps", 6)])

        for i in range(NB + 3):
            if i == 3:
                late_constants()
            if i < NB:
                A1(i)
            if 0 <= i - 1 < NB:
                A2(i - 1)
            if 0 <= i - 2 < NB:
                A3(i - 2)
            if 0 <= i - 3 < NB:
                A4(i - 3)


        def aT_range_keys(l0, l1):
            ks = []
            for tb in range(l0 // 128, (l1 - 1) // 128 + 1):
                ks += aTk(tb)
            return ks

        if DEBUG:
            dma_sp(dbg["d_aT"], R1[:, :], reads=aT_range_keys(0, LT), stream="st")

        if STOP == "A":
            P.emit(final_wait_streams="st")
            return nc
        P.barrier()
        dma_sp(gb[:, 0:1024], rng.partition_broadcast(128), writes=["gb"], stream="gb")

        b7 = bank(7)
        P.op("dve", lambda e: e.tensor_tensor(out=spt[:], in0=b6[:, 0:NB * 8], in1=bfg_s[:], op=ALU.add), reads=[("ps", 6), "bfg"], writes=["spt"])
        P.op("act", lambda e: e.activation(out=spt[:], in_=spt[:], func=AF.Exp, scale=-1.0), reads=["spt"], writes=["spt"])
        P.op("act", lambda e: e.activation(out=spt[:], in_=spt[:], func=AF.Ln, bias=1.0, scale=1.0), reads=["spt"], writes=["spt"])
        P.op("pe", lambda e: e.matmul(b7[:, 0:136], lhsT=tri_f[:], rhs=spt[:], start=True, stop=True), reads=["spt", "tri_f"], writes=[("ps", 7)])
        P.op("pe", lambda e: e.matmul(b7[:, 136:272], lhsT=ones_f[:], rhs=spt[:], start=True, stop=True), reads=["spt", "ones_f"], writes=[("ps", 7)])
        P.op("dve", lambda e: e.tensor_copy(out=cumn[:], in_=b7[:, 0:136]), reads=[("ps", 7)], writes=["cumn"])
        P.op("dve", lambda e: e.tensor_copy(out=tot[:], in_=b7[:, 136:272]), reads=[("ps", 7)], writes=["tot"])
        for b in range(1, NB):
            P.op("dve", lambda e, b=b: e.tensor_tensor(out=pre[:, b * 8:b * 8 + 8], in0=pre[:, (b - 1) * 8:b * 8], in1=tot[:, (b - 1) * 8:b * 8], op=ALU.add),
                 reads=["pre", "tot"], writes=["pre"])
        P.op("dve", lambda e: e.tensor_tensor(out=cumn[:], in0=cumn[:], in1=pre[:], op=ALU.add), reads=["cumn", "pre"], writes=["cumn"])
        P.op("dve", lambda e: e.tensor_tensor(out=biasK[:], in0=cumn[:], in1=kb_s[:, :, :].rearrange("p b h -> p (b h)"), op=ALU.add),
             reads=["cumn", "kb"], writes=["biasK"])
        for c in range(9):
            tb = 8 + c
            dst = pp[2][0:8, c * 128:(c + 1) * 128] if c < 8 else pp[3][0:8, 512:640]
            P.op("pe", lambda e, dst=dst, tb=tb: e.transpose(out=dst, in_=cumn[:, tb * 8:tb * 8 + 8], identity=ident_f[:]),
                 reads=["cumn", "ident_f"], writes=[("ps", 4), ("ps", 5)] if c < 8 else [("ps", 7)])
        P.op("dve", lambda e: e.memset(Rb[:], 0.0), writes=["Rb"])
        P.op("act", lambda e: e.activation(out=Rb[0:8, 0:1024], in_=pp[2][0:8, 0:1024], func=AF.Copy, scale=-1.0 / SCALE),
             reads=[("ps", 4), ("ps", 5)], writes=["Rb"])
        P.op("act", lambda e: e.activation(out=Rb[0:8, 1024:1152], in_=pp[3][0:8, 512:640], func=AF.Copy, scale=-1.0 / SCALE),
             reads=[("ps", 7)], writes=["Rb"])
        if DEBUG:
            dma_sp(dbg["d_cum"], cumn[:], reads=["cumn"], stream="st")
        if STOP == "B0":
            P.emit(final_wait_streams="st")
            return nc

        o_ = 0
        def carve(n):
            nonlocal o_
            a = o_
            o_ += n
            return a
        fkT = R3[:, carve(LT):o_]
        _a = carve(1028)
        fqT = R3[:, _a:_a + NOWN]
        rvfv = R3[:, carve(NB * 256):o_].rearrange("p (b n) -> p b n", b=NB)
        rkt = R3[:, carve(NB * 128):o_].rearrange("p (b n) -> p b n", b=NB)
        rqT = R3[:, carve(1152):o_]
        rkT = R3[:, carve(1152):o_]
        sg = R3[:, carve(1152):o_].rearrange("p (b n) -> p b n", b=9)
        rqt = R3[:, carve(1152):o_].rearrange("p (b n) -> p b n", b=9)
        PTt = [R3[:, carve(342):o_] for _ in range(3)]
        smt = [R3[:, carve(128):o_] for _ in range(2)]
        Sbf = [R3[:, carve(128):o_] for _ in range(2)]
        junkB = R3[:, carve(128):o_]
        assert o_ % 2 == 0
        fo = o_ // 2
        def carvef(n):
            nonlocal fo
            a = fo
            fo += n
            return a
        o_all = R3f[:, carvef(1152):fo].rearrange("p (b n) -> p b n", b=9)
        rden = R3f[:, carvef(342):fo]
        rU = R3f[:, carvef(128):fo]
        rW = R3f[:, carvef(128):fo]
        ro = R3[:, fo * 2:fo * 2 + 1152].rearrange("p (b n) -> p b n", b=9)
        assert fo * 2 + 1152 <= R3N, fo * 2

        def rotary(src, dst, tbl, dec_ap, ndec_ap, rk, wk):
            Cb = cos_s[:, tbl:tbl + 1, :].to_broadcast([128, 2, 64])
            S = sin_s[:, tbl, :]
            src3 = src[:, 0:128].rearrange("p (a b) -> p a b", a=2)
            U3 = rU.rearrange("p (a b) -> p a b", a=2)
            t1 = src[:, 0:64]
            t2 = src[:, 64:128]
            rd = list(rk) + ["cos", "sin", "dk", "dq", "ndk", "ndq"]
            P.op("dve", lambda e: e.scalar_tensor_tensor(out=U3, in0=src3, scalar=dec_ap, in1=Cb, op0=ALU.mult, op1=ALU.mult), reads=rd, writes=["rU"])
            P.op("dve", lambda e: e.scalar_tensor_tensor(out=rW[:, 0:64], in0=t2, scalar=ndec_ap, in1=S, op0=ALU.mult, op1=ALU.mult), reads=rd, writes=["rW"])
            P.op("dve", lambda e: e.scalar_tensor_tensor(out=rW[:, 64:128], in0=t1, scalar=dec_ap, in1=S, op0=ALU.mult, op1=ALU.mult), reads=rd + ["rW"], writes=["rW"])
            P.op("dve", lambda e: e.tensor_tensor(out=dst, in0=rU, in1=rW, op=ALU.add), reads=["rU", "rW"], writes=wk)

        vA = ringT[:, 0:6144].rearrange("p (c n) -> p c n", c=16)
        vB = ringT[:, 6144:10240].rearrange("p (c n) -> p c n", c=16)
        vC = ringT[:, 10240:12288].rearrange("p (c n) -> p c n", c=16)
        vD = ringT[:, 12288:14336].rearrange("p (c n) -> p c n", c=16)
        regions = {"A": (vA, 3), "B": (vB, 2), "C": (vC, 1), "D": (vD, 1)}

        def rg_keys(name):
            return [("rg" + name, si, hh) for si in range(regions[name][1]) for hh in range(2)]

        def load_region(name, col_offs, h):
            v, _ = regions[name]
            for si, c0 in enumerate(col_offs):
                for hh in range(2):
                    dma_cast(v[:, 8 * hh:8 * hh + 8, si * 128:(si + 1) * 128], w_in_v[:, 8 * hh:8 * hh + 8, c0:c0 + 128],
                             writes=[("rg" + name, si, hh)], stream="rg" + name, batch=h)

        all_rg = [k for nm in ("A", "B", "C", "D") for k in rg_keys(nm)]

        def load_wout_quarter(qp, extra=()):
            sl = []
            for m in range(4):
                s_ = next_slot()
                v_ = ring[s_].rearrange("p (c n) -> p c n", c=4)
                dma_cast(v_, w_out_v[:, 4 * m:4 * m + 4, qp * 512:(qp + 1) * 512], writes=rkeys(s_) + list(extra), stream="ring%d" % s_)
                sl.append((s_, v_))
            return sl
        wq = {}
        deferred_tail = [None]

        def load_head(h):
            load_region("A", [1024 + h * 128, 2048 + h * 128, 6144 + h * 128], h)
            load_region("B", [h * 128, 3072 + h * 128], h)
            load_region("C", [4096 + h * 128], h)
            load_region("D", [5120 + h * 128], h)
        load_head(0)
        for h in range(NHEADS_RUN):
            for tb in range(NB):
                bA = 2 * (tb % 2)
                bB = bA + 1
                for kc in range(16):
                    P.op("pe", lambda e, tb=tb, kc=kc, bA=bA: e.matmul(
                        bank(bA)[:, 0:384], lhsT=aT[:, kc, tb * 128:(tb + 1) * 128], rhs=vA[:, kc, :],
                        start=(kc == 0), stop=(kc == 15)),
                        reads=aTk(tb) + rg_keys("A"), writes=[("ps", bA)])
                    if tb >= 8:
                        P.op("pe", lambda e, tb=tb, kc=kc, bB=bB: e.matmul(
                            bank(bB)[:, 0:256], lhsT=aT[:, kc, tb * 128:(tb + 1) * 128], rhs=vB[:, kc, :],
                            start=(kc == 0), stop=(kc == 15)),
                            reads=aTk(tb) + rg_keys("B"), writes=[("ps", bB)])
                P.op("act", lambda e, tb=tb, bA=bA: e.copy(out=rvfv[:, tb, :], in_=bank(bA)[:, 128:384]), reads=[("ps", bA)], writes=[("rvfv", tb)])
                if tb >= 8:
                    P.op("act", lambda e, tb=tb, bB=bB: e.copy(out=sg[:, tb - 8, :], in_=bank(bB)[:, 128:256]), reads=[("ps", bB)], writes=["sg"])
                rotary(bank(bA), rkt[:, tb, :], tb, dk_s[:, tb, h:h + 1], ndk_s[:, tb, h:h + 1], [("ps", bA)], [("rkt", tb)])
                if tb >= 8:
                    rotary(bank(bB), rqt[:, tb - 8, :], tb, dq_s[:, tb, h:h + 1], ndq_s[:, tb, h:h + 1], [("ps", bB)], [("rqt", tb - 8)])
            P.op("act", lambda e: e.activation(out=sg[:, :, :], in_=sg[:, :, :], func=AF.Silu), reads=["sg"], writes=["sg"])
            if deferred_tail[0] is not None:
                deferred_tail[0]()
                deferred_tail[0] = None
            for nb in range(5):
                n0 = nb * 512
                nw = min(512, LT - n0)
                bk = 4 + nb % 2
                for kc in range(16):
                    P.op("pe", lambda e, kc=kc, n0=n0, nw=nw, bk=bk: e.matmul(bank(bk)[:, 0:nw], lhsT=vD[:, kc, :], rhs=aT[:, kc, n0:n0 + nw],
                                                                          start=(kc == 0), stop=(kc == 15)),
                         reads=aT_range_keys(n0, n0 + nw) + rg_keys("D"), writes=[("ps", bk)])
                if nb % 2 == 0:
                    P.op("act", lambda e, n0=n0, nw=nw, bk=bk: e.copy(out=fkT[:, n0:n0 + nw], in_=bank(bk)[:, 0:nw]), reads=[("ps", bk)], writes=[("fkT", nb)])
                else:
                    P.op("dve", lambda e, n0=n0, nw=nw, bk=bk: e.tensor_copy(out=fkT[:, n0:n0 + nw], in_=bank(bk)[:, 0:nw]), reads=[("ps", bk)], writes=[("fkT", nb)])
            for g in range(3):
                n0 = OWN0 + 342 * g
                bk = 4 + (g + 1) % 2
                for kc in range(16):
                    P.op("pe", lambda e, kc=kc, n0=n0, bk=bk: e.matmul(bank(bk)[:, 0:342], lhsT=vC[:, kc, :], rhs=aT[:, kc, n0:n0 + 342],
                                                                    start=(kc == 0), stop=(kc == 15)),
                         reads=aT_range_keys(n0, n0 + 342) + rg_keys("C"), writes=[("ps", bk)])
                P.op("act", lambda e, g=g, bk=bk: e.copy(out=fqT[:, 342 * g:342 * g + 342], in_=bank(bk)[:, 0:342]), reads=[("ps", bk)], writes=[("fqT", g)])

            if h + 1 < NHEADS_RUN:
                load_head(h + 1)
            else:
                wq[0] = load_wout_quarter(0, all_rg)
                wq[1] = load_wout_quarter(1, all_rg)
            for c in range(9):
                P.op("pe", lambda e, c=c: e.transpose(out=ppb[2][:, c * 128:(c + 1) * 128], in_=rqt[:, c, :], identity=ident_b[:]),
                     reads=[("rqt", c), "ident_b"], writes=[("ps", 4), ("ps", 5)])
                P.op("pe", lambda e, c=c: e.transpose(out=ppb[3][:, c * 128:(c + 1) * 128], in_=rkt[:, 8 + c, :], identity=ident_b[:]),
                     reads=[("rkt", 8 + c), "ident_b"], writes=[("ps", 6), ("ps", 7)])
            P.op("act", lambda e: e.copy(out=rqT, in_=ppb[2][:, 0:1152]), reads=[("ps", 4), ("ps", 5)], writes=["rqT"])
            P.op("dve", lambda e: e.tensor_copy(out=rkT, in_=ppb[3][:, 0:1152]), reads=[("ps", 6), ("ps", 7)], writes=["rkT"])
            Sps = bank(7)[:, 0:128]
            ret_steps = []

            def r_init():
                for b in range(8):
                    P.op("pe", lambda e, b=b: e.matmul(Sps, lhsT=rkt[:, b, :], rhs=rvfv[:, b, 0:128], start=(b == 0), stop=(b == 7), skip_group_check=True),
                         reads=[("rkt", b), ("rvfv", b)], writes=[("ps", 7)])
            ret_steps.append(r_init)

            def r_a(c):
                sTp = bank(6)[:, (c % 2) * 128:(c % 2) * 128 + 128]
                if c > 0:
                    P.op("pe", lambda e: e.matmul(Sps, lhsT=rkt[:, 7 + c, :], rhs=rvfv[:, 7 + c, 0:128], start=False, stop=True, skip_group_check=True),
                         reads=[("rkt", 7 + c), ("rvfv", 7 + c)], writes=[("ps", 7)])
                P.op("act", lambda e: e.copy(out=Sbf[c % 2], in_=Sps), reads=[("ps", 7)], writes=[("Sbf", c % 2)])
                P.op("pe", lambda e: e.matmul(sTp, lhsT=rkT[:, c * 128:(c + 1) * 128], rhs=rqT[:, c * 128:(c + 1) * 128], start=True, stop=True),
                     reads=["rkT", "rqT"], writes=[("ps", 6)])
                P.op("dve", lambda e: e.tensor_tensor(out=smt[c % 2], in0=sTp, in1=maskR[:], op=ALU.mult),
                     reads=[("ps", 6), "maskR"], writes=[("smt", c % 2)])

            def r_b(c):
                op_ = bank(4)[:, (c % 2) * 128:(c % 2) * 128 + 128]
                P.op("pe", lambda e: e.matmul(op_, lhsT=rqT[:, c * 128:(c + 1) * 128], rhs=Sbf[c % 2], start=True, stop=False),
                     reads=["rqT", ("Sbf", c % 2)], writes=[("ps", 4)])
                P.op("pe", lambda e: e.matmul(op_, lhsT=smt[c % 2], rhs=rvfv[:, 8 + c, 0:128], start=False, stop=True),
                     reads=[("smt", c % 2), ("rvfv", 8 + c)], writes=[("ps", 4)])
                P.op("dve", lambda e: e.tensor_copy(out=o_all[:, c, :], in_=op_), reads=[("ps", 4)], writes=[("o_all", c)])
                P.op("act", lambda e: e.activation(out=junkB, in_=op_, func=AF.Square, accum_out=st2[:, 16 + c:17 + c]),
                     reads=[("ps", 4)], writes=["junkB", ("st2q", c)])
            for c in range(9):
                ret_steps.append(lambda c=c: r_a(c))
                ret_steps.append(lambda c=c: r_b(c))

            def r_tail(h=h):
                tail_eng = "dve" if h == NHEADS_RUN - 1 else "pool"
                oall_keys = [("o_all", c) for c in range(9)]
                sq_keys = [("st2q", c) for c in range(9)]
                mean = st2[:, 0:9]
                ssq = st2[:, 16:25]
                msq = st2[:, 32:41]
                rstd = st2[:, 48:57]
                P.op("dve", lambda e: e.reduce_sum(out=mean, in_=o_all[:, :, :], axis=AX.X), reads=oall_keys, writes=["st2m"])
                P.op(tail_eng, lambda e: e.tensor_scalar_mul(out=mean, in0=mean, scalar1=1.0 / 128), reads=["st2m"], writes=["st2m"])
                P.op(tail_eng, lambda e: e.tensor_tensor(out=msq, in0=mean, in1=mean, op=ALU.mult), reads=["st2m"], writes=["st2s"])
                P.op(tail_eng, lambda e: e.tensor_scalar(out=rstd, in0=ssq, scalar1=1.0 / 128, scalar2=EPS, op0=ALU.mult, op1=ALU.add), reads=sq_keys, writes=["st2r"])
                P.op(tail_eng, lambda e: e.tensor_tensor(out=rstd, in0=rstd, in1=msq, op=ALU.subtract), reads=["st2r", "st2s"], writes=["st2r"])
                P.op("act", lambda e: e.activation(out=rstd, in_=rstd, func=AF.Ln), reads=["st2r"], writes=["st2r"])
                P.op("act", lambda e: e.activation(out=rstd, in_=rstd, func=AF.Exp, scale=-0.5), reads=["st2r"], writes=["st2r"])
                for c in range(9):
                    P.op(tail_eng, lambda e, c=c: e.tensor_scalar(out=o_all[:, c, :], in0=o_all[:, c, :], scalar1=st2[:, c:c + 1], scalar2=st2[:, 48 + c:49 + c],
                                                               op0=ALU.subtract, op1=ALU.mult),
                         reads=[("o_all", c), "st2m", "st2r"], writes=[("o_all", c)])
                    P.op(tail_eng, lambda e, c=c: e.tensor_tensor(out=o_all[:, c, :], in0=o_all[:, c, :], in1=gb[:, h * 128:(h + 1) * 128], op=ALU.mult),
                         reads=[("o_all", c), "gb"], writes=[("o_all", c)])
                    P.op(tail_eng, lambda e, c=c: e.tensor_tensor(out=ro[:, c, :], in0=o_all[:, c, :], in1=sg[:, c, :], op=ALU.mult),
                         reads=[("o_all", c), "sg"], writes=[("ro", c)])

            def r_tail_pe(h=h):
                for c in range(9):
                    P.op("pe", lambda e, c=c: e.transpose(out=ppb[2][:, c * 128:(c + 1) * 128], in_=ro[:, c, :], identity=ident_b[:]),
                         reads=[("ro", c), "ident_b"], writes=[("ps", 4), ("ps", 5)])
                P.op("act", lambda e: e.copy(out=mixT[:, h, :], in_=ppb[2][:, 126:1152]), reads=[("ps", 4), ("ps", 5)], writes=[("mixh", h)])

            tiles = []
            for g in range(3):
                q0 = OWN0 + 342 * g
                kmax = (q0 + 342 - 1) // 128
                for kb in range(kmax + 1):
                    tiles.append((g, kb, kmax, q0))
            oTp = bank(2)[:, 0:342]
            dnp = bank(3)[:, 0:342]

            def f_s(ti, h=h):
                g, kb, kmax, q0 = tiles[ti]
                sbk = (0, 1, 5)[ti % 3]
                sb_ = bank(sbk)[:, 0:342]
                delta = 128 * kb - q0
                need_mask = (128 * kb + 127) > q0
                P.op("pe", lambda e: e.matmul(sb_, lhsT=fkT[:, kb * 128:(kb + 1) * 128], rhs=fqT[:, 342 * g:342 * g + 342], start=True, stop=False),
                     reads=[("fkT", kb // 4), ("fqT", g)], writes=[("ps", sbk)])
                P.op("pe", lambda e: e.matmul(sb_, lhsT=sel[:, h * 128:(h + 1) * 128], rhs=Rb[:, 126 + 342 * g:126 + 342 * g + 342],
                                              start=False, stop=(not need_mask)),
                     reads=["sel", "Rb"], writes=[("ps", sbk)])
                if need_mask:
                    off = XOFF - delta
                    assert 0 <= off and off + 342 <= MASKW, off
                    P.op("pe", lambda e: e.matmul(sb_, lhsT=ident_b[:], rhs=maskT[:, off:off + 342], start=False, stop=True),
                         reads=["ident_b", "maskT"], writes=[("ps", sbk)])
                pt = PTt[ti % 3]
                P.op("act", lambda e: e.activation(out=pt, in_=sb_, func=AF.Exp, bias=biasK[:, kb * 8 + h:kb * 8 + h + 1], scale=SCALE),
                     reads=[("ps", sbk), "biasK"], writes=[("PT", ti % 3)])

            def f_pv(ti, h=h):
                g, kb, kmax, q0 = tiles[ti]
                pt = PTt[ti % 3]
                ptk = ("PT", ti % 3)
                P.op("pe", lambda e: e.matmul(oTp, lhsT=rvfv[:, kb, 128:256], rhs=pt, start=(kb == 0), stop=(kb == kmax)),
                     reads=[("rvfv", kb), ptk], writes=[("ps", 2)])
                P.op("pe", lambda e: e.matmul(dnp, lhsT=ones_b[:], rhs=pt, start=(kb == 0), stop=(kb == kmax)),
                     reads=["ones_b", ptk], writes=[("ps", 3)])
                if kb == kmax:
                    P.op("dve", lambda e: e.reciprocal(out=rden, in_=dnp), reads=[("ps", 3)], writes=["rden"])
                    P.op("dve", lambda e: e.tensor_tensor(out=mixT[:, 8 + h, 342 * g:342 * g + 342], in0=oTp, in1=rden, op=ALU.mult),
                         reads=[("ps", 2), "rden"], writes=[("mixf", h, g)])

            fox_steps = []
            nt_ = len(tiles)
            def f_first():
                f_s(0)
                f_s(1)
            fox_steps.append(f_first)
            for ti in range(nt_):
                def st(ti=ti):
                    if ti + 2 < nt_:
                        f_s(ti + 2)
                    f_pv(ti)
                fox_steps.append(st)
            fi = 0
            last_head = (h == NHEADS_RUN - 1)
            per = [1, 1] if last_head else [3, 2]
            for ri, rs in enumerate(ret_steps):
                rs()
                k = 1 if ri == 0 else per[ri % 2]
                for _ in range(k):
                    if fi < len(fox_steps):
                        fox_steps[fi]()
                        fi += 1
            if last_head:
                r_tail()
            while fi < len(fox_steps):
                fox_steps[fi]()
                fi += 1
            if not last_head:
                r_tail()
            deferred_tail[0] = r_tail_pe
        deferred_tail[0]()


        mix_all = [("mixh", h) for h in range(NH)] + [("mixf", h, g) for h in range(NH) for g in range(3)]
        if DEBUG:
            dma_sp(dbg["d_mix"], mixT[:, :, :].rearrange("p c t -> p (c t)"), reads=mix_all, stream="st")
        if STOP == "B":
            P.emit(final_wait_streams="st")
            return nc

        P.barrier()

        dma_sp(gb[:], g2.partition_broadcast(128), writes=["gb"], stream="gb")
        xsC = [R3f[:, 0:512], R3f[:, 512:1024], R3f[:, 1024:1536]]
        hnC = [R3[:, 4096:6144], R3[:, 6144:8192]]
        junkC = R3[:, 8192:10240]
        hh = R3f[:, 6144:8192]
        blocks = [(-1, 0, OWN0)] + [(tb, 2 + 128 * tb, 1152 + 128 * tb) for tb in range(8)]
        xi = 0
        ubk = 0

        def c_norm1(bi, tb):
            src = hh[:, :] if tb < 0 else h2[:, tb, :]
            col = 4 + bi % 2
            hk = [("h1", bi, q) for q in range(4)]
            c = rms_rstd(src, junkC, col, hk, "junkC")
            hn = hnC[bi % 2]
            P.op("dve", lambda e: e.scalar_tensor_tensor(out=hn, in0=src, scalar=c, in1=gb[:], op0=ALU.mult, op1=ALU.mult),
                 reads=hk + [("st1", col), "gb"], writes=[("hnC", bi % 2)])

        def c_norm2(bi, tb, c0):
            hn = hnC[bi % 2]
            pv = ppb[2 + bi % 2]
            pk = [("ps", 4 + 2 * (bi % 2)), ("ps", 5 + 2 * (bi % 2))]
            for cc in range(16):
                P.op("pe", lambda e, cc=cc: e.transpose(out=pv[:, cc * 128:(cc + 1) * 128], in_=hn[:, cc * 128:(cc + 1) * 128], identity=ident_b[:]),
                     reads=[("hnC", bi % 2), "ident_b"], writes=pk)
            pv3 = pv[:, 0:2048].rearrange("p (c t) -> p c t", c=16)
            if tb < 0:
                P.op("act", lambda e: e.copy(out=mixT[:, :, 0:2], in_=pv3[:, :, 0:2]), reads=pk + mix_all, writes=[("cT", bi)])
            else:
                P.op("act", lambda e: e.copy(out=mixT[:, 0:8, c0:c0 + 128], in_=pv3[:, 0:8, :]), reads=pk[0:1] + mix_all, writes=[("cT", bi)])
                P.op("dve", lambda e: e.tensor_copy(out=mixT[:, 8:16, c0:c0 + 128], in_=pv3[:, 8:16, :]), reads=pk[1:2] + mix_all, writes=[("cTb", bi)])

        for qp in range(4):
            slots = wq[qp]
            pend = None
            for bi, (tb, c0, l0) in enumerate(blocks):
                xs = xsC[xi % 3]
                xk = ("xs", xi % 3)
                xstream = "xs%d" % (xi % 3)
                xi += 1
                dma_sp(xs, xl[l0:l0 + 128, qp * 512:(qp + 1) * 512], writes=[xk], stream=xstream)
                bk = ubk % 4
                ubk += 1
                ck = [("cT", bi), ("cTb", bi)] + ([("cT", 1), ("cTb", 1)] if tb < 0 else [])
                for kc in range(16):
                    s_, v_ = slots[kc // 4]
                    P.op("pe", lambda e, kc=kc, v_=v_, c0=c0, bk=bk: e.matmul(
                        bank(bk), lhsT=mixT[:, kc, c0:c0 + 128], rhs=v_[:, kc % 4, :], start=(kc == 0), stop=(kc == 15)),
                        reads=mix_all + ck + rkeys(s_), writes=[("ps", bk)])
                dst = hh[:, qp * 512:(qp + 1) * 512] if tb < 0 else h2[:, tb, qp * 512:(qp + 1) * 512]
                P.op("dve", lambda e, dst=dst, bk=bk, xs=xs: e.tensor_tensor(out=dst, in0=bank(bk), in1=xs, op=ALU.add),
                     reads=[("ps", bk), xk], writes=[("h1", bi, qp)])
                if qp == 3:
                    c_norm1(bi, tb)
                    if pend is not None:
                        c_norm2(*pend)
                    pend = (bi, tb, c0)
            if qp == 3:
                c_norm2(*pend)
            if qp + 2 < 4:
                wq[qp + 2] = load_wout_quarter(qp + 2)

        cT = mixT
        cT_all = [("cT", bi) for bi in range(9)] + [("cTb", bi) for bi in range(1, 9)]
        if DEBUG:
            dma_sp(dbg["d_h1"], R1f[:, 0:8 * D], reads=[("h1", bi, hp) for bi in range(1, 9) for hp in range(4)], stream="st")
        if DEBUG:
            dma_sp(dbg["d_cT"], mixT[:, :, :].rearrange("p c t -> p (c t)"), reads=cT_all, stream="st")
            dma_sp(dbg["d_hh"], hh[0:2, :], reads=[("h1", 0, q) for q in range(4)], stream="st")
        if STOP == "C":
            P.emit(final_wait_streams="st")
            return nc
        P.barrier()

        gated = [[R3[:, (gs * GRP + jj) * 1024:(gs * GRP + jj + 1) * 1024] for jj in range(GRP)] for gs in range(2)]
        fb = 2 * GRP * 1024 // 2
        Yg = [R3f[:, fb + i * 1024:fb + (i + 1) * 1024] for i in range(2)]
        Yv = [R3f[:, fb + 2048 + i * 1024:fb + 2048 + (i + 1) * 1024] for i in range(2)]
        sb0 = 2 * (fb + 4096)
        Sg = [R3[:, sb0 + i * 1024:sb0 + (i + 1) * 1024] for i in range(2)]
        assert sb0 + 2048 <= R3N
        nblk = [(0, 342), (342, 683), (683, 1024)]
        ub = [0]

        h2_keys = lambda tb: [("h2", tb, n) for n in range(4)]
        mixflat = mixT[:, :, :].rearrange("p c t -> p (c t)")
        otE = [mixflat[:, 0:4096].bitcast(F32), mixflat[:, 4096:8192].bitcast(F32)]
        junkE = mixflat[:, 8192:10240]

        def final_block(tb):
            col = 8 + tb % 2
            c = rms_rstd(h2[:, tb, :], junkE, col, h2_keys(tb), "junkE", extra_writes=cT_all)
            ot = otE[tb % 2]
            if tb % 2 == 0 or tb == 7:
                P.op("dve", lambda e: e.scalar_tensor_tensor(out=ot, in0=h2[:, tb, :], scalar=c, in1=gb[:], op0=ALU.mult, op1=ALU.mult),
                     reads=h2_keys(tb) + [("st1", col), "gb"], writes=[("ot", tb % 2)] + cT_all)
            else:
                P.op("act", lambda e: e.activation(out=ot, in_=h2[:, tb, :], func=AF.Copy, scale=c),
                     reads=h2_keys(tb) + [("st1", col)], writes=[("ot", tb % 2)] + cT_all)
                P.op("pool", lambda e: e.tensor_tensor(out=ot, in0=ot, in1=gb[:], op=ALU.mult),
                     reads=[("ot", tb % 2), "gb"], writes=[("ot", tb % 2)])
            dma_sp(y[tb * 128:(tb + 1) * 128, :], ot, reads=[("ot", tb % 2)], stream="st%d" % (tb % 2))

        def wdown_group(gi, dslots, last=False):
            gs = gi % 2
            for tb in range(8):
                b0 = 4 if tb % 2 == 0 else 0
                for jj in range(GRP):
                    s, v = dslots[jj]
                    for n in range(4):
                        bk = b0 + n
                        P.op("pe", lambda e, jj=jj, v=v, tb=tb, n=n, bk=bk, gs=gs: e.matmul(
                            bank(bk), lhsT=gated[gs][jj][:, tb * 128:(tb + 1) * 128], rhs=v[:, n * 512:(n + 1) * 512],
                            start=(jj == 0), stop=(jj == GRP - 1)),
                            reads=[("gated", gs, jj)] + rkeys(s), writes=[("ps", bk)])
                for n in range(4):
                    bk = b0 + n
                    P.op("dve", lambda e, tb=tb, n=n, bk=bk: e.tensor_tensor(out=h2[:, tb, n * 512:(n + 1) * 512], in0=h2[:, tb, n * 512:(n + 1) * 512], in1=bank(bk), op=ALU.add),
                         reads=[("ps", bk), ("h2", tb, n)], writes=[("h2", tb, n)])
                if last:
                    final_block(tb)

        def load_down(gi):
            dslots = []
            for jj in range(GRP):
                j = gi * GRP + jj
                s = next_slot()
                dma_cast(ring[s][:, :], w_down[j * 128:(j + 1) * 128, :], writes=rkeys(s), stream="ring%d" % s)
                dslots.append((s, ring[s]))
            return dslots

        prev = None
        pair_i = 0
        for gi in range(NFF // GRP):
            gs = gi % 2
            for jj in range(GRP):
                j = gi * GRP + jj
                pi = pair_i % 2
                pair_i += 1
                for half in range(2):
                    cidx = half * NFF + j
                    s, v = load_slice(w_up_v, half * DFF + j * 128)
                    Y = (Yg if half == 0 else Yv)[pi]
                    yk = ("Y", half, pi)
                    for (r0, r1) in nblk:
                        ln = r1 - r0
                        bk = ub[0] % 4
                        ub[0] += 1
                        for kc in range(16):
                            P.op("pe", lambda e, kc=kc, v=v, r0=r0, ln=ln, bk=bk: e.matmul(bank(bk)[:, 0:ln + 2], lhsT=v[:, kc, :], rhs=cT[:, kc, r0:r0 + ln + 2],
                                                                                    start=(kc == 0), stop=(kc == 15)),
                                 reads=cT_all + rkeys(s), writes=[("ps", bk)])
                        u = bank(bk)
                        P.op("act", lambda e, u=u, Y=Y, r0=r0, r1=r1, ln=ln, cidx=cidx: e.activation(
                            out=Y[:, r0:r1], in_=u[:, 2:ln + 2], func=AF.Identity, bias=convb_s[:, cidx:cidx + 1], scale=convw_s[:, cidx * 3 + 2:cidx * 3 + 3]),
                            reads=[("ps", bk), "convw", "convb"], writes=[yk])
                        P.op("dve", lambda e, u=u, Y=Y, r0=r0, r1=r1, ln=ln, cidx=cidx: e.scalar_tensor_tensor(
                            out=Y[:, r0:r1], in0=u[:, 1:ln + 1], scalar=convw_s[:, cidx * 3 + 1:cidx * 3 + 2], in1=Y[:, r0:r1], op0=ALU.mult, op1=ALU.add),
                            reads=[("ps", bk), "convw", yk], writes=[yk])
                        P.op("dve", lambda e, u=u, Y=Y, r0=r0, r1=r1, ln=ln, cidx=cidx: e.scalar_tensor_tensor(
                            out=Y[:, r0:r1], in0=u[:, 0:ln], scalar=convw_s[:, cidx * 3:cidx * 3 + 1], in1=Y[:, r0:r1], op0=ALU.mult, op1=ALU.add),
                            reads=[("ps", bk), "convw", yk], writes=[yk])
                    if half == 0:
                        P.op("act", lambda e, Y=Y, pi=pi: e.activation(out=Sg[pi], in_=Y, func=AF.Silu), reads=[yk], writes=[("Sg", pi)])
                    else:
                        P.op("dve", lambda e, Y=Y, pi=pi, gs=gs, jj=jj: e.tensor_tensor(out=gated[gs][jj], in0=Y, in1=Sg[pi], op=ALU.mult),
                             reads=[yk, ("Sg", pi)], writes=[("gated", gs, jj)])
            if prev is not None:
                wdown_group(prev, load_down(prev))
            prev = gi
        dma_sp(gb[:], gf.partition_broadcast(128), writes=["gb"], stream="gb")
        wdown_group(prev, load_down(prev), last=True)

        P.emit(final_wait_streams="st")
    return nc


_NC_CACHE = {}


def _consts():
    c = {}
    c["c_ident"] = np.eye(128, dtype=np.float32)
    c["c_tri"] = np.triu(np.ones((128, 128), np.float32))
    c["c_ones"] = np.ones((128, 128), np.float32)
    c["c_maskR"] = np.triu(np.ones((128, 128), np.float32))
    p = np.arange(128)[:, None]
    xx = np.arange(MASKW)[None, :]
    c["c_maskT"] = np.where(xx - XOFF < p, NEG, 0.0).astype(np.float32)
    sel = np.zeros((128, 8, 128), np.float32)
    for h in range(8):
        sel[h, h, :] = 1.0
    c["c_sel"] = sel.reshape(128, 1024)
    return c


def _core_tables(T0):
    l = np.arange(LT)
    t = l - 1152 + T0
    valid = t >= 0
    inv_freq = 1.0 / (10000.0 ** (np.arange(0, 128, 2, dtype=np.float64) / 128.0))
    ang = np.where(valid, t, 0)[:, None].astype(np.float64) * inv_freq[None, :]
    def pm(a):
        n = a.shape[1]
        return np.ascontiguousarray(a.reshape(NB, 128, n).transpose(1, 0, 2).reshape(128, NB * n))
    tabs = {"cosT": pm(np.cos(ang).astype(np.float32)), "sinT": pm(np.sin(ang).astype(np.float32))}
    log_g = np.log1p(-np.exp2(-5.0 - np.arange(8, dtype=np.float64)))
    rel = (l - 1152).astype(np.float64)
    tabs["dqT"] = pm(np.exp(rel[:, None] * log_g[None, :]).astype(np.float32))
    tabs["dkT"] = pm((np.exp(-rel[:, None] * log_g[None, :]) * SCALE).astype(np.float32))
    tabs["kbT"] = pm(np.repeat(np.where(valid, 0.0, NEG).astype(np.float32)[:, None], 8, axis=1))
    return tabs


def kernel(x, meta_tokens, norm1_gain, w_in, b_forget, ret_norm_gain, w_out, norm2_gain, w_up,
           conv_w, conv_b, w_down, final_norm_gain):
    f32 = np.float32
    x = np.asarray(x, f32)
    B = x.shape[0]
    if "nc" not in _NC_CACHE:
        _NC_CACHE["nc"] = build_nc()
    nc = _NC_CACHE["nc"]
    consts = _consts()
    shared = {
        "w_in": np.ascontiguousarray(np.asarray(w_in, f32)[0]),
        "w_out": np.ascontiguousarray(np.asarray(w_out, f32)[0]),
        "w_up": np.ascontiguousarray(np.asarray(w_up, f32)[0]),
        "w_down": np.ascontiguousarray(np.asarray(w_down, f32)[0]),
        "g1": np.ascontiguousarray(np.asarray(norm1_gain, f32)[0]),
        "g2": np.ascontiguousarray(np.asarray(norm2_gain, f32)[0]),
        "gf": np.ascontiguousarray(np.asarray(final_norm_gain, f32)),
        "rng": np.ascontiguousarray(np.asarray(ret_norm_gain, f32)[0]),
        "wffd": np.ascontiguousarray(np.asarray(w_in, f32)[0][:, 7168:7176].reshape(16, 128, 8).transpose(1, 0, 2).reshape(128, 128)),
        "bfg": np.ascontiguousarray(np.tile(np.asarray(b_forget, f32)[0], NB)),
        "convw": np.ascontiguousarray(np.asarray(conv_w, f32)[0].reshape(3, 2 * NFF, 128).transpose(2, 1, 0).reshape(128, 2 * NFF * 3)),
        "convb": np.ascontiguousarray(np.asarray(conv_b, f32)[0].reshape(2 * NFF, 128).T),
    }
    shared.update(consts)
    meta = np.asarray(meta_tokens, f32)
    in_maps = []
    for core in range(8):
        b, s = core // 2, core % 2
        T0 = 16 + 1024 * s
        full = np.concatenate([meta, x[b]], axis=0)
        xl = np.zeros((LT, D), f32)
        t_lo = T0 - 1152
        src_lo = max(t_lo, 0)
        xl[src_lo - t_lo:, :] = full[src_lo:T0 + 1024]
        m = dict(shared)
        m["xl"] = xl
        m.update(_core_tables(T0))
        in_maps.append(m)
    res = run_bass_kernel_spmd(nc, in_maps[:NCORES_RUN], core_ids=list(range(NCORES_RUN)))
    out = np.zeros((B, 2048, D), f32)
    for core in range(NCORES_RUN):
        b, s = core // 2, core % 2
        out[b, 1024 * s:1024 * (s + 1), :] = res.results[core]["y"]
    if DEBUG:
        kernel.debug = res.results
    return out
```

```python
import contextlib
import numpy as np
import concourse.bass as bass
import concourse.mybir as mybir
from concourse.bass_utils import run_bass_kernel_spmd

F32 = mybir.dt.float32
BF16 = mybir.dt.bfloat16
AF = mybir.ActivationFunctionType
ALU = mybir.AluOpType
AX = mybir.AxisListType

D = 2048
NB = 17
LT = NB * 128
OWN0 = 1150
NOWN = 1026
NH = 8
DFF = 5632
NFF = 44
IN_DIM = 7176
SCALE = 128 ** -0.5
EPS = 1e-6
NEG = -30000.0
XOFF = 300
MASKW = 768
NSLOT = 8
GRP = 4
DEBUG = False
STOP = None
NHEADS_RUN = 8
NCORES_RUN = 8
STOP2 = None
SKIP = set()

ENGS = ("sp", "act", "pool", "dve", "pe")
SEM_LIMIT = 12000
WAIT_ALL_STREAMS = ("const", "constp")


class _Op:
    __slots__ = ("eng", "fn", "deps", "signal", "stream", "sem_i", "val", "inc", "idx", "batch")


class Prog:
    def __init__(self, nc):
        self.nc = nc
        self.ops = []
        self.eng_ops = {e: [] for e in ENGS}
        self.last_w = {}
        self.readers = {}
        self.last_in_stream = {}
        self.barrier_deps = set()

    def op(self, eng, fn, reads=(), writes=(), dma=None, batch=None):
        o = _Op()
        o.batch = batch
        o.eng = eng
        o.fn = fn
        o.idx = len(self.ops)
        o.stream = ("dma", dma) if dma is not None else ("eng", eng)
        o.inc = 16 if dma is not None else 1
        o.signal = dma is not None
        deps = set(self.barrier_deps)
        writes = list(writes) + [k for k in reads if isinstance(k, tuple) and k[0] == "ps"]
        reads = [k for k in reads if not (isinstance(k, tuple) and k[0] == "ps")]
        for k in reads:
            w = self.last_w.get(k)
            if w is not None:
                deps.add(w)
        for k in writes:
            w = self.last_w.get(k)
            if w is not None:
                deps.add(w)
            for r in self.readers.get(k, ()):
                deps.add(r)
        o.deps = deps
        for k in reads:
            self.readers.setdefault(k, []).append(o.idx)
        for k in writes:
            self.last_w[k] = o.idx
            self.readers[k] = []
        self.ops.append(o)
        self.eng_ops[eng].append(o)
        self.last_in_stream[o.stream] = o.idx
        return o

    def barrier(self):
        self.barrier_deps = set(self.last_in_stream.values())

    def emit(self, final_wait_streams=()):
        nc = self.nc
        ops = self.ops
        for o in ops:
            for d in o.deps:
                p = ops[d]
                if p.stream == ("eng", "pe") and o.eng == "pe":
                    continue
                p.signal = True
        streams = {}
        for o in ops:
            if not o.signal:
                continue
            st = streams.setdefault(o.stream, {"n": 0, "cur": 0})
            if st["cur"] + o.inc > SEM_LIMIT:
                st["n"] += 1
                st["cur"] = 0
            st["cur"] += o.inc
            o.sem_i = (o.stream, st["n"])
            o.val = st["cur"]
        batch_max = {}
        for o in ops:
            if o.signal and o.batch is not None:
                k = (o.sem_i, o.batch)
                batch_max[k] = max(batch_max.get(k, 0), o.val)
        sem_keys = []
        seen = set()
        for o in ops:
            if o.signal and o.sem_i not in seen:
                seen.add(o.sem_i)
                sem_keys.append(o.sem_i)
        with contextlib.ExitStack() as es:
            sems = {}
            for i, k in enumerate(sem_keys):
                sems[k] = es.enter_context(nc.semaphore("s%d" % i))
            last_val = {}
            for o in ops:
                if o.signal:
                    last_val[o.sem_i] = max(last_val.get(o.sem_i, 0), o.val)
            block = es.enter_context(nc.Block())
            handles = {"sp": block.sync, "act": block.scalar, "pool": block.gpsimd,
                       "dve": block.vector, "pe": block.tensor}

            def make(engname):
                def body(eng):
                    waited = {}
                    for o in self.eng_ops[engname]:
                        need = {}
                        for d in o.deps:
                            p = ops[d]
                            if not p.signal:
                                continue
                            if p.stream == ("eng", "pe") and engname == "pe":
                                continue
                            v_ = last_val[p.sem_i] if (p.stream[0] == "dma" and p.stream[1] in WAIT_ALL_STREAMS) else p.val
                            if p.batch is not None:
                                v_ = batch_max[(p.sem_i, p.batch)]
                            if v_ > need.get(p.sem_i, 0):
                                need[p.sem_i] = v_
                        for k, v in need.items():
                            if waited.get(k, 0) < v:
                                eng.wait_ge(sems[k], v)
                                waited[k] = v
                        ins = o.fn(eng)
                        if o.signal:
                            ins.then_inc(sems[o.sem_i], o.inc)
                    if engname == "sp":
                        for k in sem_keys:
                            if k[0][0] == "dma" and k[0][1].startswith(final_wait_streams):
                                eng.wait_ge(sems[k], last_val[k])
                return body

            for e in ENGS:
                handles[e](make(e))
        return len(ops)


def build_nc():
    nc = bass.Bass("TRN2", target_bir_lowering=False)

    def din(name, shape):
        return nc.dram_tensor(name, list(shape), F32, kind="ExternalInput").ap()

    xl = din("xl", [LT, D])
    w_in = din("w_in", [D, IN_DIM])
    w_out = din("w_out", [D, D])
    w_up = din("w_up", [D, 2 * DFF])
    w_down = din("w_down", [DFF, D])
    g1 = din("g1", [D]); g2 = din("g2", [D]); gf = din("gf", [D])
    rng = din("rng", [1024])
    bfg = din("bfg", [NB * 8])
    convw = din("convw", [128, 2 * NFF * 3])
    convb = din("convb", [128, 2 * NFF])
    cosT = din("cosT", [128, NB * 64]); sinT = din("sinT", [128, NB * 64])
    dkT = din("dkT", [128, NB * 8]); dqT = din("dqT", [128, NB * 8]); kbT = din("kbT", [128, NB * 8])
    wffd = din("wffd", [128, 16 * 8])
    c_ident = din("c_ident", [128, 128]); c_tri = din("c_tri", [128, 128]); c_ones = din("c_ones", [128, 128])
    c_maskR = din("c_maskR", [128, 128]); c_maskT = din("c_maskT", [128, MASKW]); c_sel = din("c_sel", [128, 1024])
    y = nc.dram_tensor("y", [1024, D], F32, kind="ExternalOutput").ap()
    dbg = {}
    if DEBUG:
        dbg["d_aT"] = nc.dram_tensor("d_aT", [128, 16 * LT], BF16, kind="ExternalOutput").ap()
        dbg["d_mix"] = nc.dram_tensor("d_mix", [128, 16 * NOWN], BF16, kind="ExternalOutput").ap()
        dbg["d_h1"] = nc.dram_tensor("d_h1", [128, 8 * D], F32, kind="ExternalOutput").ap()
        dbg["d_cum"] = nc.dram_tensor("d_cum", [128, NB * 8], F32, kind="ExternalOutput").ap()
        dbg["d_cT"] = nc.dram_tensor("d_cT", [128, 16 * NOWN], BF16, kind="ExternalOutput").ap()
        dbg["d_hh"] = nc.dram_tensor("d_hh", [2, D], F32, kind="ExternalOutput").ap()

    w_in_v = w_in.rearrange("(c p) n -> p c n", p=128)
    w_out_v = w_out.rearrange("(c p) n -> p c n", p=128)
    w_up_v = w_up.rearrange("(c p) n -> p c n", p=128)

    with contextlib.ExitStack() as es:
        def sb(name, shape, dt):
            return es.enter_context(nc.sbuf_tensor(name, list(shape), dt))

        def ps(name, shape, dt):
            return es.enter_context(nc.psum_tensor(name, list(shape), dt))

        R1 = sb("R1", [128, 16 * LT], BF16)
        aT = R1[:, :].rearrange("p (c t) -> p c t", c=16)
        R1f = R1.bitcast(F32)
        h2 = R1f[:, 0:8 * D].rearrange("p (b f) -> p b f", b=8)
        mixT = sb("mixT", [128, 16, NOWN], BF16)
        ringT = sb("ringT", [128, NSLOT * 2048], BF16)
        ring = [ringT[:, i * 2048:(i + 1) * 2048] for i in range(NSLOT)]
        R3N = 20992
        R3 = sb("R3", [128, R3N], BF16)
        R3f = R3.bitcast(F32)
        gb = sb("gb", [128, D], F32)
        ident_b = sb("ident_b", [128, 128], BF16)
        ones_b = sb("ones_b", [128, 128], BF16)
        maskT = sb("maskT", [128, MASKW], BF16)
        sel = sb("sel", [128, 1024], BF16)
        ident_f = sb("ident_f", [128, 128], F32)
        tri_f = sb("tri_f", [128, 128], F32)
        ones_f = sb("ones_f", [128, 128], F32)
        maskR = sb("maskR", [128, 128], F32)
        convw_s = sb("convw_s", [128, 2 * NFF * 3], F32)
        convb_s = sb("convb_s", [128, 2 * NFF], F32)
        bfg_s = sb("bfg_s", [128, NB * 8], F32)
        kb_s = sb("kb_s", [128, NB, 8], F32)
        cos_s = sb("cos_s", [128, NB, 64], F32)
        sin_s = sb("sin_s", [128, NB, 64], F32)
        dk_s = sb("dk_s", [128, NB, 8], F32)
        dq_s = sb("dq_s", [128, NB, 8], F32)
        ndk_s = sb("ndk_s", [128, NB, 8], F32)
        ndq_s = sb("ndq_s", [128, NB, 8], F32)
        spt = sb("spt", [128, NB * 8], F32)
        cumn = sb("cumn", [128, NB * 8], F32)
        tot = sb("tot", [128, NB * 8], F32)
        pre = sb("pre", [128, NB * 8], F32)
        biasK = sb("biasK", [128, NB * 8], F32)
        Rb = sb("Rb", [128, 9 * 128], BF16)
        wff = sb("wff", [128, 16, 8], BF16)
        st1 = sb("st1", [128, 32], F32)
        eps_t = sb("eps_t", [128, 1], F32)
        st2 = sb("st2", [128, 64], F32)

        pp = [ps("pp%d" % i, [128, 1024], F32) for i in range(4)]
        ppb = [p.bitcast(BF16) for p in pp]

        def bank(i):
            return pp[i // 2][:, (i % 2) * 512:(i % 2) * 512 + 512]

        P = Prog(nc)
        slot_ctr = [0]

        def next_slot():
            s = slot_ctr[0] % NSLOT
            slot_ctr[0] += 1
            return s

        def dma_sp(out, in_, reads=(), writes=(), stream="const"):
            P.op("sp", lambda e: e.dma_start(out=out, in_=in_), reads=reads, writes=writes, dma=stream)

        def dma_cast(out, in_, reads=(), writes=(), stream="constp", batch=None):
            P.op("pool", lambda e: e.dma_start(out=out, in_=in_), reads=reads, writes=writes, dma=stream, batch=batch)

        def load_slice(wv, c0, ncols=128):
            s = next_slot()
            v = ring[s][:, 0:16 * ncols].rearrange("p (c n) -> p c n", c=16)
            for hh in range(2):
                dma_cast(v[:, 8 * hh:8 * hh + 8, :], wv[:, 8 * hh:8 * hh + 8, c0:c0 + ncols], writes=[("ring", s, hh)], stream="ring%d" % s,
                         batch=slot_ctr[0])
            return s, v

        def rkeys(s):
            return [("ring", s, 0), ("ring", s, 1)]

        dma_sp(gb[:], g1.partition_broadcast(128), writes=["gb"], stream="gb")
        P.op("dve", lambda e: e.memset(eps_t[:], EPS), writes=["eps_t"])
        P.op("dve", lambda e: e.memset(pre[:], 0.0), writes=["pre"])
        dma_cast(ident_b[:], c_ident, writes=["ident_b"])
        dma_cast(wff[:, :, :].rearrange("p c n -> p (c n)"), wffd, writes=[("wff", 0), ("wff", 1)])
        dma_cast(ones_b[:], c_ones, writes=["ones_b"])
        dma_cast(maskT[:], c_maskT, writes=["maskT"])
        dma_cast(sel[:], c_sel, writes=["sel"])

        def late_constants():
            dma_sp(cos_s[:, :, :].rearrange("p b d -> p (b d)"), cosT, writes=["cos"])
            dma_sp(sin_s[:, :, :].rearrange("p b d -> p (b d)"), sinT, writes=["sin"])
            dma_sp(dk_s[:, :, :].rearrange("p b h -> p (b h)"), dkT, writes=["dk"])
            dma_sp(dq_s[:, :, :].rearrange("p b h -> p (b h)"), dqT, writes=["dq"])
            P.op("dve", lambda e: e.tensor_scalar_mul(out=ndk_s[:, :, :], in0=dk_s[:, :, :], scalar1=-1.0), reads=["dk"], writes=["ndk"])
            P.op("dve", lambda e: e.tensor_scalar_mul(out=ndq_s[:, :, :], in0=dq_s[:, :, :], scalar1=-1.0), reads=["dq"], writes=["ndq"])
            dma_sp(ident_f[:], c_ident, writes=["ident_f"])
            dma_sp(tri_f[:], c_tri, writes=["tri_f"])
            dma_sp(ones_f[:], c_ones, writes=["ones_f"])
            dma_sp(maskR[:], c_maskR, writes=["maskR"])
            dma_sp(bfg_s[:], bfg.partition_broadcast(128), writes=["bfg"])
            dma_sp(kb_s[:, :, :].rearrange("p b h -> p (b h)"), kbT, writes=["kb"])
            dma_sp(convw_s[:], convw, writes=["convw"])
            dma_sp(convb_s[:], convb, writes=["convb"])

        def rms_rstd(src_ap, junk_ap, col, rkeys_, jkey, extra_writes=()):
            npart = src_ap.shape[0]
            c = st1[0:npart, col:col + 1]
            P.op("act", lambda e: e.activation(out=junk_ap, in_=src_ap, func=AF.Square, accum_out=c),
                 reads=rkeys_, writes=[jkey, ("st1", col)] + list(extra_writes))
            P.op("act", lambda e: e.activation(out=c, in_=c, func=AF.Ln, bias=eps_t[0:npart, 0:1], scale=1.0 / D),
                 reads=[("st1", col), "eps_t"], writes=[("st1", col)])
            P.op("act", lambda e: e.activation(out=c, in_=c, func=AF.Exp, scale=-0.5), reads=[("st1", col)], writes=[("st1", col)])
            return c

        xsA = [R3f[:, 0:2048], R3f[:, 2048:4096], R3f[:, 4096:6144]]
        xnA = [R3[:, 12288:14336], R3[:, 14336:16384]]
        junkA = R3[:, 16384:18432]
        b6 = bank(6)

        def aTk(tb):
            return [("aT", tb, 0), ("aT", tb, 1)]

        def A1(tb):
            xs = xsA[tb % 3]
            dma_sp(xs, xl[tb * 128:(tb + 1) * 128, :], writes=[("xs", tb % 3)], stream="xs%d" % (tb % 3))
            rms_rstd(xs, junkA, tb % 3, [("xs", tb % 3)], "junkA")

        def A2(tb):
            xs = xsA[tb % 3]
            c = st1[:, tb % 3:tb % 3 + 1]
            xn = xnA[tb % 2]
            P.op("dve", lambda e: e.scalar_tensor_tensor(out=xn, in0=xs, scalar=c, in1=gb[:], op0=ALU.mult, op1=ALU.mult),
                 reads=[("xs", tb % 3), ("st1", tb % 3), "gb"], writes=[("xnA", tb % 2)])

        def A3(tb):
            xn = xnA[tb % 2]
            pv = ppb[tb % 2]
            for cc in range(16):
                P.op("pe", lambda e, cc=cc: e.transpose(out=pv[:, cc * 128:(cc + 1) * 128], in_=xn[:, cc * 128:(cc + 1) * 128], identity=ident_b[:]),
                     reads=[("xnA", tb % 2), "ident_b"], writes=[("ps", 2 * (tb % 2)), ("ps", 2 * (tb % 2) + 1)])
            pv3 = pv[:, 0:2048].rearrange("p (c t) -> p c t", c=16)
            P.op("act", lambda e: e.copy(out=aT[:, 0:8, tb * 128:(tb + 1) * 128], in_=pv3[:, 0:8, :]),
                 reads=[("ps", 2 * (tb % 2))], writes=[("aT", tb, 0)])
            P.op("dve", lambda e: e.tensor_copy(out=aT[:, 8:16, tb * 128:(tb + 1) * 128], in_=pv3[:, 8:16, :]),
                 reads=[("ps", 2 * (tb % 2) + 1)], writes=[("aT", tb, 1)])

        def A4(tb):
            for kc in range(16):
                P.op("pe", lambda e, kc=kc: e.matmul(b6[:, tb * 8:tb * 8 + 8], lhsT=aT[:, kc, tb * 128:(tb + 1) * 128], rhs=wff[:, kc, :],
                                                    start=(kc == 0), stop=(kc == 15)),
                     reads=aTk(tb) + [("wff", 0), ("wff", 1)], writes=[("ps", 6)])

        for i in range(NB + 3):
            if i == 3:
                late_constants()
            if i < NB:
                A1(i)
            if 0 <= i - 1 < NB:
                A2(i - 1)
            if 0 <= i - 2 < NB:
                A3(i - 2)
            if 0 <= i - 3 < NB:
                A4(i - 3)


        def aT_range_keys(l0, l1):
            ks = []
            for tb in range(l0 // 128, (l1 - 1) // 128 + 1):
                ks += aTk(tb)
            return ks

        if DEBUG:
            dma_sp(dbg["d_aT"], R1[:, :], reads=aT_range_keys(0, LT), stream="st")

        if STOP == "A":
            P.emit(final_wait_streams="st")
            return nc
        P.barrier()
        dma_sp(gb[:, 0:1024], rng.partition_broadcast(128), writes=["gb"], stream="gb")

        b7 = bank(7)
        P.op("dve", lambda e: e.tensor_tensor(out=spt[:], in0=b6[:, 0:NB * 8], in1=bfg_s[:], op=ALU.add), reads=[("ps", 6), "bfg"], writes=["spt"])
        P.op("act", lambda e: e.activation(out=spt[:], in_=spt[:], func=AF.Exp, scale=-1.0), reads=["spt"], writes=["spt"])
        P.op("act", lambda e: e.activation(out=spt[:], in_=spt[:], func=AF.Ln, bias=1.0, scale=1.0), reads=["spt"], writes=["spt"])
        P.op("pe", lambda e: e.matmul(b7[:, 0:136], lhsT=tri_f[:], rhs=spt[:], start=True, stop=True), reads=["spt", "tri_f"], writes=[("ps", 7)])
        P.op("pe", lambda e: e.matmul(b7[:, 136:272], lhsT=ones_f[:], rhs=spt[:], start=True, stop=True), reads=["spt", "ones_f"], writes=[("ps", 7)])
        P.op("dve", lambda e: e.tensor_copy(out=cumn[:], in_=b7[:, 0:136]), reads=[("ps", 7)], writes=["cumn"])
        P.op("dve", lambda e: e.tensor_copy(out=tot[:], in_=b7[:, 136:272]), reads=[("ps", 7)], writes=["tot"])
        for b in range(1, NB):
            P.op("dve", lambda e, b=b: e.tensor_tensor(out=pre[:, b * 8:b * 8 + 8], in0=pre[:, (b - 1) * 8:b * 8], in1=tot[:, (b - 1) * 8:b * 8], op=ALU.add),
                 reads=["pre", "tot"], writes=["pre"])
        P.op("dve", lambda e: e.tensor_tensor(out=cumn[:], in0=cumn[:], in1=pre[:], op=ALU.add), reads=["cumn", "pre"], writes=["cumn"])
        P.op("dve", lambda e: e.tensor_tensor(out=biasK[:], in0=cumn[:], in1=kb_s[:, :, :].rearrange("p b h -> p (b h)"), op=ALU.add),
             reads=["cumn", "kb"], writes=["biasK"])
        for c in range(9):
            tb = 8 + c
            dst = pp[2][0:8, c * 128:(c + 1) * 128] if c < 8 else pp[3][0:8, 512:640]
            P.op("pe", lambda e, dst=dst, tb=tb: e.transpose(out=dst, in_=cumn[:, tb * 8:tb * 8 + 8], identity=ident_f[:]),
                 reads=["cumn", "ident_f"], writes=[("ps", 4), ("ps", 5)] if c < 8 else [("ps", 7)])
        P.op("dve", lambda e: e.memset(Rb[:], 0.0), writes=["Rb"])
        P.op("act", lambda e: e.activation(out=Rb[0:8, 0:1024], in_=pp[2][0:8, 0:1024], func=AF.Copy, scale=-1.0 / SCALE),
             reads=[("ps", 4), ("ps", 5)], writes=["Rb"])
        P.op("act", lambda e: e.activation(out=Rb[0:8, 1024:1152], in_=pp[3][0:8, 512:640], func=AF.Copy, scale=-1.0 / SCALE),
             reads=[("ps", 7)], writes=["Rb"])
        if DEBUG:
            dma_sp(dbg["d_cum"], cumn[:], reads=["cumn"], stream="st")
        if STOP == "B0":
            P.emit(final_wait_streams="st")
            return nc

        o_ = 0
        def carve(n):
            nonlocal o_
            a = o_
            o_ += n
            return a
        fkT = R3[:, carve(LT):o_]
        _a = carve(1028)
        fqT = R3[:, _a:_a + NOWN]
        rvfv = R3[:, carve(NB * 256):o_].rearrange("p (b n) -> p b n", b=NB)
        rkt = R3[:, carve(NB * 128):o_].rearrange("p (b n) -> p b n", b=NB)
        rqT = R3[:, carve(1152):o_]
        rkT = R3[:, carve(1152):o_]
        sg = R3[:, carve(1152):o_].rearrange("p (b n) -> p b n", b=9)
        rqt = R3[:, carve(1152):o_].rearrange("p (b n) -> p b n", b=9)
        PTt = [R3[:, carve(342):o_] for _ in range(3)]
        smt = [R3[:, carve(128):o_] for _ in range(2)]
        Sbf = [R3[:, carve(128):o_] for _ in range(2)]
        junkB = R3[:, carve(128):o_]
        assert o_ % 2 == 0
        fo = o_ // 2
        def carvef(n):
            nonlocal fo
            a = fo
            fo += n
            return a
        o_all = R3f[:, carvef(1152):fo].rearrange("p (b n) -> p b n", b=9)
        rden = R3f[:, carvef(342):fo]
        rU = R3f[:, carvef(128):fo]
        rW = R3f[:, carvef(128):fo]
        ro = R3[:, fo * 2:fo * 2 + 1152].rearrange("p (b n) -> p b n", b=9)
        assert fo * 2 + 1152 <= R3N, fo * 2

        def rotary(src, dst, tbl, dec_ap, ndec_ap, rk, wk):
            Cb = cos_s[:, tbl:tbl + 1, :].to_broadcast([128, 2, 64])
            S = sin_s[:, tbl, :]
            src3 = src[:, 0:128].rearrange("p (a b) -> p a b", a=2)
            U3 = rU.rearrange("p (a b) -> p a b", a=2)
            t1 = src[:, 0:64]
            t2 = src[:, 64:128]
            rd = list(rk) + ["cos", "sin", "dk", "dq", "ndk", "ndq"]
            P.op("dve", lambda e: e.scalar_tensor_tensor(out=U3, in0=src3, scalar=dec_ap, in1=Cb, op0=ALU.mult, op1=ALU.mult), reads=rd, writes=["rU"])
            P.op("dve", lambda e: e.scalar_tensor_tensor(out=rW[:, 0:64], in0=t2, scalar=ndec_ap, in1=S, op0=ALU.mult, op1=ALU.mult), reads=rd, writes=["rW"])
            P.op("dve", lambda e: e.scalar_tensor_tensor(out=rW[:, 64:128], in0=t1, scalar=dec_ap, in1=S, op0=ALU.mult, op1=ALU.mult), reads=rd + ["rW"], writes=["rW"])
            P.op("dve", lambda e: e.tensor_tensor(out=dst, in0=rU, in1=rW, op=ALU.add), reads=["rU", "rW"], writes=wk)

        vA = ringT[:, 0:6144].rearrange("p (c n) -> p c n", c=16)
        vB = ringT[:, 6144:10240].rearrange("p (c n) -> p c n", c=16)
        vC = ringT[:, 10240:12288].rearrange("p (c n) -> p c n", c=16)
        vD = ringT[:, 12288:14336].rearrange("p (c n) -> p c n", c=16)
        regions = {"A": (vA, 3), "B": (vB, 2), "C": (vC, 1), "D": (vD, 1)}

        def rg_keys(name):
            return [("rg" + name, si, hh) for si in range(regions[name][1]) for hh in range(2)]

        def load_region(name, col_offs, h):
            v, _ = regions[name]
            for si, c0 in enumerate(col_offs):
                for hh in range(2):
                    dma_cast(v[:, 8 * hh:8 * hh + 8, si * 128:(si + 1) * 128], w_in_v[:, 8 * hh:8 * hh + 8, c0:c0 + 128],
                             writes=[("rg" + name, si, hh)], stream="rg" + name, batch=h)

        all_rg = [k for nm in ("A", "B", "C", "D") for k in rg_keys(nm)]

        def load_wout_quarter(qp, extra=()):
            sl = []
            for m in range(4):
                s_ = next_slot()
                v_ = ring[s_].rearrange("p (c n) -> p c n", c=4)
                dma_cast(v_, w_out_v[:, 4 * m:4 * m + 4, qp * 512:(qp + 1) * 512], writes=rkeys(s_) + list(extra), stream="ring%d" % s_)
                sl.append((s_, v_))
            return sl
        wq = {}
        deferred_tail = [None]
        pending_tail_ops = []

        def load_head(h):
            load_region("A", [1024 + h * 128, 2048 + h * 128, 6144 + h * 128], h)
            load_region("B", [h * 128, 3072 + h * 128], h)
            load_region("C", [4096 + h * 128], h)
            load_region("D", [5120 + h * 128], h)
        load_head(0)
        for h in range(NHEADS_RUN):
            for tb in range(NB):
                bA = 2 * (tb % 2)
                bB = bA + 1
                for kc in range(16):
                    P.op("pe", lambda e, tb=tb, kc=kc, bA=bA: e.matmul(
                        bank(bA)[:, 0:384], lhsT=aT[:, kc, tb * 128:(tb + 1) * 128], rhs=vA[:, kc, :],
                        start=(kc == 0), stop=(kc == 15)),
                        reads=aTk(tb) + rg_keys("A"), writes=[("ps", bA)])
                    if tb >= 8:
                        P.op("pe", lambda e, tb=tb, kc=kc, bB=bB: e.matmul(
                            bank(bB)[:, 0:256], lhsT=aT[:, kc, tb * 128:(tb + 1) * 128], rhs=vB[:, kc, :],
                            start=(kc == 0), stop=(kc == 15)),
                            reads=aTk(tb) + rg_keys("B"), writes=[("ps", bB)])
                P.op("act", lambda e, tb=tb, bA=bA: e.copy(out=rvfv[:, tb, :], in_=bank(bA)[:, 128:384]), reads=[("ps", bA)], writes=[("rvfv", tb)])
                if tb >= 8:
                    P.op("act", lambda e, tb=tb, bB=bB: e.copy(out=sg[:, tb - 8, :], in_=bank(bB)[:, 128:256]), reads=[("ps", bB)], writes=["sg"])
                rotary(bank(bA), rkt[:, tb, :], tb, dk_s[:, tb, h:h + 1], ndk_s[:, tb, h:h + 1], [("ps", bA)], [("rkt", tb)])
                for _ in range(4 if tb < 7 else 99):
                    if pending_tail_ops and tb < 8:
                        pending_tail_ops.pop(0)()
                if tb >= 8:
                    rotary(bank(bB), rqt[:, tb - 8, :], tb, dq_s[:, tb, h:h + 1], ndq_s[:, tb, h:h + 1], [("ps", bB)], [("rqt", tb - 8)])
            P.op("act", lambda e: e.activation(out=sg[:, :, :], in_=sg[:, :, :], func=AF.Silu), reads=["sg"], writes=["sg"])
            if deferred_tail[0] is not None:
                deferred_tail[0]()
                deferred_tail[0] = None
            for nb in range(5):
                n0 = nb * 512
                nw = min(512, LT - n0)
                bk = 4 + nb % 2
                for kc in range(16):
                    P.op("pe", lambda e, kc=kc, n0=n0, nw=nw, bk=bk: e.matmul(bank(bk)[:, 0:nw], lhsT=vD[:, kc, :], rhs=aT[:, kc, n0:n0 + nw],
                                                                          start=(kc == 0), stop=(kc == 15)),
                         reads=aT_range_keys(n0, n0 + nw) + rg_keys("D"), writes=[("ps", bk)])
                if nb % 2 == 0:
                    P.op("act", lambda e, n0=n0, nw=nw, bk=bk: e.copy(out=fkT[:, n0:n0 + nw], in_=bank(bk)[:, 0:nw]), reads=[("ps", bk)], writes=[("fkT", nb)])
                else:
                    P.op("dve", lambda e, n0=n0, nw=nw, bk=bk: e.tensor_copy(out=fkT[:, n0:n0 + nw], in_=bank(bk)[:, 0:nw]), reads=[("ps", bk)], writes=[("fkT", nb)])
            for g in range(3):
                n0 = OWN0 + 342 * g
                bk = 4 + (g + 1) % 2
                for kc in range(16):
                    P.op("pe", lambda e, kc=kc, n0=n0, bk=bk: e.matmul(bank(bk)[:, 0:342], lhsT=vC[:, kc, :], rhs=aT[:, kc, n0:n0 + 342],
                                                                    start=(kc == 0), stop=(kc == 15)),
                         reads=aT_range_keys(n0, n0 + 342) + rg_keys("C"), writes=[("ps", bk)])
                P.op("act", lambda e, g=g, bk=bk: e.copy(out=fqT[:, 342 * g:342 * g + 342], in_=bank(bk)[:, 0:342]), reads=[("ps", bk)], writes=[("fqT", g)])

            if h + 1 < NHEADS_RUN:
                load_head(h + 1)
            else:
                wq[0] = load_wout_quarter(0, all_rg)
                wq[1] = load_wout_quarter(1, all_rg)
            for c in range(9):
                P.op("pe", lambda e, c=c: e.transpose(out=ppb[2][:, c * 128:(c + 1) * 128], in_=rqt[:, c, :], identity=ident_b[:]),
                     reads=[("rqt", c), "ident_b"], writes=[("ps", 4), ("ps", 5)])
                P.op("pe", lambda e, c=c: e.transpose(out=ppb[3][:, c * 128:(c + 1) * 128], in_=rkt[:, 8 + c, :], identity=ident_b[:]),
                     reads=[("rkt", 8 + c), "ident_b"], writes=[("ps", 6), ("ps", 7)])
            P.op("act", lambda e: e.copy(out=rqT, in_=ppb[2][:, 0:1152]), reads=[("ps", 4), ("ps", 5)], writes=["rqT"])
            P.op("dve", lambda e: e.tensor_copy(out=rkT, in_=ppb[3][:, 0:1152]), reads=[("ps", 6), ("ps", 7)], writes=["rkT"])
            Sps = bank(7)[:, 0:128]
            ret_steps = []

            def r_init():
                for b in range(8):
                    P.op("pe", lambda e, b=b: e.matmul(Sps, lhsT=rkt[:, b, :], rhs=rvfv[:, b, 0:128], start=(b == 0), stop=(b == 7), skip_group_check=True),
                         reads=[("rkt", b), ("rvfv", b)], writes=[("ps", 7)])
            ret_steps.append(r_init)

            def r_a(c):
                sTp = bank(6)[:, (c % 2) * 128:(c % 2) * 128 + 128]
                if c > 0:
                    P.op("pe", lambda e: e.matmul(Sps, lhsT=rkt[:, 7 + c, :], rhs=rvfv[:, 7 + c, 0:128], start=False, stop=True, skip_group_check=True),
                         reads=[("rkt", 7 + c), ("rvfv", 7 + c)], writes=[("ps", 7)])
                P.op("act", lambda e: e.copy(out=Sbf[c % 2], in_=Sps), reads=[("ps", 7)], writes=[("Sbf", c % 2)])
                P.op("pe", lambda e: e.matmul(sTp, lhsT=rkT[:, c * 128:(c + 1) * 128], rhs=rqT[:, c * 128:(c + 1) * 128], start=True, stop=True),
                     reads=["rkT", "rqT"], writes=[("ps", 6)])
                P.op("dve", lambda e: e.tensor_tensor(out=smt[c % 2], in0=sTp, in1=maskR[:], op=ALU.mult),
                     reads=[("ps", 6), "maskR"], writes=[("smt", c % 2)])

            def r_b(c):
                op_ = bank(4)[:, (c % 2) * 128:(c % 2) * 128 + 128]
                P.op("pe", lambda e: e.matmul(op_, lhsT=rqT[:, c * 128:(c + 1) * 128], rhs=Sbf[c % 2], start=True, stop=False),
                     reads=["rqT", ("Sbf", c % 2)], writes=[("ps", 4)])
                P.op("pe", lambda e: e.matmul(op_, lhsT=smt[c % 2], rhs=rvfv[:, 8 + c, 0:128], start=False, stop=True),
                     reads=[("smt", c % 2), ("rvfv", 8 + c)], writes=[("ps", 4)])
                P.op("dve", lambda e: e.tensor_copy(out=o_all[:, c, :], in_=op_), reads=[("ps", 4)], writes=[("o_all", c)])
                P.op("act", lambda e: e.activation(out=junkB, in_=op_, func=AF.Square, accum_out=st2[:, 16 + c:17 + c]),
                     reads=[("ps", 4)], writes=["junkB", ("st2q", c)])
            for c in range(9):
                ret_steps.append(lambda c=c: r_a(c))
                ret_steps.append(lambda c=c: r_b(c))

            def r_tail(h=h):
                oall_keys = [("o_all", c) for c in range(9)]
                sq_keys = [("st2q", c) for c in range(9)]
                mean = st2[:, 0:9]
                ssq = st2[:, 16:25]
                msq = st2[:, 32:41]
                rstd = st2[:, 48:57]
                P.op("dve", lambda e: e.reduce_sum(out=mean, in_=o_all[:, :, :], axis=AX.X), reads=oall_keys, writes=["st2m"])
                P.op("dve", lambda e: e.tensor_scalar_mul(out=mean, in0=mean, scalar1=1.0 / 128), reads=["st2m"], writes=["st2m"])
                P.op("dve", lambda e: e.tensor_tensor(out=msq, in0=mean, in1=mean, op=ALU.mult), reads=["st2m"], writes=["st2s"])
                P.op("dve", lambda e: e.tensor_scalar(out=rstd, in0=ssq, scalar1=1.0 / 128, scalar2=EPS, op0=ALU.mult, op1=ALU.add), reads=sq_keys, writes=["st2r"])
                P.op("dve", lambda e: e.tensor_tensor(out=rstd, in0=rstd, in1=msq, op=ALU.subtract), reads=["st2r", "st2s"], writes=["st2r"])
                P.op("act", lambda e: e.activation(out=rstd, in_=rstd, func=AF.Ln), reads=["st2r"], writes=["st2r"])
                P.op("act", lambda e: e.activation(out=rstd, in_=rstd, func=AF.Exp, scale=-0.5), reads=["st2r"], writes=["st2r"])
                ops_ = []
                for c in range(9):
                    ops_.append(lambda c=c: P.op("dve", lambda e: e.tensor_scalar(out=o_all[:, c, :], in0=o_all[:, c, :], scalar1=st2[:, c:c + 1], scalar2=st2[:, 48 + c:49 + c],
                                                                              op0=ALU.subtract, op1=ALU.mult),
                                                 reads=[("o_all", c), "st2m", "st2r"], writes=[("o_all", c)]))
                    ops_.append(lambda c=c: P.op("dve", lambda e: e.tensor_tensor(out=o_all[:, c, :], in0=o_all[:, c, :], in1=gb[:, h * 128:(h + 1) * 128], op=ALU.mult),
                                                 reads=[("o_all", c), "gb"], writes=[("o_all", c)]))
                    ops_.append(lambda c=c: P.op("dve", lambda e: e.tensor_tensor(out=ro[:, c, :], in0=o_all[:, c, :], in1=sg[:, c, :], op=ALU.mult),
                                                 reads=[("o_all", c), "sg"], writes=[("ro", c)]))
                return ops_

            def r_tail_pe(h=h):
                for c in range(9):
                    P.op("pe", lambda e, c=c: e.transpose(out=ppb[2][:, c * 128:(c + 1) * 128], in_=ro[:, c, :], identity=ident_b[:]),
                         reads=[("ro", c), "ident_b"], writes=[("ps", 4), ("ps", 5)])
                P.op("act", lambda e: e.copy(out=mixT[:, h, :], in_=ppb[2][:, 126:1152]), reads=[("ps", 4), ("ps", 5)], writes=[("mixh", h)])

            tiles = []
            for g in range(3):
                q0 = OWN0 + 342 * g
                kmax = (q0 + 342 - 1) // 128
                for kb in range(kmax + 1):
                    tiles.append((g, kb, kmax, q0))
            oTp = bank(2)[:, 0:342]
            dnp = bank(3)[:, 0:342]

            def f_s(ti, h=h):
                g, kb, kmax, q0 = tiles[ti]
                sbk = (0, 1, 5)[ti % 3]
                sb_ = bank(sbk)[:, 0:342]
                delta = 128 * kb - q0
                need_mask = (128 * kb + 127) > q0
                P.op("pe", lambda e: e.matmul(sb_, lhsT=fkT[:, kb * 128:(kb + 1) * 128], rhs=fqT[:, 342 * g:342 * g + 342], start=True, stop=False),
                     reads=[("fkT", kb // 4), ("fqT", g)], writes=[("ps", sbk)])
                P.op("pe", lambda e: e.matmul(sb_, lhsT=sel[:, h * 128:(h + 1) * 128], rhs=Rb[:, 126 + 342 * g:126 + 342 * g + 342],
                                              start=False, stop=(not need_mask)),
                     reads=["sel", "Rb"], writes=[("ps", sbk)])
                if need_mask:
                    off = XOFF - delta
                    assert 0 <= off and off + 342 <= MASKW, off
                    P.op("pe", lambda e: e.matmul(sb_, lhsT=ident_b[:], rhs=maskT[:, off:off + 342], start=False, stop=True),
                         reads=["ident_b", "maskT"], writes=[("ps", sbk)])
                pt = PTt[ti % 3]
                P.op("act", lambda e: e.activation(out=pt, in_=sb_, func=AF.Exp, bias=biasK[:, kb * 8 + h:kb * 8 + h + 1], scale=SCALE),
                     reads=[("ps", sbk), "biasK"], writes=[("PT", ti % 3)])

            def f_pv(ti, h=h):
                g, kb, kmax, q0 = tiles[ti]
                pt = PTt[ti % 3]
                ptk = ("PT", ti % 3)
                P.op("pe", lambda e: e.matmul(oTp, lhsT=rvfv[:, kb, 128:256], rhs=pt, start=(kb == 0), stop=(kb == kmax)),
                     reads=[("rvfv", kb), ptk], writes=[("ps", 2)])
                P.op("pe", lambda e: e.matmul(dnp, lhsT=ones_b[:], rhs=pt, start=(kb == 0), stop=(kb == kmax)),
                     reads=["ones_b", ptk], writes=[("ps", 3)])
                if kb == kmax:
                    P.op("dve", lambda e: e.reciprocal(out=rden, in_=dnp), reads=[("ps", 3)], writes=["rden"])
                    P.op("dve", lambda e: e.tensor_tensor(out=mixT[:, 8 + h, 342 * g:342 * g + 342], in0=oTp, in1=rden, op=ALU.mult),
                         reads=[("ps", 2), "rden"], writes=[("mixf", h, g)])

            fox_steps = []
            nt_ = len(tiles)
            def f_first():
                f_s(0)
                f_s(1)
            fox_steps.append(f_first)
            for ti in range(nt_):
                def st(ti=ti):
                    if ti + 2 < nt_:
                        f_s(ti + 2)
                    f_pv(ti)
                fox_steps.append(st)
            fi = 0
            last_head = (h == NHEADS_RUN - 1)
            per = [1, 1] if last_head else [3, 2]
            for ri, rs in enumerate(ret_steps):
                rs()
                k = 1 if ri == 0 else per[ri % 2]
                for _ in range(k):
                    if fi < len(fox_steps):
                        fox_steps[fi]()
                        fi += 1
            if last_head:
                for f_ in r_tail():
                    f_()
            while fi < len(fox_steps):
                fox_steps[fi]()
                fi += 1
            if not last_head:
                pending_tail_ops[:] = r_tail()
            deferred_tail[0] = r_tail_pe
        deferred_tail[0]()


        mix_all = [("mixh", h) for h in range(NH)] + [("mixf", h, g) for h in range(NH) for g in range(3)]
        if DEBUG:
            dma_sp(dbg["d_mix"], mixT[:, :, :].rearrange("p c t -> p (c t)"), reads=mix_all, stream="st")
        if STOP == "B":
            P.emit(final_wait_streams="st")
            return nc

        P.barrier()

        dma_sp(gb[:], g2.partition_broadcast(128), writes=["gb"], stream="gb")
        xsC = [R3f[:, 0:512], R3f[:, 512:1024], R3f[:, 1024:1536]]
        hnC = [R3[:, 4096:6144], R3[:, 6144:8192]]
        junkC = R3[:, 8192:10240]
        hh = R3f[:, 6144:8192]
        blocks = [(-1, 0, OWN0)] + [(tb, 2 + 128 * tb, 1152 + 128 * tb) for tb in range(8)]
        xi = 0
        ubk = 0

        def c_norm1(bi, tb):
            src = hh[:, :] if tb < 0 else h2[:, tb, :]
            col = 4 + bi % 2
            hk = [("h1", bi, q) for q in range(4)]
            c = rms_rstd(src, junkC, col, hk, "junkC")
            hn = hnC[bi % 2]
            P.op("dve", lambda e: e.scalar_tensor_tensor(out=hn, in0=src, scalar=c, in1=gb[:], op0=ALU.mult, op1=ALU.mult),
                 reads=hk + [("st1", col), "gb"], writes=[("hnC", bi % 2)])

        def c_norm2(bi, tb, c0):
            hn = hnC[bi % 2]
            pv = ppb[2 + bi % 2]
            pk = [("ps", 4 + 2 * (bi % 2)), ("ps", 5 + 2 * (bi % 2))]
            for cc in range(16):
                P.op("pe", lambda e, cc=cc: e.transpose(out=pv[:, cc * 128:(cc + 1) * 128], in_=hn[:, cc * 128:(cc + 1) * 128], identity=ident_b[:]),
                     reads=[("hnC", bi % 2), "ident_b"], writes=pk)
            pv3 = pv[:, 0:2048].rearrange("p (c t) -> p c t", c=16)
            if tb < 0:
                P.op("act", lambda e: e.copy(out=mixT[:, :, 0:2], in_=pv3[:, :, 0:2]), reads=pk + mix_all, writes=[("cT", bi)])
            else:
                P.op("act", lambda e: e.copy(out=mixT[:, 0:8, c0:c0 + 128], in_=pv3[:, 0:8, :]), reads=pk[0:1] + mix_all, writes=[("cT", bi)])
                P.op("dve", lambda e: e.tensor_copy(out=mixT[:, 8:16, c0:c0 + 128], in_=pv3[:, 8:16, :]), reads=pk[1:2] + mix_all, writes=[("cTb", bi)])

        for qp in range(4):
            slots = wq[qp]
            pend = None
            for bi, (tb, c0, l0) in enumerate(blocks):
                xs = xsC[xi % 3]
                xk = ("xs", xi % 3)
                xstream = "xs%d" % (xi % 3)
                xi += 1
                dma_sp(xs, xl[l0:l0 + 128, qp * 512:(qp + 1) * 512], writes=[xk], stream=xstream)
                bk = ubk % 4
                ubk += 1
                ck = [("cT", bi), ("cTb", bi)] + ([("cT", 1), ("cTb", 1)] if tb < 0 else [])
                for kc in range(16):
                    s_, v_ = slots[kc // 4]
                    P.op("pe", lambda e, kc=kc, v_=v_, c0=c0, bk=bk: e.matmul(
                        bank(bk), lhsT=mixT[:, kc, c0:c0 + 128], rhs=v_[:, kc % 4, :], start=(kc == 0), stop=(kc == 15)),
                        reads=mix_all + ck + rkeys(s_), writes=[("ps", bk)])
                dst = hh[:, qp * 512:(qp + 1) * 512] if tb < 0 else h2[:, tb, qp * 512:(qp + 1) * 512]
                P.op("dve", lambda e, dst=dst, bk=bk, xs=xs: e.tensor_tensor(out=dst, in0=bank(bk), in1=xs, op=ALU.add),
                     reads=[("ps", bk), xk], writes=[("h1", bi, qp)])
                if qp == 3:
                    c_norm1(bi, tb)
                    if pend is not None:
                        c_norm2(*pend)
                    pend = (bi, tb, c0)
            if qp == 3:
                c_norm2(*pend)
            if qp + 2 < 4:
                wq[qp + 2] = load_wout_quarter(qp + 2)

        cT = mixT
        cT_all = [("cT", bi) for bi in range(9)] + [("cTb", bi) for bi in range(1, 9)]
        if DEBUG:
            dma_sp(dbg["d_h1"], R1f[:, 0:8 * D], reads=[("h1", bi, hp) for bi in range(1, 9) for hp in range(4)], stream="st")
        if DEBUG:
            dma_sp(dbg["d_cT"], mixT[:, :, :].rearrange("p c t -> p (c t)"), reads=cT_all, stream="st")
            dma_sp(dbg["d_hh"], hh[0:2, :], reads=[("h1", 0, q) for q in range(4)], stream="st")
        if STOP == "C":
            P.emit(final_wait_streams="st")
            return nc
        P.barrier()

        gated = [[R3[:, (gs * GRP + jj) * 1024:(gs * GRP + jj + 1) * 1024] for jj in range(GRP)] for gs in range(2)]
        fb = 2 * GRP * 1024 // 2
        Yg = [R3f[:, fb + i * 1024:fb + (i + 1) * 1024] for i in range(2)]
        Yv = [R3f[:, fb + 2048 + i * 1024:fb + 2048 + (i + 1) * 1024] for i in range(2)]
        sb0 = 2 * (fb + 4096)
        Sg = [R3[:, sb0 + i * 1024:sb0 + (i + 1) * 1024] for i in range(2)]
        assert sb0 + 2048 <= R3N
        nblk = [(0, 342), (342, 683), (683, 1024)]
        ub = [0]

        h2_keys = lambda tb: [("h2", tb, n) for n in range(4)]
        mixflat = mixT[:, :, :].rearrange("p c t -> p (c t)")
        otE = [mixflat[:, 0:4096].bitcast(F32), mixflat[:, 4096:8192].bitcast(F32)]
        junkE = mixflat[:, 8192:10240]

        def final_block(tb):
            col = 8 + tb % 2
            c = rms_rstd(h2[:, tb, :], junkE, col, h2_keys(tb), "junkE", extra_writes=cT_all)
            ot = otE[tb % 2]
            if tb % 2 == 0 or tb == 7:
                P.op("dve", lambda e: e.scalar_tensor_tensor(out=ot, in0=h2[:, tb, :], scalar=c, in1=gb[:], op0=ALU.mult, op1=ALU.mult),
                     reads=h2_keys(tb) + [("st1", col), "gb"], writes=[("ot", tb % 2)] + cT_all)
            else:
                P.op("act", lambda e: e.activation(out=ot, in_=h2[:, tb, :], func=AF.Copy, scale=c),
                     reads=h2_keys(tb) + [("st1", col)], writes=[("ot", tb % 2)] + cT_all)
                P.op("pool", lambda e: e.tensor_tensor(out=ot, in0=ot, in1=gb[:], op=ALU.mult),
                     reads=[("ot", tb % 2), "gb"], writes=[("ot", tb % 2)])
            dma_sp(y[tb * 128:(tb + 1) * 128, :], ot, reads=[("ot", tb % 2)], stream="st%d" % (tb % 2))

        def wdown_group(gi, dslots, last=False):
            gs = gi % 2
            for tb in range(8):
                b0 = 4 if tb % 2 == 0 else 0
                for jj in range(GRP):
                    s, v = dslots[jj]
                    for n in range(4):
                        bk = b0 + n
                        P.op("pe", lambda e, jj=jj, v=v, tb=tb, n=n, bk=bk, gs=gs: e.matmul(
                            bank(bk), lhsT=gated[gs][jj][:, tb * 128:(tb + 1) * 128], rhs=v[:, n * 512:(n + 1) * 512],
                            start=(jj == 0), stop=(jj == GRP - 1)),
                            reads=[("gated", gs, jj)] + rkeys(s), writes=[("ps", bk)])
                for n in range(4):
                    bk = b0 + n
                    P.op("dve", lambda e, tb=tb, n=n, bk=bk: e.tensor_tensor(out=h2[:, tb, n * 512:(n + 1) * 512], in0=h2[:, tb, n * 512:(n + 1) * 512], in1=bank(bk), op=ALU.add),
                         reads=[("ps", bk), ("h2", tb, n)], writes=[("h2", tb, n)])
                if last:
                    final_block(tb)

        def load_down(gi):
            dslots = []
            for jj in range(GRP):
                j = gi * GRP + jj
                s = next_slot()
                dma_cast(ring[s][:, :], w_down[j * 128:(j + 1) * 128, :], writes=rkeys(s), stream="ring%d" % s)
                dslots.append((s, ring[s]))
            return dslots

        prev = None
        pair_i = 0
        for gi in range(NFF // GRP):
            gs = gi % 2
            for jj in range(GRP):
                j = gi * GRP + jj
                pi = pair_i % 2
                pair_i += 1
                for half in range(2):
                    cidx = half * NFF + j
                    s, v = load_slice(w_up_v, half * DFF + j * 128)
                    Y = (Yg if half == 0 else Yv)[pi]
                    yk = ("Y", half, pi)
                    for (r0, r1) in nblk:
                        ln = r1 - r0
                        bk = ub[0] % 4
                        ub[0] += 1
                        for kc in range(16):
                            P.op("pe", lambda e, kc=kc, v=v, r0=r0, ln=ln, bk=bk: e.matmul(bank(bk)[:, 0:ln + 2], lhsT=v[:, kc, :], rhs=cT[:, kc, r0:r0 + ln + 2],
                                                                                    start=(kc == 0), stop=(kc == 15)),
                                 reads=cT_all + rkeys(s), writes=[("ps", bk)])
                        u = bank(bk)
                        P.op("act", lambda e, u=u, Y=Y, r0=r0, r1=r1, ln=ln, cidx=cidx: e.activation(
                            out=Y[:, r0:r1], in_=u[:, 2:ln + 2], func=AF.Identity, bias=convb_s[:, cidx:cidx + 1], scale=convw_s[:, cidx * 3 + 2:cidx * 3 + 3]),
                            reads=[("ps", bk), "convw", "convb"], writes=[yk])
                        P.op("dve", lambda e, u=u, Y=Y, r0=r0, r1=r1, ln=ln, cidx=cidx: e.scalar_tensor_tensor(
                            out=Y[:, r0:r1], in0=u[:, 1:ln + 1], scalar=convw_s[:, cidx * 3 + 1:cidx * 3 + 2], in1=Y[:, r0:r1], op0=ALU.mult, op1=ALU.add),
                            reads=[("ps", bk), "convw", yk], writes=[yk])
                        P.op("dve", lambda e, u=u, Y=Y, r0=r0, r1=r1, ln=ln, cidx=cidx: e.scalar_tensor_tensor(
                            out=Y[:, r0:r1], in0=u[:, 0:ln], scalar=convw_s[:, cidx * 3:cidx * 3 + 1], in1=Y[:, r0:r1], op0=ALU.mult, op1=ALU.add),
                            reads=[("ps", bk), "convw", yk], writes=[yk])
                    if half == 0:
                        P.op("act", lambda e, Y=Y, pi=pi: e.activation(out=Sg[pi], in_=Y, func=AF.Silu), reads=[yk], writes=[("Sg", pi)])
                    else:
                        P.op("dve", lambda e, Y=Y, pi=pi, gs=gs, jj=jj: e.tensor_tensor(out=gated[gs][jj], in0=Y, in1=Sg[pi], op=ALU.mult),
                             reads=[yk, ("Sg", pi)], writes=[("gated", gs, jj)])
            if prev is not None:
                wdown_group(prev, load_down(prev))
            prev = gi
        dma_sp(gb[:], gf.partition_broadcast(128), writes=["gb"], stream="gb")
        wdown_group(prev, load_down(prev), last=True)

        P.emit(final_wait_streams="st")
    return nc


_NC_CACHE = {}


def _consts():
    c = {}
    c["c_ident"] = np.eye(128, dtype=np.float32)
    c["c_tri"] = np.triu(np.ones((128, 128), np.float32))
    c["c_ones"] = np.ones((128, 128), np.float32)
    c["c_maskR"] = np.triu(np.ones((128, 128), np.float32))
    p = np.arange(128)[:, None]
    xx = np.arange(MASKW)[None, :]
    c["c_maskT"] = np.where(xx - XOFF < p, NEG, 0.0).astype(np.float32)
    sel = np.zeros((128, 8, 128), np.float32)
    for h in range(8):
        sel[h, h, :] = 1.0
    c["c_sel"] = sel.reshape(128, 1024)
    return c


def _core_tables(T0):
    l = np.arange(LT)
    t = l - 1152 + T0
    valid = t >= 0
    inv_freq = 1.0 / (10000.0 ** (np.arange(0, 128, 2, dtype=np.float64) / 128.0))
    ang = np.where(valid, t, 0)[:, None].astype(np.float64) * inv_freq[None, :]
    def pm(a):
        n = a.shape[1]
        return np.ascontiguousarray(a.reshape(NB, 128, n).transpose(1, 0, 2).reshape(128, NB * n))
    tabs = {"cosT": pm(np.cos(ang).astype(np.float32)), "sinT": pm(np.sin(ang).astype(np.float32))}
    log_g = np.log1p(-np.exp2(-5.0 - np.arange(8, dtype=np.float64)))
    rel = (l - 1152).astype(np.float64)
    tabs["dqT"] = pm(np.exp(rel[:, None] * log_g[None, :]).astype(np.float32))
    tabs["dkT"] = pm((np.exp(-rel[:, None] * log_g[None, :]) * SCALE).astype(np.float32))
    tabs["kbT"] = pm(np.repeat(np.where(valid, 0.0, NEG).astype(np.float32)[:, None], 8, axis=1))
    return tabs


def kernel(x, meta_tokens, norm1_gain, w_in, b_forget, ret_norm_gain, w_out, norm2_gain, w_up,
           conv_w, conv_b, w_down, final_norm_gain):
    f32 = np.float32
    x = np.asarray(x, f32)
    B = x.shape[0]
    if "nc" not in _NC_CACHE:
        _NC_CACHE["nc"] = build_nc()
    nc = _NC_CACHE["nc"]
    consts = _consts()
    shared = {
        "w_in": np.ascontiguousarray(np.asarray(w_in, f32)[0]),
        "w_out": np.ascontiguousarray(np.asarray(w_out, f32)[0]),
        "w_up": np.ascontiguousarray(np.asarray(w_up, f32)[0]),
        "w_down": np.ascontiguousarray(np.asarray(w_down, f32)[0]),
        "g1": np.ascontiguousarray(np.asarray(norm1_gain, f32)[0]),
        "g2": np.ascontiguousarray(np.asarray(norm2_gain, f32)[0]),
        "gf": np.ascontiguousarray(np.asarray(final_norm_gain, f32)),
        "rng": np.ascontiguousarray(np.asarray(ret_norm_gain, f32)[0]),
        "wffd": np.ascontiguousarray(np.asarray(w_in, f32)[0][:, 7168:7176].reshape(16, 128, 8).transpose(1, 0, 2).reshape(128, 128)),
        "bfg": np.ascontiguousarray(np.tile(np.asarray(b_forget, f32)[0], NB)),
        "convw": np.ascontiguousarray(np.asarray(conv_w, f32)[0].reshape(3, 2 * NFF, 128).transpose(2, 1, 0).reshape(128, 2 * NFF * 3)),
        "convb": np.ascontiguousarray(np.asarray(conv_b, f32)[0].reshape(2 * NFF, 128).T),
    }
    shared.update(consts)
    meta = np.asarray(meta_tokens, f32)
    in_maps = []
    for core in range(8):
        b, s = core // 2, core % 2
        T0 = 16 + 1024 * s
        full = np.concatenate([meta, x[b]], axis=0)
        xl = np.zeros((LT, D), f32)
        t_lo = T0 - 1152
        src_lo = max(t_lo, 0)
        xl[src_lo - t_lo:, :] = full[src_lo:T0 + 1024]
        m = dict(shared)
        m["xl"] = xl
        m.update(_core_tables(T0))
        in_maps.append(m)
    res = run_bass_kernel_spmd(nc, in_maps[:NCORES_RUN], core_ids=list(range(NCORES_RUN)))
    out = np.zeros((B, 2048, D), f32)
    for core in range(NCORES_RUN):
        b, s = core // 2, core % 2
        out[b, 1024 * s:1024 * (s + 1), :] = res.results[core]["y"]
    if DEBUG:
        kernel.debug = res.results
    return out
```

```python
import contextlib
import numpy as np
import concourse.bass as bass
import concourse.mybir as mybir
from concourse.bass_utils import run_bass_kernel_spmd

F32 = mybir.dt.float32
BF16 = mybir.dt.bfloat16
AF = mybir.ActivationFunctionType
ALU = mybir.AluOpType
AX = mybir.AxisListType

D = 2048
NB = 17
LT = NB * 128
OWN0 = 1150
NOWN = 1026
NH = 8
DFF = 5632
NFF = 44
IN_DIM = 7176
SCALE = 128 ** -0.5
EPS = 1e-6
NEG = -30000.0
XOFF = 300
MASKW = 768
NSLOT = 8
GRP = 4
DEBUG = False
STOP = None
NHEADS_RUN = 8
NCORES_RUN = 8
STOP2 = None
SKIP = set()

ENGS = ("sp", "act", "pool", "dve", "pe")
SEM_LIMIT = 12000
WAIT_ALL_STREAMS = ("const", "constp")


class _Op:
    __slots__ = ("eng", "fn", "deps", "signal", "stream", "sem_i", "val", "inc", "idx", "batch")


class Prog:
    def __init__(self, nc):
        self.nc = nc
        self.ops = []
        self.eng_ops = {e: [] for e in ENGS}
        self.last_w = {}
        self.readers = {}
        self.last_in_stream = {}
        self.barrier_deps = set()

    def op(self, eng, fn, reads=(), writes=(), dma=None, batch=None):
        o = _Op()
        o.batch = batch
        o.eng = eng
        o.fn = fn
        o.idx = len(self.ops)
        o.stream = ("dma", dma) if dma is not None else ("eng", eng)
        o.inc = 16 if dma is not None else 1
        o.signal = dma is not None
        deps = set(self.barrier_deps)
        writes = list(writes) + [k for k in reads if isinstance(k, tuple) and k[0] == "ps"]
        reads = [k for k in reads if not (isinstance(k, tuple) and k[0] == "ps")]
        for k in reads:
            w = self.last_w.get(k)
            if w is not None:
                deps.add(w)
        for k in writes:
            w = self.last_w.get(k)
            if w is not None:
                deps.add(w)
            for r in self.readers.get(k, ()):
                deps.add(r)
        o.deps = deps
        for k in reads:
            self.readers.setdefault(k, []).append(o.idx)
        for k in writes:
            self.last_w[k] = o.idx
            self.readers[k] = []
        self.ops.append(o)
        self.eng_ops[eng].append(o)
        self.last_in_stream[o.stream] = o.idx
        return o

    def barrier(self):
        self.barrier_deps = set(self.last_in_stream.values())

    def emit(self, final_wait_streams=()):
        nc = self.nc
        ops = self.ops
        for o in ops:
            for d in o.deps:
                p = ops[d]
                if p.stream == ("eng", "pe") and o.eng == "pe":
                    continue
                p.signal = True
        streams = {}
        for o in ops:
            if not o.signal:
                continue
            st = streams.setdefault(o.stream, {"n": 0, "cur": 0})
            if st["cur"] + o.inc > SEM_LIMIT:
                st["n"] += 1
                st["cur"] = 0
            st["cur"] += o.inc
            o.sem_i = (o.stream, st["n"])
            o.val = st["cur"]
        batch_max = {}
        for o in ops:
            if o.signal and o.batch is not None:
                k = (o.sem_i, o.batch)
                batch_max[k] = max(batch_max.get(k, 0), o.val)
        sem_keys = []
        seen = set()
        for o in ops:
            if o.signal and o.sem_i not in seen:
                seen.add(o.sem_i)
                sem_keys.append(o.sem_i)
        with contextlib.ExitStack() as es:
            sems = {}
            for i, k in enumerate(sem_keys):
                sems[k] = es.enter_context(nc.semaphore("s%d" % i))
            last_val = {}
            for o in ops:
                if o.signal:
                    last_val[o.sem_i] = max(last_val.get(o.sem_i, 0), o.val)
            block = es.enter_context(nc.Block())
            handles = {"sp": block.sync, "act": block.scalar, "pool": block.gpsimd,
                       "dve": block.vector, "pe": block.tensor}

            def make(engname):
                def body(eng):
                    waited = {}
                    for o in self.eng_ops[engname]:
                        need = {}
                        for d in o.deps:
                            p = ops[d]
                            if not p.signal:
                                continue
                            if p.stream == ("eng", "pe") and engname == "pe":
                                continue
                            v_ = last_val[p.sem_i] if (p.stream[0] == "dma" and p.stream[1] in WAIT_ALL_STREAMS) else p.val
                            if p.batch is not None:
                                v_ = batch_max[(p.sem_i, p.batch)]
                            if v_ > need.get(p.sem_i, 0):
                                need[p.sem_i] = v_
                        for k, v in need.items():
                            if waited.get(k, 0) < v:
                                eng.wait_ge(sems[k], v)
                                waited[k] = v
                        ins = o.fn(eng)
                        if o.signal:
                            ins.then_inc(sems[o.sem_i], o.inc)
                    if engname == "sp":
                        for k in sem_keys:
                            if k[0][0] == "dma" and k[0][1].startswith(final_wait_streams):
                                eng.wait_ge(sems[k], last_val[k])
                return body

            for e in ENGS:
                handles[e](make(e))
        return len(ops)


def build_nc():
    nc = bass.Bass("TRN2", target_bir_lowering=False)

    def din(name, shape):
        return nc.dram_tensor(name, list(shape), F32, kind="ExternalInput").ap()

    xl = din("xl", [LT, D])
    w_in = din("w_in", [D, IN_DIM])
    w_out = din("w_out", [D, D])
    w_up = din("w_up", [D, 2 * DFF])
    w_down = din("w_down", [DFF, D])
    g1 = din("g1", [D]); g2 = din("g2", [D]); gf = din("gf", [D])
    rng = din("rng", [1024])
    bfg = din("bfg", [NB * 8])
    convw = din("convw", [128, 2 * NFF * 3])
    convb = din("convb", [128, 2 * NFF])
    cosT = din("cosT", [128, NB * 64]); sinT = din("sinT", [128, NB * 64])
    dkT = din("dkT", [128, NB * 8]); dqT = din("dqT", [128, NB * 8]); kbT = din("kbT", [128, NB * 8])
    wffd = din("wffd", [128, 16 * 8])
    c_ident = din("c_ident", [128, 128]); c_tri = din("c_tri", [128, 128]); c_ones = din("c_ones", [128, 128])
    c_maskR = din("c_maskR", [128, 128]); c_maskT = din("c_maskT", [128, MASKW]); c_sel = din("c_sel", [128, 1024])
    y = nc.dram_tensor("y", [1024, D], F32, kind="ExternalOutput").ap()
    dbg = {}
    if DEBUG:
        dbg["d_aT"] = nc.dram_tensor("d_aT", [128, 16 * LT], BF16, kind="ExternalOutput").ap()
        dbg["d_mix"] = nc.dram_tensor("d_mix", [128, 16 * NOWN], BF16, kind="ExternalOutput").ap()
        dbg["d_h1"] = nc.dram_tensor("d_h1", [128, 8 * D], F32, kind="ExternalOutput").ap()
        dbg["d_cum"] = nc.dram_tensor("d_cum", [128, NB * 8], F32, kind="ExternalOutput").ap()
        dbg["d_cT"] = nc.dram_tensor("d_cT", [128, 16 * NOWN], BF16, kind="ExternalOutput").ap()
        dbg["d_hh"] = nc.dram_tensor("d_hh", [2, D], F32, kind="ExternalOutput").ap()

    w_in_v = w_in.rearrange("(c p) n -> p c n", p=128)
    w_out_v = w_out.rearrange("(c p) n -> p c n", p=128)
    w_up_v = w_up.rearrange("(c p) n -> p c n", p=128)

    with contextlib.ExitStack() as es:
        def sb(name, shape, dt):
            return es.enter_context(nc.sbuf_tensor(name, list(shape), dt))

        def ps(name, shape, dt):
            return es.enter_context(nc.psum_tensor(name, list(shape), dt))

        R1 = sb("R1", [128, 16 * LT], BF16)
        aT = R1[:, :].rearrange("p (c t) -> p c t", c=16)
        R1f = R1.bitcast(F32)
        h2 = R1f[:, 0:8 * D].rearrange("p (b f) -> p b f", b=8)
        mixT = sb("mixT", [128, 16, NOWN], BF16)
        ringT = sb("ringT", [128, NSLOT * 2048], BF16)
        ring = [ringT[:, i * 2048:(i + 1) * 2048] for i in range(NSLOT)]
        R3N = 20992
        R3 = sb("R3", [128, R3N], BF16)
        R3f = R3.bitcast(F32)
        gb = sb("gb", [128, D], F32)
        ident_b = sb("ident_b", [128, 128], BF16)
        ones_b = sb("ones_b", [128, 128], BF16)
        maskT = sb("maskT", [128, MASKW], BF16)
        sel = sb("sel", [128, 1024], BF16)
        ident_f = sb("ident_f", [128, 128], F32)
        tri_f = sb("tri_f", [128, 128], F32)
        ones_f = sb("ones_f", [128, 128], F32)
        maskR = sb("maskR", [128, 128], F32)
        convw_s = sb("convw_s", [128, 2 * NFF * 3], F32)
        convb_s = sb("convb_s", [128, 2 * NFF], F32)
        bfg_s = sb("bfg_s", [128, NB * 8], F32)
        kb_s = sb("kb_s", [128, NB, 8], F32)
        cos_s = sb("cos_s", [128, NB, 64], F32)
        sin_s = sb("sin_s", [128, NB, 64], F32)
        dk_s = sb("dk_s", [128, NB, 8], F32)
        dq_s = sb("dq_s", [128, NB, 8], F32)
        ndk_s = sb("ndk_s", [128, NB, 8], F32)
        ndq_s = sb("ndq_s", [128, NB, 8], F32)
        spt = sb("spt", [128, NB * 8], F32)
        cumn = sb("cumn", [128, NB * 8], F32)
        tot = sb("tot", [128, NB * 8], F32)
        pre = sb("pre", [128, NB * 8], F32)
        biasK = sb("biasK", [128, NB * 8], F32)
        Rb = sb("Rb", [128, 9 * 128], BF16)
        wff = sb("wff", [128, 16, 8], BF16)
        st1 = sb("st1", [128, 32], F32)
        eps_t = sb("eps_t", [128, 1], F32)
        st2 = sb("st2", [128, 64], F32)

        pp = [ps("pp%d" % i, [128, 1024], F32) for i in range(4)]
        ppb = [p.bitcast(BF16) for p in pp]

        def bank(i):
            return pp[i // 2][:, (i % 2) * 512:(i % 2) * 512 + 512]

        P = Prog(nc)
        slot_ctr = [0]

        def next_slot():
            s = slot_ctr[0] % NSLOT
            slot_ctr[0] += 1
            return s

        def dma_sp(out, in_, reads=(), writes=(), stream="const"):
            P.op("sp", lambda e: e.dma_start(out=out, in_=in_), reads=reads, writes=writes, dma=stream)

        def dma_cast(out, in_, reads=(), writes=(), stream="constp", batch=None):
            P.op("pool", lambda e: e.dma_start(out=out, in_=in_), reads=reads, writes=writes, dma=stream, batch=batch)

        def load_slice(wv, c0, ncols=128):
            s = next_slot()
            v = ring[s][:, 0:16 * ncols].rearrange("p (c n) -> p c n", c=16)
            for hh in range(2):
                dma_cast(v[:, 8 * hh:8 * hh + 8, :], wv[:, 8 * hh:8 * hh + 8, c0:c0 + ncols], writes=[("ring", s, hh)], stream="ring%d" % s,
                         batch=slot_ctr[0])
            return s, v

        def rkeys(s):
            return [("ring", s, 0), ("ring", s, 1)]

        dma_sp(gb[:], g1.partition_broadcast(128), writes=["gb"], stream="gb")
        P.op("dve", lambda e: e.memset(eps_t[:], EPS), writes=["eps_t"])
        P.op("dve", lambda e: e.memset(pre[:], 0.0), writes=["pre"])
        dma_cast(ident_b[:], c_ident, writes=["ident_b"])
        dma_cast(wff[:, :, :].rearrange("p c n -> p (c n)"), wffd, writes=[("wff", 0), ("wff", 1)])
        dma_cast(ones_b[:], c_ones, writes=["ones_b"])
        dma_cast(maskT[:], c_maskT, writes=["maskT"])
        dma_cast(sel[:], c_sel, writes=["sel"])

        def late_constants():
            dma_sp(cos_s[:, :, :].rearrange("p b d -> p (b d)"), cosT, writes=["cos"])
            dma_sp(sin_s[:, :, :].rearrange("p b d -> p (b d)"), sinT, writes=["sin"])
            dma_sp(dk_s[:, :, :].rearrange("p b h -> p (b h)"), dkT, writes=["dk"])
            dma_sp(dq_s[:, :, :].rearrange("p b h -> p (b h)"), dqT, writes=["dq"])
            P.op("dve", lambda e: e.tensor_scalar_mul(out=ndk_s[:, :, :], in0=dk_s[:, :, :], scalar1=-1.0), reads=["dk"], writes=["ndk"])
            P.op("dve", lambda e: e.tensor_scalar_mul(out=ndq_s[:, :, :], in0=dq_s[:, :, :], scalar1=-1.0), reads=["dq"], writes=["ndq"])
            dma_sp(ident_f[:], c_ident, writes=["ident_f"])
            dma_sp(tri_f[:], c_tri, writes=["tri_f"])
            dma_sp(ones_f[:], c_ones, writes=["ones_f"])
            dma_sp(maskR[:], c_maskR, writes=["maskR"])
            dma_sp(bfg_s[:], bfg.partition_broadcast(128), writes=["bfg"])
            dma_sp(kb_s[:, :, :].rearrange("p b h -> p (b h)"), kbT, writes=["kb"])
            dma_sp(convw_s[:], convw, writes=["convw"])
            dma_sp(convb_s[:], convb, writes=["convb"])

        def rms_rstd(src_ap, junk_ap, col, rkeys_, jkey, extra_writes=()):
            npart = src_ap.shape[0]
            c = st1[0:npart, col:col + 1]
            P.op("act", lambda e: e.activation(out=junk_ap, in_=src_ap, func=AF.Square, accum_out=c),
                 reads=rkeys_, writes=[jkey, ("st1", col)] + list(extra_writes))
            P.op("act", lambda e: e.activation(out=c, in_=c, func=AF.Ln, bias=eps_t[0:npart, 0:1], scale=1.0 / D),
                 reads=[("st1", col), "eps_t"], writes=[("st1", col)])
            P.op("act", lambda e: e.activation(out=c, in_=c, func=AF.Exp, scale=-0.5), reads=[("st1", col)], writes=[("st1", col)])
            return c

        xsA = [R3f[:, 0:2048], R3f[:, 2048:4096], R3f[:, 4096:6144]]
        xnA = [R3[:, 12288:14336], R3[:, 14336:16384]]
        junkA = R3[:, 16384:18432]
        b6 = bank(6)

        def aTk(tb):
            return [("aT", tb, 0), ("aT", tb, 1)]

        def A1(tb):
            xs = xsA[tb % 3]
            dma_sp(xs, xl[tb * 128:(tb + 1) * 128, :], writes=[("xs", tb % 3)], stream="xs%d" % (tb % 3))
            rms_rstd(xs, junkA, tb % 3, [("xs", tb % 3)], "junkA")

        def A2(tb):
            xs = xsA[tb % 3]
            c = st1[:, tb % 3:tb % 3 + 1]
            xn = xnA[tb % 2]
            P.op("dve", lambda e: e.scalar_tensor_tensor(out=xn, in0=xs, scalar=c, in1=gb[:], op0=ALU.mult, op1=ALU.mult),
                 reads=[("xs", tb % 3), ("st1", tb % 3), "gb"], writes=[("xnA", tb % 2)])

        def A3(tb):
            xn = xnA[tb % 2]
            pv = ppb[tb % 2]
            for cc in range(16):
                P.op("pe", lambda e, cc=cc: e.transpose(out=pv[:, cc * 128:(cc + 1) * 128], in_=xn[:, cc * 128:(cc + 1) * 128], identity=ident_b[:]),
                     reads=[("xnA", tb % 2), "ident_b"], writes=[("ps", 2 * (tb % 2)), ("ps", 2 * (tb % 2) + 1)])
            pv3 = pv[:, 0:2048].rearrange("p (c t) -> p c t", c=16)
            P.op("act", lambda e: e.copy(out=aT[:, 0:8, tb * 128:(tb + 1) * 128], in_=pv3[:, 0:8, :]),
                 reads=[("ps", 2 * (tb % 2))], writes=[("aT", tb, 0)])
            P.op("dve", lambda e: e.tensor_copy(out=aT[:, 8:16, tb * 128:(tb + 1) * 128], in_=pv3[:, 8:16, :]),
                 reads=[("ps", 2 * (tb % 2) + 1)], writes=[("aT", tb, 1)])

        def A4(tb):
            for kc in range(16):
                P.op("pe", lambda e, kc=kc: e.matmul(b6[:, tb * 8:tb * 8 + 8], lhsT=aT[:, kc, tb * 128:(tb + 1) * 128], rhs=wff[:, kc, :],
                                                    start=(kc == 0), stop=(kc == 15)),
                     reads=aTk(tb) + [("wff", 0), ("wff", 1)], writes=[("ps", 6)])

        for i in range(NB + 3):
            if i == 3:
                late_constants()
            if i < NB:
                A1(i)
            if 0 <= i - 1 < NB:
                A2(i - 1)
            if 0 <= i - 2 < NB:
                A3(i - 2)
            if 0 <= i - 3 < NB:
                A4(i - 3)


        def aT_range_keys(l0, l1):
            ks = []
            for tb in range(l0 // 128, (l1 - 1) // 128 + 1):
                ks += aTk(tb)
            return ks

        if DEBUG:
            dma_sp(dbg["d_aT"], R1[:, :], reads=aT_range_keys(0, LT), stream="st")

        if STOP == "A":
            P.emit(final_wait_streams="st")
            return nc
        P.barrier()
        dma_sp(gb[:, 0:1024], rng.partition_broadcast(128), writes=["gb"], stream="gb")

        b7 = bank(7)
        P.op("dve", lambda e: e.tensor_tensor(out=spt[:], in0=b6[:, 0:NB * 8], in1=bfg_s[:], op=ALU.add), reads=[("ps", 6), "bfg"], writes=["spt"])
        P.op("act", lambda e: e.activation(out=spt[:], in_=spt[:], func=AF.Exp, scale=-1.0), reads=["spt"], writes=["spt"])
        P.op("act", lambda e: e.activation(out=spt[:], in_=spt[:], func=AF.Ln, bias=1.0, scale=1.0), reads=["spt"], writes=["spt"])
        P.op("pe", lambda e: e.matmul(b7[:, 0:136], lhsT=tri_f[:], rhs=spt[:], start=True, stop=True), reads=["spt", "tri_f"], writes=[("ps", 7)])
        P.op("pe", lambda e: e.matmul(b7[:, 136:272], lhsT=ones_f[:], rhs=spt[:], start=True, stop=True), reads=["spt", "ones_f"], writes=[("ps", 7)])
        P.op("dve", lambda e: e.tensor_copy(out=cumn[:], in_=b7[:, 0:136]), reads=[("ps", 7)], writes=["cumn"])
        P.op("dve", lambda e: e.tensor_copy(out=tot[:], in_=b7[:, 136:272]), reads=[("ps", 7)], writes=["tot"])
        for b in range(1, NB):
            P.op("dve", lambda e, b=b: e.tensor_tensor(out=pre[:, b * 8:b * 8 + 8], in0=pre[:, (b - 1) * 8:b * 8], in1=tot[:, (b - 1) * 8:b * 8], op=ALU.add),
                 reads=["pre", "tot"], writes=["pre"])
        P.op("dve", lambda e: e.tensor_tensor(out=cumn[:], in0=cumn[:], in1=pre[:], op=ALU.add), reads=["cumn", "pre"], writes=["cumn"])
        P.op("dve", lambda e: e.tensor_tensor(out=biasK[:], in0=cumn[:], in1=kb_s[:, :, :].rearrange("p b h -> p (b h)"), op=ALU.add),
             reads=["cumn", "kb"], writes=["biasK"])
        for c in range(9):
            tb = 8 + c
            dst = pp[2][0:8, c * 128:(c + 1) * 128] if c < 8 else pp[3][0:8, 512:640]
            P.op("pe", lambda e, dst=dst, tb=tb: e.transpose(out=dst, in_=cumn[:, tb * 8:tb * 8 + 8], identity=ident_f[:]),
                 reads=["cumn", "ident_f"], writes=[("ps", 4), ("ps", 5)] if c < 8 else [("ps", 7)])
        P.op("dve", lambda e: e.memset(Rb[:], 0.0), writes=["Rb"])
        P.op("act", lambda e: e.activation(out=Rb[0:8, 0:1024], in_=pp[2][0:8, 0:1024], func=AF.Copy, scale=-1.0 / SCALE),
             reads=[("ps", 4), ("ps", 5)], writes=["Rb"])
        P.op("act", lambda e: e.activation(out=Rb[0:8, 1024:1152], in_=pp[3][0:8, 512:640], func=AF.Copy, scale=-1.0 / SCALE),
             reads=[("ps", 7)], writes=["Rb"])
        if DEBUG:
            dma_sp(dbg["d_cum"], cumn[:], reads=["cumn"], stream="st")
        if STOP == "B0":
            P.emit(final_wait_streams="st")
            return nc

        o_ = 0
        def carve(n):
            nonlocal o_
            a = o_
            o_ += n
            return a
        fkT = R3[:, carve(LT):o_]
        _a = carve(1028)
        fqT = R3[:, _a:_a + NOWN]
        rvfv = R3[:, carve(NB * 256):o_].rearrange("p (b n) -> p b n", b=NB)
        rkt = R3[:, carve(NB * 128):o_].rearrange("p (b n) -> p b n", b=NB)
        rqT = R3[:, carve(1152):o_]
        rkT = R3[:, carve(1152):o_]
        sg = R3[:, carve(1152):o_].rearrange("p (b n) -> p b n", b=9)
        rqt = R3[:, carve(1152):o_].rearrange("p (b n) -> p b n", b=9)
        PTt = [R3[:, carve(342):o_] for _ in range(3)]
        smt = [R3[:, carve(128):o_] for _ in range(2)]
        Sbf = [R3[:, carve(128):o_] for _ in range(2)]
        junkB = R3[:, carve(128):o_]
        assert o_ % 2 == 0
        fo = o_ // 2
        def carvef(n):
            nonlocal fo
            a = fo
            fo += n
            return a
        o_all = R3f[:, carvef(1152):fo].rearrange("p (b n) -> p b n", b=9)
        rden = R3f[:, carvef(342):fo]
        rU = R3f[:, carvef(128):fo]
        rW = R3f[:, carvef(128):fo]
        ro = R3[:, fo * 2:fo * 2 + 1152].rearrange("p (b n) -> p b n", b=9)
        assert fo * 2 + 1152 <= R3N, fo * 2

        def rotary(src, dst, tbl, dec_ap, ndec_ap, rk, wk):
            Cb = cos_s[:, tbl:tbl + 1, :].to_broadcast([128, 2, 64])
            S = sin_s[:, tbl, :]
            src3 = src[:, 0:128].rearrange("p (a b) -> p a b", a=2)
            U3 = rU.rearrange("p (a b) -> p a b", a=2)
            t1 = src[:, 0:64]
            t2 = src[:, 64:128]
            rd = list(rk) + ["cos", "sin", "dk", "dq", "ndk", "ndq"]
            P.op("dve", lambda e: e.scalar_tensor_tensor(out=U3, in0=src3, scalar=dec_ap, in1=Cb, op0=ALU.mult, op1=ALU.mult), reads=rd, writes=["rU"])
            P.op("dve", lambda e: e.scalar_tensor_tensor(out=rW[:, 0:64], in0=t2, scalar=ndec_ap, in1=S, op0=ALU.mult, op1=ALU.mult), reads=rd, writes=["rW"])
            P.op("dve", lambda e: e.scalar_tensor_tensor(out=rW[:, 64:128], in0=t1, scalar=dec_ap, in1=S, op0=ALU.mult, op1=ALU.mult), reads=rd + ["rW"], writes=["rW"])
            P.op("dve", lambda e: e.tensor_tensor(out=dst, in0=rU, in1=rW, op=ALU.add), reads=["rU", "rW"], writes=wk)

        vA = ringT[:, 0:6144].rearrange("p (c n) -> p c n", c=16)
        vB = ringT[:, 6144:10240].rearrange("p (c n) -> p c n", c=16)
        vC = ringT[:, 10240:12288].rearrange("p (c n) -> p c n", c=16)
        vD = ringT[:, 12288:14336].rearrange("p (c n) -> p c n", c=16)
        regions = {"A": (vA, 3), "B": (vB, 2), "C": (vC, 1), "D": (vD, 1)}

        def rg_keys(name):
            return [("rg" + name, si, hh) for si in range(regions[name][1]) for hh in range(2)]

        def load_region(name, col_offs, h):
            v, _ = regions[name]
            for si, c0 in enumerate(col_offs):
                for hh in range(2):
                    dma_cast(v[:, 8 * hh:8 * hh + 8, si * 128:(si + 1) * 128], w_in_v[:, 8 * hh:8 * hh + 8, c0:c0 + 128],
                             writes=[("rg" + name, si, hh)], stream="rg" + name, batch=h)

        all_rg = [k for nm in ("A", "B", "C", "D") for k in rg_keys(nm)]

        def load_wout_quarter(qp, extra=()):
            sl = []
            for m in range(4):
                s_ = next_slot()
                v_ = ring[s_].rearrange("p (c n) -> p c n", c=4)
                dma_cast(v_, w_out_v[:, 4 * m:4 * m + 4, qp * 512:(qp + 1) * 512], writes=rkeys(s_) + list(extra), stream="ring%d" % s_)
                sl.append((s_, v_))
            return sl
        wq = {}
        deferred_tail = [None]
        pending_tail_ops = []

        def load_head(h):
            load_region("A", [1024 + h * 128, 2048 + h * 128, 6144 + h * 128], h)
            load_region("B", [h * 128, 3072 + h * 128], h)
            load_region("C", [4096 + h * 128], h)
            load_region("D", [5120 + h * 128], h)
        load_head(0)
        for h in range(NHEADS_RUN):
            for tb in range(NB):
                bA = 2 * (tb % 2)
                bB = bA + 1
                for kc in range(16):
                    P.op("pe", lambda e, tb=tb, kc=kc, bA=bA: e.matmul(
                        bank(bA)[:, 0:384], lhsT=aT[:, kc, tb * 128:(tb + 1) * 128], rhs=vA[:, kc, :],
                        start=(kc == 0), stop=(kc == 15)),
                        reads=aTk(tb) + rg_keys("A"), writes=[("ps", bA)])
                    if tb >= 8:
                        P.op("pe", lambda e, tb=tb, kc=kc, bB=bB: e.matmul(
                            bank(bB)[:, 0:256], lhsT=aT[:, kc, tb * 128:(tb + 1) * 128], rhs=vB[:, kc, :],
                            start=(kc == 0), stop=(kc == 15)),
                            reads=aTk(tb) + rg_keys("B"), writes=[("ps", bB)])
                P.op("act", lambda e, tb=tb, bA=bA: e.copy(out=rvfv[:, tb, :], in_=bank(bA)[:, 128:384]), reads=[("ps", bA)], writes=[("rvfv", tb)])
                if tb >= 8:
                    P.op("act", lambda e, tb=tb, bB=bB: e.copy(out=sg[:, tb - 8, :], in_=bank(bB)[:, 128:256]), reads=[("ps", bB)], writes=["sg"])
                rotary(bank(bA), rkt[:, tb, :], tb, dk_s[:, tb, h:h + 1], ndk_s[:, tb, h:h + 1], [("ps", bA)], [("rkt", tb)])
                for _ in range(4 if tb < 7 else 99):
                    if pending_tail_ops and tb < 8:
                        pending_tail_ops.pop(0)()
                if tb >= 8:
                    rotary(bank(bB), rqt[:, tb - 8, :], tb, dq_s[:, tb, h:h + 1], ndq_s[:, tb, h:h + 1], [("ps", bB)], [("rqt", tb - 8)])
            P.op("act", lambda e: e.activation(out=sg[:, :, :], in_=sg[:, :, :], func=AF.Silu), reads=["sg"], writes=["sg"])
            if deferred_tail[0] is not None:
                deferred_tail[0]()
                deferred_tail[0] = None
            for nb in range(5):
                n0 = nb * 512
                nw = min(512, LT - n0)
                bk = 4 + nb % 2
                for kc in range(16):
                    P.op("pe", lambda e, kc=kc, n0=n0, nw=nw, bk=bk: e.matmul(bank(bk)[:, 0:nw], lhsT=vD[:, kc, :], rhs=aT[:, kc, n0:n0 + nw],
                                                                          start=(kc == 0), stop=(kc == 15)),
                         reads=aT_range_keys(n0, n0 + nw) + rg_keys("D"), writes=[("ps", bk)])
                P.op("dve", lambda e, n0=n0, nw=nw, bk=bk: e.tensor_copy(out=fkT[:, n0:n0 + nw], in_=bank(bk)[:, 0:nw]), reads=[("ps", bk)], writes=[("fkT", nb)])
            for g in range(3):
                n0 = OWN0 + 342 * g
                bk = 4 + (g + 1) % 2
                for kc in range(16):
                    P.op("pe", lambda e, kc=kc, n0=n0, bk=bk: e.matmul(bank(bk)[:, 0:342], lhsT=vC[:, kc, :], rhs=aT[:, kc, n0:n0 + 342],
                                                                    start=(kc == 0), stop=(kc == 15)),
                         reads=aT_range_keys(n0, n0 + 342) + rg_keys("C"), writes=[("ps", bk)])
                P.op("dve", lambda e, g=g, bk=bk: e.tensor_copy(out=fqT[:, 342 * g:342 * g + 342], in_=bank(bk)[:, 0:342]), reads=[("ps", bk)], writes=[("fqT", g)])

            if h + 1 < NHEADS_RUN:
                load_head(h + 1)
            else:
                wq[0] = load_wout_quarter(0, all_rg)
                wq[1] = load_wout_quarter(1, all_rg)
            for c in range(9):
                P.op("pe", lambda e, c=c: e.transpose(out=ppb[2][:, c * 128:(c + 1) * 128], in_=rqt[:, c, :], identity=ident_b[:]),
                     reads=[("rqt", c), "ident_b"], writes=[("ps", 4), ("ps", 5)])
                P.op("pe", lambda e, c=c: e.transpose(out=ppb[3][:, c * 128:(c + 1) * 128], in_=rkt[:, 8 + c, :], identity=ident_b[:]),
                     reads=[("rkt", 8 + c), "ident_b"], writes=[("ps", 6), ("ps", 7)])
            P.op("dve", lambda e: e.tensor_copy(out=rqT, in_=ppb[2][:, 0:1152]), reads=[("ps", 4), ("ps", 5)], writes=["rqT"])
            P.op("dve", lambda e: e.tensor_copy(out=rkT, in_=ppb[3][:, 0:1152]), reads=[("ps", 6), ("ps", 7)], writes=["rkT"])
            Sps = bank(7)[:, 0:128]
            ret_steps = []

            def r_init():
                for b in range(8):
                    P.op("pe", lambda e, b=b: e.matmul(Sps, lhsT=rkt[:, b, :], rhs=rvfv[:, b, 0:128], start=(b == 0), stop=(b == 7), skip_group_check=True),
                         reads=[("rkt", b), ("rvfv", b)], writes=[("ps", 7)])
            ret_steps.append(r_init)

            def r_a(c):
                sTp = bank(6)[:, (c % 2) * 128:(c % 2) * 128 + 128]
                if c > 0:
                    P.op("pe", lambda e: e.matmul(Sps, lhsT=rkt[:, 7 + c, :], rhs=rvfv[:, 7 + c, 0:128], start=False, stop=True, skip_group_check=True),
                         reads=[("rkt", 7 + c), ("rvfv", 7 + c)], writes=[("ps", 7)])
                P.op("dve", lambda e: e.tensor_copy(out=Sbf[c % 2], in_=Sps), reads=[("ps", 7)], writes=[("Sbf", c % 2)])
                P.op("pe", lambda e: e.matmul(sTp, lhsT=rkT[:, c * 128:(c + 1) * 128], rhs=rqT[:, c * 128:(c + 1) * 128], start=True, stop=True),
                     reads=["rkT", "rqT"], writes=[("ps", 6)])
                P.op("dve", lambda e: e.tensor_tensor(out=smt[c % 2], in0=sTp, in1=maskR[:], op=ALU.mult),
                     reads=[("ps", 6), "maskR"], writes=[("smt", c % 2)])

            def r_b(c):
                op_ = bank(4)[:, (c % 2) * 128:(c % 2) * 128 + 128]
                P.op("pe", lambda e: e.matmul(op_, lhsT=rqT[:, c * 128:(c + 1) * 128], rhs=Sbf[c % 2], start=True, stop=False),
                     reads=["rqT", ("Sbf", c % 2)], writes=[("ps", 4)])
                P.op("pe", lambda e: e.matmul(op_, lhsT=smt[c % 2], rhs=rvfv[:, 8 + c, 0:128], start=False, stop=True),
                     reads=[("smt", c % 2), ("rvfv", 8 + c)], writes=[("ps", 4)])
                P.op("dve", lambda e: e.tensor_copy(out=o_all[:, c, :], in_=op_), reads=[("ps", 4)], writes=[("o_all", c)])
                P.op("act", lambda e: e.activation(out=junkB, in_=op_, func=AF.Square, accum_out=st2[:, 16 + c:17 + c]),
                     reads=[("ps", 4)], writes=["junkB", ("st2q", c)])
            for c in range(9):
                ret_steps.append(lambda c=c: r_a(c))
                ret_steps.append(lambda c=c: r_b(c))

            def r_tail(h=h):
                oall_keys = [("o_all", c) for c in range(9)]
                sq_keys = [("st2q", c) for c in range(9)]
                mean = st2[:, 0:9]
                ssq = st2[:, 16:25]
                msq = st2[:, 32:41]
                rstd = st2[:, 48:57]
                P.op("dve", lambda e: e.reduce_sum(out=mean, in_=o_all[:, :, :], axis=AX.X), reads=oall_keys, writes=["st2m"])
                P.op("dve", lambda e: e.tensor_scalar_mul(out=mean, in0=mean, scalar1=1.0 / 128), reads=["st2m"], writes=["st2m"])
                P.op("dve", lambda e: e.tensor_tensor(out=msq, in0=mean, in1=mean, op=ALU.mult), reads=["st2m"], writes=["st2s"])
                P.op("dve", lambda e: e.tensor_scalar(out=rstd, in0=ssq, scalar1=1.0 / 128, scalar2=EPS, op0=ALU.mult, op1=ALU.add), reads=sq_keys, writes=["st2r"])
                P.op("dve", lambda e: e.tensor_tensor(out=rstd, in0=rstd, in1=msq, op=ALU.subtract), reads=["st2r", "st2s"], writes=["st2r"])
                P.op("act", lambda e: e.activation(out=rstd, in_=rstd, func=AF.Ln), reads=["st2r"], writes=["st2r"])
                P.op("act", lambda e: e.activation(out=rstd, in_=rstd, func=AF.Exp, scale=-0.5), reads=["st2r"], writes=["st2r"])
                ops_ = []
                for c in range(9):
                    ops_.append(lambda c=c: P.op("dve", lambda e: e.tensor_scalar(out=o_all[:, c, :], in0=o_all[:, c, :], scalar1=st2[:, c:c + 1], scalar2=st2[:, 48 + c:49 + c],
                                                                              op0=ALU.subtract, op1=ALU.mult),
                                                 reads=[("o_all", c), "st2m", "st2r"], writes=[("o_all", c)]))
                    ops_.append(lambda c=c: P.op("dve", lambda e: e.tensor_tensor(out=o_all[:, c, :], in0=o_all[:, c, :], in1=gb[:, h * 128:(h + 1) * 128], op=ALU.mult),
                                                 reads=[("o_all", c), "gb"], writes=[("o_all", c)]))
                    ops_.append(lambda c=c: P.op("dve", lambda e: e.tensor_tensor(out=ro[:, c, :], in0=o_all[:, c, :], in1=sg[:, c, :], op=ALU.mult),
                                                 reads=[("o_all", c), "sg"], writes=[("ro", c)]))
                return ops_

            def r_tail_pe(h=h):
                for c in range(9):
                    P.op("pe", lambda e, c=c: e.transpose(out=ppb[2][:, c * 128:(c + 1) * 128], in_=ro[:, c, :], identity=ident_b[:]),
                         reads=[("ro", c), "ident_b"], writes=[("ps", 4), ("ps", 5)])
                P.op("act", lambda e: e.copy(out=mixT[:, h, :], in_=ppb[2][:, 126:1152]), reads=[("ps", 4), ("ps", 5)], writes=[("mixh", h)])

            tiles = []
            for g in range(3):
                q0 = OWN0 + 342 * g
                kmax = (q0 + 342 - 1) // 128
                for kb in range(kmax + 1):
                    tiles.append((g, kb, kmax, q0))
            oTp = bank(2)[:, 0:342]
            dnp = bank(3)[:, 0:342]

            def f_s(ti, h=h):
                g, kb, kmax, q0 = tiles[ti]
                sbk = (0, 1, 5)[ti % 3]
                sb_ = bank(sbk)[:, 0:342]
                delta = 128 * kb - q0
                need_mask = (128 * kb + 127) > q0
                P.op("pe", lambda e: e.matmul(sb_, lhsT=fkT[:, kb * 128:(kb + 1) * 128], rhs=fqT[:, 342 * g:342 * g + 342], start=True, stop=False),
                     reads=[("fkT", kb // 4), ("fqT", g)], writes=[("ps", sbk)])
                P.op("pe", lambda e: e.matmul(sb_, lhsT=sel[:, h * 128:(h + 1) * 128], rhs=Rb[:, 126 + 342 * g:126 + 342 * g + 342],
                                              start=False, stop=(not need_mask)),
                     reads=["sel", "Rb"], writes=[("ps", sbk)])
                if need_mask:
                    off = XOFF - delta
                    assert 0 <= off and off + 342 <= MASKW, off
                    P.op("pe", lambda e: e.matmul(sb_, lhsT=ident_b[:], rhs=maskT[:, off:off + 342], start=False, stop=True),
                         reads=["ident_b", "maskT"], writes=[("ps", sbk)])
                pt = PTt[ti % 3]
                P.op("act", lambda e: e.activation(out=pt, in_=sb_, func=AF.Exp, bias=biasK[:, kb * 8 + h:kb * 8 + h + 1], scale=SCALE),
                     reads=[("ps", sbk), "biasK"], writes=[("PT", ti % 3)])

            def f_pv(ti, h=h):
                g, kb, kmax, q0 = tiles[ti]
                pt = PTt[ti % 3]
                ptk = ("PT", ti % 3)
                P.op("pe", lambda e: e.matmul(oTp, lhsT=rvfv[:, kb, 128:256], rhs=pt, start=(kb == 0), stop=(kb == kmax)),
                     reads=[("rvfv", kb), ptk], writes=[("ps", 2)])
                P.op("pe", lambda e: e.matmul(dnp, lhsT=ones_b[:], rhs=pt, start=(kb == 0), stop=(kb == kmax)),
                     reads=["ones_b", ptk], writes=[("ps", 3)])
                if kb == kmax:
                    P.op("dve", lambda e: e.reciprocal(out=rden, in_=dnp), reads=[("ps", 3)], writes=["rden"])
                    P.op("dve", lambda e: e.tensor_tensor(out=mixT[:, 8 + h, 342 * g:342 * g + 342], in0=oTp, in1=rden, op=ALU.mult),
                         reads=[("ps", 2), "rden"], writes=[("mixf", h, g)])

            fox_steps = []
            nt_ = len(tiles)
            def f_first():
                f_s(0)
                f_s(1)
            fox_steps.append(f_first)
            for ti in range(nt_):
                def st(ti=ti):
                    if ti + 2 < nt_:
                        f_s(ti + 2)
                    f_pv(ti)
                fox_steps.append(st)
            fi = 0
            last_head = (h == NHEADS_RUN - 1)
            per = [1, 1] if last_head else [3, 2]
            for ri, rs in enumerate(ret_steps):
                rs()
                k = 1 if ri == 0 else per[ri % 2]
                for _ in range(k):
                    if fi < len(fox_steps):
                        fox_steps[fi]()
                        fi += 1
            if last_head:
                for f_ in r_tail():
                    f_()
            while fi < len(fox_steps):
                fox_steps[fi]()
                fi += 1
            if not last_head:
                pending_tail_ops[:] = r_tail()
            deferred_tail[0] = r_tail_pe
        deferred_tail[0]()


        mix_all = [("mixh", h) for h in range(NH)] + [("mixf", h, g) for h in range(NH) for g in range(3)]
        if DEBUG:
            dma_sp(dbg["d_mix"], mixT[:, :, :].rearrange("p c t -> p (c t)"), reads=mix_all, stream="st")
        if STOP == "B":
            P.emit(final_wait_streams="st")
            return nc

        P.barrier()

        dma_sp(gb[:], g2.partition_broadcast(128), writes=["gb"], stream="gb")
        xsC = [R3f[:, 0:512], R3f[:, 512:1024], R3f[:, 1024:1536]]
        hnC = [R3[:, 4096:6144], R3[:, 6144:8192]]
        junkC = R3[:, 8192:10240]
        hh = R3f[:, 6144:8192]
        blocks = [(-1, 0, OWN0)] + [(tb, 2 + 128 * tb, 1152 + 128 * tb) for tb in range(8)]
        xi = 0
        ubk = 0

        def c_norm1(bi, tb):
            src = hh[:, :] if tb < 0 else h2[:, tb, :]
            col = 4 + bi % 2
            hk = [("h1", bi, q) for q in range(4)]
            c = rms_rstd(src, junkC, col, hk, "junkC")
            hn = hnC[bi % 2]
            P.op("dve", lambda e: e.scalar_tensor_tensor(out=hn, in0=src, scalar=c, in1=gb[:], op0=ALU.mult, op1=ALU.mult),
                 reads=hk + [("st1", col), "gb"], writes=[("hnC", bi % 2)])

        def c_norm2(bi, tb, c0):
            hn = hnC[bi % 2]
            pv = ppb[2 + bi % 2]
            pk = [("ps", 4 + 2 * (bi % 2)), ("ps", 5 + 2 * (bi % 2))]
            for cc in range(16):
                P.op("pe", lambda e, cc=cc: e.transpose(out=pv[:, cc * 128:(cc + 1) * 128], in_=hn[:, cc * 128:(cc + 1) * 128], identity=ident_b[:]),
                     reads=[("hnC", bi % 2), "ident_b"], writes=pk)
            pv3 = pv[:, 0:2048].rearrange("p (c t) -> p c t", c=16)
            if tb < 0:
                P.op("act", lambda e: e.copy(out=mixT[:, :, 0:2], in_=pv3[:, :, 0:2]), reads=pk + mix_all, writes=[("cT", bi)])
            else:
                P.op("act", lambda e: e.copy(out=mixT[:, 0:8, c0:c0 + 128], in_=pv3[:, 0:8, :]), reads=pk[0:1] + mix_all, writes=[("cT", bi)])
                P.op("dve", lambda e: e.tensor_copy(out=mixT[:, 8:16, c0:c0 + 128], in_=pv3[:, 8:16, :]), reads=pk[1:2] + mix_all, writes=[("cTb", bi)])

        for qp in range(4):
            slots = wq[qp]
            pend = None
            for bi, (tb, c0, l0) in enumerate(blocks):
                xs = xsC[xi % 3]
                xk = ("xs", xi % 3)
                xstream = "xs%d" % (xi % 3)
                xi += 1
                dma_sp(xs, xl[l0:l0 + 128, qp * 512:(qp + 1) * 512], writes=[xk], stream=xstream)
                bk = ubk % 4
                ubk += 1
                ck = [("cT", bi), ("cTb", bi)] + ([("cT", 1), ("cTb", 1)] if tb < 0 else [])
                for kc in range(16):
                    s_, v_ = slots[kc // 4]
                    P.op("pe", lambda e, kc=kc, v_=v_, c0=c0, bk=bk: e.matmul(
                        bank(bk), lhsT=mixT[:, kc, c0:c0 + 128], rhs=v_[:, kc % 4, :], start=(kc == 0), stop=(kc == 15)),
                        reads=mix_all + ck + rkeys(s_), writes=[("ps", bk)])
                dst = hh[:, qp * 512:(qp + 1) * 512] if tb < 0 else h2[:, tb, qp * 512:(qp + 1) * 512]
                P.op("dve", lambda e, dst=dst, bk=bk, xs=xs: e.tensor_tensor(out=dst, in0=bank(bk), in1=xs, op=ALU.add),
                     reads=[("ps", bk), xk], writes=[("h1", bi, qp)])
                if qp == 3:
                    c_norm1(bi, tb)
                    if pend is not None:
                        c_norm2(*pend)
                    pend = (bi, tb, c0)
            if qp == 3:
                c_norm2(*pend)
            if qp + 2 < 4:
                wq[qp + 2] = load_wout_quarter(qp + 2)

        cT = mixT
        cT_all = [("cT", bi) for bi in range(9)] + [("cTb", bi) for bi in range(1, 9)]
        if DEBUG:
            dma_sp(dbg["d_h1"], R1f[:, 0:8 * D], reads=[("h1", bi, hp) for bi in range(1, 9) for hp in range(4)], stream="st")
        if DEBUG:
            dma_sp(dbg["d_cT"], mixT[:, :, :].rearrange("p c t -> p (c t)"), reads=cT_all, stream="st")
            dma_sp(dbg["d_hh"], hh[0:2, :], reads=[("h1", 0, q) for q in range(4)], stream="st")
        if STOP == "C":
            P.emit(final_wait_streams="st")
            return nc
        P.barrier()

        gated = [[R3[:, (gs * GRP + jj) * 1024:(gs * GRP + jj + 1) * 1024] for jj in range(GRP)] for gs in range(2)]
        fb = 2 * GRP * 1024 // 2
        Yg = [R3f[:, fb + i * 1024:fb + (i + 1) * 1024] for i in range(2)]
        Yv = [R3f[:, fb + 2048 + i * 1024:fb + 2048 + (i + 1) * 1024] for i in range(2)]
        sb0 = 2 * (fb + 4096)
        Sg = [R3[:, sb0 + i * 1024:sb0 + (i + 1) * 1024] for i in range(2)]
        assert sb0 + 2048 <= R3N
        nblk = [(0, 342), (342, 683), (683, 1024)]
        ub = [0]

        h2_keys = lambda tb: [("h2", tb, n) for n in range(4)]
        mixflat = mixT[:, :, :].rearrange("p c t -> p (c t)")
        otE = [mixflat[:, 0:4096].bitcast(F32), mixflat[:, 4096:8192].bitcast(F32)]
        junkE = mixflat[:, 8192:10240]

        def final_block(tb):
            col = 8 + tb % 2
            c = rms_rstd(h2[:, tb, :], junkE, col, h2_keys(tb), "junkE", extra_writes=cT_all)
            ot = otE[tb % 2]
            if tb % 2 == 0 or tb == 7:
                P.op("dve", lambda e: e.scalar_tensor_tensor(out=ot, in0=h2[:, tb, :], scalar=c, in1=gb[:], op0=ALU.mult, op1=ALU.mult),
                     reads=h2_keys(tb) + [("st1", col), "gb"], writes=[("ot", tb % 2)] + cT_all)
            else:
                P.op("act", lambda e: e.activation(out=ot, in_=h2[:, tb, :], func=AF.Copy, scale=c),
                     reads=h2_keys(tb) + [("st1", col)], writes=[("ot", tb % 2)] + cT_all)
                P.op("pool", lambda e: e.tensor_tensor(out=ot, in0=ot, in1=gb[:], op=ALU.mult),
                     reads=[("ot", tb % 2), "gb"], writes=[("ot", tb % 2)])
            dma_sp(y[tb * 128:(tb + 1) * 128, :], ot, reads=[("ot", tb % 2)], stream="st%d" % (tb % 2))

        def wdown_group(gi, dslots, last=False):
            gs = gi % 2
            for tb in range(8):
                b0 = 4 if tb % 2 == 0 else 0
                for jj in range(GRP):
                    s, v = dslots[jj]
                    for n in range(4):
                        bk = b0 + n
                        P.op("pe", lambda e, jj=jj, v=v, tb=tb, n=n, bk=bk, gs=gs: e.matmul(
                            bank(bk), lhsT=gated[gs][jj][:, tb * 128:(tb + 1) * 128], rhs=v[:, n * 512:(n + 1) * 512],
                            start=(jj == 0), stop=(jj == GRP - 1)),
                            reads=[("gated", gs, jj)] + rkeys(s), writes=[("ps", bk)])
                for n in range(4):
                    bk = b0 + n
                    P.op("dve", lambda e, tb=tb, n=n, bk=bk: e.tensor_tensor(out=h2[:, tb, n * 512:(n + 1) * 512], in0=h2[:, tb, n * 512:(n + 1) * 512], in1=bank(bk), op=ALU.add),
                         reads=[("ps", bk), ("h2", tb, n)], writes=[("h2", tb, n)])
                if last:
                    final_block(tb)

        def load_down(gi):
            dslots = []
            for jj in range(GRP):
                j = gi * GRP + jj
                s = next_slot()
                dma_cast(ring[s][:, :], w_down[j * 128:(j + 1) * 128, :], writes=rkeys(s), stream="ring%d" % s)
                dslots.append((s, ring[s]))
            return dslots

        prev = None
        pair_i = 0
        for gi in range(NFF // GRP):
            gs = gi % 2
            for jj in range(GRP):
                j = gi * GRP + jj
                pi = pair_i % 2
                pair_i += 1
                for half in range(2):
                    cidx = half * NFF + j
                    s, v = load_slice(w_up_v, half * DFF + j * 128)
                    Y = (Yg if half == 0 else Yv)[pi]
                    yk = ("Y", half, pi)
                    for (r0, r1) in nblk:
                        ln = r1 - r0
                        bk = ub[0] % 4
                        ub[0] += 1
                        for kc in range(16):
                            P.op("pe", lambda e, kc=kc, v=v, r0=r0, ln=ln, bk=bk: e.matmul(bank(bk)[:, 0:ln + 2], lhsT=v[:, kc, :], rhs=cT[:, kc, r0:r0 + ln + 2],
                                                                                    start=(kc == 0), stop=(kc == 15)),
                                 reads=cT_all + rkeys(s), writes=[("ps", bk)])
                        u = bank(bk)
                        P.op("act", lambda e, u=u, Y=Y, r0=r0, r1=r1, ln=ln, cidx=cidx: e.activation(
                            out=Y[:, r0:r1], in_=u[:, 2:ln + 2], func=AF.Identity, bias=convb_s[:, cidx:cidx + 1], scale=convw_s[:, cidx * 3 + 2:cidx * 3 + 3]),
                            reads=[("ps", bk), "convw", "convb"], writes=[yk])
                        P.op("dve", lambda e, u=u, Y=Y, r0=r0, r1=r1, ln=ln, cidx=cidx: e.scalar_tensor_tensor(
                            out=Y[:, r0:r1], in0=u[:, 1:ln + 1], scalar=convw_s[:, cidx * 3 + 1:cidx * 3 + 2], in1=Y[:, r0:r1], op0=ALU.mult, op1=ALU.add),
                            reads=[("ps", bk), "convw", yk], writes=[yk])
                        P.op("dve", lambda e, u=u, Y=Y, r0=r0, r1=r1, ln=ln, cidx=cidx: e.scalar_tensor_tensor(
                            out=Y[:, r0:r1], in0=u[:, 0:ln], scalar=convw_s[:, cidx * 3:cidx * 3 + 1], in1=Y[:, r0:r1], op0=ALU.mult, op1=ALU.add),
                            reads=[("ps", bk), "convw", yk], writes=[yk])
                    if half == 0:
                        P.op("act", lambda e, Y=Y, pi=pi: e.activation(out=Sg[pi], in_=Y, func=AF.Silu), reads=[yk], writes=[("Sg", pi)])
                    else:
                        P.op("dve", lambda e, Y=Y, pi=pi, gs=gs, jj=jj: e.tensor_tensor(out=gated[gs][jj], in0=Y, in1=Sg[pi], op=ALU.mult),
                             reads=[yk, ("Sg", pi)], writes=[("gated", gs, jj)])
            if prev is not None:
                wdown_group(prev, load_down(prev))
            prev = gi
        dma_sp(gb[:], gf.partition_broadcast(128), writes=["gb"], stream="gb")
        wdown_group(prev, load_down(prev), last=True)

        P.emit(final_wait_streams="st")
    return nc


_NC_CACHE = {}


def _consts():
    c = {}
    c["c_ident"] = np.eye(128, dtype=np.float32)
    c["c_tri"] = np.triu(np.ones((128, 128), np.float32))
    c["c_ones"] = np.ones((128, 128), np.float32)
    c["c_maskR"] = np.triu(np.ones((128, 128), np.float32))
    p = np.arange(128)[:, None]
    xx = np.arange(MASKW)[None, :]
    c["c_maskT"] = np.where(xx - XOFF < p, NEG, 0.0).astype(np.float32)
    sel = np.zeros((128, 8, 128), np.float32)
    for h in range(8):
        sel[h, h, :] = 1.0
    c["c_sel"] = sel.reshape(128, 1024)
    return c


def _core_tables(T0):
    l = np.arange(LT)
    t = l - 1152 + T0
    valid = t >= 0
    inv_freq = 1.0 / (10000.0 ** (np.arange(0, 128, 2, dtype=np.float64) / 128.0))
    ang = np.where(valid, t, 0)[:, None].astype(np.float64) * inv_freq[None, :]
    def pm(a):
        n = a.shape[1]
        return np.ascontiguousarray(a.reshape(NB, 128, n).transpose(1, 0, 2).reshape(128, NB * n))
    tabs = {"cosT": pm(np.cos(ang).astype(np.float32)), "sinT": pm(np.sin(ang).astype(np.float32))}
    log_g = np.log1p(-np.exp2(-5.0 - np.arange(8, dtype=np.float64)))
    rel = (l - 1152).astype(np.float64)
    tabs["dqT"] = pm(np.exp(rel[:, None] * log_g[None, :]).astype(np.float32))
    tabs["dkT"] = pm((np.exp(-rel[:, None] * log_g[None, :]) * SCALE).astype(np.float32))
    tabs["kbT"] = pm(np.repeat(np.where(valid, 0.0, NEG).astype(np.float32)[:, None], 8, axis=1))
    return tabs


def kernel(x, meta_tokens, norm1_gain, w_in, b_forget, ret_norm_gain, w_out, norm2_gain, w_up,
           conv_w, conv_b, w_down, final_norm_gain):
    f32 = np.float32
    x = np.asarray(x, f32)
    B = x.shape[0]
    if "nc" not in _NC_CACHE:
        _NC_CACHE["nc"] = build_nc()
    nc = _NC_CACHE["nc"]
    consts = _consts()
    shared = {
        "w_in": np.ascontiguousarray(np.asarray(w_in, f32)[0]),
        "w_out": np.ascontiguousarray(np.asarray(w_out, f32)[0]),
        "w_up": np.ascontiguousarray(np.asarray(w_up, f32)[0]),
        "w_down": np.ascontiguousarray(np.asarray(w_down, f32)[0]),
        "g1": np.ascontiguousarray(np.asarray(norm1_gain, f32)[0]),
        "g2": np.ascontiguousarray(np.asarray(norm2_gain, f32)[0]),
        "gf": np.ascontiguousarray(np.asarray(final_norm_gain, f32)),
        "rng": np.ascontiguousarray(np.asarray(ret_norm_gain, f32)[0]),
        "wffd": np.ascontiguousarray(np.asarray(w_in, f32)[0][:, 7168:7176].reshape(16, 128, 8).transpose(1, 0, 2).reshape(128, 128)),
        "bfg": np.ascontiguousarray(np.tile(np.asarray(b_forget, f32)[0], NB)),
        "convw": np.ascontiguousarray(np.asarray(conv_w, f32)[0].reshape(3, 2 * NFF, 128).transpose(2, 1, 0).reshape(128, 2 * NFF * 3)),
        "convb": np.ascontiguousarray(np.asarray(conv_b, f32)[0].reshape(2 * NFF, 128).T),
    }
    shared.update(consts)
    meta = np.asarray(meta_tokens, f32)
    in_maps = []
    for core in range(8):
        b, s = core // 2, core % 2
        T0 = 16 + 1024 * s
        full = np.concatenate([meta, x[b]], axis=0)
        xl = np.zeros((LT, D), f32)
        t_lo = T0 - 1152
        src_lo = max(t_lo, 0)
        xl[src_lo - t_lo:, :] = full[src_lo:T0 + 1024]
        m = dict(shared)
        m["xl"] = xl
        m.update(_core_tables(T0))
        in_maps.append(m)
    res = run_bass_kernel_spmd(nc, in_maps[:NCORES_RUN], core_ids=list(range(NCORES_RUN)))
    out = np.zeros((B, 2048, D), f32)
    for core in range(NCORES_RUN):
        b, s = core // 2, core % 2
        out[b, 1024 * s:1024 * (s + 1), :] = res.results[core]["y"]
    if DEBUG:
        kernel.debug = res.results
    return out
```

```python
import contextlib
import numpy as np
import concourse.bass as bass
import concourse.mybir as mybir
from concourse.bass_utils import run_bass_kernel_spmd

F32 = mybir.dt.float32
BF16 = mybir.dt.bfloat16
AF = mybir.ActivationFunctionType
ALU = mybir.AluOpType
AX = mybir.AxisListType

D = 2048
NB = 17
LT = NB * 128
OWN0 = 1150
NOWN = 1026
NH = 8
DFF = 5632
NFF = 44
IN_DIM = 7176
SCALE = 128 ** -0.5
EPS = 1e-6
NEG = -30000.0
XOFF = 300
MASKW = 768
NSLOT = 8
GRP = 4
DEBUG = False
STOP = None
NHEADS_RUN = 8
NCORES_RUN = 8
STOP2 = None
SKIP = set()

ENGS = ("sp", "act", "pool", "dve", "pe")
SEM_LIMIT = 12000
WAIT_ALL_STREAMS = ("const", "constp")


class _Op:
    __slots__ = ("eng", "fn", "deps", "signal", "stream", "sem_i", "val", "inc", "idx", "batch")


class Prog:
    def __init__(self, nc):
        self.nc = nc
        self.ops = []
        self.eng_ops = {e: [] for e in ENGS}
        self.last_w = {}
        self.readers = {}
        self.last_in_stream = {}
        self.barrier_deps = set()

    def op(self, eng, fn, reads=(), writes=(), dma=None, batch=None):
        o = _Op()
        o.batch = batch
        o.eng = eng
        o.fn = fn
        o.idx = len(self.ops)
        o.stream = ("dma", dma) if dma is not None else ("eng", eng)
        o.inc = 16 if dma is not None else 1
        o.signal = dma is not None
        deps = set(self.barrier_deps)
        writes = list(writes) + [k for k in reads if isinstance(k, tuple) and k[0] == "ps"]
        reads = [k for k in reads if not (isinstance(k, tuple) and k[0] == "ps")]
        for k in reads:
            w = self.last_w.get(k)
            if w is not None:
                deps.add(w)
        for k in writes:
            w = self.last_w.get(k)
            if w is not None:
                deps.add(w)
            for r in self.readers.get(k, ()):
                deps.add(r)
        o.deps = deps
        for k in reads:
            self.readers.setdefault(k, []).append(o.idx)
        for k in writes:
            self.last_w[k] = o.idx
            self.readers[k] = []
        self.ops.append(o)
        self.eng_ops[eng].append(o)
        self.last_in_stream[o.stream] = o.idx
        return o

    def barrier(self):
        self.barrier_deps = set(self.last_in_stream.values())

    def emit(self, final_wait_streams=()):
        nc = self.nc
        ops = self.ops
        for o in ops:
            for d in o.deps:
                p = ops[d]
                if p.stream == ("eng", "pe") and o.eng == "pe":
                    continue
                p.signal = True
        streams = {}
        for o in ops:
            if not o.signal:
                continue
            st = streams.setdefault(o.stream, {"n": 0, "cur": 0})
            if st["cur"] + o.inc > SEM_LIMIT:
                st["n"] += 1
                st["cur"] = 0
            st["cur"] += o.inc
            o.sem_i = (o.stream, st["n"])
            o.val = st["cur"]
        batch_max = {}
        for o in ops:
            if o.signal and o.batch is not None:
                k = (o.sem_i, o.batch)
                batch_max[k] = max(batch_max.get(k, 0), o.val)
        sem_keys = []
        seen = set()
        for o in ops:
            if o.signal and o.sem_i not in seen:
                seen.add(o.sem_i)
                sem_keys.append(o.sem_i)
        with contextlib.ExitStack() as es:
            sems = {}
            for i, k in enumerate(sem_keys):
                sems[k] = es.enter_context(nc.semaphore("s%d" % i))
            last_val = {}
            for o in ops:
                if o.signal:
                    last_val[o.sem_i] = max(last_val.get(o.sem_i, 0), o.val)
            block = es.enter_context(nc.Block())
            handles = {"sp": block.sync, "act": block.scalar, "pool": block.gpsimd,
                       "dve": block.vector, "pe": block.tensor}

            def make(engname):
                def body(eng):
                    waited = {}
                    for o in self.eng_ops[engname]:
                        need = {}
                        for d in o.deps:
                            p = ops[d]
                            if not p.signal:
                                continue
                            if p.stream == ("eng", "pe") and engname == "pe":
                                continue
                            v_ = last_val[p.sem_i] if (p.stream[0] == "dma" and p.stream[1] in WAIT_ALL_STREAMS) else p.val
                            if p.batch is not None:
                                v_ = batch_max[(p.sem_i, p.batch)]
                            if v_ > need.get(p.sem_i, 0):
                                need[p.sem_i] = v_
                        for k, v in need.items():
                            if waited.get(k, 0) < v:
                                eng.wait_ge(sems[k], v)
                                waited[k] = v
                        ins = o.fn(eng)
                        if o.signal:
                            ins.then_inc(sems[o.sem_i], o.inc)
                    if engname == "sp":
                        for k in sem_keys:
                            if k[0][0] == "dma" and k[0][1].startswith(final_wait_streams):
                                eng.wait_ge(sems[k], last_val[k])
                return body

            for e in ENGS:
                handles[e](make(e))
        return len(ops)


def build_nc():
    nc = bass.Bass("TRN2", target_bir_lowering=False)

    def din(name, shape):
        return nc.dram_tensor(name, list(shape), F32, kind="ExternalInput").ap()

    xl = din("xl", [LT, D])
    w_in = din("w_in", [D, IN_DIM])
    w_out = din("w_out", [D, D])
    w_up = din("w_up", [D, 2 * DFF])
    w_down = din("w_down", [DFF, D])
    g1 = din("g1", [D]); g2 = din("g2", [D]); gf = din("gf", [D])
    rng = din("rng", [1024])
    bfg = din("bfg", [NB * 8])
    convw = din("convw", [128, 2 * NFF * 3])
    convb = din("convb", [128, 2 * NFF])
    cosT = din("cosT", [128, NB * 64]); sinT = din("sinT", [128, NB * 64])
    dkT = din("dkT", [128, NB * 8]); dqT = din("dqT", [128, NB * 8]); kbT = din("kbT", [128, NB * 8])
    wffd = din("wffd", [128, 16 * 8])
    c_ident = din("c_ident", [128, 128]); c_tri = din("c_tri", [128, 128]); c_ones = din("c_ones", [128, 128])
    c_maskR = din("c_maskR", [128, 128]); c_maskT = din("c_maskT", [128, MASKW]); c_sel = din("c_sel", [128, 1024])
    y = nc.dram_tensor("y", [1024, D], F32, kind="ExternalOutput").ap()
    dbg = {}
    if DEBUG:
        dbg["d_aT"] = nc.dram_tensor("d_aT", [128, 16 * LT], BF16, kind="ExternalOutput").ap()
        dbg["d_mix"] = nc.dram_tensor("d_mix", [128, 16 * NOWN], BF16, kind="ExternalOutput").ap()
        dbg["d_h1"] = nc.dram_tensor("d_h1", [128, 8 * D], F32, kind="ExternalOutput").ap()
        dbg["d_cum"] = nc.dram_tensor("d_cum", [128, NB * 8], F32, kind="ExternalOutput").ap()
        dbg["d_cT"] = nc.dram_tensor("d_cT", [128, 16 * NOWN], BF16, kind="ExternalOutput").ap()
        dbg["d_hh"] = nc.dram_tensor("d_hh", [2, D], F32, kind="ExternalOutput").ap()

    w_in_v = w_in.rearrange("(c p) n -> p c n", p=128)
    w_out_v = w_out.rearrange("(c p) n -> p c n", p=128)
    w_up_v = w_up.rearrange("(c p) n -> p c n", p=128)

    with contextlib.ExitStack() as es:
        def sb(name, shape, dt):
            return es.enter_context(nc.sbuf_tensor(name, list(shape), dt))

        def ps(name, shape, dt):
            return es.enter_context(nc.psum_tensor(name, list(shape), dt))

        R1 = sb("R1", [128, 16 * LT], BF16)
        aT = R1[:, :].rearrange("p (c t) -> p c t", c=16)
        R1f = R1.bitcast(F32)
        h2 = R1f[:, 0:8 * D].rearrange("p (b f) -> p b f", b=8)
        mixT = sb("mixT", [128, 16, NOWN], BF16)
        ringT = sb("ringT", [128, NSLOT * 2048], BF16)
        ring = [ringT[:, i * 2048:(i + 1) * 2048] for i in range(NSLOT)]
        R3N = 20992
        R3 = sb("R3", [128, R3N], BF16)
        R3f = R3.bitcast(F32)
        gb = sb("gb", [128, D], F32)
        ident_b = sb("ident_b", [128, 128], BF16)
        ones_b = sb("ones_b", [128, 128], BF16)
        maskT = sb("maskT", [128, MASKW], BF16)
        sel = sb("sel", [128, 1024], BF16)
        ident_f = sb("ident_f", [128, 128], F32)
        tri_f = sb("tri_f", [128, 128], F32)
        ones_f = sb("ones_f", [128, 128], F32)
        maskR = sb("maskR", [128, 128], F32)
        convw_s = sb("convw_s", [128, 2 * NFF * 3], F32)
        convb_s = sb("convb_s", [128, 2 * NFF], F32)
        bfg_s = sb("bfg_s", [128, NB * 8], F32)
        kb_s = sb("kb_s", [128, NB, 8], F32)
        cos_s = sb("cos_s", [128, NB, 64], F32)
        sin_s = sb("sin_s", [128, NB, 64], F32)
        dk_s = sb("dk_s", [128, NB, 8], F32)
        dq_s = sb("dq_s", [128, NB, 8], F32)
        ndk_s = sb("ndk_s", [128, NB, 8], F32)
        ndq_s = sb("ndq_s", [128, NB, 8], F32)
        spt = sb("spt", [128, NB * 8], F32)
        cumn = sb("cumn", [128, NB * 8], F32)
        tot = sb("tot", [128, NB * 8], F32)
        pre = sb("pre", [128, NB * 8], F32)
        biasK = sb("biasK", [128, NB * 8], F32)
        Rb = sb("Rb", [128, 9 * 128], BF16)
        wff = sb("wff", [128, 16, 8], BF16)
        st1 = sb("st1", [128, 32], F32)
        eps_t = sb("eps_t", [128, 1], F32)
        st2 = sb("st2", [128, 64], F32)

        pp = [ps("pp%d" % i, [128, 1024], F32) for i in range(4)]
        ppb = [p.bitcast(BF16) for p in pp]

        def bank(i):
            return pp[i // 2][:, (i % 2) * 512:(i % 2) * 512 + 512]

        P = Prog(nc)
        slot_ctr = [0]

        def next_slot():
            s = slot_ctr[0] % NSLOT
            slot_ctr[0] += 1
            return s

        def dma_sp(out, in_, reads=(), writes=(), stream="const"):
            P.op("sp", lambda e: e.dma_start(out=out, in_=in_), reads=reads, writes=writes, dma=stream)

        def dma_cast(out, in_, reads=(), writes=(), stream="constp", batch=None):
            P.op("pool", lambda e: e.dma_start(out=out, in_=in_), reads=reads, writes=writes, dma=stream, batch=batch)

        def load_slice(wv, c0, ncols=128):
            s = next_slot()
            v = ring[s][:, 0:16 * ncols].rearrange("p (c n) -> p c n", c=16)
            for hh in range(2):
                dma_cast(v[:, 8 * hh:8 * hh + 8, :], wv[:, 8 * hh:8 * hh + 8, c0:c0 + ncols], writes=[("ring", s, hh)], stream="ring%d" % s,
                         batch=slot_ctr[0])
            return s, v

        def rkeys(s):
            return [("ring", s, 0), ("ring", s, 1)]

        dma_sp(gb[:], g1.partition_broadcast(128), writes=["gb"], stream="gb")
        P.op("dve", lambda e: e.memset(eps_t[:], EPS), writes=["eps_t"])
        P.op("dve", lambda e: e.memset(pre[:], 0.0), writes=["pre"])
        dma_cast(ident_b[:], c_ident, writes=["ident_b"])
        dma_cast(wff[:, :, :].rearrange("p c n -> p (c n)"), wffd, writes=[("wff", 0), ("wff", 1)])
        dma_cast(ones_b[:], c_ones, writes=["ones_b"])
        dma_cast(maskT[:], c_maskT, writes=["maskT"])
        dma_cast(sel[:], c_sel, writes=["sel"])

        def late_constants():
            dma_sp(cos_s[:, :, :].rearrange("p b d -> p (b d)"), cosT, writes=["cos"])
            dma_sp(sin_s[:, :, :].rearrange("p b d -> p (b d)"), sinT, writes=["sin"])
            dma_sp(dk_s[:, :, :].rearrange("p b h -> p (b h)"), dkT, writes=["dk"])
            dma_sp(dq_s[:, :, :].rearrange("p b h -> p (b h)"), dqT, writes=["dq"])
            P.op("dve", lambda e: e.tensor_scalar_mul(out=ndk_s[:, :, :], in0=dk_s[:, :, :], scalar1=-1.0), reads=["dk"], writes=["ndk"])
            P.op("dve", lambda e: e.tensor_scalar_mul(out=ndq_s[:, :, :], in0=dq_s[:, :, :], scalar1=-1.0), reads=["dq"], writes=["ndq"])
            dma_sp(ident_f[:], c_ident, writes=["ident_f"])
            dma_sp(tri_f[:], c_tri, writes=["tri_f"])
            dma_sp(ones_f[:], c_ones, writes=["ones_f"])
            dma_sp(maskR[:], c_maskR, writes=["maskR"])
            dma_sp(bfg_s[:], bfg.partition_broadcast(128), writes=["bfg"])
            dma_sp(kb_s[:, :, :].rearrange("p b h -> p (b h)"), kbT, writes=["kb"])
            dma_sp(convw_s[:], convw, writes=["convw"])
            dma_sp(convb_s[:], convb, writes=["convb"])

        def rms_rstd(src_ap, junk_ap, col, rkeys_, jkey, extra_writes=()):
            npart = src_ap.shape[0]
            c = st1[0:npart, col:col + 1]
            P.op("act", lambda e: e.activation(out=junk_ap, in_=src_ap, func=AF.Square, accum_out=c),
                 reads=rkeys_, writes=[jkey, ("st1", col)] + list(extra_writes))
            P.op("act", lambda e: e.activation(out=c, in_=c, func=AF.Ln, bias=eps_t[0:npart, 0:1], scale=1.0 / D),
                 reads=[("st1", col), "eps_t"], writes=[("st1", col)])
            P.op("act", lambda e: e.activation(out=c, in_=c, func=AF.Exp, scale=-0.5), reads=[("st1", col)], writes=[("st1", col)])
            return c

        xsA = [R3f[:, 0:2048], R3f[:, 2048:4096], R3f[:, 4096:6144]]
        xnA = [R3[:, 12288:14336], R3[:, 14336:16384]]
        junkA = R3[:, 16384:18432]
        b6 = bank(6)

        def aTk(tb):
            return [("aT", tb, 0), ("aT", tb, 1)]

        def A1(tb):
            xs = xsA[tb % 3]
            dma_sp(xs, xl[tb * 128:(tb + 1) * 128, :], writes=[("xs", tb % 3)], stream="xs%d" % (tb % 3))
            rms_rstd(xs, junkA, tb % 3, [("xs", tb % 3)], "junkA")

        def A2(tb):
            xs = xsA[tb % 3]
            c = st1[:, tb % 3:tb % 3 + 1]
            xn = xnA[tb % 2]
            P.op("dve", lambda e: e.scalar_tensor_tensor(out=xn, in0=xs, scalar=c, in1=gb[:], op0=ALU.mult, op1=ALU.mult),
                 reads=[("xs", tb % 3), ("st1", tb % 3), "gb"], writes=[("xnA", tb % 2)])

        def A3(tb):
            xn = xnA[tb % 2]
            pv = ppb[tb % 2]
            for cc in range(16):
                P.op("pe", lambda e, cc=cc: e.transpose(out=pv[:, cc * 128:(cc + 1) * 128], in_=xn[:, cc * 128:(cc + 1) * 128], identity=ident_b[:]),
                     reads=[("xnA", tb % 2), "ident_b"], writes=[("ps", 2 * (tb % 2)), ("ps", 2 * (tb % 2) + 1)])
            pv3 = pv[:, 0:2048].rearrange("p (c t) -> p c t", c=16)
            P.op("act", lambda e: e.copy(out=aT[:, 0:8, tb * 128:(tb + 1) * 128], in_=pv3[:, 0:8, :]),
                 reads=[("ps", 2 * (tb % 2))], writes=[("aT", tb, 0)])
            P.op("dve", lambda e: e.tensor_copy(out=aT[:, 8:16, tb * 128:(tb + 1) * 128], in_=pv3[:, 8:16, :]),
                 reads=[("ps", 2 * (tb % 2) + 1)], writes=[("aT", tb, 1)])

        def A4(tb):
            for kc in range(16):
                P.op("pe", lambda e, kc=kc: e.matmul(b6[:, tb * 8:tb * 8 + 8], lhsT=aT[:, kc, tb * 128:(tb + 1) * 128], rhs=wff[:, kc, :],
                                                    start=(kc == 0), stop=(kc == 15)),
                     reads=aTk(tb) + [("wff", 0), ("wff", 1)], writes=[("ps", 6)])

        for i in range(NB + 3):
            if i == 3:
                late_constants()
            if i < NB:
                A1(i)
            if 0 <= i - 1 < NB:
                A2(i - 1)
            if 0 <= i - 2 < NB:
                A3(i - 2)
            if 0 <= i - 3 < NB:
                A4(i - 3)


        def aT_range_keys(l0, l1):
            ks = []
            for tb in range(l0 // 128, (l1 - 1) // 128 + 1):
                ks += aTk(tb)
            return ks

        if DEBUG:
            dma_sp(dbg["d_aT"], R1[:, :], reads=aT_range_keys(0, LT), stream="st")

        if STOP == "A":
            P.emit(final_wait_streams="st")
            return nc
        P.barrier()
        dma_sp(gb[:, 0:1024], rng.partition_broadcast(128), writes=["gb"], stream="gb")

        b7 = bank(7)
        P.op("dve", lambda e: e.tensor_tensor(out=spt[:], in0=b6[:, 0:NB * 8], in1=bfg_s[:], op=ALU.add), reads=[("ps", 6), "bfg"], writes=["spt"])
        P.op("act", lambda e: e.activation(out=spt[:], in_=spt[:], func=AF.Exp, scale=-1.0), reads=["spt"], writes=["spt"])
        P.op("act", lambda e: e.activation(out=spt[:], in_=spt[:], func=AF.Ln, bias=1.0, scale=1.0), reads=["spt"], writes=["spt"])
        P.op("pe", lambda e: e.matmul(b7[:, 0:136], lhsT=tri_f[:], rhs=spt[:], start=True, stop=True), reads=["spt", "tri_f"], writes=[("ps", 7)])
        P.op("pe", lambda e: e.matmul(b7[:, 136:272], lhsT=ones_f[:], rhs=spt[:], start=True, stop=True), reads=["spt", "ones_f"], writes=[("ps", 7)])
        P.op("dve", lambda e: e.tensor_copy(out=cumn[:], in_=b7[:, 0:136]), reads=[("ps", 7)], writes=["cumn"])
        P.op("dve", lambda e: e.tensor_copy(out=tot[:], in_=b7[:, 136:272]), reads=[("ps", 7)], writes=["tot"])
        for b in range(1, NB):
            P.op("dve", lambda e, b=b: e.tensor_tensor(out=pre[:, b * 8:b * 8 + 8], in0=pre[:, (b - 1) * 8:b * 8], in1=tot[:, (b - 1) * 8:b * 8], op=ALU.add),
                 reads=["pre", "tot"], writes=["pre"])
        P.op("dve", lambda e: e.tensor_tensor(out=cumn[:], in0=cumn[:], in1=pre[:], op=ALU.add), reads=["cumn", "pre"], writes=["cumn"])
        P.op("dve", lambda e: e.tensor_tensor(out=biasK[:], in0=cumn[:], in1=kb_s[:, :, :].rearrange("p b h -> p (b h)"), op=ALU.add),
             reads=["cumn", "kb"], writes=["biasK"])
        for c in range(9):
            tb = 8 + c
            dst = pp[2][0:8, c * 128:(c + 1) * 128] if c < 8 else pp[3][0:8, 512:640]
            P.op("pe", lambda e, dst=dst, tb=tb: e.transpose(out=dst, in_=cumn[:, tb * 8:tb * 8 + 8], identity=ident_f[:]),
                 reads=["cumn", "ident_f"], writes=[("ps", 4), ("ps", 5)] if c < 8 else [("ps", 7)])
        P.op("dve", lambda e: e.memset(Rb[:], 0.0), writes=["Rb"])
        P.op("act", lambda e: e.activation(out=Rb[0:8, 0:1024], in_=pp[2][0:8, 0:1024], func=AF.Copy, scale=-1.0 / SCALE),
             reads=[("ps", 4), ("ps", 5)], writes=["Rb"])
        P.op("act", lambda e: e.activation(out=Rb[0:8, 1024:1152], in_=pp[3][0:8, 512:640], func=AF.Copy, scale=-1.0 / SCALE),
             reads=[("ps", 7)], writes=["Rb"])
        if DEBUG:
            dma_sp(dbg["d_cum"], cumn[:], reads=["cumn"], stream="st")
        if STOP == "B0":
            P.emit(final_wait_streams="st")
            return nc

        o_ = 0
        def carve(n):
            nonlocal o_
            a = o_
            o_ += n
            return a
        fkT = R3[:, carve(LT):o_]
        _a = carve(1028)
        fqT = R3[:, _a:_a + NOWN]
        rvfv = R3[:, carve(NB * 256):o_].rearrange("p (b n) -> p b n", b=NB)
        rkt = R3[:, carve(NB * 128):o_].rearrange("p (b n) -> p b n", b=NB)
        rqT = R3[:, carve(1152):o_]
        rkT = R3[:, carve(1152):o_]
        sg = R3[:, carve(1152):o_].rearrange("p (b n) -> p b n", b=9)
        rqt = R3[:, carve(1152):o_].rearrange("p (b n) -> p b n", b=9)
        PTt = [R3[:, carve(342):o_] for _ in range(3)]
        smt = [R3[:, carve(128):o_] for _ in range(2)]
        Sbf = [R3[:, carve(128):o_] for _ in range(2)]
        junkB = R3[:, carve(128):o_]
        assert o_ % 2 == 0
        fo = o_ // 2
        def carvef(n):
            nonlocal fo
            a = fo
            fo += n
            return a
        o_all = R3f[:, carvef(1152):fo].rearrange("p (b n) -> p b n", b=9)
        rden = R3f[:, carvef(342):fo]
        rU = R3f[:, carvef(128):fo]
        rW = R3f[:, carvef(128):fo]
        ro = R3[:, fo * 2:fo * 2 + 1152].rearrange("p (b n) -> p b n", b=9)
        assert fo * 2 + 1152 <= R3N, fo * 2

        def rotary(src, dst, tbl, dec_ap, ndec_ap, rk, wk):
            Cb = cos_s[:, tbl:tbl + 1, :].to_broadcast([128, 2, 64])
            S = sin_s[:, tbl, :]
            src3 = src[:, 0:128].rearrange("p (a b) -> p a b", a=2)
            U3 = rU.rearrange("p (a b) -> p a b", a=2)
            t1 = src[:, 0:64]
            t2 = src[:, 64:128]
            rd = list(rk) + ["cos", "sin", "dk", "dq", "ndk", "ndq"]
            P.op("dve", lambda e: e.scalar_tensor_tensor(out=U3, in0=src3, scalar=dec_ap, in1=Cb, op0=ALU.mult, op1=ALU.mult), reads=rd, writes=["rU"])
            P.op("dve", lambda e: e.scalar_tensor_tensor(out=rW[:, 0:64], in0=t2, scalar=ndec_ap, in1=S, op0=ALU.mult, op1=ALU.mult), reads=rd, writes=["rW"])
            P.op("dve", lambda e: e.scalar_tensor_tensor(out=rW[:, 64:128], in0=t1, scalar=dec_ap, in1=S, op0=ALU.mult, op1=ALU.mult), reads=rd + ["rW"], writes=["rW"])
            P.op("dve", lambda e: e.tensor_tensor(out=dst, in0=rU, in1=rW, op=ALU.add), reads=["rU", "rW"], writes=wk)

        vA = ringT[:, 0:6144].rearrange("p (c n) -> p c n", c=16)
        vB = ringT[:, 6144:10240].rearrange("p (c n) -> p c n", c=16)
        vC = ringT[:, 10240:12288].rearrange("p (c n) -> p c n", c=16)
        vD = ringT[:, 12288:14336].rearrange("p (c n) -> p c n", c=16)
        regions = {"A": (vA, 3), "B": (vB, 2), "C": (vC, 1), "D": (vD, 1)}

        def rg_keys(name):
            return [("rg" + name, si, hh) for si in range(regions[name][1]) for hh in range(2)]

        def load_region(name, col_offs, h):
            v, _ = regions[name]
            for si, c0 in enumerate(col_offs):
                for hh in range(2):
                    dma_cast(v[:, 8 * hh:8 * hh + 8, si * 128:(si + 1) * 128], w_in_v[:, 8 * hh:8 * hh + 8, c0:c0 + 128],
                             writes=[("rg" + name, si, hh)], stream="rg" + name, batch=h)

        all_rg = [k for nm in ("A", "B", "C", "D") for k in rg_keys(nm)]

        def load_wout_quarter(qp, extra=()):
            sl = []
            for m in range(4):
                s_ = next_slot()
                v_ = ring[s_].rearrange("p (c n) -> p c n", c=4)
                dma_cast(v_, w_out_v[:, 4 * m:4 * m + 4, qp * 512:(qp + 1) * 512], writes=rkeys(s_) + list(extra), stream="ring%d" % s_)
                sl.append((s_, v_))
            return sl
        wq = {}
        deferred_tail = [None]
        pending_tail_ops = []

        def load_head(h):
            load_region("A", [1024 + h * 128, 2048 + h * 128, 6144 + h * 128], h)
            load_region("B", [h * 128, 3072 + h * 128], h)
            load_region("C", [4096 + h * 128], h)
            load_region("D", [5120 + h * 128], h)
        load_head(0)
        for h in range(NHEADS_RUN):
            for tb in range(NB):
                bA = 2 * (tb % 2)
                bB = bA + 1
                for kc in range(16):
                    P.op("pe", lambda e, tb=tb, kc=kc, bA=bA: e.matmul(
                        bank(bA)[:, 0:384], lhsT=aT[:, kc, tb * 128:(tb + 1) * 128], rhs=vA[:, kc, :],
                        start=(kc == 0), stop=(kc == 15)),
                        reads=aTk(tb) + rg_keys("A"), writes=[("ps", bA)])
                    if tb >= 8:
                        P.op("pe", lambda e, tb=tb, kc=kc, bB=bB: e.matmul(
                            bank(bB)[:, 0:256], lhsT=aT[:, kc, tb * 128:(tb + 1) * 128], rhs=vB[:, kc, :],
                            start=(kc == 0), stop=(kc == 15)),
                            reads=aTk(tb) + rg_keys("B"), writes=[("ps", bB)])
                P.op("act", lambda e, tb=tb, bA=bA: e.copy(out=rvfv[:, tb, :], in_=bank(bA)[:, 128:384]), reads=[("ps", bA)], writes=[("rvfv", tb)])
                if tb >= 8:
                    P.op("act", lambda e, tb=tb, bB=bB: e.copy(out=sg[:, tb - 8, :], in_=bank(bB)[:, 128:256]), reads=[("ps", bB)], writes=["sg"])
                rotary(bank(bA), rkt[:, tb, :], tb, dk_s[:, tb, h:h + 1], ndk_s[:, tb, h:h + 1], [("ps", bA)], [("rkt", tb)])
                for _ in range(4 if tb < 7 else 99):
                    if pending_tail_ops and tb < 8:
                        pending_tail_ops.pop(0)()
                if tb >= 8:
                    rotary(bank(bB), rqt[:, tb - 8, :], tb, dq_s[:, tb, h:h + 1], ndq_s[:, tb, h:h + 1], [("ps", bB)], [("rqt", tb - 8)])
            P.op("act", lambda e: e.activation(out=sg[:, :, :], in_=sg[:, :, :], func=AF.Silu), reads=["sg"], writes=["sg"])
            if deferred_tail[0] is not None:
                deferred_tail[0]()
                deferred_tail[0] = None
            for nb in range(5):
                n0 = nb * 512
                nw = min(512, LT - n0)
                bk = 4 + nb % 2
                for kc in range(16):
                    P.op("pe", lambda e, kc=kc, n0=n0, nw=nw, bk=bk: e.matmul(bank(bk)[:, 0:nw], lhsT=vD[:, kc, :], rhs=aT[:, kc, n0:n0 + nw],
                                                                          start=(kc == 0), stop=(kc == 15)),
                         reads=aT_range_keys(n0, n0 + nw) + rg_keys("D"), writes=[("ps", bk)])
                P.op("dve", lambda e, n0=n0, nw=nw, bk=bk: e.tensor_copy(out=fkT[:, n0:n0 + nw], in_=bank(bk)[:, 0:nw]), reads=[("ps", bk)], writes=[("fkT", nb)])
            for g in range(3):
                n0 = OWN0 + 342 * g
                bk = 4 + (g + 1) % 2
                for kc in range(16):
                    P.op("pe", lambda e, kc=kc, n0=n0, bk=bk: e.matmul(bank(bk)[:, 0:342], lhsT=vC[:, kc, :], rhs=aT[:, kc, n0:n0 + 342],
                                                                    start=(kc == 0), stop=(kc == 15)),
                         reads=aT_range_keys(n0, n0 + 342) + rg_keys("C"), writes=[("ps", bk)])
                P.op("dve", lambda e, g=g, bk=bk: e.tensor_copy(out=fqT[:, 342 * g:342 * g + 342], in_=bank(bk)[:, 0:342]), reads=[("ps", bk)], writes=[("fqT", g)])

            if h + 1 < NHEADS_RUN:
                load_head(h + 1)
            else:
                wq[0] = load_wout_quarter(0, all_rg)
                wq[1] = load_wout_quarter(1, all_rg)
            for c in range(9):
                P.op("pe", lambda e, c=c: e.transpose(out=ppb[2][:, c * 128:(c + 1) * 128], in_=rqt[:, c, :], identity=ident_b[:]),
                     reads=[("rqt", c), "ident_b"], writes=[("ps", 4), ("ps", 5)])
                P.op("pe", lambda e, c=c: e.transpose(out=ppb[3][:, c * 128:(c + 1) * 128], in_=rkt[:, 8 + c, :], identity=ident_b[:]),
                     reads=[("rkt", 8 + c), "ident_b"], writes=[("ps", 6), ("ps", 7)])
            P.op("dve", lambda e: e.tensor_copy(out=rqT, in_=ppb[2][:, 0:1152]), reads=[("ps", 4), ("ps", 5)], writes=["rqT"])
            P.op("dve", lambda e: e.tensor_copy(out=rkT, in_=ppb[3][:, 0:1152]), reads=[("ps", 6), ("ps", 7)], writes=["rkT"])
            Sps = bank(7)[:, 0:128]
            ret_steps = []

            def r_init():
                for b in range(8):
                    P.op("pe", lambda e, b=b: e.matmul(Sps, lhsT=rkt[:, b, :], rhs=rvfv[:, b, 0:128], start=(b == 0), stop=(b == 7), skip_group_check=True),
                         reads=[("rkt", b), ("rvfv", b)], writes=[("ps", 7)])
            ret_steps.append(r_init)

            def r_a(c):
                sTp = bank(6)[:, (c % 2) * 128:(c % 2) * 128 + 128]
                if c > 0:
                    P.op("pe", lambda e: e.matmul(Sps, lhsT=rkt[:, 7 + c, :], rhs=rvfv[:, 7 + c, 0:128], start=False, stop=True, skip_group_check=True),
                         reads=[("rkt", 7 + c), ("rvfv", 7 + c)], writes=[("ps", 7)])
                P.op("dve", lambda e: e.tensor_copy(out=Sbf[c % 2], in_=Sps), reads=[("ps", 7)], writes=[("Sbf", c % 2)])
                P.op("pe", lambda e: e.matmul(sTp, lhsT=rkT[:, c * 128:(c + 1) * 128], rhs=rqT[:, c * 128:(c + 1) * 128], start=True, stop=True),
                     reads=["rkT", "rqT"], writes=[("ps", 6)])
                P.op("dve", lambda e: e.tensor_tensor(out=smt[c % 2], in0=sTp, in1=maskR[:], op=ALU.mult),
                     reads=[("ps", 6), "maskR"], writes=[("smt", c % 2)])

            def r_b(c):
                op_ = bank(4)[:, (c % 2) * 128:(c % 2) * 128 + 128]
                P.op("pe", lambda e: e.matmul(op_, lhsT=rqT[:, c * 128:(c + 1) * 128], rhs=Sbf[c % 2], start=True, stop=False),
                     reads=["rqT", ("Sbf", c % 2)], writes=[("ps", 4)])
                P.op("pe", lambda e: e.matmul(op_, lhsT=smt[c % 2], rhs=rvfv[:, 8 + c, 0:128], start=False, stop=True),
                     reads=[("smt", c % 2), ("rvfv", 8 + c)], writes=[("ps", 4)])
                P.op("dve", lambda e: e.tensor_copy(out=o_all[:, c, :], in_=op_), reads=[("ps", 4)], writes=[("o_all", c)])
                P.op("act", lambda e: e.activation(out=junkB, in_=op_, func=AF.Square, accum_out=st2[:, 16 + c:17 + c]),
                     reads=[("ps", 4)], writes=["junkB", ("st2q", c)])
            for c in range(9):
                ret_steps.append(lambda c=c: r_a(c))
                ret_steps.append(lambda c=c: r_b(c))

            def r_tail(h=h):
                oall_keys = [("o_all", c) for c in range(9)]
                sq_keys = [("st2q", c) for c in range(9)]
                mean = st2[:, 0:9]
                ssq = st2[:, 16:25]
                msq = st2[:, 32:41]
                rstd = st2[:, 48:57]
                P.op("dve", lambda e: e.reduce_sum(out=mean, in_=o_all[:, :, :], axis=AX.X), reads=oall_keys, writes=["st2m"])
                P.op("dve", lambda e: e.tensor_scalar_mul(out=mean, in0=mean, scalar1=1.0 / 128), reads=["st2m"], writes=["st2m"])
                P.op("dve", lambda e: e.tensor_tensor(out=msq, in0=mean, in1=mean, op=ALU.mult), reads=["st2m"], writes=["st2s"])
                P.op("dve", lambda e: e.tensor_scalar(out=rstd, in0=ssq, scalar1=1.0 / 128, scalar2=EPS, op0=ALU.mult, op1=ALU.add), reads=sq_keys, writes=["st2r"])
                P.op("dve", lambda e: e.tensor_tensor(out=rstd, in0=rstd, in1=msq, op=ALU.subtract), reads=["st2r", "st2s"], writes=["st2r"])
                P.op("act", lambda e: e.activation(out=rstd, in_=rstd, func=AF.Ln), reads=["st2r"], writes=["st2r"])
                P.op("act", lambda e: e.activation(out=rstd, in_=rstd, func=AF.Exp, scale=-0.5), reads=["st2r"], writes=["st2r"])
                ops_ = []
                for c in range(9):
                    ops_.append(lambda c=c: P.op("dve", lambda e: e.tensor_scalar(out=o_all[:, c, :], in0=o_all[:, c, :], scalar1=st2[:, c:c + 1], scalar2=st2[:, 48 + c:49 + c],
                                                                              op0=ALU.subtract, op1=ALU.mult),
                                                 reads=[("o_all", c), "st2m", "st2r"], writes=[("o_all", c)]))
                    ops_.append(lambda c=c: P.op("dve", lambda e: e.tensor_tensor(out=o_all[:, c, :], in0=o_all[:, c, :], in1=gb[:, h * 128:(h + 1) * 128], op=ALU.mult),
                                                 reads=[("o_all", c), "gb"], writes=[("o_all", c)]))
                    ops_.append(lambda c=c: P.op("dve", lambda e: e.tensor_tensor(out=ro[:, c, :], in0=o_all[:, c, :], in1=sg[:, c, :], op=ALU.mult),
                                                 reads=[("o_all", c), "sg"], writes=[("ro", c)]))
                return ops_

            def r_tail_pe(h=h):
                for c in range(9):
                    P.op("pe", lambda e, c=c: e.transpose(out=ppb[3][:, c * 128:(c + 1) * 128], in_=ro[:, c, :], identity=ident_b[:]),
                         reads=[("ro", c), "ident_b"], writes=[("ps", 6), ("ps", 7)])
                P.op("act", lambda e: e.copy(out=mixT[:, h, :], in_=ppb[3][:, 126:1152]), reads=[("ps", 6), ("ps", 7)], writes=[("mixh", h)])

            tiles = []
            for g in range(3):
                q0 = OWN0 + 342 * g
                kmax = (q0 + 342 - 1) // 128
                for kb in range(kmax + 1):
                    tiles.append((g, kb, kmax, q0))
            oTp = bank(2)[:, 0:342]
            dnp = bank(3)[:, 0:342]

            def f_s(ti, h=h):
                g, kb, kmax, q0 = tiles[ti]
                sbk = (0, 1, 5)[ti % 3]
                sb_ = bank(sbk)[:, 0:342]
                delta = 128 * kb - q0
                need_mask = (128 * kb + 127) > q0
                P.op("pe", lambda e: e.matmul(sb_, lhsT=fkT[:, kb * 128:(kb + 1) * 128], rhs=fqT[:, 342 * g:342 * g + 342], start=True, stop=False),
                     reads=[("fkT", kb // 4), ("fqT", g)], writes=[("ps", sbk)])
                P.op("pe", lambda e: e.matmul(sb_, lhsT=sel[:, h * 128:(h + 1) * 128], rhs=Rb[:, 126 + 342 * g:126 + 342 * g + 342],
                                              start=False, stop=(not need_mask)),
                     reads=["sel", "Rb"], writes=[("ps", sbk)])
                if need_mask:
                    off = XOFF - delta
                    assert 0 <= off and off + 342 <= MASKW, off
                    P.op("pe", lambda e: e.matmul(sb_, lhsT=ident_b[:], rhs=maskT[:, off:off + 342], start=False, stop=True),
                         reads=["ident_b", "maskT"], writes=[("ps", sbk)])
                pt = PTt[ti % 3]
                P.op("act", lambda e: e.activation(out=pt, in_=sb_, func=AF.Exp, bias=biasK[:, kb * 8 + h:kb * 8 + h + 1], scale=SCALE),
                     reads=[("ps", sbk), "biasK"], writes=[("PT", ti % 3)])

            def f_pv(ti, h=h):
                g, kb, kmax, q0 = tiles[ti]
                pt = PTt[ti % 3]
                ptk = ("PT", ti % 3)
                P.op("pe", lambda e: e.matmul(oTp, lhsT=rvfv[:, kb, 128:256], rhs=pt, start=(kb == 0), stop=(kb == kmax)),
                     reads=[("rvfv", kb), ptk], writes=[("ps", 2)])
                P.op("pe", lambda e: e.matmul(dnp, lhsT=ones_b[:], rhs=pt, start=(kb == 0), stop=(kb == kmax)),
                     reads=["ones_b", ptk], writes=[("ps", 3)])
                if kb == kmax:
                    P.op("dve", lambda e: e.reciprocal(out=rden, in_=dnp), reads=[("ps", 3)], writes=["rden"])
                    P.op("dve", lambda e: e.tensor_tensor(out=mixT[:, 8 + h, 342 * g:342 * g + 342], in0=oTp, in1=rden, op=ALU.mult),
                         reads=[("ps", 2), "rden"], writes=[("mixf", h, g)])

            fox_steps = []
            nt_ = len(tiles)
            def f_first():
                f_s(0)
                f_s(1)
            fox_steps.append(f_first)
            for ti in range(nt_):
                def st(ti=ti):
                    if ti + 2 < nt_:
                        f_s(ti + 2)
                    f_pv(ti)
                fox_steps.append(st)
            fi = 0
            last_head = (h == NHEADS_RUN - 1)
            per = [1, 1] if last_head else [3, 2]
            for ri, rs in enumerate(ret_steps):
                rs()
                k = 1 if ri == 0 else per[ri % 2]
                for _ in range(k):
                    if fi < len(fox_steps):
                        fox_steps[fi]()
                        fi += 1
            if last_head:
                for f_ in r_tail():
                    f_()
            while fi < len(fox_steps):
                fox_steps[fi]()
                fi += 1
            if not last_head:
                pending_tail_ops[:] = r_tail()
            deferred_tail[0] = r_tail_pe
        deferred_tail[0]()


        mix_all = [("mixh", h) for h in range(NH)] + [("mixf", h, g) for h in range(NH) for g in range(3)]
        if DEBUG:
            dma_sp(dbg["d_mix"], mixT[:, :, :].rearrange("p c t -> p (c t)"), reads=mix_all, stream="st")
        if STOP == "B":
            P.emit(final_wait_streams="st")
            return nc

        P.barrier()

        dma_sp(gb[:], g2.partition_broadcast(128), writes=["gb"], stream="gb")
        xsC = [R3f[:, 0:512], R3f[:, 512:1024], R3f[:, 1024:1536]]
        hnC = [R3[:, 4096:6144], R3[:, 6144:8192]]
        junkC = R3[:, 8192:10240]
        hh = R3f[:, 6144:8192]
        blocks = [(-1, 0, OWN0)] + [(tb, 2 + 128 * tb, 1152 + 128 * tb) for tb in range(8)]
        xi = 0
        ubk = 0

        def c_norm1(bi, tb):
            src = hh[:, :] if tb < 0 else h2[:, tb, :]
            col = 4 + bi % 2
            hk = [("h1", bi, q) for q in range(4)]
            c = rms_rstd(src, junkC, col, hk, "junkC")
            hn = hnC[bi % 2]
            P.op("dve", lambda e: e.scalar_tensor_tensor(out=hn, in0=src, scalar=c, in1=gb[:], op0=ALU.mult, op1=ALU.mult),
                 reads=hk + [("st1", col), "gb"], writes=[("hnC", bi % 2)])

        def c_norm2(bi, tb, c0):
            hn = hnC[bi % 2]
            pv = ppb[2 + bi % 2]
            pk = [("ps", 4 + 2 * (bi % 2)), ("ps", 5 + 2 * (bi % 2))]
            for cc in range(16):
                P.op("pe", lambda e, cc=cc: e.transpose(out=pv[:, cc * 128:(cc + 1) * 128], in_=hn[:, cc * 128:(cc + 1) * 128], identity=ident_b[:]),
                     reads=[("hnC", bi % 2), "ident_b"], writes=pk)
            pv3 = pv[:, 0:2048].rearrange("p (c t) -> p c t", c=16)
            if tb < 0:
                P.op("act", lambda e: e.copy(out=mixT[:, :, 0:2], in_=pv3[:, :, 0:2]), reads=pk + mix_all, writes=[("cT", bi)])
            else:
                P.op("act", lambda e: e.copy(out=mixT[:, 0:8, c0:c0 + 128], in_=pv3[:, 0:8, :]), reads=pk[0:1] + mix_all, writes=[("cT", bi)])
                P.op("dve", lambda e: e.tensor_copy(out=mixT[:, 8:16, c0:c0 + 128], in_=pv3[:, 8:16, :]), reads=pk[1:2] + mix_all, writes=[("cTb", bi)])

        for qp in range(4):
            slots = wq[qp]
            pend = None
            for bi, (tb, c0, l0) in enumerate(blocks):
                xs = xsC[xi % 3]
                xk = ("xs", xi % 3)
                xstream = "xs%d" % (xi % 3)
                xi += 1
                dma_sp(xs, xl[l0:l0 + 128, qp * 512:(qp + 1) * 512], writes=[xk], stream=xstream)
                bk = ubk % 4
                ubk += 1
                ck = [("cT", bi), ("cTb", bi)] + ([("cT", 1), ("cTb", 1)] if tb < 0 else [])
                for kc in range(16):
                    s_, v_ = slots[kc // 4]
                    P.op("pe", lambda e, kc=kc, v_=v_, c0=c0, bk=bk: e.matmul(
                        bank(bk), lhsT=mixT[:, kc, c0:c0 + 128], rhs=v_[:, kc % 4, :], start=(kc == 0), stop=(kc == 15)),
                        reads=mix_all + ck + rkeys(s_), writes=[("ps", bk)])
                dst = hh[:, qp * 512:(qp + 1) * 512] if tb < 0 else h2[:, tb, qp * 512:(qp + 1) * 512]
                P.op("dve", lambda e, dst=dst, bk=bk, xs=xs: e.tensor_tensor(out=dst, in0=bank(bk), in1=xs, op=ALU.add),
                     reads=[("ps", bk), xk], writes=[("h1", bi, qp)])
                if qp == 3:
                    c_norm1(bi, tb)
                    if pend is not None:
                        c_norm2(*pend)
                    pend = (bi, tb, c0)
            if qp == 3:
                c_norm2(*pend)
            if qp + 2 < 4:
                wq[qp + 2] = load_wout_quarter(qp + 2)

        cT = mixT
        cT_all = [("cT", bi) for bi in range(9)] + [("cTb", bi) for bi in range(1, 9)]
        if DEBUG:
            dma_sp(dbg["d_h1"], R1f[:, 0:8 * D], reads=[("h1", bi, hp) for bi in range(1, 9) for hp in range(4)], stream="st")
        if DEBUG:
            dma_sp(dbg["d_cT"], mixT[:, :, :].rearrange("p c t -> p (c t)"), reads=cT_all, stream="st")
            dma_sp(dbg["d_hh"], hh[0:2, :], reads=[("h1", 0, q) for q in range(4)], stream="st")
        if STOP == "C":
            P.emit(final_wait_streams="st")
            return nc
        pre_up = [load_slice(w_up_v, half * DFF) for half in range(2)]
        P.barrier()

        gated = [[R3[:, (gs * GRP + jj) * 1024:(gs * GRP + jj + 1) * 1024] for jj in range(GRP)] for gs in range(2)]
        fb = 2 * GRP * 1024 // 2
        Yg = [R3f[:, fb + i * 1024:fb + (i + 1) * 1024] for i in range(2)]
        Yv = [R3f[:, fb + 2048 + i * 1024:fb + 2048 + (i + 1) * 1024] for i in range(2)]
        sb0 = 2 * (fb + 4096)
        Sg = [R3[:, sb0 + i * 1024:sb0 + (i + 1) * 1024] for i in range(2)]
        assert sb0 + 2048 <= R3N
        nblk = [(0, 342), (342, 683), (683, 1024)]
        ub = [0]

        h2_keys = lambda tb: [("h2", tb, n) for n in range(4)]
        mixflat = mixT[:, :, :].rearrange("p c t -> p (c t)")
        otE = [mixflat[:, 0:4096].bitcast(F32), mixflat[:, 4096:8192].bitcast(F32)]
        junkE = mixflat[:, 8192:10240]

        def final_block(tb):
            col = 8 + tb % 2
            c = rms_rstd(h2[:, tb, :], junkE, col, h2_keys(tb), "junkE", extra_writes=cT_all)
            ot = otE[tb % 2]
            if tb % 2 == 0 or tb == 7:
                P.op("dve", lambda e: e.scalar_tensor_tensor(out=ot, in0=h2[:, tb, :], scalar=c, in1=gb[:], op0=ALU.mult, op1=ALU.mult),
                     reads=h2_keys(tb) + [("st1", col), "gb"], writes=[("ot", tb % 2)] + cT_all)
            else:
                P.op("act", lambda e: e.activation(out=ot, in_=h2[:, tb, :], func=AF.Copy, scale=c),
                     reads=h2_keys(tb) + [("st1", col)], writes=[("ot", tb % 2)] + cT_all)
                P.op("pool", lambda e: e.tensor_tensor(out=ot, in0=ot, in1=gb[:], op=ALU.mult),
                     reads=[("ot", tb % 2), "gb"], writes=[("ot", tb % 2)])
            dma_sp(y[tb * 128:(tb + 1) * 128, :], ot, reads=[("ot", tb % 2)], stream="st%d" % (tb % 2))

        def wdown_group(gi, dslots, last=False):
            gs = gi % 2
            for tb in range(8):
                b0 = 4 if tb % 2 == 0 else 0
                for jj in range(GRP):
                    s, v = dslots[jj]
                    for n in range(4):
                        bk = b0 + n
                        P.op("pe", lambda e, jj=jj, v=v, tb=tb, n=n, bk=bk, gs=gs: e.matmul(
                            bank(bk), lhsT=gated[gs][jj][:, tb * 128:(tb + 1) * 128], rhs=v[:, n * 512:(n + 1) * 512],
                            start=(jj == 0), stop=(jj == GRP - 1)),
                            reads=[("gated", gs, jj)] + rkeys(s), writes=[("ps", bk)])
                for n in range(4):
                    bk = b0 + n
                    P.op("dve", lambda e, tb=tb, n=n, bk=bk: e.tensor_tensor(out=h2[:, tb, n * 512:(n + 1) * 512], in0=h2[:, tb, n * 512:(n + 1) * 512], in1=bank(bk), op=ALU.add),
                         reads=[("ps", bk), ("h2", tb, n)], writes=[("h2", tb, n)])
                if last:
                    final_block(tb)

        def load_down(gi):
            dslots = []
            for jj in range(GRP):
                j = gi * GRP + jj
                s = next_slot()
                dma_cast(ring[s][:, :], w_down[j * 128:(j + 1) * 128, :], writes=rkeys(s), stream="ring%d" % s)
                dslots.append((s, ring[s]))
            return dslots

        prev = None
        pair_i = 0
        for gi in range(NFF // GRP):
            gs = gi % 2
            for jj in range(GRP):
                j = gi * GRP + jj
                pi = pair_i % 2
                pair_i += 1
                for half in range(2):
                    cidx = half * NFF + j
                    s, v = pre_up[half] if j == 0 else load_slice(w_up_v, half * DFF + j * 128)
                    Y = (Yg if half == 0 else Yv)[pi]
                    yk = ("Y", half, pi)
                    for (r0, r1) in nblk:
                        ln = r1 - r0
                        bk = ub[0] % 4
                        ub[0] += 1
                        for kc in range(16):
                            P.op("pe", lambda e, kc=kc, v=v, r0=r0, ln=ln, bk=bk: e.matmul(bank(bk)[:, 0:ln + 2], lhsT=v[:, kc, :], rhs=cT[:, kc, r0:r0 + ln + 2],
                                                                                    start=(kc == 0), stop=(kc == 15)),
                                 reads=cT_all + rkeys(s), writes=[("ps", bk)])
                        u = bank(bk)
                        P.op("act", lambda e, u=u, Y=Y, r0=r0, r1=r1, ln=ln, cidx=cidx: e.activation(
                            out=Y[:, r0:r1], in_=u[:, 2:ln + 2], func=AF.Identity, bias=convb_s[:, cidx:cidx + 1], scale=convw_s[:, cidx * 3 + 2:cidx * 3 + 3]),
                            reads=[("ps", bk), "convw", "convb"], writes=[yk])
                        P.op("dve", lambda e, u=u, Y=Y, r0=r0, r1=r1, ln=ln, cidx=cidx: e.scalar_tensor_tensor(
                            out=Y[:, r0:r1], in0=u[:, 1:ln + 1], scalar=convw_s[:, cidx * 3 + 1:cidx * 3 + 2], in1=Y[:, r0:r1], op0=ALU.mult, op1=ALU.add),
                            reads=[("ps", bk), "convw", yk], writes=[yk])
                        P.op("dve", lambda e, u=u, Y=Y, r0=r0, r1=r1, ln=ln, cidx=cidx: e.scalar_tensor_tensor(
                            out=Y[:, r0:r1], in0=u[:, 0:ln], scalar=convw_s[:, cidx * 3:cidx * 3 + 1], in1=Y[:, r0:r1], op0=ALU.mult, op1=ALU.add),
                            reads=[("ps", bk), "convw", yk], writes=[yk])
                    if half == 0:
                        P.op("act", lambda e, Y=Y, pi=pi: e.activation(out=Sg[pi], in_=Y, func=AF.Silu), reads=[yk], writes=[("Sg", pi)])
                    else:
                        P.op("dve", lambda e, Y=Y, pi=pi, gs=gs, jj=jj: e.tensor_tensor(out=gated[gs][jj], in0=Y, in1=Sg[pi], op=ALU.mult),
                             reads=[yk, ("Sg", pi)], writes=[("gated", gs, jj)])
            if prev is not None:
                wdown_group(prev, load_down(prev))
            prev = gi
        dma_sp(gb[:], gf.partition_broadcast(128), writes=["gb"], stream="gb")
        wdown_group(prev, load_down(prev), last=True)

        P.emit(final_wait_streams="st")
    return nc


_NC_CACHE = {}


def _consts():
    c = {}
    c["c_ident"] = np.eye(128, dtype=np.float32)
    c["c_tri"] = np.triu(np.ones((128, 128), np.float32))
    c["c_ones"] = np.ones((128, 128), np.float32)
    c["c_maskR"] = np.triu(np.ones((128, 128), np.float32))
    p = np.arange(128)[:, None]
    xx = np.arange(MASKW)[None, :]
    c["c_maskT"] = np.where(xx - XOFF < p, NEG, 0.0).astype(np.float32)
    sel = np.zeros((128, 8, 128), np.float32)
    for h in range(8):
        sel[h, h, :] = 1.0
    c["c_sel"] = sel.reshape(128, 1024)
    return c


def _core_tables(T0):
    l = np.arange(LT)
    t = l - 1152 + T0
    valid = t >= 0
    inv_freq = 1.0 / (10000.0 ** (np.arange(0, 128, 2, dtype=np.float64) / 128.0))
    ang = np.where(valid, t, 0)[:, None].astype(np.float64) * inv_freq[None, :]
    def pm(a):
        n = a.shape[1]
        return np.ascontiguousarray(a.reshape(NB, 128, n).transpose(1, 0, 2).reshape(128, NB * n))
    tabs = {"cosT": pm(np.cos(ang).astype(np.float32)), "sinT": pm(np.sin(ang).astype(np.float32))}
    log_g = np.log1p(-np.exp2(-5.0 - np.arange(8, dtype=np.float64)))
    rel = (l - 1152).astype(np.float64)
    tabs["dqT"] = pm(np.exp(rel[:, None] * log_g[None, :]).astype(np.float32))
    tabs["dkT"] = pm((np.exp(-rel[:, None] * log_g[None, :]) * SCALE).astype(np.float32))
    tabs["kbT"] = pm(np.repeat(np.where(valid, 0.0, NEG).astype(np.float32)[:, None], 8, axis=1))
    return tabs


def kernel(x, meta_tokens, norm1_gain, w_in, b_forget, ret_norm_gain, w_out, norm2_gain, w_up,
           conv_w, conv_b, w_down, final_norm_gain):
    f32 = np.float32
    x = np.asarray(x, f32)
    B = x.shape[0]
    if "nc" not in _NC_CACHE:
        _NC_CACHE["nc"] = build_nc()
    nc = _NC_CACHE["nc"]
    consts = _consts()
    shared = {
        "w_in": np.ascontiguousarray(np.asarray(w_in, f32)[0]),
        "w_out": np.ascontiguousarray(np.asarray(w_out, f32)[0]),
        "w_up": np.ascontiguousarray(np.asarray(w_up, f32)[0]),
        "w_down": np.ascontiguousarray(np.asarray(w_down, f32)[0]),
        "g1": np.ascontiguousarray(np.asarray(norm1_gain, f32)[0]),
        "g2": np.ascontiguousarray(np.asarray(norm2_gain, f32)[0]),
        "gf": np.ascontiguousarray(np.asarray(final_norm_gain, f32)),
        "rng": np.ascontiguousarray(np.asarray(ret_norm_gain, f32)[0]),
        "wffd": np.ascontiguousarray(np.asarray(w_in, f32)[0][:, 7168:7176].reshape(16, 128, 8).transpose(1, 0, 2).reshape(128, 128)),
        "bfg": np.ascontiguousarray(np.tile(np.asarray(b_forget, f32)[0], NB)),
        "convw": np.ascontiguousarray(np.asarray(conv_w, f32)[0].reshape(3, 2 * NFF, 128).transpose(2, 1, 0).reshape(128, 2 * NFF * 3)),
        "convb": np.ascontiguousarray(np.asarray(conv_b, f32)[0].reshape(2 * NFF, 128).T),
    }
    shared.update(consts)
    meta = np.asarray(meta_tokens, f32)
    in_maps = []
    for core in range(8):
        b, s = core // 2, core % 2
        T0 = 16 + 1024 * s
        full = np.concatenate([meta, x[b]], axis=0)
        xl = np.zeros((LT, D), f32)
        t_lo = T0 - 1152
        src_lo = max(t_lo, 0)
        xl[src_lo - t_lo:, :] = full[src_lo:T0 + 1024]
        m = dict(shared)
        m["xl"] = xl
        m.update(_core_tables(T0))
        in_maps.append(m)
    res = run_bass_kernel_spmd(nc, in_maps[:NCORES_RUN], core_ids=list(range(NCORES_RUN)))
    out = np.zeros((B, 2048, D), f32)
    for core in range(NCORES_RUN):
        b, s = core // 2, core % 2
        out[b, 1024 * s:1024 * (s + 1), :] = res.results[core]["y"]
    if DEBUG:
        kernel.debug = res.results
    return out
```

```python
import contextlib
import numpy as np
import concourse.bass as bass
import concourse.mybir as mybir
from concourse.bass_utils import run_bass_kernel_spmd

F32 = mybir.dt.float32
BF16 = mybir.dt.bfloat16
AF = mybir.ActivationFunctionType
ALU = mybir.AluOpType
AX = mybir.AxisListType

D = 2048
NB = 17
LT = NB * 128
OWN0 = 1150
NOWN = 1026
NH = 8
DFF = 5632
NFF = 44
IN_DIM = 7176
SCALE = 128 ** -0.5
EPS = 1e-6
NEG = -30000.0
XOFF = 300
MASKW = 768
NSLOT = 8
GRP = 4
DEBUG = False
STOP = None
NHEADS_RUN = 8
NCORES_RUN = 8
STOP2 = None
SKIP = set()

ENGS = ("sp", "act", "pool", "dve", "pe")
SEM_LIMIT = 12000
WAIT_ALL_STREAMS = ("const", "constp")


class _Op:
    __slots__ = ("eng", "fn", "deps", "signal", "stream", "sem_i", "val", "inc", "idx", "batch")


class Prog:
    def __init__(self, nc):
        self.nc = nc
        self.ops = []
        self.eng_ops = {e: [] for e in ENGS}
        self.last_w = {}
        self.readers = {}
        self.last_in_stream = {}
        self.barrier_deps = set()

    def op(self, eng, fn, reads=(), writes=(), dma=None, batch=None):
        o = _Op()
        o.batch = batch
        o.eng = eng
        o.fn = fn
        o.idx = len(self.ops)
        o.stream = ("dma", dma) if dma is not None else ("eng", eng)
        o.inc = 16 if dma is not None else 1
        o.signal = dma is not None
        deps = set(self.barrier_deps)
        writes = list(writes) + [k for k in reads if isinstance(k, tuple) and k[0] == "ps"]
        reads = [k for k in reads if not (isinstance(k, tuple) and k[0] == "ps")]
        for k in reads:
            w = self.last_w.get(k)
            if w is not None:
                deps.add(w)
        for k in writes:
            w = self.last_w.get(k)
            if w is not None:
                deps.add(w)
            for r in self.readers.get(k, ()):
                deps.add(r)
        o.deps = deps
        for k in reads:
            self.readers.setdefault(k, []).append(o.idx)
        for k in writes:
            self.last_w[k] = o.idx
            self.readers[k] = []
        self.ops.append(o)
        self.eng_ops[eng].append(o)
        self.last_in_stream[o.stream] = o.idx
        return o

    def barrier(self):
        self.barrier_deps = set(self.last_in_stream.values())

    def emit(self, final_wait_streams=()):
        nc = self.nc
        ops = self.ops
        for o in ops:
            for d in o.deps:
                p = ops[d]
                if p.stream == ("eng", "pe") and o.eng == "pe":
                    continue
                p.signal = True
        streams = {}
        for o in ops:
            if not o.signal:
                continue
            st = streams.setdefault(o.stream, {"n": 0, "cur": 0})
            if st["cur"] + o.inc > SEM_LIMIT:
                st["n"] += 1
                st["cur"] = 0
            st["cur"] += o.inc
            o.sem_i = (o.stream, st["n"])
            o.val = st["cur"]
        batch_max = {}
        for o in ops:
            if o.signal and o.batch is not None:
                k = (o.sem_i, o.batch)
                batch_max[k] = max(batch_max.get(k, 0), o.val)
        sem_keys = []
        seen = set()
        for o in ops:
            if o.signal and o.sem_i not in seen:
                seen.add(o.sem_i)
                sem_keys.append(o.sem_i)
        with contextlib.ExitStack() as es:
            sems = {}
            for i, k in enumerate(sem_keys):
                sems[k] = es.enter_context(nc.semaphore("s%d" % i))
            last_val = {}
            for o in ops:
                if o.signal:
                    last_val[o.sem_i] = max(last_val.get(o.sem_i, 0), o.val)
            block = es.enter_context(nc.Block())
            handles = {"sp": block.sync, "act": block.scalar, "pool": block.gpsimd,
                       "dve": block.vector, "pe": block.tensor}

            def make(engname):
                def body(eng):
                    waited = {}
                    for o in self.eng_ops[engname]:
                        need = {}
                        for d in o.deps:
                            p = ops[d]
                            if not p.signal:
                                continue
                            if p.stream == ("eng", "pe") and engname == "pe":
                                continue
                            v_ = last_val[p.sem_i] if (p.stream[0] == "dma" and p.stream[1] in WAIT_ALL_STREAMS) else p.val
                            if p.batch is not None:
                                v_ = batch_max[(p.sem_i, p.batch)]
                            if v_ > need.get(p.sem_i, 0):
                                need[p.sem_i] = v_
                        for k, v in need.items():
                            if waited.get(k, 0) < v:
                                eng.wait_ge(sems[k], v)
                                waited[k] = v
                        ins = o.fn(eng)
                        if o.signal:
                            ins.then_inc(sems[o.sem_i], o.inc)
                    if engname == "sp":
                        for k in sem_keys:
                            if k[0][0] == "dma" and k[0][1].startswith(final_wait_streams):
                                eng.wait_ge(sems[k], last_val[k])
                return body

            for e in ENGS:
                handles[e](make(e))
        return len(ops)


def build_nc():
    nc = bass.Bass("TRN2", target_bir_lowering=False)

    def din(name, shape):
        return nc.dram_tensor(name, list(shape), F32, kind="ExternalInput").ap()

    xl = din("xl", [LT, D])
    w_in = din("w_in", [D, IN_DIM])
    w_out = din("w_out", [D, D])
    w_up = din("w_up", [D, 2 * DFF])
    w_down = din("w_down", [DFF, D])
    g1 = din("g1", [D]); g2 = din("g2", [D]); gf = din("gf", [D])
    rng = din("rng", [1024])
    bfg = din("bfg", [NB * 8])
    convw = din("convw", [128, 2 * NFF * 3])
    convb = din("convb", [128, 2 * NFF])
    cosT = din("cosT", [128, NB * 64]); sinT = din("sinT", [128, NB * 64])
    dkT = din("dkT", [128, NB * 8]); dqT = din("dqT", [128, NB * 8]); kbT = din("kbT", [128, NB * 8])
    wffd = din("wffd", [128, 16 * 8])
    c_ident = din("c_ident", [128, 128]); c_tri = din("c_tri", [128, 128]); c_ones = din("c_ones", [128, 128])
    c_maskR = din("c_maskR", [128, 128]); c_maskT = din("c_maskT", [128, MASKW]); c_sel = din("c_sel", [128, 1024])
    y = nc.dram_tensor("y", [1024, D], F32, kind="ExternalOutput").ap()
    dbg = {}
    if DEBUG:
        dbg["d_aT"] = nc.dram_tensor("d_aT", [128, 16 * LT], BF16, kind="ExternalOutput").ap()
        dbg["d_mix"] = nc.dram_tensor("d_mix", [128, 16 * NOWN], BF16, kind="ExternalOutput").ap()
        dbg["d_h1"] = nc.dram_tensor("d_h1", [128, 8 * D], F32, kind="ExternalOutput").ap()
        dbg["d_cum"] = nc.dram_tensor("d_cum", [128, NB * 8], F32, kind="ExternalOutput").ap()
        dbg["d_cT"] = nc.dram_tensor("d_cT", [128, 16 * NOWN], BF16, kind="ExternalOutput").ap()
        dbg["d_hh"] = nc.dram_tensor("d_hh", [2, D], F32, kind="ExternalOutput").ap()

    w_in_v = w_in.rearrange("(c p) n -> p c n", p=128)
    w_out_v = w_out.rearrange("(c p) n -> p c n", p=128)
    w_up_v = w_up.rearrange("(c p) n -> p c n", p=128)

    with contextlib.ExitStack() as es:
        def sb(name, shape, dt):
            return es.enter_context(nc.sbuf_tensor(name, list(shape), dt))

        def ps(name, shape, dt):
            return es.enter_context(nc.psum_tensor(name, list(shape), dt))

        R1 = sb("R1", [128, 16 * LT], BF16)
        aT = R1[:, :].rearrange("p (c t) -> p c t", c=16)
        R1f = R1.bitcast(F32)
        h2 = R1f[:, 0:8 * D].rearrange("p (b f) -> p b f", b=8)
        mixT = sb("mixT", [128, 16, NOWN], BF16)
        ringT = sb("ringT", [128, NSLOT * 2048], BF16)
        ring = [ringT[:, i * 2048:(i + 1) * 2048] for i in range(NSLOT)]
        R3N = 20992
        R3 = sb("R3", [128, R3N], BF16)
        R3f = R3.bitcast(F32)
        gb = sb("gb", [128, D], F32)
        ident_b = sb("ident_b", [128, 128], BF16)
        ones_b = sb("ones_b", [128, 128], BF16)
        maskT = sb("maskT", [128, MASKW], BF16)
        sel = sb("sel", [128, 1024], BF16)
        ident_f = sb("ident_f", [128, 128], F32)
        tri_f = sb("tri_f", [128, 128], F32)
        ones_f = sb("ones_f", [128, 128], F32)
        maskR = sb("maskR", [128, 128], F32)
        convw_s = sb("convw_s", [128, 2 * NFF * 3], F32)
        convb_s = sb("convb_s", [128, 2 * NFF], F32)
        bfg_s = sb("bfg_s", [128, NB * 8], F32)
        kb_s = sb("kb_s", [128, NB, 8], F32)
        cos_s = sb("cos_s", [128, NB, 64], F32)
        sin_s = sb("sin_s", [128, NB, 64], F32)
        dk_s = sb("dk_s", [128, NB, 8], F32)
        dq_s = sb("dq_s", [128, NB, 8], F32)
        ndk_s = sb("ndk_s", [128, NB, 8], F32)
        ndq_s = sb("ndq_s", [128, NB, 8], F32)
        spt = sb("spt", [128, NB * 8], F32)
        cumn = sb("cumn", [128, NB * 8], F32)
        tot = sb("tot", [128, NB * 8], F32)
        pre = sb("pre", [128, NB * 8], F32)
        biasK = sb("biasK", [128, NB * 8], F32)
        Rb = sb("Rb", [128, 9 * 128], BF16)
        wff = sb("wff", [128, 16, 8], BF16)
        st1 = sb("st1", [128, 32], F32)
        eps_t = sb("eps_t", [128, 1], F32)
        st2 = sb("st2", [128, 64], F32)

        pp = [ps("pp%d" % i, [128, 1024], F32) for i in range(4)]
        ppb = [p.bitcast(BF16) for p in pp]

        def bank(i):
            return pp[i // 2][:, (i % 2) * 512:(i % 2) * 512 + 512]

        P = Prog(nc)
        slot_ctr = [0]

        def next_slot():
            s = slot_ctr[0] % NSLOT
            slot_ctr[0] += 1
            return s

        def dma_sp(out, in_, reads=(), writes=(), stream="const"):
            P.op("sp", lambda e: e.dma_start(out=out, in_=in_), reads=reads, writes=writes, dma=stream)

        def dma_cast(out, in_, reads=(), writes=(), stream="constp", batch=None):
            P.op("pool", lambda e: e.dma_start(out=out, in_=in_), reads=reads, writes=writes, dma=stream, batch=batch)

        def load_slice(wv, c0, ncols=128):
            s = next_slot()
            v = ring[s][:, 0:16 * ncols].rearrange("p (c n) -> p c n", c=16)
            for hh in range(2):
                dma_cast(v[:, 8 * hh:8 * hh + 8, :], wv[:, 8 * hh:8 * hh + 8, c0:c0 + ncols], writes=[("ring", s, hh)], stream="ring%d" % s,
                         batch=slot_ctr[0])
            return s, v

        def rkeys(s):
            return [("ring", s, 0), ("ring", s, 1)]

        dma_sp(gb[:], g1.partition_broadcast(128), writes=["gb"], stream="gb")
        P.op("dve", lambda e: e.memset(eps_t[:], EPS), writes=["eps_t"])
        P.op("dve", lambda e: e.memset(pre[:], 0.0), writes=["pre"])
        dma_cast(ident_b[:], c_ident, writes=["ident_b"])
        dma_cast(wff[:, :, :].rearrange("p c n -> p (c n)"), wffd, writes=[("wff", 0), ("wff", 1)])
        dma_cast(ones_b[:], c_ones, writes=["ones_b"])
        dma_cast(maskT[:], c_maskT, writes=["maskT"])
        dma_cast(sel[:], c_sel, writes=["sel"])

        def late_constants():
            dma_sp(cos_s[:, :, :].rearrange("p b d -> p (b d)"), cosT, writes=["cos"])
            dma_sp(sin_s[:, :, :].rearrange("p b d -> p (b d)"), sinT, writes=["sin"])
            dma_sp(dk_s[:, :, :].rearrange("p b h -> p (b h)"), dkT, writes=["dk"])
            dma_sp(dq_s[:, :, :].rearrange("p b h -> p (b h)"), dqT, writes=["dq"])
            P.op("dve", lambda e: e.tensor_scalar_mul(out=ndk_s[:, :, :], in0=dk_s[:, :, :], scalar1=-1.0), reads=["dk"], writes=["ndk"])
            P.op("dve", lambda e: e.tensor_scalar_mul(out=ndq_s[:, :, :], in0=dq_s[:, :, :], scalar1=-1.0), reads=["dq"], writes=["ndq"])
            dma_sp(ident_f[:], c_ident, writes=["ident_f"])
            dma_sp(tri_f[:], c_tri, writes=["tri_f"])
            dma_sp(ones_f[:], c_ones, writes=["ones_f"])
            dma_sp(maskR[:], c_maskR, writes=["maskR"])
            dma_sp(bfg_s[:], bfg.partition_broadcast(128), writes=["bfg"])
            dma_sp(kb_s[:, :, :].rearrange("p b h -> p (b h)"), kbT, writes=["kb"])
            dma_sp(convw_s[:], convw, writes=["convw"])
            dma_sp(convb_s[:], convb, writes=["convb"])

        def rms_rstd(src_ap, junk_ap, col, rkeys_, jkey, extra_writes=()):
            npart = src_ap.shape[0]
            c = st1[0:npart, col:col + 1]
            P.op("act", lambda e: e.activation(out=junk_ap, in_=src_ap, func=AF.Square, accum_out=c),
                 reads=rkeys_, writes=[jkey, ("st1", col)] + list(extra_writes))
            P.op("act", lambda e: e.activation(out=c, in_=c, func=AF.Ln, bias=eps_t[0:npart, 0:1], scale=1.0 / D),
                 reads=[("st1", col), "eps_t"], writes=[("st1", col)])
            P.op("act", lambda e: e.activation(out=c, in_=c, func=AF.Exp, scale=-0.5), reads=[("st1", col)], writes=[("st1", col)])
            return c

        xsA = [R3f[:, 0:2048], R3f[:, 2048:4096], R3f[:, 4096:6144]]
        xnA = [R3[:, 12288:14336], R3[:, 14336:16384]]
        junkA = R3[:, 16384:18432]
        b6 = bank(6)

        def aTk(tb):
            return [("aT", tb, 0), ("aT", tb, 1)]

        def A1(tb):
            xs = xsA[tb % 3]
            dma_sp(xs, xl[tb * 128:(tb + 1) * 128, :], writes=[("xs", tb % 3)], stream="xs%d" % (tb % 3))
            rms_rstd(xs, junkA, tb % 3, [("xs", tb % 3)], "junkA")

        def A2(tb):
            xs = xsA[tb % 3]
            c = st1[:, tb % 3:tb % 3 + 1]
            xn = xnA[tb % 2]
            P.op("dve", lambda e: e.scalar_tensor_tensor(out=xn, in0=xs, scalar=c, in1=gb[:], op0=ALU.mult, op1=ALU.mult),
                 reads=[("xs", tb % 3), ("st1", tb % 3), "gb"], writes=[("xnA", tb % 2)])

        def A3(tb):
            xn = xnA[tb % 2]
            pv = ppb[tb % 2]
            for cc in range(16):
                P.op("pe", lambda e, cc=cc: e.transpose(out=pv[:, cc * 128:(cc + 1) * 128], in_=xn[:, cc * 128:(cc + 1) * 128], identity=ident_b[:]),
                     reads=[("xnA", tb % 2), "ident_b"], writes=[("ps", 2 * (tb % 2)), ("ps", 2 * (tb % 2) + 1)])
            pv3 = pv[:, 0:2048].rearrange("p (c t) -> p c t", c=16)
            P.op("act", lambda e: e.copy(out=aT[:, 0:8, tb * 128:(tb + 1) * 128], in_=pv3[:, 0:8, :]),
                 reads=[("ps", 2 * (tb % 2))], writes=[("aT", tb, 0)])
            P.op("dve", lambda e: e.tensor_copy(out=aT[:, 8:16, tb * 128:(tb + 1) * 128], in_=pv3[:, 8:16, :]),
                 reads=[("ps", 2 * (tb % 2) + 1)], writes=[("aT", tb, 1)])

        def A4(tb):
            for kc in range(16):
                P.op("pe", lambda e, kc=kc: e.matmul(b6[:, tb * 8:tb * 8 + 8], lhsT=aT[:, kc, tb * 128:(tb + 1) * 128], rhs=wff[:, kc, :],
                                                    start=(kc == 0), stop=(kc == 15)),
                     reads=aTk(tb) + [("wff", 0), ("wff", 1)], writes=[("ps", 6)])

        for i in range(NB + 3):
            if i == 3:
                late_constants()
            if i < NB:
                A1(i)
            if 0 <= i - 1 < NB:
                A2(i - 1)
            if 0 <= i - 2 < NB:
                A3(i - 2)
            if 0 <= i - 3 < NB:
                A4(i - 3)


        def aT_range_keys(l0, l1):
            ks = []
            for tb in range(l0 // 128, (l1 - 1) // 128 + 1):
                ks += aTk(tb)
            return ks

        if DEBUG:
            dma_sp(dbg["d_aT"], R1[:, :], reads=aT_range_keys(0, LT), stream="st")

        if STOP == "A":
            P.emit(final_wait_streams="st")
            return nc
        P.barrier()
        dma_sp(gb[:, 0:1024], rng.partition_broadcast(128), writes=["gb"], stream="gb")

        b7 = bank(7)
        P.op("dve", lambda e: e.tensor_tensor(out=spt[:], in0=b6[:, 0:NB * 8], in1=bfg_s[:], op=ALU.add), reads=[("ps", 6), "bfg"], writes=["spt"])
        P.op("act", lambda e: e.activation(out=spt[:], in_=spt[:], func=AF.Exp, scale=-1.0), reads=["spt"], writes=["spt"])
        P.op("act", lambda e: e.activation(out=spt[:], in_=spt[:], func=AF.Ln, bias=1.0, scale=1.0), reads=["spt"], writes=["spt"])
        P.op("pe", lambda e: e.matmul(b7[:, 0:136], lhsT=tri_f[:], rhs=spt[:], start=True, stop=True), reads=["spt", "tri_f"], writes=[("ps", 7)])
        P.op("pe", lambda e: e.matmul(b7[:, 136:272], lhsT=ones_f[:], rhs=spt[:], start=True, stop=True), reads=["spt", "ones_f"], writes=[("ps", 7)])
        P.op("dve", lambda e: e.tensor_copy(out=cumn[:], in_=b7[:, 0:136]), reads=[("ps", 7)], writes=["cumn"])
        P.op("dve", lambda e: e.tensor_copy(out=tot[:], in_=b7[:, 136:272]), reads=[("ps", 7)], writes=["tot"])
        for b in range(1, NB):
            P.op("dve", lambda e, b=b: e.tensor_tensor(out=pre[:, b * 8:b * 8 + 8], in0=pre[:, (b - 1) * 8:b * 8], in1=tot[:, (b - 1) * 8:b * 8], op=ALU.add),
                 reads=["pre", "tot"], writes=["pre"])
        P.op("dve", lambda e: e.tensor_tensor(out=cumn[:], in0=cumn[:], in1=pre[:], op=ALU.add), reads=["cumn", "pre"], writes=["cumn"])
        P.op("dve", lambda e: e.tensor_tensor(out=biasK[:], in0=cumn[:], in1=kb_s[:, :, :].rearrange("p b h -> p (b h)"), op=ALU.add),
             reads=["cumn", "kb"], writes=["biasK"])
        for c in range(9):
            tb = 8 + c
            dst = pp[2][0:8, c * 128:(c + 1) * 128] if c < 8 else pp[3][0:8, 512:640]
            P.op("pe", lambda e, dst=dst, tb=tb: e.transpose(out=dst, in_=cumn[:, tb * 8:tb * 8 + 8], identity=ident_f[:]),
                 reads=["cumn", "ident_f"], writes=[("ps", 4), ("ps", 5)] if c < 8 else [("ps", 7)])
        P.op("dve", lambda e: e.memset(Rb[:], 0.0), writes=["Rb"])
        P.op("act", lambda e: e.activation(out=Rb[0:8, 0:1024], in_=pp[2][0:8, 0:1024], func=AF.Copy, scale=-1.0 / SCALE),
             reads=[("ps", 4), ("ps", 5)], writes=["Rb"])
        P.op("act", lambda e: e.activation(out=Rb[0:8, 1024:1152], in_=pp[3][0:8, 512:640], func=AF.Copy, scale=-1.0 / SCALE),
             reads=[("ps", 7)], writes=["Rb"])
        if DEBUG:
            dma_sp(dbg["d_cum"], cumn[:], reads=["cumn"], stream="st")
        if STOP == "B0":
            P.emit(final_wait_streams="st")
            return nc

        o_ = 0
        def carve(n):
            nonlocal o_
            a = o_
            o_ += n
            return a
        fkT = R3[:, carve(LT):o_]
        _a = carve(1028)
        fqT = R3[:, _a:_a + NOWN]
        rvfv = R3[:, carve(NB * 256):o_].rearrange("p (b n) -> p b n", b=NB)
        rkt = R3[:, carve(NB * 128):o_].rearrange("p (b n) -> p b n", b=NB)
        rqT = R3[:, carve(1152):o_]
        rkT = R3[:, carve(1152):o_]
        sg = R3[:, carve(1152):o_].rearrange("p (b n) -> p b n", b=9)
        rqt = R3[:, carve(1152):o_].rearrange("p (b n) -> p b n", b=9)
        PTt = [R3[:, carve(342):o_] for _ in range(3)]
        smt = [R3[:, carve(128):o_] for _ in range(2)]
        Sbf = [R3[:, carve(128):o_] for _ in range(2)]
        junkB = R3[:, carve(128):o_]
        assert o_ % 2 == 0
        fo = o_ // 2
        def carvef(n):
            nonlocal fo
            a = fo
            fo += n
            return a
        o_all = R3f[:, carvef(1152):fo].rearrange("p (b n) -> p b n", b=9)
        rden = R3f[:, carvef(342):fo]
        rU = R3f[:, carvef(128):fo]
        rW = R3f[:, carvef(128):fo]
        ro = R3[:, fo * 2:fo * 2 + 1152].rearrange("p (b n) -> p b n", b=9)
        assert fo * 2 + 1152 <= R3N, fo * 2

        def rotary(src, dst, tbl, dec_ap, ndec_ap, rk, wk):
            Cb = cos_s[:, tbl:tbl + 1, :].to_broadcast([128, 2, 64])
            S = sin_s[:, tbl, :]
            src3 = src[:, 0:128].rearrange("p (a b) -> p a b", a=2)
            U3 = rU.rearrange("p (a b) -> p a b", a=2)
            t1 = src[:, 0:64]
            t2 = src[:, 64:128]
            rd = list(rk) + ["cos", "sin", "dk", "dq", "ndk", "ndq"]
            P.op("dve", lambda e: e.scalar_tensor_tensor(out=U3, in0=src3, scalar=dec_ap, in1=Cb, op0=ALU.mult, op1=ALU.mult), reads=rd, writes=["rU"])
            P.op("dve", lambda e: e.scalar_tensor_tensor(out=rW[:, 0:64], in0=t2, scalar=ndec_ap, in1=S, op0=ALU.mult, op1=ALU.mult), reads=rd, writes=["rW"])
            P.op("dve", lambda e: e.scalar_tensor_tensor(out=rW[:, 64:128], in0=t1, scalar=dec_ap, in1=S, op0=ALU.mult, op1=ALU.mult), reads=rd + ["rW"], writes=["rW"])
            P.op("dve", lambda e: e.tensor_tensor(out=dst, in0=rU, in1=rW, op=ALU.add), reads=["rU", "rW"], writes=wk)

        vA = ringT[:, 0:6144].rearrange("p (c n) -> p c n", c=16)
        vB = ringT[:, 6144:10240].rearrange("p (c n) -> p c n", c=16)
        vC = ringT[:, 10240:12288].rearrange("p (c n) -> p c n", c=16)
        vD = ringT[:, 12288:14336].rearrange("p (c n) -> p c n", c=16)
        regions = {"A": (vA, 3), "B": (vB, 2), "C": (vC, 1), "D": (vD, 1)}

        def rg_keys(name):
            return [("rg" + name, si, hh) for si in range(regions[name][1]) for hh in range(2)]

        def load_region(name, col_offs, h):
            v, _ = regions[name]
            for si, c0 in enumerate(col_offs):
                for hh in range(2):
                    dma_cast(v[:, 8 * hh:8 * hh + 8, si * 128:(si + 1) * 128], w_in_v[:, 8 * hh:8 * hh + 8, c0:c0 + 128],
                             writes=[("rg" + name, si, hh)], stream="rg" + name, batch=h)

        all_rg = [k for nm in ("A", "B", "C", "D") for k in rg_keys(nm)]

        def load_wout_quarter(qp, extra=()):
            sl = []
            for m in range(4):
                s_ = next_slot()
                v_ = ring[s_].rearrange("p (c n) -> p c n", c=4)
                dma_cast(v_, w_out_v[:, 4 * m:4 * m + 4, qp * 512:(qp + 1) * 512], writes=rkeys(s_) + list(extra), stream="ring%d" % s_)
                sl.append((s_, v_))
            return sl
        wq = {}
        deferred_tail = [None]
        pending_tail_ops = []

        def load_head(h):
            load_region("A", [1024 + h * 128, 2048 + h * 128, 6144 + h * 128], h)
            load_region("B", [h * 128, 3072 + h * 128], h)
            load_region("C", [4096 + h * 128], h)
            load_region("D", [5120 + h * 128], h)
        load_head(0)
        for h in range(NHEADS_RUN):
            for tb in range(NB):
                bA = 2 * (tb % 2)
                bB = bA + 1
                for kc in range(16):
                    P.op("pe", lambda e, tb=tb, kc=kc, bA=bA: e.matmul(
                        bank(bA)[:, 0:384], lhsT=aT[:, kc, tb * 128:(tb + 1) * 128], rhs=vA[:, kc, :],
                        start=(kc == 0), stop=(kc == 15)),
                        reads=aTk(tb) + rg_keys("A"), writes=[("ps", bA)])
                    if tb >= 8:
                        P.op("pe", lambda e, tb=tb, kc=kc, bB=bB: e.matmul(
                            bank(bB)[:, 0:256], lhsT=aT[:, kc, tb * 128:(tb + 1) * 128], rhs=vB[:, kc, :],
                            start=(kc == 0), stop=(kc == 15)),
                            reads=aTk(tb) + rg_keys("B"), writes=[("ps", bB)])
                P.op("act", lambda e, tb=tb, bA=bA: e.copy(out=rvfv[:, tb, :], in_=bank(bA)[:, 128:384]), reads=[("ps", bA)], writes=[("rvfv", tb)])
                if tb >= 8:
                    P.op("act", lambda e, tb=tb, bB=bB: e.copy(out=sg[:, tb - 8, :], in_=bank(bB)[:, 128:256]), reads=[("ps", bB)], writes=["sg"])
                rotary(bank(bA), rkt[:, tb, :], tb, dk_s[:, tb, h:h + 1], ndk_s[:, tb, h:h + 1], [("ps", bA)], [("rkt", tb)])
                for _ in range(4 if tb < 7 else 99):
                    if pending_tail_ops and tb < 8:
                        pending_tail_ops.pop(0)()
                if tb >= 8:
                    rotary(bank(bB), rqt[:, tb - 8, :], tb, dq_s[:, tb, h:h + 1], ndq_s[:, tb, h:h + 1], [("ps", bB)], [("rqt", tb - 8)])
            P.op("act", lambda e: e.activation(out=sg[:, :, :], in_=sg[:, :, :], func=AF.Silu), reads=["sg"], writes=["sg"])
            if deferred_tail[0] is not None:
                deferred_tail[0]()
                deferred_tail[0] = None
            def qk_transposes():
                for c in range(9):
                    P.op("pe", lambda e, c=c: e.transpose(out=ppb[0][:, c * 128:(c + 1) * 128], in_=rqt[:, c, :], identity=ident_b[:]),
                         reads=[("rqt", c), "ident_b"], writes=[("ps", 0), ("ps", 1)])
                    P.op("pe", lambda e, c=c: e.transpose(out=ppb[1][:, c * 128:(c + 1) * 128], in_=rkt[:, 8 + c, :], identity=ident_b[:]),
                         reads=[("rkt", 8 + c), "ident_b"], writes=[("ps", 2), ("ps", 3)])
                P.op("dve", lambda e: e.tensor_copy(out=rqT, in_=ppb[0][:, 0:1152]), reads=[("ps", 0), ("ps", 1)], writes=["rqT"])
                P.op("dve", lambda e: e.tensor_copy(out=rkT, in_=ppb[1][:, 0:1152]), reads=[("ps", 2), ("ps", 3)], writes=["rkT"])

            for nb in range(5):
                if nb == 3:
                    qk_transposes()
                n0 = nb * 512
                nw = min(512, LT - n0)
                bk = 4 + nb % 2
                for kc in range(16):
                    P.op("pe", lambda e, kc=kc, n0=n0, nw=nw, bk=bk: e.matmul(bank(bk)[:, 0:nw], lhsT=vD[:, kc, :], rhs=aT[:, kc, n0:n0 + nw],
                                                                          start=(kc == 0), stop=(kc == 15)),
                         reads=aT_range_keys(n0, n0 + nw) + rg_keys("D"), writes=[("ps", bk)])
                P.op("dve", lambda e, n0=n0, nw=nw, bk=bk: e.tensor_copy(out=fkT[:, n0:n0 + nw], in_=bank(bk)[:, 0:nw]), reads=[("ps", bk)], writes=[("fkT", nb)])
            for g in range(3):
                n0 = OWN0 + 342 * g
                bk = 4 + (g + 1) % 2
                for kc in range(16):
                    P.op("pe", lambda e, kc=kc, n0=n0, bk=bk: e.matmul(bank(bk)[:, 0:342], lhsT=vC[:, kc, :], rhs=aT[:, kc, n0:n0 + 342],
                                                                    start=(kc == 0), stop=(kc == 15)),
                         reads=aT_range_keys(n0, n0 + 342) + rg_keys("C"), writes=[("ps", bk)])
                P.op("dve", lambda e, g=g, bk=bk: e.tensor_copy(out=fqT[:, 342 * g:342 * g + 342], in_=bank(bk)[:, 0:342]), reads=[("ps", bk)], writes=[("fqT", g)])

            if h + 1 < NHEADS_RUN:
                load_head(h + 1)
            else:
                wq[0] = load_wout_quarter(0, all_rg)
                wq[1] = load_wout_quarter(1, all_rg)
            Sps = bank(7)[:, 0:128]
            ret_steps = []

            def r_init():
                for b in range(8):
                    P.op("pe", lambda e, b=b: e.matmul(Sps, lhsT=rkt[:, b, :], rhs=rvfv[:, b, 0:128], start=(b == 0), stop=(b == 7), skip_group_check=True),
                         reads=[("rkt", b), ("rvfv", b)], writes=[("ps", 7)])
            ret_steps.append(r_init)

            def r_a(c):
                sTp = bank(6)[:, (c % 2) * 128:(c % 2) * 128 + 128]
                if c > 0:
                    P.op("pe", lambda e: e.matmul(Sps, lhsT=rkt[:, 7 + c, :], rhs=rvfv[:, 7 + c, 0:128], start=False, stop=True, skip_group_check=True),
                         reads=[("rkt", 7 + c), ("rvfv", 7 + c)], writes=[("ps", 7)])
                P.op("dve", lambda e: e.tensor_copy(out=Sbf[c % 2], in_=Sps), reads=[("ps", 7)], writes=[("Sbf", c % 2)])
                P.op("pe", lambda e: e.matmul(sTp, lhsT=rkT[:, c * 128:(c + 1) * 128], rhs=rqT[:, c * 128:(c + 1) * 128], start=True, stop=True),
                     reads=["rkT", "rqT"], writes=[("ps", 6)])
                P.op("dve", lambda e: e.tensor_tensor(out=smt[c % 2], in0=sTp, in1=maskR[:], op=ALU.mult),
                     reads=[("ps", 6), "maskR"], writes=[("smt", c % 2)])

            def r_b(c):
                op_ = bank(4)[:, (c % 2) * 128:(c % 2) * 128 + 128]
                P.op("pe", lambda e: e.matmul(op_, lhsT=rqT[:, c * 128:(c + 1) * 128], rhs=Sbf[c % 2], start=True, stop=False),
                     reads=["rqT", ("Sbf", c % 2)], writes=[("ps", 4)])
                P.op("pe", lambda e: e.matmul(op_, lhsT=smt[c % 2], rhs=rvfv[:, 8 + c, 0:128], start=False, stop=True),
                     reads=[("smt", c % 2), ("rvfv", 8 + c)], writes=[("ps", 4)])
                P.op("dve", lambda e: e.tensor_copy(out=o_all[:, c, :], in_=op_), reads=[("ps", 4)], writes=[("o_all", c)])
                P.op("act", lambda e: e.activation(out=junkB, in_=op_, func=AF.Square, accum_out=st2[:, 16 + c:17 + c]),
                     reads=[("ps", 4)], writes=["junkB", ("st2q", c)])
            for c in range(9):
                ret_steps.append(lambda c=c: r_a(c))
                ret_steps.append(lambda c=c: r_b(c))

            def r_tail(h=h):
                oall_keys = [("o_all", c) for c in range(9)]
                sq_keys = [("st2q", c) for c in range(9)]
                mean = st2[:, 0:9]
                ssq = st2[:, 16:25]
                msq = st2[:, 32:41]
                rstd = st2[:, 48:57]
                P.op("dve", lambda e: e.reduce_sum(out=mean, in_=o_all[:, :, :], axis=AX.X), reads=oall_keys, writes=["st2m"])
                P.op("dve", lambda e: e.tensor_scalar_mul(out=mean, in0=mean, scalar1=1.0 / 128), reads=["st2m"], writes=["st2m"])
                P.op("dve", lambda e: e.tensor_tensor(out=msq, in0=mean, in1=mean, op=ALU.mult), reads=["st2m"], writes=["st2s"])
                P.op("dve", lambda e: e.tensor_scalar(out=rstd, in0=ssq, scalar1=1.0 / 128, scalar2=EPS, op0=ALU.mult, op1=ALU.add), reads=sq_keys, writes=["st2r"])
                P.op("dve", lambda e: e.tensor_tensor(out=rstd, in0=rstd, in1=msq, op=ALU.subtract), reads=["st2r", "st2s"], writes=["st2r"])
                P.op("act", lambda e: e.activation(out=rstd, in_=rstd, func=AF.Ln), reads=["st2r"], writes=["st2r"])
                P.op("act", lambda e: e.activation(out=rstd, in_=rstd, func=AF.Exp, scale=-0.5), reads=["st2r"], writes=["st2r"])
                ops_ = []
                for c in range(9):
                    ops_.append(lambda c=c: P.op("dve", lambda e: e.tensor_scalar(out=o_all[:, c, :], in0=o_all[:, c, :], scalar1=st2[:, c:c + 1], scalar2=st2[:, 48 + c:49 + c],
                                                                              op0=ALU.subtract, op1=ALU.mult),
                                                 reads=[("o_all", c), "st2m", "st2r"], writes=[("o_all", c)]))
                    ops_.append(lambda c=c: P.op("dve", lambda e: e.tensor_tensor(out=o_all[:, c, :], in0=o_all[:, c, :], in1=gb[:, h * 128:(h + 1) * 128], op=ALU.mult),
                                                 reads=[("o_all", c), "gb"], writes=[("o_all", c)]))
                    ops_.append(lambda c=c: P.op("dve", lambda e: e.tensor_tensor(out=ro[:, c, :], in0=o_all[:, c, :], in1=sg[:, c, :], op=ALU.mult),
                                                 reads=[("o_all", c), "sg"], writes=[("ro", c)]))
                return ops_

            def r_tail_pe(h=h):
                for c in range(9):
                    P.op("pe", lambda e, c=c: e.transpose(out=ppb[3][:, c * 128:(c + 1) * 128], in_=ro[:, c, :], identity=ident_b[:]),
                         reads=[("ro", c), "ident_b"], writes=[("ps", 6), ("ps", 7)])
                P.op("act", lambda e: e.copy(out=mixT[:, h, :], in_=ppb[3][:, 126:1152]), reads=[("ps", 6), ("ps", 7)], writes=[("mixh", h)])

            tiles = []
            for g in range(3):
                q0 = OWN0 + 342 * g
                kmax = (q0 + 342 - 1) // 128
                for kb in range(kmax + 1):
                    tiles.append((g, kb, kmax, q0))
            oTp = bank(2)[:, 0:342]
            dnp = bank(3)[:, 0:342]

            def f_s(ti, h=h):
                g, kb, kmax, q0 = tiles[ti]
                sbk = (0, 1, 5)[ti % 3]
                sb_ = bank(sbk)[:, 0:342]
                delta = 128 * kb - q0
                need_mask = (128 * kb + 127) > q0
                P.op("pe", lambda e: e.matmul(sb_, lhsT=fkT[:, kb * 128:(kb + 1) * 128], rhs=fqT[:, 342 * g:342 * g + 342], start=True, stop=False),
                     reads=[("fkT", kb // 4), ("fqT", g)], writes=[("ps", sbk)])
                P.op("pe", lambda e: e.matmul(sb_, lhsT=sel[:, h * 128:(h + 1) * 128], rhs=Rb[:, 126 + 342 * g:126 + 342 * g + 342],
                                              start=False, stop=(not need_mask)),
                     reads=["sel", "Rb"], writes=[("ps", sbk)])
                if need_mask:
                    off = XOFF - delta
                    assert 0 <= off and off + 342 <= MASKW, off
                    P.op("pe", lambda e: e.matmul(sb_, lhsT=ident_b[:], rhs=maskT[:, off:off + 342], start=False, stop=True),
                         reads=["ident_b", "maskT"], writes=[("ps", sbk)])
                pt = PTt[ti % 3]
                P.op("act", lambda e: e.activation(out=pt, in_=sb_, func=AF.Exp, bias=biasK[:, kb * 8 + h:kb * 8 + h + 1], scale=SCALE),
                     reads=[("ps", sbk), "biasK"], writes=[("PT", ti % 3)])

            def f_pv(ti, h=h):
                g, kb, kmax, q0 = tiles[ti]
                pt = PTt[ti % 3]
                ptk = ("PT", ti % 3)
                P.op("pe", lambda e: e.matmul(oTp, lhsT=rvfv[:, kb, 128:256], rhs=pt, start=(kb == 0), stop=(kb == kmax)),
                     reads=[("rvfv", kb), ptk], writes=[("ps", 2)])
                P.op("pe", lambda e: e.matmul(dnp, lhsT=ones_b[:], rhs=pt, start=(kb == 0), stop=(kb == kmax)),
                     reads=["ones_b", ptk], writes=[("ps", 3)])
                if kb == kmax:
                    P.op("dve", lambda e: e.reciprocal(out=rden, in_=dnp), reads=[("ps", 3)], writes=["rden"])
                    P.op("dve", lambda e: e.tensor_tensor(out=mixT[:, 8 + h, 342 * g:342 * g + 342], in0=oTp, in1=rden, op=ALU.mult),
                         reads=[("ps", 2), "rden"], writes=[("mixf", h, g)])

            fox_steps = []
            nt_ = len(tiles)
            def f_first():
                f_s(0)
                f_s(1)
            fox_steps.append(f_first)
            for ti in range(nt_):
                def st(ti=ti):
                    if ti + 2 < nt_:
                        f_s(ti + 2)
                    f_pv(ti)
                fox_steps.append(st)
            fi = 0
            last_head = (h == NHEADS_RUN - 1)
            per = [1, 1] if last_head else [3, 2]
            for ri, rs in enumerate(ret_steps):
                rs()
                k = 1 if ri == 0 else per[ri % 2]
                for _ in range(k):
                    if fi < len(fox_steps):
                        fox_steps[fi]()
                        fi += 1
            if last_head:
                for f_ in r_tail():
                    f_()
            while fi < len(fox_steps):
                fox_steps[fi]()
                fi += 1
            if not last_head:
                pending_tail_ops[:] = r_tail()
            deferred_tail[0] = r_tail_pe
        deferred_tail[0]()


        mix_all = [("mixh", h) for h in range(NH)] + [("mixf", h, g) for h in range(NH) for g in range(3)]
        if DEBUG:
            dma_sp(dbg["d_mix"], mixT[:, :, :].rearrange("p c t -> p (c t)"), reads=mix_all, stream="st")
        if STOP == "B":
            P.emit(final_wait_streams="st")
            return nc

        P.barrier()

        dma_sp(gb[:], g2.partition_broadcast(128), writes=["gb"], stream="gb")
        xsC = [R3f[:, 0:512], R3f[:, 512:1024], R3f[:, 1024:1536]]
        hnC = [R3[:, 4096:6144], R3[:, 6144:8192]]
        junkC = R3[:, 8192:10240]
        hh = R3f[:, 6144:8192]
        blocks = [(-1, 0, OWN0)] + [(tb, 2 + 128 * tb, 1152 + 128 * tb) for tb in range(8)]
        xi = 0
        ubk = 0

        def c_norm1(bi, tb):
            src = hh[:, :] if tb < 0 else h2[:, tb, :]
            col = 4 + bi % 2
            hk = [("h1", bi, q) for q in range(4)]
            c = rms_rstd(src, junkC, col, hk, "junkC")
            hn = hnC[bi % 2]
            P.op("dve", lambda e: e.scalar_tensor_tensor(out=hn, in0=src, scalar=c, in1=gb[:], op0=ALU.mult, op1=ALU.mult),
                 reads=hk + [("st1", col), "gb"], writes=[("hnC", bi % 2)])

        def c_norm2(bi, tb, c0):
            hn = hnC[bi % 2]
            pv = ppb[2 + bi % 2]
            pk = [("ps", 4 + 2 * (bi % 2)), ("ps", 5 + 2 * (bi % 2))]
            for cc in range(16):
                P.op("pe", lambda e, cc=cc: e.transpose(out=pv[:, cc * 128:(cc + 1) * 128], in_=hn[:, cc * 128:(cc + 1) * 128], identity=ident_b[:]),
                     reads=[("hnC", bi % 2), "ident_b"], writes=pk)
            pv3 = pv[:, 0:2048].rearrange("p (c t) -> p c t", c=16)
            if tb < 0:
                P.op("act", lambda e: e.copy(out=mixT[:, :, 0:2], in_=pv3[:, :, 0:2]), reads=pk + mix_all, writes=[("cT", bi)])
            else:
                P.op("act", lambda e: e.copy(out=mixT[:, 0:8, c0:c0 + 128], in_=pv3[:, 0:8, :]), reads=pk[0:1] + mix_all, writes=[("cT", bi)])
                P.op("dve", lambda e: e.tensor_copy(out=mixT[:, 8:16, c0:c0 + 128], in_=pv3[:, 8:16, :]), reads=pk[1:2] + mix_all, writes=[("cTb", bi)])

        for qp in range(4):
            slots = wq[qp]
            pend = None
            for bi, (tb, c0, l0) in enumerate(blocks):
                xs = xsC[xi % 3]
                xk = ("xs", xi % 3)
                xstream = "xs%d" % (xi % 3)
                xi += 1
                dma_sp(xs, xl[l0:l0 + 128, qp * 512:(qp + 1) * 512], writes=[xk], stream=xstream)
                bk = ubk % 4
                ubk += 1
                ck = [("cT", bi), ("cTb", bi)] + ([("cT", 1), ("cTb", 1)] if tb < 0 else [])
                for kc in range(16):
                    s_, v_ = slots[kc // 4]
                    P.op("pe", lambda e, kc=kc, v_=v_, c0=c0, bk=bk: e.matmul(
                        bank(bk), lhsT=mixT[:, kc, c0:c0 + 128], rhs=v_[:, kc % 4, :], start=(kc == 0), stop=(kc == 15)),
                        reads=mix_all + ck + rkeys(s_), writes=[("ps", bk)])
                dst = hh[:, qp * 512:(qp + 1) * 512] if tb < 0 else h2[:, tb, qp * 512:(qp + 1) * 512]
                P.op("dve", lambda e, dst=dst, bk=bk, xs=xs: e.tensor_tensor(out=dst, in0=bank(bk), in1=xs, op=ALU.add),
                     reads=[("ps", bk), xk], writes=[("h1", bi, qp)])
                if qp == 3:
                    c_norm1(bi, tb)
                    if pend is not None:
                        c_norm2(*pend)
                    pend = (bi, tb, c0)
            if qp == 3:
                c_norm2(*pend)
            if qp + 2 < 4:
                wq[qp + 2] = load_wout_quarter(qp + 2)

        cT = mixT
        cT_all = [("cT", bi) for bi in range(9)] + [("cTb", bi) for bi in range(1, 9)]
        if DEBUG:
            dma_sp(dbg["d_h1"], R1f[:, 0:8 * D], reads=[("h1", bi, hp) for bi in range(1, 9) for hp in range(4)], stream="st")
        if DEBUG:
            dma_sp(dbg["d_cT"], mixT[:, :, :].rearrange("p c t -> p (c t)"), reads=cT_all, stream="st")
            dma_sp(dbg["d_hh"], hh[0:2, :], reads=[("h1", 0, q) for q in range(4)], stream="st")
        if STOP == "C":
            P.emit(final_wait_streams="st")
            return nc
        pre_up = [load_slice(w_up_v, half * DFF) for half in range(2)]
        P.barrier()

        gated = [[R3[:, (gs * GRP + jj) * 1024:(gs * GRP + jj + 1) * 1024] for jj in range(GRP)] for gs in range(2)]
        fb = 2 * GRP * 1024 // 2
        Yg = [R3f[:, fb + i * 1024:fb + (i + 1) * 1024] for i in range(2)]
        Yv = [R3f[:, fb + 2048 + i * 1024:fb + 2048 + (i + 1) * 1024] for i in range(2)]
        sb0 = 2 * (fb + 4096)
        Sg = [R3[:, sb0 + i * 1024:sb0 + (i + 1) * 1024] for i in range(2)]
        assert sb0 + 2048 <= R3N
        nblk = [(0, 342), (342, 683), (683, 1024)]
        ub = [0]

        h2_keys = lambda tb: [("h2", tb, n) for n in range(4)]
        mixflat = mixT[:, :, :].rearrange("p c t -> p (c t)")
        otE = [mixflat[:, 0:4096].bitcast(F32), mixflat[:, 4096:8192].bitcast(F32)]
        junkE = mixflat[:, 8192:10240]

        def final_block(tb):
            col = 8 + tb % 2
            c = rms_rstd(h2[:, tb, :], junkE, col, h2_keys(tb), "junkE", extra_writes=cT_all)
            ot = otE[tb % 2]
            if tb % 2 == 0 or tb == 7:
                P.op("dve", lambda e: e.scalar_tensor_tensor(out=ot, in0=h2[:, tb, :], scalar=c, in1=gb[:], op0=ALU.mult, op1=ALU.mult),
                     reads=h2_keys(tb) + [("st1", col), "gb"], writes=[("ot", tb % 2)] + cT_all)
            else:
                P.op("act", lambda e: e.activation(out=ot, in_=h2[:, tb, :], func=AF.Copy, scale=c),
                     reads=h2_keys(tb) + [("st1", col)], writes=[("ot", tb % 2)] + cT_all)
                P.op("pool", lambda e: e.tensor_tensor(out=ot, in0=ot, in1=gb[:], op=ALU.mult),
                     reads=[("ot", tb % 2), "gb"], writes=[("ot", tb % 2)])
            dma_sp(y[tb * 128:(tb + 1) * 128, :], ot, reads=[("ot", tb % 2)], stream="st%d" % (tb % 2))

        def wdown_group(gi, dslots, last=False):
            gs = gi % 2
            for tb in range(8):
                b0 = 4 if tb % 2 == 0 else 0
                for jj in range(GRP):
                    s, v = dslots[jj]
                    for n in range(4):
                        bk = b0 + n
                        P.op("pe", lambda e, jj=jj, v=v, tb=tb, n=n, bk=bk, gs=gs: e.matmul(
                            bank(bk), lhsT=gated[gs][jj][:, tb * 128:(tb + 1) * 128], rhs=v[:, n * 512:(n + 1) * 512],
                            start=(jj == 0), stop=(jj == GRP - 1)),
                            reads=[("gated", gs, jj)] + rkeys(s), writes=[("ps", bk)])
                for n in range(4):
                    bk = b0 + n
                    P.op("dve", lambda e, tb=tb, n=n, bk=bk: e.tensor_tensor(out=h2[:, tb, n * 512:(n + 1) * 512], in0=h2[:, tb, n * 512:(n + 1) * 512], in1=bank(bk), op=ALU.add),
                         reads=[("ps", bk), ("h2", tb, n)], writes=[("h2", tb, n)])
                if last:
                    final_block(tb)

        def load_down(gi):
            dslots = []
            for jj in range(GRP):
                j = gi * GRP + jj
                s = next_slot()
                dma_cast(ring[s][:, :], w_down[j * 128:(j + 1) * 128, :], writes=rkeys(s), stream="ring%d" % s)
                dslots.append((s, ring[s]))
            return dslots

        prev = None
        pair_i = 0
        for gi in range(NFF // GRP):
            gs = gi % 2
            for jj in range(GRP):
                j = gi * GRP + jj
                pi = pair_i % 2
                pair_i += 1
                for half in range(2):
                    cidx = half * NFF + j
                    s, v = pre_up[half] if j == 0 else load_slice(w_up_v, half * DFF + j * 128)
                    Y = (Yg if half == 0 else Yv)[pi]
                    yk = ("Y", half, pi)
                    for (r0, r1) in nblk:
                        ln = r1 - r0
                        bk = ub[0] % 4
                        ub[0] += 1
                        for kc in range(16):
                            P.op("pe", lambda e, kc=kc, v=v, r0=r0, ln=ln, bk=bk: e.matmul(bank(bk)[:, 0:ln + 2], lhsT=v[:, kc, :], rhs=cT[:, kc, r0:r0 + ln + 2],
                                                                                    start=(kc == 0), stop=(kc == 15)),
                                 reads=cT_all + rkeys(s), writes=[("ps", bk)])
                        u = bank(bk)
                        P.op("act", lambda e, u=u, Y=Y, r0=r0, r1=r1, ln=ln, cidx=cidx: e.activation(
                            out=Y[:, r0:r1], in_=u[:, 2:ln + 2], func=AF.Identity, bias=convb_s[:, cidx:cidx + 1], scale=convw_s[:, cidx * 3 + 2:cidx * 3 + 3]),
                            reads=[("ps", bk), "convw", "convb"], writes=[yk])
                        P.op("dve", lambda e, u=u, Y=Y, r0=r0, r1=r1, ln=ln, cidx=cidx: e.scalar_tensor_tensor(
                            out=Y[:, r0:r1], in0=u[:, 1:ln + 1], scalar=convw_s[:, cidx * 3 + 1:cidx * 3 + 2], in1=Y[:, r0:r1], op0=ALU.mult, op1=ALU.add),
                            reads=[("ps", bk), "convw", yk], writes=[yk])
                        P.op("dve", lambda e, u=u, Y=Y, r0=r0, r1=r1, ln=ln, cidx=cidx: e.scalar_tensor_tensor(
                            out=Y[:, r0:r1], in0=u[:, 0:ln], scalar=convw_s[:, cidx * 3:cidx * 3 + 1], in1=Y[:, r0:r1], op0=ALU.mult, op1=ALU.add),
                            reads=[("ps", bk), "convw", yk], writes=[yk])
                    if half == 0:
                        P.op("act", lambda e, Y=Y, pi=pi: e.activation(out=Sg[pi], in_=Y, func=AF.Silu), reads=[yk], writes=[("Sg", pi)])
                    else:
                        P.op("dve", lambda e, Y=Y, pi=pi, gs=gs, jj=jj: e.tensor_tensor(out=gated[gs][jj], in0=Y, in1=Sg[pi], op=ALU.mult),
                             reads=[yk, ("Sg", pi)], writes=[("gated", gs, jj)])
            if prev is not None:
                wdown_group(prev, load_down(prev))
            prev = gi
        dma_sp(gb[:], gf.partition_broadcast(128), writes=["gb"], stream="gb")
        wdown_group(prev, load_down(prev), last=True)

        P.emit(final_wait_streams="st")
    return nc


_NC_CACHE = {}


def _consts():
    c = {}
    c["c_ident"] = np.eye(128, dtype=np.float32)
    c["c_tri"] = np.triu(np.ones((128, 128), np.float32))
    c["c_ones"] = np.ones((128, 128), np.float32)
    c["c_maskR"] = np.triu(np.ones((128, 128), np.float32))
    p = np.arange(128)[:, None]
    xx = np.arange(MASKW)[None, :]
    c["c_maskT"] = np.where(xx - XOFF < p, NEG, 0.0).astype(np.float32)
    sel = np.zeros((128, 8, 128), np.float32)
    for h in range(8):
        sel[h, h, :] = 1.0
    c["c_sel"] = sel.reshape(128, 1024)
    return c


def _core_tables(T0):
    l = np.arange(LT)
    t = l - 1152 + T0
    valid = t >= 0
    inv_freq = 1.0 / (10000.0 ** (np.arange(0, 128, 2, dtype=np.float64) / 128.0))
    ang = np.where(valid, t, 0)[:, None].astype(np.float64) * inv_freq[None, :]
    def pm(a):
        n = a.shape[1]
        return np.ascontiguousarray(a.reshape(NB, 128, n).transpose(1, 0, 2).reshape(128, NB * n))
    tabs = {"cosT": pm(np.cos(ang).astype(np.float32)), "sinT": pm(np.sin(ang).astype(np.float32))}
    log_g = np.log1p(-np.exp2(-5.0 - np.arange(8, dtype=np.float64)))
    rel = (l - 1152).astype(np.float64)
    tabs["dqT"] = pm(np.exp(rel[:, None] * log_g[None, :]).astype(np.float32))
    tabs["dkT"] = pm((np.exp(-rel[:, None] * log_g[None, :]) * SCALE).astype(np.float32))
    tabs["kbT"] = pm(np.repeat(np.where(valid, 0.0, NEG).astype(np.float32)[:, None], 8, axis=1))
    return tabs


def kernel(x, meta_tokens, norm1_gain, w_in, b_forget, ret_norm_gain, w_out, norm2_gain, w_up,
           conv_w, conv_b, w_down, final_norm_gain):
    f32 = np.float32
    x = np.asarray(x, f32)
    B = x.shape[0]
    if "nc" not in _NC_CACHE:
        _NC_CACHE["nc"] = build_nc()
    nc = _NC_CACHE["nc"]
    consts = _consts()
    shared = {
        "w_in": np.ascontiguousarray(np.asarray(w_in, f32)[0]),
        "w_out": np.ascontiguousarray(np.asarray(w_out, f32)[0]),
        "w_up": np.ascontiguousarray(np.asarray(w_up, f32)[0]),
        "w_down": np.ascontiguousarray(np.asarray(w_down, f32)[0]),
        "g1": np.ascontiguousarray(np.asarray(norm1_gain, f32)[0]),
        "g2": np.ascontiguousarray(np.asarray(norm2_gain, f32)[0]),
        "gf": np.ascontiguousarray(np.asarray(final_norm_gain, f32)),
        "rng": np.ascontiguousarray(np.asarray(ret_norm_gain, f32)[0]),
        "wffd": np.ascontiguousarray(np.asarray(w_in, f32)[0][:, 7168:7176].reshape(16, 128, 8).transpose(1, 0, 2).reshape(128, 128)),
        "bfg": np.ascontiguousarray(np.tile(np.asarray(b_forget, f32)[0], NB)),
        "convw": np.ascontiguousarray(np.asarray(conv_w, f32)[0].reshape(3, 2 * NFF, 128).transpose(2, 1, 0).reshape(128, 2 * NFF * 3)),
        "convb": np.ascontiguousarray(np.asarray(conv_b, f32)[0].reshape(2 * NFF, 128).T),
    }
    shared.update(consts)
    meta = np.asarray(meta_tokens, f32)
    in_maps = []
    for core in range(8):
        b, s = core // 2, core % 2
        T0 = 16 + 1024 * s
        full = np.concatenate([meta, x[b]], axis=0)
        xl = np.zeros((LT, D), f32)
        t_lo = T0 - 1152
        src_lo = max(t_lo, 0)
        xl[src_lo - t_lo:, :] = full[src_lo:T0 + 1024]
        m = dict(shared)
        m["xl"] = xl
        m.update(_core_tables(T0))
        in_maps.append(m)
    res = run_bass_kernel_spmd(nc, in_maps[:NCORES_RUN], core_ids=list(range(NCORES_RUN)))
    out = np.zeros((B, 2048, D), f32)
    for core in range(NCORES_RUN):
        b, s = core // 2, core % 2
        out[b, 1024 * s:1024 * (s + 1), :] = res.results[core]["y"]
    if DEBUG:
        kernel.debug = res.results
    return out
```

```python
import contextlib
import numpy as np
import concourse.bass as bass
import concourse.mybir as mybir
from concourse.bass_utils import run_bass_kernel_spmd

F32 = mybir.dt.float32
BF16 = mybir.dt.bfloat16
AF = mybir.ActivationFunctionType
ALU = mybir.AluOpType
AX = mybir.AxisListType

D = 2048
NB = 17
LT = NB * 128
OWN0 = 1150
NOWN = 1026
NH = 8
DFF = 5632
NFF = 44
IN_DIM = 7176
SCALE = 128 ** -0.5
EPS = 1e-6
NEG = -30000.0
XOFF = 300
MASKW = 768
NSLOT = 8
GRP = 4
DEBUG = False
STOP = None
NHEADS_RUN = 8
NCORES_RUN = 8
STOP2 = None
SKIP = set()

ENGS = ("sp", "act", "pool", "dve", "pe")
SEM_LIMIT = 12000
WAIT_ALL_STREAMS = ("const", "constp")


class _Op:
    __slots__ = ("eng", "fn", "deps", "signal", "stream", "sem_i", "val", "inc", "idx", "batch")


class Prog:
    def __init__(self, nc):
        self.nc = nc
        self.ops = []
        self.eng_ops = {e: [] for e in ENGS}
        self.last_w = {}
        self.readers = {}
        self.last_in_stream = {}
        self.barrier_deps = set()

    def op(self, eng, fn, reads=(), writes=(), dma=None, batch=None):
        o = _Op()
        o.batch = batch
        o.eng = eng
        o.fn = fn
        o.idx = len(self.ops)
        o.stream = ("dma", dma) if dma is not None else ("eng", eng)
        o.inc = 16 if dma is not None else 1
        o.signal = dma is not None
        deps = set(self.barrier_deps)
        writes = list(writes) + [k for k in reads if isinstance(k, tuple) and k[0] == "ps"]
        reads = [k for k in reads if not (isinstance(k, tuple) and k[0] == "ps")]
        for k in reads:
            w = self.last_w.get(k)
            if w is not None:
                deps.add(w)
        for k in writes:
            w = self.last_w.get(k)
            if w is not None:
                deps.add(w)
            for r in self.readers.get(k, ()):
                deps.add(r)
        o.deps = deps
        for k in reads:
            self.readers.setdefault(k, []).append(o.idx)
        for k in writes:
            self.last_w[k] = o.idx
            self.readers[k] = []
        self.ops.append(o)
        self.eng_ops[eng].append(o)
        self.last_in_stream[o.stream] = o.idx
        return o

    def barrier(self):
        self.barrier_deps = set(self.last_in_stream.values())

    def emit(self, final_wait_streams=()):
        nc = self.nc
        ops = self.ops
        for o in ops:
            for d in o.deps:
                p = ops[d]
                if p.stream == ("eng", "pe") and o.eng == "pe":
                    continue
                p.signal = True
        streams = {}
        for o in ops:
            if not o.signal:
                continue
            st = streams.setdefault(o.stream, {"n": 0, "cur": 0})
            if st["cur"] + o.inc > SEM_LIMIT:
                st["n"] += 1
                st["cur"] = 0
            st["cur"] += o.inc
            o.sem_i = (o.stream, st["n"])
            o.val = st["cur"]
        batch_max = {}
        for o in ops:
            if o.signal and o.batch is not None:
                k = (o.sem_i, o.batch)
                batch_max[k] = max(batch_max.get(k, 0), o.val)
        sem_keys = []
        seen = set()
        for o in ops:
            if o.signal and o.sem_i not in seen:
                seen.add(o.sem_i)
                sem_keys.append(o.sem_i)
        with contextlib.ExitStack() as es:
            sems = {}
            for i, k in enumerate(sem_keys):
                sems[k] = es.enter_context(nc.semaphore("s%d" % i))
            last_val = {}
            for o in ops:
                if o.signal:
                    last_val[o.sem_i] = max(last_val.get(o.sem_i, 0), o.val)
            block = es.enter_context(nc.Block())
            handles = {"sp": block.sync, "act": block.scalar, "pool": block.gpsimd,
                       "dve": block.vector, "pe": block.tensor}

            def make(engname):
                def body(eng):
                    waited = {}
                    for o in self.eng_ops[engname]:
                        need = {}
                        for d in o.deps:
                            p = ops[d]
                            if not p.signal:
                                continue
                            if p.stream == ("eng", "pe") and engname == "pe":
                                continue
                            v_ = last_val[p.sem_i] if (p.stream[0] == "dma" and p.stream[1] in WAIT_ALL_STREAMS) else p.val
                            if p.batch is not None:
                                v_ = batch_max[(p.sem_i, p.batch)]
                            if v_ > need.get(p.sem_i, 0):
                                need[p.sem_i] = v_
                        for k, v in need.items():
                            if waited.get(k, 0) < v:
                                eng.wait_ge(sems[k], v)
                                waited[k] = v
                        ins = o.fn(eng)
                        if o.signal:
                            ins.then_inc(sems[o.sem_i], o.inc)
                    if engname == "sp":
                        for k in sem_keys:
                            if k[0][0] == "dma" and k[0][1].startswith(final_wait_streams):
                                eng.wait_ge(sems[k], last_val[k])
                return body

            for e in ENGS:
                handles[e](make(e))
        return len(ops)


def build_nc():
    nc = bass.Bass("TRN2", target_bir_lowering=False)

    def din(name, shape):
        return nc.dram_tensor(name, list(shape), F32, kind="ExternalInput").ap()

    xl = din("xl", [LT, D])
    w_in = din("w_in", [D, IN_DIM])
    w_out = din("w_out", [D, D])
    w_up = din("w_up", [D, 2 * DFF])
    w_down = din("w_down", [DFF, D])
    g1 = din("g1", [D]); g2 = din("g2", [D]); gf = din("gf", [D])
    rng = din("rng", [1024])
    bfg = din("bfg", [NB * 8])
    convw = din("convw", [128, 2 * NFF * 3])
    convb = din("convb", [128, 2 * NFF])
    cosT = din("cosT", [128, NB * 64]); sinT = din("sinT", [128, NB * 64])
    dkT = din("dkT", [128, NB * 8]); dqT = din("dqT", [128, NB * 8]); kbT = din("kbT", [128, NB * 8])
    wffd = din("wffd", [128, 16 * 8])
    c_ident = din("c_ident", [128, 128]); c_tri = din("c_tri", [128, 128]); c_ones = din("c_ones", [128, 128])
    c_maskR = din("c_maskR", [128, 128]); c_maskT = din("c_maskT", [128, MASKW]); c_sel = din("c_sel", [128, 1024])
    y = nc.dram_tensor("y", [1024, D], F32, kind="ExternalOutput").ap()
    dbg = {}
    if DEBUG:
        dbg["d_aT"] = nc.dram_tensor("d_aT", [128, 16 * LT], BF16, kind="ExternalOutput").ap()
        dbg["d_mix"] = nc.dram_tensor("d_mix", [128, 16 * NOWN], BF16, kind="ExternalOutput").ap()
        dbg["d_h1"] = nc.dram_tensor("d_h1", [128, 8 * D], F32, kind="ExternalOutput").ap()
        dbg["d_cum"] = nc.dram_tensor("d_cum", [128, NB * 8], F32, kind="ExternalOutput").ap()
        dbg["d_cT"] = nc.dram_tensor("d_cT", [128, 16 * NOWN], BF16, kind="ExternalOutput").ap()
        dbg["d_hh"] = nc.dram_tensor("d_hh", [2, D], F32, kind="ExternalOutput").ap()

    w_in_v = w_in.rearrange("(c p) n -> p c n", p=128)
    w_out_v = w_out.rearrange("(c p) n -> p c n", p=128)
    w_up_v = w_up.rearrange("(c p) n -> p c n", p=128)

    with contextlib.ExitStack() as es:
        def sb(name, shape, dt):
            return es.enter_context(nc.sbuf_tensor(name, list(shape), dt))

        def ps(name, shape, dt):
            return es.enter_context(nc.psum_tensor(name, list(shape), dt))

        R1 = sb("R1", [128, 16 * LT], BF16)
        aT = R1[:, :].rearrange("p (c t) -> p c t", c=16)
        R1f = R1.bitcast(F32)
        h2 = R1f[:, 0:8 * D].rearrange("p (b f) -> p b f", b=8)
        mixT = sb("mixT", [128, 16, NOWN], BF16)
        ringT = sb("ringT", [128, NSLOT * 2048], BF16)
        ring = [ringT[:, i * 2048:(i + 1) * 2048] for i in range(NSLOT)]
        R3N = 20992
        R3 = sb("R3", [128, R3N], BF16)
        R3f = R3.bitcast(F32)
        gb = sb("gb", [128, D], F32)
        ident_b = sb("ident_b", [128, 128], BF16)
        ones_b = sb("ones_b", [128, 128], BF16)
        maskT = sb("maskT", [128, MASKW], BF16)
        sel = sb("sel", [128, 1024], BF16)
        ident_f = sb("ident_f", [128, 128], F32)
        tri_f = sb("tri_f", [128, 128], F32)
        ones_f = sb("ones_f", [128, 128], F32)
        maskR = sb("maskR", [128, 128], F32)
        convw_s = sb("convw_s", [128, 2 * NFF * 3], F32)
        convb_s = sb("convb_s", [128, 2 * NFF], F32)
        bfg_s = sb("bfg_s", [128, NB * 8], F32)
        kb_s = sb("kb_s", [128, NB, 8], F32)
        cos_s = sb("cos_s", [128, NB, 64], F32)
        sin_s = sb("sin_s", [128, NB, 64], F32)
        dk_s = sb("dk_s", [128, NB, 8], F32)
        dq_s = sb("dq_s", [128, NB, 8], F32)
        ndk_s = sb("ndk_s", [128, NB, 8], F32)
        ndq_s = sb("ndq_s", [128, NB, 8], F32)
        spt = sb("spt", [128, NB * 8], F32)
        cumn = sb("cumn", [128, NB * 8], F32)
        tot = sb("tot", [128, NB * 8], F32)
        pre = sb("pre", [128, NB * 8], F32)
        biasK = sb("biasK", [128, NB * 8], F32)
        Rb = sb("Rb", [128, 9 * 128], BF16)
        wff = sb("wff", [128, 16, 8], BF16)
        st1 = sb("st1", [128, 32], F32)
        eps_t = sb("eps_t", [128, 1], F32)
        st2 = sb("st2", [128, 64], F32)

        pp = [ps("pp%d" % i, [128, 1024], F32) for i in range(4)]
        ppb = [p.bitcast(BF16) for p in pp]

        def bank(i):
            return pp[i // 2][:, (i % 2) * 512:(i % 2) * 512 + 512]

        P = Prog(nc)
        slot_ctr = [0]

        def next_slot():
            s = slot_ctr[0] % NSLOT
            slot_ctr[0] += 1
            return s

        def dma_sp(out, in_, reads=(), writes=(), stream="const"):
            P.op("sp", lambda e: e.dma_start(out=out, in_=in_), reads=reads, writes=writes, dma=stream)

        def dma_cast(out, in_, reads=(), writes=(), stream="constp", batch=None):
            P.op("pool", lambda e: e.dma_start(out=out, in_=in_), reads=reads, writes=writes, dma=stream, batch=batch)

        def load_slice(wv, c0, ncols=128):
            s = next_slot()
            v = ring[s][:, 0:16 * ncols].rearrange("p (c n) -> p c n", c=16)
            for hh in range(2):
                dma_cast(v[:, 8 * hh:8 * hh + 8, :], wv[:, 8 * hh:8 * hh + 8, c0:c0 + ncols], writes=[("ring", s, hh)], stream="ring%d" % s,
                         batch=slot_ctr[0])
            return s, v

        def rkeys(s):
            return [("ring", s, 0), ("ring", s, 1)]

        dma_sp(gb[:], g1.partition_broadcast(128), writes=["gb"], stream="gb")
        P.op("dve", lambda e: e.memset(eps_t[:], EPS), writes=["eps_t"])
        P.op("dve", lambda e: e.memset(pre[:], 0.0), writes=["pre"])
        dma_cast(ident_b[:], c_ident, writes=["ident_b"])
        dma_cast(wff[:, :, :].rearrange("p c n -> p (c n)"), wffd, writes=[("wff", 0), ("wff", 1)])
        dma_cast(ones_b[:], c_ones, writes=["ones_b"])
        dma_cast(maskT[:], c_maskT, writes=["maskT"])
        dma_cast(sel[:], c_sel, writes=["sel"])

        def late_constants():
            dma_sp(cos_s[:, :, :].rearrange("p b d -> p (b d)"), cosT, writes=["cos"])
            dma_sp(sin_s[:, :, :].rearrange("p b d -> p (b d)"), sinT, writes=["sin"])
            dma_sp(dk_s[:, :, :].rearrange("p b h -> p (b h)"), dkT, writes=["dk"])
            dma_sp(dq_s[:, :, :].rearrange("p b h -> p (b h)"), dqT, writes=["dq"])
            P.op("dve", lambda e: e.tensor_scalar_mul(out=ndk_s[:, :, :], in0=dk_s[:, :, :], scalar1=-1.0), reads=["dk"], writes=["ndk"])
            P.op("dve", lambda e: e.tensor_scalar_mul(out=ndq_s[:, :, :], in0=dq_s[:, :, :], scalar1=-1.0), reads=["dq"], writes=["ndq"])
            dma_sp(ident_f[:], c_ident, writes=["ident_f"])
            dma_sp(tri_f[:], c_tri, writes=["tri_f"])
            dma_sp(ones_f[:], c_ones, writes=["ones_f"])
            dma_sp(maskR[:], c_maskR, writes=["maskR"])
            dma_sp(bfg_s[:], bfg.partition_broadcast(128), writes=["bfg"])
            dma_sp(kb_s[:, :, :].rearrange("p b h -> p (b h)"), kbT, writes=["kb"])
            dma_sp(convw_s[:], convw, writes=["convw"])
            dma_sp(convb_s[:], convb, writes=["convb"])

        def rms_rstd(src_ap, junk_ap, col, rkeys_, jkey, extra_writes=()):
            npart = src_ap.shape[0]
            c = st1[0:npart, col:col + 1]
            P.op("act", lambda e: e.activation(out=junk_ap, in_=src_ap, func=AF.Square, accum_out=c),
                 reads=rkeys_, writes=[jkey, ("st1", col)] + list(extra_writes))
            P.op("act", lambda e: e.activation(out=c, in_=c, func=AF.Ln, bias=eps_t[0:npart, 0:1], scale=1.0 / D),
                 reads=[("st1", col), "eps_t"], writes=[("st1", col)])
            P.op("act", lambda e: e.activation(out=c, in_=c, func=AF.Exp, scale=-0.5), reads=[("st1", col)], writes=[("st1", col)])
            return c

        xsA = [R3f[:, 0:2048], R3f[:, 2048:4096], R3f[:, 4096:6144]]
        xnA = [R3[:, 12288:14336], R3[:, 14336:16384]]
        junkA = R3[:, 16384:18432]
        b6 = bank(6)

        def aTk(tb):
            return [("aT", tb, 0), ("aT", tb, 1)]

        def A1(tb):
            xs = xsA[tb % 3]
            dma_sp(xs, xl[tb * 128:(tb + 1) * 128, :], writes=[("xs", tb % 3)], stream="xs%d" % (tb % 3))
            rms_rstd(xs, junkA, tb % 3, [("xs", tb % 3)], "junkA")

        def A2(tb):
            xs = xsA[tb % 3]
            c = st1[:, tb % 3:tb % 3 + 1]
            xn = xnA[tb % 2]
            P.op("dve", lambda e: e.scalar_tensor_tensor(out=xn, in0=xs, scalar=c, in1=gb[:], op0=ALU.mult, op1=ALU.mult),
                 reads=[("xs", tb % 3), ("st1", tb % 3), "gb"], writes=[("xnA", tb % 2)])

        def A3(tb):
            xn = xnA[tb % 2]
            pv = ppb[tb % 2]
            for cc in range(16):
                P.op("pe", lambda e, cc=cc: e.transpose(out=pv[:, cc * 128:(cc + 1) * 128], in_=xn[:, cc * 128:(cc + 1) * 128], identity=ident_b[:]),
                     reads=[("xnA", tb % 2), "ident_b"], writes=[("ps", 2 * (tb % 2)), ("ps", 2 * (tb % 2) + 1)])
            pv3 = pv[:, 0:2048].rearrange("p (c t) -> p c t", c=16)
            P.op("act", lambda e: e.copy(out=aT[:, 0:8, tb * 128:(tb + 1) * 128], in_=pv3[:, 0:8, :]),
                 reads=[("ps", 2 * (tb % 2))], writes=[("aT", tb, 0)])
            P.op("dve", lambda e: e.tensor_copy(out=aT[:, 8:16, tb * 128:(tb + 1) * 128], in_=pv3[:, 8:16, :]),
                 reads=[("ps", 2 * (tb % 2) + 1)], writes=[("aT", tb, 1)])

        def A4(tb):
            for kc in range(16):
                P.op("pe", lambda e, kc=kc: e.matmul(b6[:, tb * 8:tb * 8 + 8], lhsT=aT[:, kc, tb * 128:(tb + 1) * 128], rhs=wff[:, kc, :],
                                                    start=(kc == 0), stop=(kc == 15)),
                     reads=aTk(tb) + [("wff", 0), ("wff", 1)], writes=[("ps", 6)])

        for i in range(NB + 3):
            if i == 3:
                late_constants()
            if i < NB:
                A1(i)
            if 0 <= i - 1 < NB:
                A2(i - 1)
            if 0 <= i - 2 < NB:
                A3(i - 2)
            if 0 <= i - 3 < NB:
                A4(i - 3)


        def aT_range_keys(l0, l1):
            ks = []
            for tb in range(l0 // 128, (l1 - 1) // 128 + 1):
                ks += aTk(tb)
            return ks

        if DEBUG:
            dma_sp(dbg["d_aT"], R1[:, :], reads=aT_range_keys(0, LT), stream="st")

        if STOP == "A":
            P.emit(final_wait_streams="st")
            return nc
        P.barrier()
        dma_sp(gb[:, 0:1024], rng.partition_broadcast(128), writes=["gb"], stream="gb")

        b7 = bank(7)
        P.op("dve", lambda e: e.tensor_tensor(out=spt[:], in0=b6[:, 0:NB * 8], in1=bfg_s[:], op=ALU.add), reads=[("ps", 6), "bfg"], writes=["spt"])
        P.op("act", lambda e: e.activation(out=spt[:], in_=spt[:], func=AF.Exp, scale=-1.0), reads=["spt"], writes=["spt"])
        P.op("act", lambda e: e.activation(out=spt[:], in_=spt[:], func=AF.Ln, bias=1.0, scale=1.0), reads=["spt"], writes=["spt"])
        P.op("pe", lambda e: e.matmul(b7[:, 0:136], lhsT=tri_f[:], rhs=spt[:], start=True, stop=True), reads=["spt", "tri_f"], writes=[("ps", 7)])
        P.op("pe", lambda e: e.matmul(b7[:, 136:272], lhsT=ones_f[:], rhs=spt[:], start=True, stop=True), reads=["spt", "ones_f"], writes=[("ps", 7)])
        P.op("dve", lambda e: e.tensor_copy(out=cumn[:], in_=b7[:, 0:136]), reads=[("ps", 7)], writes=["cumn"])
        P.op("dve", lambda e: e.tensor_copy(out=tot[:], in_=b7[:, 136:272]), reads=[("ps", 7)], writes=["tot"])
        for b in range(1, NB):
            P.op("dve", lambda e, b=b: e.tensor_tensor(out=pre[:, b * 8:b * 8 + 8], in0=pre[:, (b - 1) * 8:b * 8], in1=tot[:, (b - 1) * 8:b * 8], op=ALU.add),
                 reads=["pre", "tot"], writes=["pre"])
        P.op("dve", lambda e: e.tensor_tensor(out=cumn[:], in0=cumn[:], in1=pre[:], op=ALU.add), reads=["cumn", "pre"], writes=["cumn"])
        P.op("dve", lambda e: e.tensor_tensor(out=biasK[:], in0=cumn[:], in1=kb_s[:, :, :].rearrange("p b h -> p (b h)"), op=ALU.add),
             reads=["cumn", "kb"], writes=["biasK"])
        for c in range(9):
            tb = 8 + c
            dst = pp[2][0:8, c * 128:(c + 1) * 128] if c < 8 else pp[3][0:8, 512:640]
            P.op("pe", lambda e, dst=dst, tb=tb: e.transpose(out=dst, in_=cumn[:, tb * 8:tb * 8 + 8], identity=ident_f[:]),
                 reads=["cumn", "ident_f"], writes=[("ps", 4), ("ps", 5)] if c < 8 else [("ps", 7)])
        P.op("dve", lambda e: e.memset(Rb[:], 0.0), writes=["Rb"])
        P.op("act", lambda e: e.activation(out=Rb[0:8, 0:1024], in_=pp[2][0:8, 0:1024], func=AF.Copy, scale=-1.0 / SCALE),
             reads=[("ps", 4), ("ps", 5)], writes=["Rb"])
        P.op("act", lambda e: e.activation(out=Rb[0:8, 1024:1152], in_=pp[3][0:8, 512:640], func=AF.Copy, scale=-1.0 / SCALE),
             reads=[("ps", 7)], writes=["Rb"])
        if DEBUG:
            dma_sp(dbg["d_cum"], cumn[:], reads=["cumn"], stream="st")
        if STOP == "B0":
            P.emit(final_wait_streams="st")
            return nc

        o_ = 0
        def carve(n):
            nonlocal o_
            a = o_
            o_ += n
            return a
        fkT = R3[:, carve(LT):o_]
        _a = carve(1028)
        fqT = R3[:, _a:_a + NOWN]
        rvfv = R3[:, carve(NB * 256):o_].rearrange("p (b n) -> p b n", b=NB)
        rkt = R3[:, carve(NB * 128):o_].rearrange("p (b n) -> p b n", b=NB)
        rqT = R3[:, carve(1152):o_]
        rkT = R3[:, carve(1152):o_]
        sg = R3[:, carve(1152):o_].rearrange("p (b n) -> p b n", b=9)
        rqt = R3[:, carve(1152):o_].rearrange("p (b n) -> p b n", b=9)
        PTt = [R3[:, carve(342):o_] for _ in range(3)]
        smt = [R3[:, carve(128):o_] for _ in range(2)]
        Sbf = [R3[:, carve(128):o_] for _ in range(2)]
        junkB = R3[:, carve(128):o_]
        assert o_ % 2 == 0
        fo = o_ // 2
        def carvef(n):
            nonlocal fo
            a = fo
            fo += n
            return a
        o_all = R3f[:, carvef(1152):fo].rearrange("p (b n) -> p b n", b=9)
        rden = R3f[:, carvef(342):fo]
        rU = R3f[:, carvef(128):fo]
        rW = R3f[:, carvef(128):fo]
        ro = R3[:, fo * 2:fo * 2 + 1152].rearrange("p (b n) -> p b n", b=9)
        assert fo * 2 + 1152 <= R3N, fo * 2

        def rotary(src, dst, tbl, dec_ap, ndec_ap, rk, wk):
            Cb = cos_s[:, tbl:tbl + 1, :].to_broadcast([128, 2, 64])
            S = sin_s[:, tbl, :]
            src3 = src[:, 0:128].rearrange("p (a b) -> p a b", a=2)
            U3 = rU.rearrange("p (a b) -> p a b", a=2)
            t1 = src[:, 0:64]
            t2 = src[:, 64:128]
            rd = list(rk) + ["cos", "sin", "dk", "dq", "ndk", "ndq"]
            P.op("dve", lambda e: e.scalar_tensor_tensor(out=U3, in0=src3, scalar=dec_ap, in1=Cb, op0=ALU.mult, op1=ALU.mult), reads=rd, writes=["rU"])
            P.op("dve", lambda e: e.scalar_tensor_tensor(out=rW[:, 0:64], in0=t2, scalar=ndec_ap, in1=S, op0=ALU.mult, op1=ALU.mult), reads=rd, writes=["rW"])
            P.op("dve", lambda e: e.scalar_tensor_tensor(out=rW[:, 64:128], in0=t1, scalar=dec_ap, in1=S, op0=ALU.mult, op1=ALU.mult), reads=rd + ["rW"], writes=["rW"])
            P.op("dve", lambda e: e.tensor_tensor(out=dst, in0=rU, in1=rW, op=ALU.add), reads=["rU", "rW"], writes=wk)

        vA = ringT[:, 0:6144].rearrange("p (c n) -> p c n", c=16)
        vB = ringT[:, 6144:10240].rearrange("p (c n) -> p c n", c=16)
        vC = ringT[:, 10240:12288].rearrange("p (c n) -> p c n", c=16)
        vD = ringT[:, 12288:14336].rearrange("p (c n) -> p c n", c=16)
        regions = {"A": (vA, 3), "B": (vB, 2), "C": (vC, 1), "D": (vD, 1)}

        def rg_keys(name):
            return [("rg" + name, si, hh) for si in range(regions[name][1]) for hh in range(2)]

        def load_region(name, col_offs, h):
            v, _ = regions[name]
            for si, c0 in enumerate(col_offs):
                for hh in range(2):
                    dma_cast(v[:, 8 * hh:8 * hh + 8, si * 128:(si + 1) * 128], w_in_v[:, 8 * hh:8 * hh + 8, c0:c0 + 128],
                             writes=[("rg" + name, si, hh)], stream="rg" + name, batch=h)

        all_rg = [k for nm in ("A", "B", "C", "D") for k in rg_keys(nm)]

        def load_wout_quarter(qp, extra=()):
            sl = []
            for m in range(4):
                s_ = next_slot()
                v_ = ring[s_].rearrange("p (c n) -> p c n", c=4)
                dma_cast(v_, w_out_v[:, 4 * m:4 * m + 4, qp * 512:(qp + 1) * 512], writes=rkeys(s_) + list(extra), stream="ring%d" % s_)
                sl.append((s_, v_))
            return sl
        wq = {}
        deferred_tail = [None]
        pending_tail_ops = []

        def load_head(h):
            load_region("A", [1024 + h * 128, 2048 + h * 128, 6144 + h * 128], h)
            load_region("B", [h * 128, 3072 + h * 128], h)
            load_region("C", [4096 + h * 128], h)
            load_region("D", [5120 + h * 128], h)
        load_head(0)
        for h in range(NHEADS_RUN):
            for tb in range(NB):
                bA = 2 * (tb % 2)
                bB = bA + 1
                for kc in range(16):
                    P.op("pe", lambda e, tb=tb, kc=kc, bA=bA: e.matmul(
                        bank(bA)[:, 0:384], lhsT=aT[:, kc, tb * 128:(tb + 1) * 128], rhs=vA[:, kc, :],
                        start=(kc == 0), stop=(kc == 15)),
                        reads=aTk(tb) + rg_keys("A"), writes=[("ps", bA)])
                    if tb >= 8:
                        P.op("pe", lambda e, tb=tb, kc=kc, bB=bB: e.matmul(
                            bank(bB)[:, 0:256], lhsT=aT[:, kc, tb * 128:(tb + 1) * 128], rhs=vB[:, kc, :],
                            start=(kc == 0), stop=(kc == 15)),
                            reads=aTk(tb) + rg_keys("B"), writes=[("ps", bB)])
                P.op("act", lambda e, tb=tb, bA=bA: e.copy(out=rvfv[:, tb, :], in_=bank(bA)[:, 128:384]), reads=[("ps", bA)], writes=[("rvfv", tb)])
                if tb >= 8:
                    P.op("act", lambda e, tb=tb, bB=bB: e.copy(out=sg[:, tb - 8, :], in_=bank(bB)[:, 128:256]), reads=[("ps", bB)], writes=["sg"])
                rotary(bank(bA), rkt[:, tb, :], tb, dk_s[:, tb, h:h + 1], ndk_s[:, tb, h:h + 1], [("ps", bA)], [("rkt", tb)])
                for _ in range(4 if tb < 7 else 99):
                    if pending_tail_ops and tb < 8:
                        pending_tail_ops.pop(0)()
                if tb >= 8:
                    rotary(bank(bB), rqt[:, tb - 8, :], tb, dq_s[:, tb, h:h + 1], ndq_s[:, tb, h:h + 1], [("ps", bB)], [("rqt", tb - 8)])
            P.op("act", lambda e: e.activation(out=sg[:, :, :], in_=sg[:, :, :], func=AF.Silu), reads=["sg"], writes=["sg"])
            if deferred_tail[0] is not None:
                deferred_tail[0]()
                deferred_tail[0] = None
            def qk_transposes():
                for c in range(9):
                    P.op("pe", lambda e, c=c: e.transpose(out=ppb[0][:, c * 128:(c + 1) * 128], in_=rqt[:, c, :], identity=ident_b[:]),
                         reads=[("rqt", c), "ident_b"], writes=[("ps", 0), ("ps", 1)])
                    P.op("pe", lambda e, c=c: e.transpose(out=ppb[1][:, c * 128:(c + 1) * 128], in_=rkt[:, 8 + c, :], identity=ident_b[:]),
                         reads=[("rkt", 8 + c), "ident_b"], writes=[("ps", 2), ("ps", 3)])
                P.op("dve", lambda e: e.tensor_copy(out=rqT, in_=ppb[0][:, 0:1152]), reads=[("ps", 0), ("ps", 1)], writes=["rqT"])
                P.op("dve", lambda e: e.tensor_copy(out=rkT, in_=ppb[1][:, 0:1152]), reads=[("ps", 2), ("ps", 3)], writes=["rkT"])

            for nb in range(5):
                if nb == 3:
                    qk_transposes()
                n0 = nb * 512
                nw = min(512, LT - n0)
                bk = 4 + nb % 2
                for kc in range(16):
                    P.op("pe", lambda e, kc=kc, n0=n0, nw=nw, bk=bk: e.matmul(bank(bk)[:, 0:nw], lhsT=vD[:, kc, :], rhs=aT[:, kc, n0:n0 + nw],
                                                                          start=(kc == 0), stop=(kc == 15)),
                         reads=aT_range_keys(n0, n0 + nw) + rg_keys("D"), writes=[("ps", bk)])
                P.op("dve", lambda e, n0=n0, nw=nw, bk=bk: e.tensor_copy(out=fkT[:, n0:n0 + nw], in_=bank(bk)[:, 0:nw]), reads=[("ps", bk)], writes=[("fkT", nb)])
            for g in range(3):
                n0 = OWN0 + 342 * g
                bk = 4 + (g + 1) % 2
                for kc in range(16):
                    P.op("pe", lambda e, kc=kc, n0=n0, bk=bk: e.matmul(bank(bk)[:, 0:342], lhsT=vC[:, kc, :], rhs=aT[:, kc, n0:n0 + 342],
                                                                    start=(kc == 0), stop=(kc == 15)),
                         reads=aT_range_keys(n0, n0 + 342) + rg_keys("C"), writes=[("ps", bk)])
                P.op("dve", lambda e, g=g, bk=bk: e.tensor_copy(out=fqT[:, 342 * g:342 * g + 342], in_=bank(bk)[:, 0:342]), reads=[("ps", bk)], writes=[("fqT", g)])

            if h + 1 < NHEADS_RUN:
                load_head(h + 1)
            else:
                wq[0] = load_wout_quarter(0, all_rg)
                wq[1] = load_wout_quarter(1, all_rg)
            Sps = bank(7)[:, 0:128]
            ret_steps = []

            def r_init():
                for b in range(8):
                    P.op("pe", lambda e, b=b: e.matmul(Sps, lhsT=rkt[:, b, :], rhs=rvfv[:, b, 0:128], start=(b == 0), stop=(b == 7), skip_group_check=True),
                         reads=[("rkt", b), ("rvfv", b)], writes=[("ps", 7)])
            ret_steps.append(r_init)

            def r_a(c):
                sTp = bank(6)[:, (c % 2) * 128:(c % 2) * 128 + 128]
                if c > 0:
                    P.op("pe", lambda e: e.matmul(Sps, lhsT=rkt[:, 7 + c, :], rhs=rvfv[:, 7 + c, 0:128], start=False, stop=True, skip_group_check=True),
                         reads=[("rkt", 7 + c), ("rvfv", 7 + c)], writes=[("ps", 7)])
                P.op("dve", lambda e: e.tensor_copy(out=Sbf[c % 2], in_=Sps), reads=[("ps", 7)], writes=[("Sbf", c % 2)])
                P.op("pe", lambda e: e.matmul(sTp, lhsT=rkT[:, c * 128:(c + 1) * 128], rhs=rqT[:, c * 128:(c + 1) * 128], start=True, stop=True),
                     reads=["rkT", "rqT"], writes=[("ps", 6)])
                P.op("dve", lambda e: e.tensor_tensor(out=smt[c % 2], in0=sTp, in1=maskR[:], op=ALU.mult),
                     reads=[("ps", 6), "maskR"], writes=[("smt", c % 2)])

            def r_b(c):
                op_ = bank(4)[:, (c % 2) * 128:(c % 2) * 128 + 128]
                P.op("pe", lambda e: e.matmul(op_, lhsT=rqT[:, c * 128:(c + 1) * 128], rhs=Sbf[c % 2], start=True, stop=False),
                     reads=["rqT", ("Sbf", c % 2)], writes=[("ps", 4)])
                P.op("pe", lambda e: e.matmul(op_, lhsT=smt[c % 2], rhs=rvfv[:, 8 + c, 0:128], start=False, stop=True),
                     reads=[("smt", c % 2), ("rvfv", 8 + c)], writes=[("ps", 4)])
                P.op("dve", lambda e: e.tensor_copy(out=o_all[:, c, :], in_=op_), reads=[("ps", 4)], writes=[("o_all", c)])
                P.op("act", lambda e: e.activation(out=junkB, in_=op_, func=AF.Square, accum_out=st2[:, 16 + c:17 + c]),
                     reads=[("ps", 4)], writes=["junkB", ("st2q", c)])
            for c in range(9):
                ret_steps.append(lambda c=c: r_a(c))
                ret_steps.append(lambda c=c: r_b(c))

            def r_tail(h=h):
                oall_keys = [("o_all", c) for c in range(9)]
                sq_keys = [("st2q", c) for c in range(9)]
                mean = st2[:, 0:9]
                ssq = st2[:, 16:25]
                msq = st2[:, 32:41]
                rstd = st2[:, 48:57]
                P.op("dve", lambda e: e.reduce_sum(out=mean, in_=o_all[:, :, :], axis=AX.X), reads=oall_keys, writes=["st2m"])
                P.op("dve", lambda e: e.tensor_scalar_mul(out=mean, in0=mean, scalar1=1.0 / 128), reads=["st2m"], writes=["st2m"])
                P.op("dve", lambda e: e.tensor_tensor(out=msq, in0=mean, in1=mean, op=ALU.mult), reads=["st2m"], writes=["st2s"])
                P.op("dve", lambda e: e.tensor_scalar(out=rstd, in0=ssq, scalar1=1.0 / 128, scalar2=EPS, op0=ALU.mult, op1=ALU.add), reads=sq_keys, writes=["st2r"])
                P.op("dve", lambda e: e.tensor_tensor(out=rstd, in0=rstd, in1=msq, op=ALU.subtract), reads=["st2r", "st2s"], writes=["st2r"])
                P.op("act", lambda e: e.activation(out=rstd, in_=rstd, func=AF.Ln), reads=["st2r"], writes=["st2r"])
                P.op("act", lambda e: e.activation(out=rstd, in_=rstd, func=AF.Exp, scale=-0.5), reads=["st2r"], writes=["st2r"])
                ops_ = []
                for c in range(9):
                    ops_.append(lambda c=c: P.op("dve", lambda e: e.tensor_scalar(out=o_all[:, c, :], in0=o_all[:, c, :], scalar1=st2[:, c:c + 1], scalar2=st2[:, 48 + c:49 + c],
                                                                              op0=ALU.subtract, op1=ALU.mult),
                                                 reads=[("o_all", c), "st2m", "st2r"], writes=[("o_all", c)]))
                    ops_.append(lambda c=c: P.op("dve", lambda e: e.tensor_tensor(out=o_all[:, c, :], in0=o_all[:, c, :], in1=gb[:, h * 128:(h + 1) * 128], op=ALU.mult),
                                                 reads=[("o_all", c), "gb"], writes=[("o_all", c)]))
                    ops_.append(lambda c=c: P.op("dve", lambda e: e.tensor_tensor(out=ro[:, c, :], in0=o_all[:, c, :], in1=sg[:, c, :], op=ALU.mult),
                                                 reads=[("o_all", c), "sg"], writes=[("ro", c)]))
                return ops_

            def r_tail_pe(h=h):
                for c in range(9):
                    P.op("pe", lambda e, c=c: e.transpose(out=ppb[3][:, c * 128:(c + 1) * 128], in_=ro[:, c, :], identity=ident_b[:]),
                         reads=[("ro", c), "ident_b"], writes=[("ps", 6), ("ps", 7)])
                P.op("act", lambda e: e.copy(out=mixT[:, h, :], in_=ppb[3][:, 126:1152]), reads=[("ps", 6), ("ps", 7)], writes=[("mixh", h)])

            tiles = []
            for g in range(3):
                q0 = OWN0 + 342 * g
                kmax = (q0 + 342 - 1) // 128
                for kb in range(kmax + 1):
                    tiles.append((g, kb, kmax, q0))
            oTp = bank(2)[:, 0:342]
            dnp = bank(3)[:, 0:342]

            def f_s(ti, h=h):
                g, kb, kmax, q0 = tiles[ti]
                sbk = (0, 1, 5)[ti % 3]
                sb_ = bank(sbk)[:, 0:342]
                delta = 128 * kb - q0
                need_mask = (128 * kb + 127) > q0
                P.op("pe", lambda e: e.matmul(sb_, lhsT=fkT[:, kb * 128:(kb + 1) * 128], rhs=fqT[:, 342 * g:342 * g + 342], start=True, stop=False),
                     reads=[("fkT", kb // 4), ("fqT", g)], writes=[("ps", sbk)])
                P.op("pe", lambda e: e.matmul(sb_, lhsT=sel[:, h * 128:(h + 1) * 128], rhs=Rb[:, 126 + 342 * g:126 + 342 * g + 342],
                                              start=False, stop=(not need_mask)),
                     reads=["sel", "Rb"], writes=[("ps", sbk)])
                if need_mask:
                    off = XOFF - delta
                    assert 0 <= off and off + 342 <= MASKW, off
                    P.op("pe", lambda e: e.matmul(sb_, lhsT=ident_b[:], rhs=maskT[:, off:off + 342], start=False, stop=True),
                         reads=["ident_b", "maskT"], writes=[("ps", sbk)])
                pt = PTt[ti % 3]
                P.op("act", lambda e: e.activation(out=pt, in_=sb_, func=AF.Exp, bias=biasK[:, kb * 8 + h:kb * 8 + h + 1], scale=SCALE),
                     reads=[("ps", sbk), "biasK"], writes=[("PT", ti % 3)])

            def f_pv(ti, h=h):
                g, kb, kmax, q0 = tiles[ti]
                pt = PTt[ti % 3]
                ptk = ("PT", ti % 3)
                P.op("pe", lambda e: e.matmul(oTp, lhsT=rvfv[:, kb, 128:256], rhs=pt, start=(kb == 0), stop=(kb == kmax)),
                     reads=[("rvfv", kb), ptk], writes=[("ps", 2)])
                P.op("pe", lambda e: e.matmul(dnp, lhsT=ones_b[:], rhs=pt, start=(kb == 0), stop=(kb == kmax)),
                     reads=["ones_b", ptk], writes=[("ps", 3)])
                if kb == kmax:
                    P.op("dve", lambda e: e.reciprocal(out=rden, in_=dnp), reads=[("ps", 3)], writes=["rden"])
                    P.op("dve", lambda e: e.tensor_tensor(out=mixT[:, 8 + h, 342 * g:342 * g + 342], in0=oTp, in1=rden, op=ALU.mult),
                         reads=[("ps", 2), "rden"], writes=[("mixf", h, g)])

            fox_steps = []
            nt_ = len(tiles)
            def f_first():
                f_s(0)
                f_s(1)
            fox_steps.append(f_first)
            for ti in range(nt_):
                def st(ti=ti):
                    if ti + 2 < nt_:
                        f_s(ti + 2)
                    f_pv(ti)
                fox_steps.append(st)
            fi = 0
            last_head = (h == NHEADS_RUN - 1)
            per = [1, 1] if last_head else [3, 2]
            for ri, rs in enumerate(ret_steps):
                rs()
                k = 1 if ri == 0 else per[ri % 2]
                for _ in range(k):
                    if fi < len(fox_steps):
                        fox_steps[fi]()
                        fi += 1
            if last_head:
                for f_ in r_tail():
                    f_()
            while fi < len(fox_steps):
                fox_steps[fi]()
                fi += 1
            if not last_head:
                pending_tail_ops[:] = r_tail()
            deferred_tail[0] = r_tail_pe
        deferred_tail[0]()


        mix_all = [("mixh", h) for h in range(NH)] + [("mixf", h, g) for h in range(NH) for g in range(3)]
        if DEBUG:
            dma_sp(dbg["d_mix"], mixT[:, :, :].rearrange("p c t -> p (c t)"), reads=mix_all, stream="st")
        if STOP == "B":
            P.emit(final_wait_streams="st")
            return nc

        P.barrier()

        dma_sp(gb[:], g2.partition_broadcast(128), writes=["gb"], stream="gb")
        xsC = [R3f[:, 0:512], R3f[:, 512:1024], R3f[:, 1024:1536]]
        hnC = [R3[:, 4096:6144], R3[:, 6144:8192]]
        junkC = R3[:, 8192:10240]
        hh = R3f[:, 6144:8192]
        blocks = [(-1, 0, OWN0)] + [(tb, 2 + 128 * tb, 1152 + 128 * tb) for tb in range(8)]
        xi = 0
        ubk = 0

        def c_norm1(bi, tb):
            src = hh[:, :] if tb < 0 else h2[:, tb, :]
            col = 4 + bi % 2
            hk = [("h1", bi, q) for q in range(4)]
            c = rms_rstd(src, junkC, col, hk, "junkC")
            hn = hnC[bi % 2]
            P.op("dve", lambda e: e.scalar_tensor_tensor(out=hn, in0=src, scalar=c, in1=gb[:], op0=ALU.mult, op1=ALU.mult),
                 reads=hk + [("st1", col), "gb"], writes=[("hnC", bi % 2)])

        def c_norm2(bi, tb, c0):
            hn = hnC[bi % 2]
            pv = ppb[2 + bi % 2]
            pk = [("ps", 4 + 2 * (bi % 2)), ("ps", 5 + 2 * (bi % 2))]
            for cc in range(16):
                P.op("pe", lambda e, cc=cc: e.transpose(out=pv[:, cc * 128:(cc + 1) * 128], in_=hn[:, cc * 128:(cc + 1) * 128], identity=ident_b[:]),
                     reads=[("hnC", bi % 2), "ident_b"], writes=pk)
            pv3 = pv[:, 0:2048].rearrange("p (c t) -> p c t", c=16)
            if tb < 0:
                P.op("act", lambda e: e.copy(out=mixT[:, :, 0:2], in_=pv3[:, :, 0:2]), reads=pk + mix_all, writes=[("cT", bi)])
            else:
                P.op("act", lambda e: e.copy(out=mixT[:, 0:8, c0:c0 + 128], in_=pv3[:, 0:8, :]), reads=pk[0:1] + mix_all, writes=[("cT", bi)])
                P.op("dve", lambda e: e.tensor_copy(out=mixT[:, 8:16, c0:c0 + 128], in_=pv3[:, 8:16, :]), reads=pk[1:2] + mix_all, writes=[("cTb", bi)])

        for qp in range(4):
            slots = wq[qp]
            pend = None
            for bi, (tb, c0, l0) in enumerate(blocks):
                xs = xsC[xi % 3]
                xk = ("xs", xi % 3)
                xstream = "xs%d" % (xi % 3)
                xi += 1
                dma_sp(xs, xl[l0:l0 + 128, qp * 512:(qp + 1) * 512], writes=[xk], stream=xstream)
                bk = ubk % 4
                ubk += 1
                ck = [("cT", bi), ("cTb", bi)] + ([("cT", 1), ("cTb", 1)] if tb < 0 else [])
                for kc in range(16):
                    s_, v_ = slots[kc // 4]
                    P.op("pe", lambda e, kc=kc, v_=v_, c0=c0, bk=bk: e.matmul(
                        bank(bk), lhsT=mixT[:, kc, c0:c0 + 128], rhs=v_[:, kc % 4, :], start=(kc == 0), stop=(kc == 15)),
                        reads=mix_all + ck + rkeys(s_), writes=[("ps", bk)])
                dst = hh[:, qp * 512:(qp + 1) * 512] if tb < 0 else h2[:, tb, qp * 512:(qp + 1) * 512]
                P.op("dve", lambda e, dst=dst, bk=bk, xs=xs: e.tensor_tensor(out=dst, in0=bank(bk), in1=xs, op=ALU.add),
                     reads=[("ps", bk), xk], writes=[("h1", bi, qp)])
                if qp == 3:
                    c_norm1(bi, tb)
                    if pend is not None:
                        c_norm2(*pend)
                    pend = (bi, tb, c0)
            if qp == 3:
                c_norm2(*pend)
            if qp + 2 < 4:
                wq[qp + 2] = load_wout_quarter(qp + 2)

        cT = mixT
        cT_all = [("cT", bi) for bi in range(9)] + [("cTb", bi) for bi in range(1, 9)]
        if DEBUG:
            dma_sp(dbg["d_h1"], R1f[:, 0:8 * D], reads=[("h1", bi, hp) for bi in range(1, 9) for hp in range(4)], stream="st")
        if DEBUG:
            dma_sp(dbg["d_cT"], mixT[:, :, :].rearrange("p c t -> p (c t)"), reads=cT_all, stream="st")
            dma_sp(dbg["d_hh"], hh[0:2, :], reads=[("h1", 0, q) for q in range(4)], stream="st")
        if STOP == "C":
            P.emit(final_wait_streams="st")
            return nc
        pre_up = [load_slice(w_up_v, half * DFF) for half in range(2)]
        P.barrier()

        gated = [[R3[:, (gs * GRP + jj) * 1024:(gs * GRP + jj + 1) * 1024] for jj in range(GRP)] for gs in range(2)]
        fb = 2 * GRP * 1024 // 2
        Yg = [R3f[:, fb + i * 1024:fb + (i + 1) * 1024] for i in range(2)]
        Yv = [R3f[:, fb + 2048 + i * 1024:fb + 2048 + (i + 1) * 1024] for i in range(2)]
        sb0 = 2 * (fb + 4096)
        Sg = [R3[:, sb0 + i * 1024:sb0 + (i + 1) * 1024] for i in range(2)]
        assert sb0 + 2048 <= R3N
        nblk = [(0, 342), (342, 683), (683, 1024)]
        ub = [0]

        h2_keys = lambda tb: [("h2", tb, n) for n in range(4)]
        mixflat = mixT[:, :, :].rearrange("p c t -> p (c t)")
        otE = [mixflat[:, 0:4096].bitcast(F32), mixflat[:, 4096:8192].bitcast(F32)]
        junkE = mixflat[:, 8192:10240]

        def final_block(tb):
            col = 8 + tb % 2
            c = rms_rstd(h2[:, tb, :], junkE, col, h2_keys(tb), "junkE", extra_writes=cT_all)
            ot = otE[tb % 2]
            if True:
                P.op("dve", lambda e: e.scalar_tensor_tensor(out=ot, in0=h2[:, tb, :], scalar=c, in1=gb[:], op0=ALU.mult, op1=ALU.mult),
                     reads=h2_keys(tb) + [("st1", col), "gb"], writes=[("ot", tb % 2)] + cT_all)
            else:
                P.op("act", lambda e: e.activation(out=ot, in_=h2[:, tb, :], func=AF.Copy, scale=c),
                     reads=h2_keys(tb) + [("st1", col)], writes=[("ot", tb % 2)] + cT_all)
                P.op("pool", lambda e: e.tensor_tensor(out=ot, in0=ot, in1=gb[:], op=ALU.mult),
                     reads=[("ot", tb % 2), "gb"], writes=[("ot", tb % 2)])
            dma_sp(y[tb * 128:(tb + 1) * 128, :], ot, reads=[("ot", tb % 2)], stream="st%d" % (tb % 2))

        def wdown_group(gi, dslots, last=False):
            gs = gi % 2
            for tb in range(8):
                b0 = 4 if tb % 2 == 0 else 0
                for jj in range(GRP):
                    s, v = dslots[jj]
                    for n in range(4):
                        bk = b0 + n
                        P.op("pe", lambda e, jj=jj, v=v, tb=tb, n=n, bk=bk, gs=gs: e.matmul(
                            bank(bk), lhsT=gated[gs][jj][:, tb * 128:(tb + 1) * 128], rhs=v[:, n * 512:(n + 1) * 512],
                            start=(jj == 0), stop=(jj == GRP - 1)),
                            reads=[("gated", gs, jj)] + rkeys(s), writes=[("ps", bk)])
                for n in range(4):
                    bk = b0 + n
                    P.op("dve", lambda e, tb=tb, n=n, bk=bk: e.tensor_tensor(out=h2[:, tb, n * 512:(n + 1) * 512], in0=h2[:, tb, n * 512:(n + 1) * 512], in1=bank(bk), op=ALU.add),
                         reads=[("ps", bk), ("h2", tb, n)], writes=[("h2", tb, n)])
                if last:
                    final_block(tb)

        def load_down(gi):
            dslots = []
            for jj in range(GRP):
                j = gi * GRP + jj
                s = next_slot()
                dma_cast(ring[s][:, :], w_down[j * 128:(j + 1) * 128, :], writes=rkeys(s), stream="ring%d" % s)
                dslots.append((s, ring[s]))
            return dslots

        prev = None
        pair_i = 0
        for gi in range(NFF // GRP):
            gs = gi % 2
            for jj in range(GRP):
                j = gi * GRP + jj
                pi = pair_i % 2
                pair_i += 1
                for half in range(2):
                    cidx = half * NFF + j
                    s, v = pre_up[half] if j == 0 else load_slice(w_up_v, half * DFF + j * 128)
                    Y = (Yg if half == 0 else Yv)[pi]
                    yk = ("Y", half, pi)
                    for (r0, r1) in nblk:
                        ln = r1 - r0
                        bk = ub[0] % 4
                        ub[0] += 1
                        for kc in range(16):
                            P.op("pe", lambda e, kc=kc, v=v, r0=r0, ln=ln, bk=bk: e.matmul(bank(bk)[:, 0:ln + 2], lhsT=v[:, kc, :], rhs=cT[:, kc, r0:r0 + ln + 2],
                                                                                    start=(kc == 0), stop=(kc == 15)),
                                 reads=cT_all + rkeys(s), writes=[("ps", bk)])
                        u = bank(bk)
                        P.op("act", lambda e, u=u, Y=Y, r0=r0, r1=r1, ln=ln, cidx=cidx: e.activation(
                            out=Y[:, r0:r1], in_=u[:, 2:ln + 2], func=AF.Identity, bias=convb_s[:, cidx:cidx + 1], scale=convw_s[:, cidx * 3 + 2:cidx * 3 + 3]),
                            reads=[("ps", bk), "convw", "convb"], writes=[yk])
                        P.op("dve", lambda e, u=u, Y=Y, r0=r0, r1=r1, ln=ln, cidx=cidx: e.scalar_tensor_tensor(
                            out=Y[:, r0:r1], in0=u[:, 1:ln + 1], scalar=convw_s[:, cidx * 3 + 1:cidx * 3 + 2], in1=Y[:, r0:r1], op0=ALU.mult, op1=ALU.add),
                            reads=[("ps", bk), "convw", yk], writes=[yk])
                        P.op("dve", lambda e, u=u, Y=Y, r0=r0, r1=r1, ln=ln, cidx=cidx: e.scalar_tensor_tensor(
                            out=Y[:, r0:r1], in0=u[:, 0:ln], scalar=convw_s[:, cidx * 3:cidx * 3 + 1], in1=Y[:, r0:r1], op0=ALU.mult, op1=ALU.add),
                            reads=[("ps", bk), "convw", yk], writes=[yk])
                    if half == 0:
                        P.op("act", lambda e, Y=Y, pi=pi: e.activation(out=Sg[pi], in_=Y, func=AF.Silu), reads=[yk], writes=[("Sg", pi)])
                    else:
                        P.op("dve", lambda e, Y=Y, pi=pi, gs=gs, jj=jj: e.tensor_tensor(out=gated[gs][jj], in0=Y, in1=Sg[pi], op=ALU.mult),
                             reads=[yk, ("Sg", pi)], writes=[("gated", gs, jj)])
            if prev is not None:
                wdown_group(prev, load_down(prev))
            prev = gi
        dma_sp(gb[:], gf.partition_broadcast(128), writes=["gb"], stream="gb")
        wdown_group(prev, load_down(prev), last=True)

        P.emit(final_wait_streams="st")
    return nc


_NC_CACHE = {}


def _consts():
    c = {}
    c["c_ident"] = np.eye(128, dtype=np.float32)
    c["c_tri"] = np.triu(np.ones((128, 128), np.float32))
    c["c_ones"] = np.ones((128, 128), np.float32)
    c["c_maskR"] = np.triu(np.ones((128, 128), np.float32))
    p = np.arange(128)[:, None]
    xx = np.arange(MASKW)[None, :]
    c["c_maskT"] = np.where(xx - XOFF < p, NEG, 0.0).astype(np.float32)
    sel = np.zeros((128, 8, 128), np.float32)
    for h in range(8):
        sel[h, h, :] = 1.0
    c["c_sel"] = sel.reshape(128, 1024)
    return c


def _core_tables(T0):
    l = np.arange(LT)
    t = l - 1152 + T0
    valid = t >= 0
    inv_freq = 1.0 / (10000.0 ** (np.arange(0, 128, 2, dtype=np.float64) / 128.0))
    ang = np.where(valid, t, 0)[:, None].astype(np.float64) * inv_freq[None, :]
    def pm(a):
        n = a.shape[1]
        return np.ascontiguousarray(a.reshape(NB, 128, n).transpose(1, 0, 2).reshape(128, NB * n))
    tabs = {"cosT": pm(np.cos(ang).astype(np.float32)), "sinT": pm(np.sin(ang).astype(np.float32))}
    log_g = np.log1p(-np.exp2(-5.0 - np.arange(8, dtype=np.float64)))
    rel = (l - 1152).astype(np.float64)
    tabs["dqT"] = pm(np.exp(rel[:, None] * log_g[None, :]).astype(np.float32))
    tabs["dkT"] = pm((np.exp(-rel[:, None] * log_g[None, :]) * SCALE).astype(np.float32))
    tabs["kbT"] = pm(np.repeat(np.where(valid, 0.0, NEG).astype(np.float32)[:, None], 8, axis=1))
    return tabs


def kernel(x, meta_tokens, norm1_gain, w_in, b_forget, ret_norm_gain, w_out, norm2_gain, w_up,
           conv_w, conv_b, w_down, final_norm_gain):
    f32 = np.float32
    x = np.asarray(x, f32)
    B = x.shape[0]
    if "nc" not in _NC_CACHE:
        _NC_CACHE["nc"] = build_nc()
    nc = _NC_CACHE["nc"]
    consts = _consts()
    shared = {
        "w_in": np.ascontiguousarray(np.asarray(w_in, f32)[0]),
        "w_out": np.ascontiguousarray(np.asarray(w_out, f32)[0]),
        "w_up": np.ascontiguousarray(np.asarray(w_up, f32)[0]),
        "w_down": np.ascontiguousarray(np.asarray(w_down, f32)[0]),
        "g1": np.ascontiguousarray(np.asarray(norm1_gain, f32)[0]),
        "g2": np.ascontiguousarray(np.asarray(norm2_gain, f32)[0]),
        "gf": np.ascontiguousarray(np.asarray(final_norm_gain, f32)),
        "rng": np.ascontiguousarray(np.asarray(ret_norm_gain, f32)[0]),
        "wffd": np.ascontiguousarray(np.asarray(w_in, f32)[0][:, 7168:7176].reshape(16, 128, 8).transpose(1, 0, 2).reshape(128, 128)),
        "bfg": np.ascontiguousarray(np.tile(np.asarray(b_forget, f32)[0], NB)),
        "convw": np.ascontiguousarray(np.asarray(conv_w, f32)[0].reshape(3, 2 * NFF, 128).transpose(2, 1, 0).reshape(128, 2 * NFF * 3)),
        "convb": np.ascontiguousarray(np.asarray(conv_b, f32)[0].reshape(2 * NFF, 128).T),
    }
    shared.update(consts)
    meta = np.asarray(meta_tokens, f32)
    in_maps = []
    for core in range(8):
        b, s = core // 2, core % 2
        T0 = 16 + 1024 * s
        full = np.concatenate([meta, x[b]], axis=0)
        xl = np.zeros((LT, D), f32)
        t_lo = T0 - 1152
        src_lo = max(t_lo, 0)
        xl[src_lo - t_lo:, :] = full[src_lo:T0 + 1024]
        m = dict(shared)
        m["xl"] = xl
        m.update(_core_tables(T0))
        in_maps.append(m)
    res = run_bass_kernel_spmd(nc, in_maps[:NCORES_RUN], core_ids=list(range(NCORES_RUN)))
    out = np.zeros((B, 2048, D), f32)
    for core in range(NCORES_RUN):
        b, s = core // 2, core % 2
        out[b, 1024 * s:1024 * (s + 1), :] = res.results[core]["y"]
    if DEBUG:
        kernel.debug = res.results
    return out
```

```python
import contextlib
import numpy as np
import concourse.bass as bass
import concourse.mybir as mybir
from concourse.bass_utils import run_bass_kernel_spmd

F32 = mybir.dt.float32
BF16 = mybir.dt.bfloat16
AF = mybir.ActivationFunctionType
ALU = mybir.AluOpType
AX = mybir.AxisListType

D = 2048
NB = 17
LT = NB * 128
OWN0 = 1150
NOWN = 1026
NH = 8
DFF = 5632
NFF = 44
IN_DIM = 7176
SCALE = 128 ** -0.5
EPS = 1e-6
NEG = -30000.0
XOFF = 300
MASKW = 768
NSLOT = 8
GRP = 4
DEBUG = False
STOP = None
NHEADS_RUN = 8
NCORES_RUN = 8
STOP2 = None
SKIP = set()

ENGS = ("sp", "act", "pool", "dve", "pe")
SEM_LIMIT = 12000
WAIT_ALL_STREAMS = ("const", "constp")


class _Op:
    __slots__ = ("eng", "fn", "deps", "signal", "stream", "sem_i", "val", "inc", "idx", "batch")


class Prog:
    def __init__(self, nc):
        self.nc = nc
        self.ops = []
        self.eng_ops = {e: [] for e in ENGS}
        self.last_w = {}
        self.readers = {}
        self.last_in_stream = {}
        self.barrier_deps = set()

    def op(self, eng, fn, reads=(), writes=(), dma=None, batch=None):
        o = _Op()
        o.batch = batch
        o.eng = eng
        o.fn = fn
        o.idx = len(self.ops)
        o.stream = ("dma", dma) if dma is not None else ("eng", eng)
        o.inc = 16 if dma is not None else 1
        o.signal = dma is not None
        deps = set(self.barrier_deps)
        writes = list(writes) + [k for k in reads if isinstance(k, tuple) and k[0] == "ps"]
        reads = [k for k in reads if not (isinstance(k, tuple) and k[0] == "ps")]
        for k in reads:
            w = self.last_w.get(k)
            if w is not None:
                deps.add(w)
        for k in writes:
            w = self.last_w.get(k)
            if w is not None:
                deps.add(w)
            for r in self.readers.get(k, ()):
                deps.add(r)
        o.deps = deps
        for k in reads:
            self.readers.setdefault(k, []).append(o.idx)
        for k in writes:
            self.last_w[k] = o.idx
            self.readers[k] = []
        self.ops.append(o)
        self.eng_ops[eng].append(o)
        self.last_in_stream[o.stream] = o.idx
        return o

    def barrier(self):
        self.barrier_deps = set(self.last_in_stream.values())

    def emit(self, final_wait_streams=()):
        nc = self.nc
        ops = self.ops
        for o in ops:
            for d in o.deps:
                p = ops[d]
                if p.stream == ("eng", "pe") and o.eng == "pe":
                    continue
                p.signal = True
        streams = {}
        for o in ops:
            if not o.signal:
                continue
            st = streams.setdefault(o.stream, {"n": 0, "cur": 0})
            if st["cur"] + o.inc > SEM_LIMIT:
                st["n"] += 1
                st["cur"] = 0
            st["cur"] += o.inc
            o.sem_i = (o.stream, st["n"])
            o.val = st["cur"]
        batch_max = {}
        for o in ops:
            if o.signal and o.batch is not None:
                k = (o.sem_i, o.batch)
                batch_max[k] = max(batch_max.get(k, 0), o.val)
        sem_keys = []
        seen = set()
        for o in ops:
            if o.signal and o.sem_i not in seen:
                seen.add(o.sem_i)
                sem_keys.append(o.sem_i)
        with contextlib.ExitStack() as es:
            sems = {}
            for i, k in enumerate(sem_keys):
                sems[k] = es.enter_context(nc.semaphore("s%d" % i))
            last_val = {}
            for o in ops:
                if o.signal:
                    last_val[o.sem_i] = max(last_val.get(o.sem_i, 0), o.val)
            block = es.enter_context(nc.Block())
            handles = {"sp": block.sync, "act": block.scalar, "pool": block.gpsimd,
                       "dve": block.vector, "pe": block.tensor}

            def make(engname):
                def body(eng):
                    waited = {}
                    for o in self.eng_ops[engname]:
                        need = {}
                        for d in o.deps:
                            p = ops[d]
                            if not p.signal:
                                continue
                            if p.stream == ("eng", "pe") and engname == "pe":
                                continue
                            v_ = last_val[p.sem_i] if (p.stream[0] == "dma" and p.stream[1] in WAIT_ALL_STREAMS) else p.val
                            if p.batch is not None:
                                v_ = batch_max[(p.sem_i, p.batch)]
                            if v_ > need.get(p.sem_i, 0):
                                need[p.sem_i] = v_
                        for k, v in need.items():
                            if waited.get(k, 0) < v:
                                eng.wait_ge(sems[k], v)
                                waited[k] = v
                        ins = o.fn(eng)
                        if o.signal:
                            ins.then_inc(sems[o.sem_i], o.inc)
                    if engname == "sp":
                        for k in sem_keys:
                            if k[0][0] == "dma" and k[0][1].startswith(final_wait_streams):
                                eng.wait_ge(sems[k], last_val[k])
                return body

            for e in ENGS:
                handles[e](make(e))
        return len(ops)


def build_nc():
    nc = bass.Bass("TRN2", target_bir_lowering=False)

    def din(name, shape):
        return nc.dram_tensor(name, list(shape), F32, kind="ExternalInput").ap()

    xl = din("xl", [LT, D])
    w_in = din("w_in", [D, IN_DIM])
    w_out = din("w_out", [D, D])
    w_up = din("w_up", [D, 2 * DFF])
    w_down = din("w_down", [DFF, D])
    g1 = din("g1", [D]); g2 = din("g2", [D]); gf = din("gf", [D])
    rng = din("rng", [1024])
    bfg = din("bfg", [NB * 8])
    convw = din("convw", [128, 2 * NFF * 3])
    convb = din("convb", [128, 2 * NFF])
    cosT = din("cosT", [128, NB * 64]); sinT = din("sinT", [128, NB * 64])
    dkT = din("dkT", [128, NB * 8]); dqT = din("dqT", [128, NB * 8]); kbT = din("kbT", [128, NB * 8])
    wffd = din("wffd", [128, 16 * 8])
    c_ident = din("c_ident", [128, 128]); c_tri = din("c_tri", [128, 128]); c_ones = din("c_ones", [128, 128])
    c_maskR = din("c_maskR", [128, 128]); c_maskT = din("c_maskT", [128, MASKW]); c_sel = din("c_sel", [128, 1024])
    y = nc.dram_tensor("y", [1024, D], F32, kind="ExternalOutput").ap()
    dbg = {}
    if DEBUG:
        dbg["d_aT"] = nc.dram_tensor("d_aT", [128, 16 * LT], BF16, kind="ExternalOutput").ap()
        dbg["d_mix"] = nc.dram_tensor("d_mix", [128, 16 * NOWN], BF16, kind="ExternalOutput").ap()
        dbg["d_h1"] = nc.dram_tensor("d_h1", [128, 8 * D], F32, kind="ExternalOutput").ap()
        dbg["d_cum"] = nc.dram_tensor("d_cum", [128, NB * 8], F32, kind="ExternalOutput").ap()
        dbg["d_cT"] = nc.dram_tensor("d_cT", [128, 16 * NOWN], BF16, kind="ExternalOutput").ap()
        dbg["d_hh"] = nc.dram_tensor("d_hh", [2, D], F32, kind="ExternalOutput").ap()

    w_in_v = w_in.rearrange("(c p) n -> p c n", p=128)
    w_out_v = w_out.rearrange("(c p) n -> p c n", p=128)
    w_up_v = w_up.rearrange("(c p) n -> p c n", p=128)

    with contextlib.ExitStack() as es:
        def sb(name, shape, dt):
            return es.enter_context(nc.sbuf_tensor(name, list(shape), dt))

        def ps(name, shape, dt):
            return es.enter_context(nc.psum_tensor(name, list(shape), dt))

        R1 = sb("R1", [128, 16 * LT], BF16)
        aT = R1[:, :].rearrange("p (c t) -> p c t", c=16)
        R1f = R1.bitcast(F32)
        h2 = R1f[:, 0:8 * D].rearrange("p (b f) -> p b f", b=8)
        mixT = sb("mixT", [128, 16, NOWN], BF16)
        ringT = sb("ringT", [128, NSLOT * 2048], BF16)
        ring = [ringT[:, i * 2048:(i + 1) * 2048] for i in range(NSLOT)]
        R3N = 20992
        R3 = sb("R3", [128, R3N], BF16)
        R3f = R3.bitcast(F32)
        gb = sb("gb", [128, D], F32)
        ident_b = sb("ident_b", [128, 128], BF16)
        ones_b = sb("ones_b", [128, 128], BF16)
        maskT = sb("maskT", [128, MASKW], BF16)
        sel = sb("sel", [128, 1024], BF16)
        ident_f = sb("ident_f", [128, 128], F32)
        tri_f = sb("tri_f", [128, 128], F32)
        ones_f = sb("ones_f", [128, 128], F32)
        maskR = sb("maskR", [128, 128], F32)
        convw_s = sb("convw_s", [128, 2 * NFF * 3], F32)
        convb_s = sb("convb_s", [128, 2 * NFF], F32)
        bfg_s = sb("bfg_s", [128, NB * 8], F32)
        kb_s = sb("kb_s", [128, NB, 8], F32)
        cos_s = sb("cos_s", [128, NB, 64], F32)
        sin_s = sb("sin_s", [128, NB, 64], F32)
        dk_s = sb("dk_s", [128, NB, 8], F32)
        dq_s = sb("dq_s", [128, NB, 8], F32)
        ndk_s = sb("ndk_s", [128, NB, 8], F32)
        ndq_s = sb("ndq_s", [128, NB, 8], F32)
        spt = sb("spt", [128, NB * 8], F32)
        cumn = sb("cumn", [128, NB * 8], F32)
        tot = sb("tot", [128, NB * 8], F32)
        pre = sb("pre", [128, NB * 8], F32)
        biasK = sb("biasK", [128, NB * 8], F32)
        Rb = sb("Rb", [128, 9 * 128], BF16)
        wff = sb("wff", [128, 16, 8], BF16)
        st1 = sb("st1", [128, 32], F32)
        eps_t = sb("eps_t", [128, 1], F32)
        st2 = sb("st2", [128, 64], F32)

        pp = [ps("pp%d" % i, [128, 1024], F32) for i in range(4)]
        ppb = [p.bitcast(BF16) for p in pp]

        def bank(i):
            return pp[i // 2][:, (i % 2) * 512:(i % 2) * 512 + 512]

        P = Prog(nc)
        slot_ctr = [0]

        def next_slot():
            s = slot_ctr[0] % NSLOT
            slot_ctr[0] += 1
            return s

        def dma_sp(out, in_, reads=(), writes=(), stream="const"):
            P.op("sp", lambda e: e.dma_start(out=out, in_=in_), reads=reads, writes=writes, dma=stream)

        def dma_cast(out, in_, reads=(), writes=(), stream="constp", batch=None):
            P.op("pool", lambda e: e.dma_start(out=out, in_=in_), reads=reads, writes=writes, dma=stream, batch=batch)

        def load_slice(wv, c0, ncols=128):
            s = next_slot()
            v = ring[s][:, 0:16 * ncols].rearrange("p (c n) -> p c n", c=16)
            for hh in range(2):
                dma_cast(v[:, 8 * hh:8 * hh + 8, :], wv[:, 8 * hh:8 * hh + 8, c0:c0 + ncols], writes=[("ring", s, hh)], stream="ring%d" % s,
                         batch=slot_ctr[0])
            return s, v

        def rkeys(s):
            return [("ring", s, 0), ("ring", s, 1)]

        dma_sp(gb[:], g1.partition_broadcast(128), writes=["gb"], stream="gb")
        P.op("dve", lambda e: e.memset(eps_t[:], EPS), writes=["eps_t"])
        P.op("dve", lambda e: e.memset(pre[:], 0.0), writes=["pre"])
        dma_cast(ident_b[:], c_ident, writes=["ident_b"])
        dma_cast(wff[:, :, :].rearrange("p c n -> p (c n)"), wffd, writes=[("wff", 0), ("wff", 1)])
        dma_cast(ones_b[:], c_ones, writes=["ones_b"])
        dma_cast(maskT[:], c_maskT, writes=["maskT"])
        dma_cast(sel[:], c_sel, writes=["sel"])

        def late_constants():
            dma_sp(cos_s[:, :, :].rearrange("p b d -> p (b d)"), cosT, writes=["cos"])
            dma_sp(sin_s[:, :, :].rearrange("p b d -> p (b d)"), sinT, writes=["sin"])
            dma_sp(dk_s[:, :, :].rearrange("p b h -> p (b h)"), dkT, writes=["dk"])
            dma_sp(dq_s[:, :, :].rearrange("p b h -> p (b h)"), dqT, writes=["dq"])
            P.op("dve", lambda e: e.tensor_scalar_mul(out=ndk_s[:, :, :], in0=dk_s[:, :, :], scalar1=-1.0), reads=["dk"], writes=["ndk"])
            P.op("dve", lambda e: e.tensor_scalar_mul(out=ndq_s[:, :, :], in0=dq_s[:, :, :], scalar1=-1.0), reads=["dq"], writes=["ndq"])
            dma_sp(ident_f[:], c_ident, writes=["ident_f"])
            dma_sp(tri_f[:], c_tri, writes=["tri_f"])
            dma_sp(ones_f[:], c_ones, writes=["ones_f"])
            dma_sp(maskR[:], c_maskR, writes=["maskR"])
            dma_sp(bfg_s[:], bfg.partition_broadcast(128), writes=["bfg"])
            dma_sp(kb_s[:, :, :].rearrange("p b h -> p (b h)"), kbT, writes=["kb"])
            dma_sp(convw_s[:], convw, writes=["convw"])
            dma_sp(convb_s[:], convb, writes=["convb"])

        def rms_rstd(src_ap, junk_ap, col, rkeys_, jkey, extra_writes=()):
            npart = src_ap.shape[0]
            c = st1[0:npart, col:col + 1]
            P.op("act", lambda e: e.activation(out=junk_ap, in_=src_ap, func=AF.Square, accum_out=c),
                 reads=rkeys_, writes=[jkey, ("st1", col)] + list(extra_writes))
            P.op("act", lambda e: e.activation(out=c, in_=c, func=AF.Ln, bias=eps_t[0:npart, 0:1], scale=1.0 / D),
                 reads=[("st1", col), "eps_t"], writes=[("st1", col)])
            P.op("act", lambda e: e.activation(out=c, in_=c, func=AF.Exp, scale=-0.5), reads=[("st1", col)], writes=[("st1", col)])
            return c

        xsA = [R3f[:, 0:2048], R3f[:, 2048:4096], R3f[:, 4096:6144]]
        xnA = [R3[:, 12288:14336], R3[:, 14336:16384]]
        junkA = R3[:, 16384:18432]
        b6 = bank(6)

        def aTk(tb):
            return [("aT", tb, 0), ("aT", tb, 1)]

        def A1(tb):
            xs = xsA[tb % 3]
            dma_sp(xs, xl[tb * 128:(tb + 1) * 128, :], writes=[("xs", tb % 3)], stream="xs%d" % (tb % 3))
            rms_rstd(xs, junkA, tb % 3, [("xs", tb % 3)], "junkA")

        def A2(tb):
            xs = xsA[tb % 3]
            c = st1[:, tb % 3:tb % 3 + 1]
            xn = xnA[tb % 2]
            P.op("dve", lambda e: e.scalar_tensor_tensor(out=xn, in0=xs, scalar=c, in1=gb[:], op0=ALU.mult, op1=ALU.mult),
                 reads=[("xs", tb % 3), ("st1", tb % 3), "gb"], writes=[("xnA", tb % 2)])

        def A3(tb):
            xn = xnA[tb % 2]
            pv = ppb[tb % 2]
            for cc in range(16):
                P.op("pe", lambda e, cc=cc: e.transpose(out=pv[:, cc * 128:(cc + 1) * 128], in_=xn[:, cc * 128:(cc + 1) * 128], identity=ident_b[:]),
                     reads=[("xnA", tb % 2), "ident_b"], writes=[("ps", 2 * (tb % 2)), ("ps", 2 * (tb % 2) + 1)])
            pv3 = pv[:, 0:2048].rearrange("p (c t) -> p c t", c=16)
            P.op("act", lambda e: e.copy(out=aT[:, 0:8, tb * 128:(tb + 1) * 128], in_=pv3[:, 0:8, :]),
                 reads=[("ps", 2 * (tb % 2))], writes=[("aT", tb, 0)])
            P.op("dve", lambda e: e.tensor_copy(out=aT[:, 8:16, tb * 128:(tb + 1) * 128], in_=pv3[:, 8:16, :]),
                 reads=[("ps", 2 * (tb % 2) + 1)], writes=[("aT", tb, 1)])

        def A4(tb):
            for kc in range(16):
                P.op("pe", lambda e, kc=kc: e.matmul(b6[:, tb * 8:tb * 8 + 8], lhsT=aT[:, kc, tb * 128:(tb + 1) * 128], rhs=wff[:, kc, :],
                                                    start=(kc == 0), stop=(kc == 15)),
                     reads=aTk(tb) + [("wff", 0), ("wff", 1)], writes=[("ps", 6)])

        for i in range(NB + 3):
            if i == 3:
                late_constants()
            if i < NB:
                A1(i)
            if 0 <= i - 1 < NB:
                A2(i - 1)
            if 0 <= i - 2 < NB:
                A3(i - 2)
            if 0 <= i - 3 < NB:
                A4(i - 3)


        def aT_range_keys(l0, l1):
            ks = []
            for tb in range(l0 // 128, (l1 - 1) // 128 + 1):
                ks += aTk(tb)
            return ks

        if DEBUG:
            dma_sp(dbg["d_aT"], R1[:, :], reads=aT_range_keys(0, LT), stream="st")

        if STOP == "A":
            P.emit(final_wait_streams="st")
            return nc
        P.barrier()
        dma_sp(gb[:, 0:1024], rng.partition_broadcast(128), writes=["gb"], stream="gb")

        b7 = bank(7)
        P.op("dve", lambda e: e.tensor_tensor(out=spt[:], in0=b6[:, 0:NB * 8], in1=bfg_s[:], op=ALU.add), reads=[("ps", 6), "bfg"], writes=["spt"])
        P.op("act", lambda e: e.activation(out=spt[:], in_=spt[:], func=AF.Exp, scale=-1.0), reads=["spt"], writes=["spt"])
        P.op("act", lambda e: e.activation(out=spt[:], in_=spt[:], func=AF.Ln, bias=1.0, scale=1.0), reads=["spt"], writes=["spt"])
        P.op("pe", lambda e: e.matmul(b7[:, 0:136], lhsT=tri_f[:], rhs=spt[:], start=True, stop=True), reads=["spt", "tri_f"], writes=[("ps", 7)])
        P.op("pe", lambda e: e.matmul(b7[:, 136:272], lhsT=ones_f[:], rhs=spt[:], start=True, stop=True), reads=["spt", "ones_f"], writes=[("ps", 7)])
        P.op("dve", lambda e: e.tensor_copy(out=cumn[:], in_=b7[:, 0:136]), reads=[("ps", 7)], writes=["cumn"])
        P.op("dve", lambda e: e.tensor_copy(out=tot[:], in_=b7[:, 136:272]), reads=[("ps", 7)], writes=["tot"])
        for b in range(1, NB):
            P.op("dve", lambda e, b=b: e.tensor_tensor(out=pre[:, b * 8:b * 8 + 8], in0=pre[:, (b - 1) * 8:b * 8], in1=tot[:, (b - 1) * 8:b * 8], op=ALU.add),
                 reads=["pre", "tot"], writes=["pre"])
        P.op("dve", lambda e: e.tensor_tensor(out=cumn[:], in0=cumn[:], in1=pre[:], op=ALU.add), reads=["cumn", "pre"], writes=["cumn"])
        P.op("dve", lambda e: e.tensor_tensor(out=biasK[:], in0=cumn[:], in1=kb_s[:, :, :].rearrange("p b h -> p (b h)"), op=ALU.add),
             reads=["cumn", "kb"], writes=["biasK"])
        for c in range(9):
            tb = 8 + c
            dst = pp[2][0:8, c * 128:(c + 1) * 128] if c < 8 else pp[3][0:8, 512:640]
            P.op("pe", lambda e, dst=dst, tb=tb: e.transpose(out=dst, in_=cumn[:, tb * 8:tb * 8 + 8], identity=ident_f[:]),
                 reads=["cumn", "ident_f"], writes=[("ps", 4), ("ps", 5)] if c < 8 else [("ps", 7)])
        P.op("dve", lambda e: e.memset(Rb[:], 0.0), writes=["Rb"])
        P.op("act", lambda e: e.activation(out=Rb[0:8, 0:1024], in_=pp[2][0:8, 0:1024], func=AF.Copy, scale=-1.0 / SCALE),
             reads=[("ps", 4), ("ps", 5)], writes=["Rb"])
        P.op("act", lambda e: e.activation(out=Rb[0:8, 1024:1152], in_=pp[3][0:8, 512:640], func=AF.Copy, scale=-1.0 / SCALE),
             reads=[("ps", 7)], writes=["Rb"])
        if DEBUG:
            dma_sp(dbg["d_cum"], cumn[:], reads=["cumn"], stream="st")
        if STOP == "B0":
            P.emit(final_wait_streams="st")
            return nc

        o_ = 0
        def carve(n):
            nonlocal o_
            a = o_
            o_ += n
            return a
        fkT = R3[:, carve(LT):o_]
        _a = carve(1028)
        fqT = R3[:, _a:_a + NOWN]
        rvfv = R3[:, carve(NB * 256):o_].rearrange("p (b n) -> p b n", b=NB)
        rkt = R3[:, carve(NB * 128):o_].rearrange("p (b n) -> p b n", b=NB)
        rqT = R3[:, carve(1152):o_]
        rkT = R3[:, carve(1152):o_]
        sg = R3[:, carve(1152):o_].rearrange("p (b n) -> p b n", b=9)
        rqt = R3[:, carve(1152):o_].rearrange("p (b n) -> p b n", b=9)
        PTt = [R3[:, carve(342):o_] for _ in range(3)]
        smt = [R3[:, carve(128):o_] for _ in range(2)]
        Sbf = [R3[:, carve(128):o_] for _ in range(2)]
        junkB = R3[:, carve(128):o_]
        assert o_ % 2 == 0
        fo = o_ // 2
        def carvef(n):
            nonlocal fo
            a = fo
            fo += n
            return a
        o_all = R3f[:, carvef(1152):fo].rearrange("p (b n) -> p b n", b=9)
        rden = R3f[:, carvef(342):fo]
        rU = R3f[:, carvef(128):fo]
        rW = R3f[:, carvef(128):fo]
        ro = R3[:, fo * 2:fo * 2 + 1152].rearrange("p (b n) -> p b n", b=9)
        assert fo * 2 + 1152 <= R3N, fo * 2

        def rotary(src, dst, tbl, dec_ap, ndec_ap, rk, wk):
            Cb = cos_s[:, tbl:tbl + 1, :].to_broadcast([128, 2, 64])
            S = sin_s[:, tbl, :]
            src3 = src[:, 0:128].rearrange("p (a b) -> p a b", a=2)
            U3 = rU.rearrange("p (a b) -> p a b", a=2)
            t1 = src[:, 0:64]
            t2 = src[:, 64:128]
            rd = list(rk) + ["cos", "sin", "dk", "dq", "ndk", "ndq"]
            P.op("dve", lambda e: e.scalar_tensor_tensor(out=U3, in0=src3, scalar=dec_ap, in1=Cb, op0=ALU.mult, op1=ALU.mult), reads=rd, writes=["rU"])
            P.op("dve", lambda e: e.scalar_tensor_tensor(out=rW[:, 0:64], in0=t2, scalar=ndec_ap, in1=S, op0=ALU.mult, op1=ALU.mult), reads=rd, writes=["rW"])
            P.op("dve", lambda e: e.scalar_tensor_tensor(out=rW[:, 64:128], in0=t1, scalar=dec_ap, in1=S, op0=ALU.mult, op1=ALU.mult), reads=rd + ["rW"], writes=["rW"])
            P.op("dve", lambda e: e.tensor_tensor(out=dst, in0=rU, in1=rW, op=ALU.add), reads=["rU", "rW"], writes=wk)

        vA = ringT[:, 0:6144].rearrange("p (c n) -> p c n", c=16)
        vB = ringT[:, 6144:10240].rearrange("p (c n) -> p c n", c=16)
        vC = ringT[:, 10240:12288].rearrange("p (c n) -> p c n", c=16)
        vD = ringT[:, 12288:14336].rearrange("p (c n) -> p c n", c=16)
        regions = {"A": (vA, 3), "B": (vB, 2), "C": (vC, 1), "D": (vD, 1)}

        def rg_keys(name):
            return [("rg" + name, si, hh) for si in range(regions[name][1]) for hh in range(2)]

        def load_region(name, col_offs, h):
            v, _ = regions[name]
            for si, c0 in enumerate(col_offs):
                for hh in range(2):
                    dma_cast(v[:, 8 * hh:8 * hh + 8, si * 128:(si + 1) * 128], w_in_v[:, 8 * hh:8 * hh + 8, c0:c0 + 128],
                             writes=[("rg" + name, si, hh)], stream="rg" + name, batch=h)

        all_rg = [k for nm in ("A", "B", "C", "D") for k in rg_keys(nm)]

        def load_wout_quarter(qp, extra=()):
            sl = []
            for m in range(4):
                s_ = next_slot()
                v_ = ring[s_].rearrange("p (c n) -> p c n", c=4)
                dma_cast(v_, w_out_v[:, 4 * m:4 * m + 4, qp * 512:(qp + 1) * 512], writes=rkeys(s_) + list(extra), stream="ring%d" % s_)
                sl.append((s_, v_))
            return sl
        wq = {}
        deferred_tail = [None]
        pending_tail_ops = []

        def load_head(h):
            load_region("A", [1024 + h * 128, 2048 + h * 128, 6144 + h * 128], h)
            load_region("B", [h * 128, 3072 + h * 128], h)
            load_region("C", [4096 + h * 128], h)
            load_region("D", [5120 + h * 128], h)
        load_head(0)
        for h in range(NHEADS_RUN):
            for tb in range(NB):
                bA = 2 * (tb % 2)
                bB = bA + 1
                for kc in range(16):
                    P.op("pe", lambda e, tb=tb, kc=kc, bA=bA: e.matmul(
                        bank(bA)[:, 0:384], lhsT=aT[:, kc, tb * 128:(tb + 1) * 128], rhs=vA[:, kc, :],
                        start=(kc == 0), stop=(kc == 15)),
                        reads=aTk(tb) + rg_keys("A"), writes=[("ps", bA)])
                    if tb >= 8:
                        P.op("pe", lambda e, tb=tb, kc=kc, bB=bB: e.matmul(
                            bank(bB)[:, 0:256], lhsT=aT[:, kc, tb * 128:(tb + 1) * 128], rhs=vB[:, kc, :],
                            start=(kc == 0), stop=(kc == 15)),
                            reads=aTk(tb) + rg_keys("B"), writes=[("ps", bB)])
                P.op("act", lambda e, tb=tb, bA=bA: e.copy(out=rvfv[:, tb, :], in_=bank(bA)[:, 128:384]), reads=[("ps", bA)], writes=[("rvfv", tb)])
                if tb >= 8:
                    P.op("act", lambda e, tb=tb, bB=bB: e.copy(out=sg[:, tb - 8, :], in_=bank(bB)[:, 128:256]), reads=[("ps", bB)], writes=["sg"])
                rotary(bank(bA), rkt[:, tb, :], tb, dk_s[:, tb, h:h + 1], ndk_s[:, tb, h:h + 1], [("ps", bA)], [("rkt", tb)])
                for _ in range(4 if tb < 7 else 99):
                    if pending_tail_ops and tb < 8:
                        pending_tail_ops.pop(0)()
                if tb >= 8:
                    rotary(bank(bB), rqt[:, tb - 8, :], tb, dq_s[:, tb, h:h + 1], ndq_s[:, tb, h:h + 1], [("ps", bB)], [("rqt", tb - 8)])
            P.op("act", lambda e: e.activation(out=sg[:, :, :], in_=sg[:, :, :], func=AF.Silu), reads=["sg"], writes=["sg"])
            if deferred_tail[0] is not None:
                deferred_tail[0]()
                deferred_tail[0] = None
            def qk_transposes():
                for c in range(9):
                    P.op("pe", lambda e, c=c: e.transpose(out=ppb[0][:, c * 128:(c + 1) * 128], in_=rqt[:, c, :], identity=ident_b[:]),
                         reads=[("rqt", c), "ident_b"], writes=[("ps", 0), ("ps", 1)])
                    P.op("pe", lambda e, c=c: e.transpose(out=ppb[1][:, c * 128:(c + 1) * 128], in_=rkt[:, 8 + c, :], identity=ident_b[:]),
                         reads=[("rkt", 8 + c), "ident_b"], writes=[("ps", 2), ("ps", 3)])
                P.op("dve", lambda e: e.tensor_copy(out=rqT, in_=ppb[0][:, 0:1152]), reads=[("ps", 0), ("ps", 1)], writes=["rqT"])
                P.op("dve", lambda e: e.tensor_copy(out=rkT, in_=ppb[1][:, 0:1152]), reads=[("ps", 2), ("ps", 3)], writes=["rkT"])

            for nb in range(5):
                if nb == 3:
                    qk_transposes()
                n0 = nb * 512
                nw = min(512, LT - n0)
                bk = 4 + nb % 2
                for kc in range(16):
                    P.op("pe", lambda e, kc=kc, n0=n0, nw=nw, bk=bk: e.matmul(bank(bk)[:, 0:nw], lhsT=vD[:, kc, :], rhs=aT[:, kc, n0:n0 + nw],
                                                                          start=(kc == 0), stop=(kc == 15)),
                         reads=aT_range_keys(n0, n0 + nw) + rg_keys("D"), writes=[("ps", bk)])
                P.op("dve", lambda e, n0=n0, nw=nw, bk=bk: e.tensor_copy(out=fkT[:, n0:n0 + nw], in_=bank(bk)[:, 0:nw]), reads=[("ps", bk)], writes=[("fkT", nb)])
            for g in range(3):
                n0 = OWN0 + 342 * g
                bk = 4 + (g + 1) % 2
                for kc in range(16):
                    P.op("pe", lambda e, kc=kc, n0=n0, bk=bk: e.matmul(bank(bk)[:, 0:342], lhsT=vC[:, kc, :], rhs=aT[:, kc, n0:n0 + 342],
                                                                    start=(kc == 0), stop=(kc == 15)),
                         reads=aT_range_keys(n0, n0 + 342) + rg_keys("C"), writes=[("ps", bk)])
                P.op("dve", lambda e, g=g, bk=bk: e.tensor_copy(out=fqT[:, 342 * g:342 * g + 342], in_=bank(bk)[:, 0:342]), reads=[("ps", bk)], writes=[("fqT", g)])

            if h + 1 < NHEADS_RUN:
                load_head(h + 1)
            else:
                wq[0] = load_wout_quarter(0, all_rg)
                wq[1] = load_wout_quarter(1, all_rg)
            Sps = bank(7)[:, 0:128]
            ret_steps = []

            def r_init():
                for b in range(8):
                    P.op("pe", lambda e, b=b: e.matmul(Sps, lhsT=rkt[:, b, :], rhs=rvfv[:, b, 0:128], start=(b == 0), stop=(b == 7), skip_group_check=True),
                         reads=[("rkt", b), ("rvfv", b)], writes=[("ps", 7)])
            ret_steps.append(r_init)

            def r_a(c):
                sTp = bank(6)[:, (c % 2) * 128:(c % 2) * 128 + 128]
                if c > 0:
                    P.op("pe", lambda e: e.matmul(Sps, lhsT=rkt[:, 7 + c, :], rhs=rvfv[:, 7 + c, 0:128], start=False, stop=True, skip_group_check=True),
                         reads=[("rkt", 7 + c), ("rvfv", 7 + c)], writes=[("ps", 7)])
                P.op("dve", lambda e: e.tensor_copy(out=Sbf[c % 2], in_=Sps), reads=[("ps", 7)], writes=[("Sbf", c % 2)])
                P.op("pe", lambda e: e.matmul(sTp, lhsT=rkT[:, c * 128:(c + 1) * 128], rhs=rqT[:, c * 128:(c + 1) * 128], start=True, stop=True),
                     reads=["rkT", "rqT"], writes=[("ps", 6)])
                P.op("dve", lambda e: e.tensor_tensor(out=smt[c % 2], in0=sTp, in1=maskR[:], op=ALU.mult),
                     reads=[("ps", 6), "maskR"], writes=[("smt", c % 2)])

            def r_b(c):
                op_ = bank(4)[:, (c % 2) * 128:(c % 2) * 128 + 128]
                P.op("pe", lambda e: e.matmul(op_, lhsT=rqT[:, c * 128:(c + 1) * 128], rhs=Sbf[c % 2], start=True, stop=False),
                     reads=["rqT", ("Sbf", c % 2)], writes=[("ps", 4)])
                P.op("pe", lambda e: e.matmul(op_, lhsT=smt[c % 2], rhs=rvfv[:, 8 + c, 0:128], start=False, stop=True),
                     reads=[("smt", c % 2), ("rvfv", 8 + c)], writes=[("ps", 4)])
                P.op("dve", lambda e: e.tensor_copy(out=o_all[:, c, :], in_=op_), reads=[("ps", 4)], writes=[("o_all", c)])
                P.op("act", lambda e: e.activation(out=junkB, in_=op_, func=AF.Square, accum_out=st2[:, 16 + c:17 + c]),
                     reads=[("ps", 4)], writes=["junkB", ("st2q", c)])
            for c in range(9):
                ret_steps.append(lambda c=c: r_a(c))
                ret_steps.append(lambda c=c: r_b(c))

            def r_tail(h=h):
                oall_keys = [("o_all", c) for c in range(9)]
                sq_keys = [("st2q", c) for c in range(9)]
                mean = st2[:, 0:9]
                ssq = st2[:, 16:25]
                msq = st2[:, 32:41]
                rstd = st2[:, 48:57]
                P.op("dve", lambda e: e.reduce_sum(out=mean, in_=o_all[:, :, :], axis=AX.X), reads=oall_keys, writes=["st2m"])
                P.op("dve", lambda e: e.tensor_scalar_mul(out=mean, in0=mean, scalar1=1.0 / 128), reads=["st2m"], writes=["st2m"])
                P.op("dve", lambda e: e.tensor_tensor(out=msq, in0=mean, in1=mean, op=ALU.mult), reads=["st2m"], writes=["st2s"])
                P.op("dve", lambda e: e.tensor_scalar(out=rstd, in0=ssq, scalar1=1.0 / 128, scalar2=EPS, op0=ALU.mult, op1=ALU.add), reads=sq_keys, writes=["st2r"])
                P.op("dve", lambda e: e.tensor_tensor(out=rstd, in0=rstd, in1=msq, op=ALU.subtract), reads=["st2r", "st2s"], writes=["st2r"])
                P.op("act", lambda e: e.activation(out=rstd, in_=rstd, func=AF.Ln), reads=["st2r"], writes=["st2r"])
                P.op("act", lambda e: e.activation(out=rstd, in_=rstd, func=AF.Exp, scale=-0.5), reads=["st2r"], writes=["st2r"])
                ops_ = []
                for c in range(9):
                    ops_.append(lambda c=c: P.op("dve", lambda e: e.tensor_scalar(out=o_all[:, c, :], in0=o_all[:, c, :], scalar1=st2[:, c:c + 1], scalar2=st2[:, 48 + c:49 + c],
                                                                              op0=ALU.subtract, op1=ALU.mult),
                                                 reads=[("o_all", c), "st2m", "st2r"], writes=[("o_all", c)]))
                    ops_.append(lambda c=c: P.op("dve", lambda e: e.tensor_tensor(out=o_all[:, c, :], in0=o_all[:, c, :], in1=gb[:, h * 128:(h + 1) * 128], op=ALU.mult),
                                                 reads=[("o_all", c), "gb"], writes=[("o_all", c)]))
                    ops_.append(lambda c=c: P.op("dve", lambda e: e.tensor_tensor(out=ro[:, c, :], in0=o_all[:, c, :], in1=sg[:, c, :], op=ALU.mult),
                                                 reads=[("o_all", c), "sg"], writes=[("ro", c)]))
                return ops_

            def r_tail_pe(h=h):
                for c in range(9):
                    P.op("pe", lambda e, c=c: e.transpose(out=ppb[3][:, c * 128:(c + 1) * 128], in_=ro[:, c, :], identity=ident_b[:]),
                         reads=[("ro", c), "ident_b"], writes=[("ps", 6), ("ps", 7)])
                P.op("act", lambda e: e.copy(out=mixT[:, h, :], in_=ppb[3][:, 126:1152]), reads=[("ps", 6), ("ps", 7)], writes=[("mixh", h)])

            tiles = []
            for g in range(3):
                q0 = OWN0 + 342 * g
                kmax = (q0 + 342 - 1) // 128
                for kb in range(kmax + 1):
                    tiles.append((g, kb, kmax, q0))
            oTp = bank(2)[:, 0:342]
            dnp = bank(3)[:, 0:342]

            def f_s(ti, h=h):
                g, kb, kmax, q0 = tiles[ti]
                sbk = (0, 1, 5)[ti % 3]
                sb_ = bank(sbk)[:, 0:342]
                delta = 128 * kb - q0
                need_mask = (128 * kb + 127) > q0
                P.op("pe", lambda e: e.matmul(sb_, lhsT=fkT[:, kb * 128:(kb + 1) * 128], rhs=fqT[:, 342 * g:342 * g + 342], start=True, stop=False),
                     reads=[("fkT", kb // 4), ("fqT", g)], writes=[("ps", sbk)])
                P.op("pe", lambda e: e.matmul(sb_, lhsT=sel[:, h * 128:(h + 1) * 128], rhs=Rb[:, 126 + 342 * g:126 + 342 * g + 342],
                                              start=False, stop=(not need_mask)),
                     reads=["sel", "Rb"], writes=[("ps", sbk)])
                if need_mask:
                    off = XOFF - delta
                    assert 0 <= off and off + 342 <= MASKW, off
                    P.op("pe", lambda e: e.matmul(sb_, lhsT=ident_b[:], rhs=maskT[:, off:off + 342], start=False, stop=True),
                         reads=["ident_b", "maskT"], writes=[("ps", sbk)])
                pt = PTt[ti % 3]
                P.op("act", lambda e: e.activation(out=pt, in_=sb_, func=AF.Exp, bias=biasK[:, kb * 8 + h:kb * 8 + h + 1], scale=SCALE),
                     reads=[("ps", sbk), "biasK"], writes=[("PT", ti % 3)])

            def f_pv(ti, h=h):
                g, kb, kmax, q0 = tiles[ti]
                pt = PTt[ti % 3]
                ptk = ("PT", ti % 3)
                P.op("pe", lambda e: e.matmul(oTp, lhsT=rvfv[:, kb, 128:256], rhs=pt, start=(kb == 0), stop=(kb == kmax)),
                     reads=[("rvfv", kb), ptk], writes=[("ps", 2)])
                P.op("pe", lambda e: e.matmul(dnp, lhsT=ones_b[:], rhs=pt, start=(kb == 0), stop=(kb == kmax)),
                     reads=["ones_b", ptk], writes=[("ps", 3)])
                if kb == kmax:
                    P.op("dve", lambda e: e.reciprocal(out=rden, in_=dnp), reads=[("ps", 3)], writes=["rden"])
                    P.op("dve", lambda e: e.tensor_tensor(out=mixT[:, 8 + h, 342 * g:342 * g + 342], in0=oTp, in1=rden, op=ALU.mult),
                         reads=[("ps", 2), "rden"], writes=[("mixf", h, g)])

            fox_steps = []
            nt_ = len(tiles)
            def f_first():
                f_s(0)
                f_s(1)
            fox_steps.append(f_first)
            for ti in range(nt_):
                def st(ti=ti):
                    if ti + 2 < nt_:
                        f_s(ti + 2)
                    f_pv(ti)
                fox_steps.append(st)
            fi = 0
            last_head = (h == NHEADS_RUN - 1)
            per = [1, 1] if last_head else [3, 2]
            for ri, rs in enumerate(ret_steps):
                rs()
                k = 1 if ri == 0 else per[ri % 2]
                for _ in range(k):
                    if fi < len(fox_steps):
                        fox_steps[fi]()
                        fi += 1
            if last_head:
                for f_ in r_tail():
                    f_()
            while fi < len(fox_steps):
                fox_steps[fi]()
                fi += 1
            if not last_head:
                pending_tail_ops[:] = r_tail()
            deferred_tail[0] = r_tail_pe
        deferred_tail[0]()


        mix_all = [("mixh", h) for h in range(NH)] + [("mixf", h, g) for h in range(NH) for g in range(3)]
        if DEBUG:
            dma_sp(dbg["d_mix"], mixT[:, :, :].rearrange("p c t -> p (c t)"), reads=mix_all, stream="st")
        if STOP == "B":
            P.emit(final_wait_streams="st")
            return nc

        P.barrier()

        dma_sp(gb[:], g2.partition_broadcast(128), writes=["gb"], stream="gb")
        xsC = [R3f[:, 0:512], R3f[:, 512:1024], R3f[:, 1024:1536]]
        hnC = [R3[:, 4096:6144], R3[:, 6144:8192], R3[:, 10240:12288]]
        junkC = R3[:, 8192:10240]
        hh = R3f[:, 6144:8192]
        blocks = [(-1, 0, OWN0)] + [(tb, 2 + 128 * tb, 1152 + 128 * tb) for tb in range(8)]
        xi = 0
        ubk = 0

        def c_norm1(bi, tb):
            src = hh[:, :] if tb < 0 else h2[:, tb, :]
            col = 4 + bi % 3
            hk = [("h1", bi, q) for q in range(4)]
            c = rms_rstd(src, junkC, col, hk, "junkC")
            hn = hnC[bi % 3]
            P.op("dve", lambda e: e.scalar_tensor_tensor(out=hn, in0=src, scalar=c, in1=gb[:], op0=ALU.mult, op1=ALU.mult),
                 reads=hk + [("st1", col), "gb"], writes=[("hnC", bi % 3)])

        def c_norm2(bi, tb, c0):
            hn = hnC[bi % 3]
            pv = ppb[2 + bi % 2]
            pk = [("ps", 4 + 2 * (bi % 2)), ("ps", 5 + 2 * (bi % 2))]
            for cc in range(16):
                P.op("pe", lambda e, cc=cc: e.transpose(out=pv[:, cc * 128:(cc + 1) * 128], in_=hn[:, cc * 128:(cc + 1) * 128], identity=ident_b[:]),
                     reads=[("hnC", bi % 3), "ident_b"], writes=pk)
            pv3 = pv[:, 0:2048].rearrange("p (c t) -> p c t", c=16)
            if tb < 0:
                P.op("act", lambda e: e.copy(out=mixT[:, :, 0:2], in_=pv3[:, :, 0:2]), reads=pk + mix_all, writes=[("cT", bi)])
            else:
                P.op("act", lambda e: e.copy(out=mixT[:, 0:8, c0:c0 + 128], in_=pv3[:, 0:8, :]), reads=pk[0:1] + mix_all, writes=[("cT", bi)])
                P.op("dve", lambda e: e.tensor_copy(out=mixT[:, 8:16, c0:c0 + 128], in_=pv3[:, 8:16, :]), reads=pk[1:2] + mix_all, writes=[("cTb", bi)])

        for qp in range(4):
            slots = wq[qp]
            pend = []
            for bi, (tb, c0, l0) in enumerate(blocks):
                xs = xsC[xi % 3]
                xk = ("xs", xi % 3)
                xstream = "xs%d" % (xi % 3)
                xi += 1
                dma_sp(xs, xl[l0:l0 + 128, qp * 512:(qp + 1) * 512], writes=[xk], stream=xstream)
                bk = ubk % 4
                ubk += 1
                ck = [("cT", bi), ("cTb", bi)] + ([("cT", 1), ("cTb", 1)] if tb < 0 else [])
                for kc in range(16):
                    s_, v_ = slots[kc // 4]
                    P.op("pe", lambda e, kc=kc, v_=v_, c0=c0, bk=bk: e.matmul(
                        bank(bk), lhsT=mixT[:, kc, c0:c0 + 128], rhs=v_[:, kc % 4, :], start=(kc == 0), stop=(kc == 15)),
                        reads=mix_all + ck + rkeys(s_), writes=[("ps", bk)])
                dst = hh[:, qp * 512:(qp + 1) * 512] if tb < 0 else h2[:, tb, qp * 512:(qp + 1) * 512]
                P.op("dve", lambda e, dst=dst, bk=bk, xs=xs: e.tensor_tensor(out=dst, in0=bank(bk), in1=xs, op=ALU.add),
                     reads=[("ps", bk), xk], writes=[("h1", bi, qp)])
                if qp == 3:
                    c_norm1(bi, tb)
                    pend.append((bi, tb, c0))
                    if len(pend) > 2:
                        c_norm2(*pend.pop(0))
            if qp == 3:
                while pend:
                    c_norm2(*pend.pop(0))
            if qp + 2 < 4:
                wq[qp + 2] = load_wout_quarter(qp + 2)

        cT = mixT
        cT_all = [("cT", bi) for bi in range(9)] + [("cTb", bi) for bi in range(1, 9)]
        if DEBUG:
            dma_sp(dbg["d_h1"], R1f[:, 0:8 * D], reads=[("h1", bi, hp) for bi in range(1, 9) for hp in range(4)], stream="st")
        if DEBUG:
            dma_sp(dbg["d_cT"], mixT[:, :, :].rearrange("p c t -> p (c t)"), reads=cT_all, stream="st")
            dma_sp(dbg["d_hh"], hh[0:2, :], reads=[("h1", 0, q) for q in range(4)], stream="st")
        if STOP == "C":
            P.emit(final_wait_streams="st")
            return nc
        pre_up = [load_slice(w_up_v, half * DFF) for half in range(2)]
        P.barrier()

        gated = [[R3[:, (gs * GRP + jj) * 1024:(gs * GRP + jj + 1) * 1024] for jj in range(GRP)] for gs in range(2)]
        fb = 2 * GRP * 1024 // 2
        Yg = [R3f[:, fb + i * 1024:fb + (i + 1) * 1024] for i in range(2)]
        Yv = [R3f[:, fb + 2048 + i * 1024:fb + 2048 + (i + 1) * 1024] for i in range(2)]
        sb0 = 2 * (fb + 4096)
        Sg = [R3[:, sb0 + i * 1024:sb0 + (i + 1) * 1024] for i in range(2)]
        assert sb0 + 2048 <= R3N
        nblk = [(0, 342), (342, 683), (683, 1024)]
        ub = [0]

        h2_keys = lambda tb: [("h2", tb, n) for n in range(4)]
        mixflat = mixT[:, :, :].rearrange("p c t -> p (c t)")
        otE = [mixflat[:, 0:4096].bitcast(F32), mixflat[:, 4096:8192].bitcast(F32)]
        junkE = mixflat[:, 8192:10240]

        def final_block(tb):
            col = 8 + tb % 2
            c = rms_rstd(h2[:, tb, :], junkE, col, h2_keys(tb), "junkE", extra_writes=cT_all)
            ot = otE[tb % 2]
            if True:
                P.op("dve", lambda e: e.scalar_tensor_tensor(out=ot, in0=h2[:, tb, :], scalar=c, in1=gb[:], op0=ALU.mult, op1=ALU.mult),
                     reads=h2_keys(tb) + [("st1", col), "gb"], writes=[("ot", tb % 2)] + cT_all)
            else:
                P.op("act", lambda e: e.activation(out=ot, in_=h2[:, tb, :], func=AF.Copy, scale=c),
                     reads=h2_keys(tb) + [("st1", col)], writes=[("ot", tb % 2)] + cT_all)
                P.op("pool", lambda e: e.tensor_tensor(out=ot, in0=ot, in1=gb[:], op=ALU.mult),
                     reads=[("ot", tb % 2), "gb"], writes=[("ot", tb % 2)])
            dma_sp(y[tb * 128:(tb + 1) * 128, :], ot, reads=[("ot", tb % 2)], stream="st%d" % (tb % 2))

        def wdown_group(gi, dslots, last=False):
            gs = gi % 2
            for tb in range(8):
                b0 = 4 if tb % 2 == 0 else 0
                for jj in range(GRP):
                    s, v = dslots[jj]
                    for n in range(4):
                        bk = b0 + n
                        P.op("pe", lambda e, jj=jj, v=v, tb=tb, n=n, bk=bk, gs=gs: e.matmul(
                            bank(bk), lhsT=gated[gs][jj][:, tb * 128:(tb + 1) * 128], rhs=v[:, n * 512:(n + 1) * 512],
                            start=(jj == 0), stop=(jj == GRP - 1)),
                            reads=[("gated", gs, jj)] + rkeys(s), writes=[("ps", bk)])
                for n in range(4):
                    bk = b0 + n
                    P.op("dve", lambda e, tb=tb, n=n, bk=bk: e.tensor_tensor(out=h2[:, tb, n * 512:(n + 1) * 512], in0=h2[:, tb, n * 512:(n + 1) * 512], in1=bank(bk), op=ALU.add),
                         reads=[("ps", bk), ("h2", tb, n)], writes=[("h2", tb, n)])
                if last:
                    final_block(tb)

        def load_down(gi):
            dslots = []
            for jj in range(GRP):
                j = gi * GRP + jj
                s = next_slot()
                dma_cast(ring[s][:, :], w_down[j * 128:(j + 1) * 128, :], writes=rkeys(s), stream="ring%d" % s)
                dslots.append((s, ring[s]))
            return dslots

        prev = None
        pair_i = 0
        for gi in range(NFF // GRP):
            gs = gi % 2
            for jj in range(GRP):
                j = gi * GRP + jj
                pi = pair_i % 2
                pair_i += 1
                for half in range(2):
                    cidx = half * NFF + j
                    s, v = pre_up[half] if j == 0 else load_slice(w_up_v, half * DFF + j * 128)
                    Y = (Yg if half == 0 else Yv)[pi]
                    yk = ("Y", half, pi)
                    for (r0, r1) in nblk:
                        ln = r1 - r0
                        bk = ub[0] % 4
                        ub[0] += 1
                        for kc in range(16):
                            P.op("pe", lambda e, kc=kc, v=v, r0=r0, ln=ln, bk=bk: e.matmul(bank(bk)[:, 0:ln + 2], lhsT=v[:, kc, :], rhs=cT[:, kc, r0:r0 + ln + 2],
                                                                                    start=(kc == 0), stop=(kc == 15)),
                                 reads=cT_all + rkeys(s), writes=[("ps", bk)])
                        u = bank(bk)
                        P.op("act", lambda e, u=u, Y=Y, r0=r0, r1=r1, ln=ln, cidx=cidx: e.activation(
                            out=Y[:, r0:r1], in_=u[:, 2:ln + 2], func=AF.Identity, bias=convb_s[:, cidx:cidx + 1], scale=convw_s[:, cidx * 3 + 2:cidx * 3 + 3]),
                            reads=[("ps", bk), "convw", "convb"], writes=[yk])
                        P.op("dve", lambda e, u=u, Y=Y, r0=r0, r1=r1, ln=ln, cidx=cidx: e.scalar_tensor_tensor(
                            out=Y[:, r0:r1], in0=u[:, 1:ln + 1], scalar=convw_s[:, cidx * 3 + 1:cidx * 3 + 2], in1=Y[:, r0:r1], op0=ALU.mult, op1=ALU.add),
                            reads=[("ps", bk), "convw", yk], writes=[yk])
                        P.op("dve", lambda e, u=u, Y=Y, r0=r0, r1=r1, ln=ln, cidx=cidx: e.scalar_tensor_tensor(
                            out=Y[:, r0:r1], in0=u[:, 0:ln], scalar=convw_s[:, cidx * 3:cidx * 3 + 1], in1=Y[:, r0:r1], op0=ALU.mult, op1=ALU.add),
                            reads=[("ps", bk), "convw", yk], writes=[yk])
                    if half == 0:
                        P.op("act", lambda e, Y=Y, pi=pi: e.activation(out=Sg[pi], in_=Y, func=AF.Silu), reads=[yk], writes=[("Sg", pi)])
                    else:
                        P.op("dve", lambda e, Y=Y, pi=pi, gs=gs, jj=jj: e.tensor_tensor(out=gated[gs][jj], in0=Y, in1=Sg[pi], op=ALU.mult),
                             reads=[yk, ("Sg", pi)], writes=[("gated", gs, jj)])
            if prev is not None:
                wdown_group(prev, load_down(prev))
            prev = gi
        dma_sp(gb[:], gf.partition_broadcast(128), writes=["gb"], stream="gb")
        wdown_group(prev, load_down(prev), last=True)

        P.emit(final_wait_streams="st")
    return nc


_NC_CACHE = {}


def _consts():
    c = {}
    c["c_ident"] = np.eye(128, dtype=np.float32)
    c["c_tri"] = np.triu(np.ones((128, 128), np.float32))
    c["c_ones"] = np.ones((128, 128), np.float32)
    c["c_maskR"] = np.triu(np.ones((128, 128), np.float32))
    p = np.arange(128)[:, None]
    xx = np.arange(MASKW)[None, :]
    c["c_maskT"] = np.where(xx - XOFF < p, NEG, 0.0).astype(np.float32)
    sel = np.zeros((128, 8, 128), np.float32)
    for h in range(8):
        sel[h, h, :] = 1.0
    c["c_sel"] = sel.reshape(128, 1024)
    return c


def _core_tables(T0):
    l = np.arange(LT)
    t = l - 1152 + T0
    valid = t >= 0
    inv_freq = 1.0 / (10000.0 ** (np.arange(0, 128, 2, dtype=np.float64) / 128.0))
    ang = np.where(valid, t, 0)[:, None].astype(np.float64) * inv_freq[None, :]
    def pm(a):
        n = a.shape[1]
        return np.ascontiguousarray(a.reshape(NB, 128, n).transpose(1, 0, 2).reshape(128, NB * n))
    tabs = {"cosT": pm(np.cos(ang).astype(np.float32)), "sinT": pm(np.sin(ang).astype(np.float32))}
    log_g = np.log1p(-np.exp2(-5.0 - np.arange(8, dtype=np.float64)))
    rel = (l - 1152).astype(np.float64)
    tabs["dqT"] = pm(np.exp(rel[:, None] * log_g[None, :]).astype(np.float32))
    tabs["dkT"] = pm((np.exp(-rel[:, None] * log_g[None, :]) * SCALE).astype(np.float32))
    tabs["kbT"] = pm(np.repeat(np.where(valid, 0.0, NEG).astype(np.float32)[:, None], 8, axis=1))
    return tabs


def kernel(x, meta_tokens, norm1_gain, w_in, b_forget, ret_norm_gain, w_out, norm2_gain, w_up,
           conv_w, conv_b, w_down, final_norm_gain):
    f32 = np.float32
    x = np.asarray(x, f32)
    B = x.shape[0]
    if "nc" not in _NC_CACHE:
        _NC_CACHE["nc"] = build_nc()
    nc = _NC_CACHE["nc"]
    consts = _consts()
    shared = {
        "w_in": np.ascontiguousarray(np.asarray(w_in, f32)[0]),
        "w_out": np.ascontiguousarray(np.asarray(w_out, f32)[0]),
        "w_up": np.ascontiguousarray(np.asarray(w_up, f32)[0]),
        "w_down": np.ascontiguousarray(np.asarray(w_down, f32)[0]),
        "g1": np.ascontiguousarray(np.asarray(norm1_gain, f32)[0]),
        "g2": np.ascontiguousarray(np.asarray(norm2_gain, f32)[0]),
        "gf": np.ascontiguousarray(np.asarray(final_norm_gain, f32)),
        "rng": np.ascontiguousarray(np.asarray(ret_norm_gain, f32)[0]),
        "wffd": np.ascontiguousarray(np.asarray(w_in, f32)[0][:, 7168:7176].reshape(16, 128, 8).transpose(1, 0, 2).reshape(128, 128)),
        "bfg": np.ascontiguousarray(np.tile(np.asarray(b_forget, f32)[0], NB)),
        "convw": np.ascontiguousarray(np.asarray(conv_w, f32)[0].reshape(3, 2 * NFF, 128).transpose(2, 1, 0).reshape(128, 2 * NFF * 3)),
        "convb": np.ascontiguousarray(np.asarray(conv_b, f32)[0].reshape(2 * NFF, 128).T),
    }
    shared.update(consts)
    meta = np.asarray(meta_tokens, f32)
    in_maps = []
    for core in range(8):
        b, s = core // 2, core % 2
        T0 = 16 + 1024 * s
        full = np.concatenate([meta, x[b]], axis=0)
        xl = np.zeros((LT, D), f32)
        t_lo = T0 - 1152
        src_lo = max(t_lo, 0)
        xl[src_lo - t_lo:, :] = full[src_lo:T0 + 1024]
        m = dict(shared)
        m["xl"] = xl
        m.update(_core_tables(T0))
        in_maps.append(m)
    res = run_bass_kernel_spmd(nc, in_maps[:NCORES_RUN], core_ids=list(range(NCORES_RUN)))
    out = np.zeros((B, 2048, D), f32)
    for core in range(NCORES_RUN):
        b, s = core // 2, core % 2
        out[b, 1024 * s:1024 * (s + 1), :] = res.results[core]["y"]
    if DEBUG:
        kernel.debug = res.results
    return out
```
